# Optimizing a Trainium2 kernel written in Bass

```python
import math
import jax, jax.numpy as jnp
from jax import lax
import numpy as np

D_MODEL = 2048
BATCH = 2
SEQ = 8192
DEPTH = 4

D_MIX = D_MODEL
D_MLSTM = D_MIX // 2
D_HGRN = D_MIX - D_MLSTM
M_HEADS = 4
M_DV = D_MLSTM // M_HEADS
M_DQK = M_DV // 2
H_HEADS = 8
H_DH = D_HGRN // H_HEADS
CONV_W = 4
CHUNK = 64
D_FF = int(math.ceil(8 * D_MODEL / 3 / 256)) * 256
N_MOD = 6
EPS = 1e-5
DEEPNORM_ALPHA = (2 * DEPTH) ** 0.25
DEEPNORM_BETA = (8 * DEPTH) ** -0.25

IN_COLS = (2 * M_HEADS * M_DQK,
           D_MLSTM,
           D_MLSTM,
           M_HEADS,
           M_HEADS,
           D_HGRN,
           D_HGRN,
           D_HGRN,
           D_HGRN)
D_IN = sum(IN_COLS)
SPLIT_IDX = tuple(int(s) for s in np.cumsum(IN_COLS)[:-1])

kernel_name = "hymba_mlstm_hgrn2_deepnorm_adaln"


def layer_norm(x, g, b):
    xf = x.astype(jnp.float32)
    mu = xf.mean(-1, keepdims=True)
    var = jnp.square(xf - mu).mean(-1, keepdims=True)
    return ((xf - mu) * lax.rsqrt(var + EPS)).astype(x.dtype) * g + b


def head_norm(h, w, center):
    hf = h.astype(jnp.float32)
    if center:
        hf = hf - hf.mean(-1, keepdims=True)
    hf = hf * lax.rsqrt(jnp.square(hf).mean(-1, keepdims=True) + EPS)
    return hf.reshape(h.shape[0], h.shape[1], -1) * w


def causal_dwconv(x, w, b):
    T = x.shape[1]
    xp = jnp.pad(x, ((0, 0), (CONV_W - 1, 0), (0, 0)))
    return sum(w[j] * xp[:, j:j + T] for j in range(CONV_W)) + b


def to_chunks(a):
    B, T = a.shape[:2]
    a = a.reshape((B, T // CHUNK, CHUNK) + a.shape[2:])
    return jnp.moveaxis(a, (1, 3), (0, 2))


def from_chunks(a):
    a = jnp.moveaxis(a, (0, 2), (1, 3))
    return a.reshape((a.shape[0], -1) + a.shape[3:])


def mlstm_chunkwise(q, k, v, ig, fg):
    f32 = jnp.float32
    B, T, H, dk = q.shape
    dv = v.shape[-1]
    q = q.astype(f32) * dk ** -0.5
    k = k.astype(f32)
    v = v.astype(f32)
    log_f = jax.nn.log_sigmoid(fg.astype(f32))
    ig = ig.astype(f32)
    causal = jnp.tril(jnp.ones((CHUNK, CHUNK), dtype=bool))

    def step(carry, xs):
        C_prev, n_prev, m_prev = carry
        qc, kc, vc, ic, lfc = xs
        b = jnp.cumsum(lfc, axis=-1)
        logD = jnp.where(causal, b[..., :, None] - b[..., None, :] + ic[..., None, :], -jnp.inf)
        m_inter = b + m_prev[..., None]
        m_t = jnp.maximum(m_inter, logD.max(-1))
        Dw = jnp.exp(logD - m_t[..., None])
        w_inter = jnp.exp(m_inter - m_t)
        S = jnp.einsum('bhtd,bhsd->bhts', qc, kc) * Dw
        num = jnp.einsum('bhts,bhsv->bhtv', S, vc) + w_inter[..., None] * jnp.einsum('bhtd,bhdv->bhtv', qc, C_prev)
        den = S.sum(-1) + w_inter * jnp.einsum('bhtd,bhd->bht', qc, n_prev)
        h = num / jnp.maximum(jnp.abs(den), jnp.exp(-m_t))[..., None]
        b_last = b[..., -1]
        log_w_end = b_last[..., None] - b + ic
        m_new = jnp.maximum(b_last + m_prev, log_w_end.max(-1))
        w_end = jnp.exp(log_w_end - m_new[..., None])
        decay = jnp.exp(b_last + m_prev - m_new)
        C_new = decay[..., None, None] * C_prev + jnp.einsum('bhs,bhsd,bhsv->bhdv', w_end, kc, vc)
        n_new = decay[..., None] * n_prev + jnp.einsum('bhs,bhsd->bhd', w_end, kc)
        return (C_new, n_new, m_new), h

    init = (jnp.zeros((B, H, dk, dv), f32), jnp.zeros((B, H, dk), f32), jnp.zeros((B, H), f32))
    xs = (to_chunks(q), to_chunks(k), to_chunks(v), to_chunks(ig), to_chunks(log_f))
    _, hs = lax.scan(step, init, xs)
    return from_chunks(hs)


def hgrn2_chunkwise(q, f_pre, i, lb):
    f32 = jnp.float32
    B, T, H, dh = q.shape
    q = jax.nn.silu(q.astype(f32))
    f_pre = f_pre.astype(f32)
    log_f = jnp.logaddexp(jnp.log(lb), jnp.log1p(-lb) + jax.nn.log_sigmoid(f_pre))
    k = (1.0 - lb) * jax.nn.sigmoid(-f_pre)
    v = i.astype(f32)
    causal = jnp.tril(jnp.ones((CHUNK, CHUNK), dtype=bool))[:, :, None]

    def step(S_prev, xs):
        qc, kc, vc, lfc = xs
        b = jnp.cumsum(lfc, axis=2)
        rel = jnp.where(causal, b[:, :, :, None, :] - b[:, :, None, :, :], -jnp.inf)
        A = jnp.einsum('bhtd,bhsd,bhtsd->bhts', qc, kc, jnp.exp(rel))
        o = jnp.einsum('bhts,bhsv->bhtv', A, vc) + jnp.einsum('bhtd,bhdv->bhtv', qc * jnp.exp(b), S_prev)
        b_last = b[:, :, -1:, :]
        S_new = jnp.exp(b_last[:, :, 0, :])[..., None] * S_prev + jnp.einsum('bhsd,bhsv->bhdv', kc * jnp.exp(b_last - b), vc)
        return S_new, o

    init = jnp.zeros((B, H, dh, dh), f32)
    xs = (to_chunks(q), to_chunks(k), to_chunks(v), to_chunks(log_f))
    _, os_ = lax.scan(step, init, xs)
    return from_chunks(os_)


def token_mixer(u, w_in, conv_w, conv_b, b_ig, b_fg, mnorm_w, hnorm_w, lb, w_out):
    B, T, _ = u.shape
    proj = u @ w_in
    qk_m, v_m, o_m, ig, fg, q_h, f_h, i_h, g_h = jnp.split(proj, SPLIT_IDX, axis=-1)
    qk_m = jax.nn.silu(causal_dwconv(qk_m, conv_w, conv_b))
    q_m, k_m = jnp.split(qk_m, 2, axis=-1)
    h_m = mlstm_chunkwise(q_m.reshape(B, T, M_HEADS, M_DQK), k_m.reshape(B, T, M_HEADS, M_DQK),
                          v_m.reshape(B, T, M_HEADS, M_DV), ig + b_ig, fg + b_fg)
    y_m = head_norm(h_m, mnorm_w, True) * jax.nn.sigmoid(o_m.astype(jnp.float32))
    shp = (B, T, H_HEADS, H_DH)
    h_h = hgrn2_chunkwise(q_h.reshape(shp), f_h.reshape(shp), i_h.reshape(shp), lb.reshape(H_HEADS, H_DH))
    y_h = head_norm(h_h, hnorm_w, False) * jax.nn.silu(g_h.astype(jnp.float32))
    y = jnp.concatenate([y_m, y_h], axis=-1).astype(u.dtype)
    return y @ w_out


def swiglu(u, w_gate, w_up, w_down):
    return (jax.nn.silu(u @ w_gate) * (u @ w_up)) @ w_down


def setup_inputs(seed: int = 0) -> dict:
    key = jax.random.key(seed)
    ks = jax.random.split(key, 24)
    n = jax.random.normal
    D = D_MODEL
    return {
        "x": n(ks[0], (BATCH, SEQ, D), jnp.float32),
        "c": n(ks[1], (BATCH, D), jnp.float32),
        "w_mod": n(ks[2], (DEPTH, D, N_MOD * D), jnp.float32) * (0.5 * D ** -0.5),
        "b_mod": 0.01 * n(ks[3], (DEPTH, N_MOD * D), jnp.float32),
        "w_in": n(ks[4], (DEPTH, D, D_IN), jnp.float32) * D ** -0.5,
        "conv_w": n(ks[5], (DEPTH, CONV_W, 2 * M_HEADS * M_DQK), jnp.float32) * CONV_W ** -0.5,
        "conv_b": 0.01 * n(ks[6], (DEPTH, 2 * M_HEADS * M_DQK), jnp.float32),
        "b_igate": 0.1 * n(ks[7], (DEPTH, M_HEADS), jnp.float32),
        "b_fgate": jnp.linspace(3.0, 6.0, M_HEADS, dtype=jnp.float32) + 0.1 * n(ks[8], (DEPTH, M_HEADS), jnp.float32),
        "mlstm_norm_w": 1.0 + 0.02 * n(ks[9], (DEPTH, D_MLSTM), jnp.float32),
        "hgrn_norm_w": 1.0 + 0.02 * n(ks[10], (DEPTH, D_HGRN), jnp.float32),
        "lb_logits": 0.1 * n(ks[11], (DEPTH, D_HGRN), jnp.float32),
        "w_out": n(ks[12], (DEPTH, D_MIX, D), jnp.float32) * (D_MIX ** -0.5 * DEEPNORM_BETA),
        "ln1_g": 1.0 + 0.02 * n(ks[13], (DEPTH, D), jnp.float32),
        "ln1_b": 0.01 * n(ks[14], (DEPTH, D), jnp.float32),
        "w_gate": n(ks[15], (DEPTH, D, D_FF), jnp.float32) * D ** -0.5,
        "w_up": n(ks[16], (DEPTH, D, D_FF), jnp.float32) * D ** -0.5,
        "w_down": n(ks[17], (DEPTH, D_FF, D), jnp.float32) * (D_FF ** -0.5 * DEEPNORM_BETA),
        "ln2_g": 1.0 + 0.02 * n(ks[18], (DEPTH, D), jnp.float32),
        "ln2_b": 0.01 * n(ks[19], (DEPTH, D), jnp.float32),
    }


def reference(x, c, w_mod, b_mod, w_in, conv_w, conv_b, b_igate, b_fgate, mlstm_norm_w, hgrn_norm_w,
              lb_logits, w_out, ln1_g, ln1_b, w_gate, w_up, w_down, ln2_g, ln2_b):
    p = jax.nn.softmax(lb_logits.astype(jnp.float32), axis=0)
    cs = jnp.cumsum(p, axis=0)
    lb_all = cs - cs[0:1]
    c_act = jax.nn.silu(c)
    for l in range(DEPTH):
        mod = c_act @ w_mod[l] + b_mod[l]
        sh1, sc1, g1, sh2, sc2, g2 = jnp.split(mod[:, None, :], N_MOD, axis=-1)
        u = x * (1.0 + sc1) + sh1
        y = token_mixer(u, w_in[l], conv_w[l], conv_b[l], b_igate[l], b_fgate[l],
                        mlstm_norm_w[l], hgrn_norm_w[l], lb_all[l], w_out[l])
        x = layer_norm(DEEPNORM_ALPHA * x + (1.0 + g1) * y, ln1_g[l], ln1_b[l])
        u = x * (1.0 + sc2) + sh2
        y = swiglu(u, w_gate[l], w_up[l], w_down[l])
        x = layer_norm(DEEPNORM_ALPHA * x + (1.0 + g2) * y, ln2_g[l], ln2_b[l])
    return x
```

```python
import contextlib
import math
import numpy as np
import ml_dtypes
import concourse.bass as bass
import concourse.mybir as mybir
from concourse.bass_utils import run_bass_kernel_spmd

F32 = mybir.dt.float32
BF16 = mybir.dt.bfloat16
U8 = mybir.dt.uint8
AF = mybir.ActivationFunctionType
ALU = mybir.AluOpType
AX = mybir.AxisListType

D = 2048
DFF = 5632
KC = D // 128
FC = DFF // 128
TT = 512
EPS = 1e-5
DEPTH = 4
NCORE = 8
ALPHA = (2 * DEPTH) ** 0.25
WSLOT = 8192
NWS = 3
MW = 260
NST = 2304
NPC = NST // 256
LNSC = math.log(128.0 ** -0.5)

NO_CC = False
ENGS = ("pe", "act", "dve", "pool", "sp")
SEM_EPOCH = 30000
NDS = 20


class _Op:
    __slots__ = ("eng", "fn", "waits", "is_dma", "needed", "sem", "val")

    def __init__(self, eng, fn, is_dma):
        self.eng = eng
        self.fn = fn
        self.is_dma = is_dma
        self.waits = []
        self.needed = False
        self.sem = None
        self.val = None


class Sched:
    def __init__(self, nc):
        self.nc = nc
        self.q = {e: [] for e in ENGS}
        self.last_w = {}
        self.readers = {}
        self.all_ops = []

    def op(self, eng, fn, reads=(), writes=(), dma=False, nofence=False):
        o = _Op(eng, fn, dma)
        deps = []
        seen = set()
        if not nofence:
            reads = list(reads) + ["__fence__"]

        def add(d):
            if d is None or id(d) in seen:
                return
            seen.add(id(d))
            deps.append(d)
        for k in reads:
            add(self.last_w.get(k))
        for k in writes:
            add(self.last_w.get(k))
            for r in self.readers.get(k, ()):
                add(r)
        for d in deps:
            if d.eng == "pe" and eng == "pe" and not d.is_dma and not dma:
                continue
            d.needed = True
            o.waits.append(d)
        for k in reads:
            self.readers.setdefault(k, []).append(o)
        for k in writes:
            self.last_w[k] = o
            self.readers[k] = []
        self.q[eng].append(o)
        self.all_ops.append(o)
        return o

    def fence(self, scratch):
        self.op("dve", lambda e: e.memset(scratch, 0.0), writes=["__fence__"], nofence=True)

    def barrier_all(self, eng="sp"):
        o = _Op(eng, None, False)
        for d in self.all_ops:
            if d.is_dma:
                d.needed = True
                o.waits.append(d)
        self.q[eng].append(o)
        self.all_ops.append(o)

    def run(self):
        nc = self.nc
        cnt = {e: 0 for e in ENGS}
        di = {e: 0 for e in ENGS}
        dlast = {}
        dcount = {}
        for o in self.all_ops:
            if not o.needed:
                continue
            e = o.eng
            if o.is_dma == "cc":
                o.sem, o.val = f"cc_{di[e]}", 1
                di[e] += 1
            elif o.is_dma:
                name = f"d_{e}_{di[e] % NDS}"
                di[e] += 1
                prev = dlast.get(name)
                if prev is not None and prev not in o.waits:
                    o.waits.append(prev)
                dcount[name] = dcount.get(name, 0) + 16
                o.sem, o.val = name, dcount[name]
                dlast[name] = o
            else:
                n = cnt[e]
                o.sem = f"c_{e}_{n // SEM_EPOCH}"
                o.val = n % SEM_EPOCH + 1
                cnt[e] = n + 1
        names = sorted({o.sem for o in self.all_ops if o.sem is not None})
        for o in self.all_ops:
            best = {}
            for d in o.waits:
                if d.sem not in best or d.val > best[d.sem].val:
                    best[d.sem] = d
            o.waits = list(best.values())
        with contextlib.ExitStack() as es:
            S = {n: es.enter_context(nc.semaphore(n)) for n in names}
            block = es.enter_context(nc.Block())

            def body(eng_name):
                def _f(eng):
                    known = {}
                    for o in self.q[eng_name]:
                        for d in o.waits:
                            if known.get(d.sem, 0) >= d.val:
                                continue
                            eng.wait_ge(S[d.sem], d.val)
                            known[d.sem] = d.val
                        if o.fn is None:
                            continue
                        ins = o.fn(eng)
                        if o.needed:
                            ins.then_inc(S[o.sem], 16 if (o.is_dma and o.is_dma != "cc") else 1)
                return _f
            block.tensor(body("pe"))
            block.scalar(body("act"))
            block.vector(body("dve"))
            block.gpsimd(body("pool"))
            block.sync(body("sp"))
        return len(names)


class WStream:
    def __init__(self, S, buf, record=None):
        self.S = S
        self.buf = buf
        self.seq = [] if record is None else record
        self.recording = record is None
        self.k = 0
        self.issued = 0

    def _issue(self, j):
        ap, n = self.seq[j]
        slot = j % NWS
        dst = self.buf[:, slot * WSLOT: slot * WSLOT + n]
        self.S.op("pool", lambda e, dst=dst, ap=ap: e.dma_start(out=dst, in_=ap),
                  writes=[("w", slot)], dma=True, nofence=True)

    def get(self, ap, n):
        if self.recording:
            self.seq.append((ap, n))
            return self.buf[:, 0:n], ("w", 0)
        k = self.k
        while self.issued < min(len(self.seq), k + NWS):
            self._issue(self.issued)
            self.issued += 1
        self.k += 1
        slot = k % NWS
        return self.buf[:, slot * WSLOT: slot * WSLOT + n], ("w", slot)


def _bank(ps, b):
    return ps[:, b * 512:(b + 1) * 512]


def emit_transpose_affine(S, T, src_key, gcol, bcol, extra_reads=()):
    X, aT, ps, ident, vf = T["X"], T["aT"], T["ps"], T["ident"], T["vf"]
    for kc in range(KC):
        bank = 4 + (kc % 2)
        pst = _bank(ps, bank)
        for m in range(4):
            S.op("pe", lambda e, pst=pst, m=m, kc=kc: e.transpose(
                pst[:, m * 128:(m + 1) * 128], X[:, m, kc * 128:(kc + 1) * 128], ident[:]),
                reads=[src_key, "ident"], writes=[("ps", bank)])
        S.op("act", lambda e, pst=pst, kc=kc: e.activation(
            out=aT[:, kc, :], in_=pst, func=AF.Identity,
            scale=vf[:, gcol + kc:gcol + kc + 1], bias=vf[:, bcol + kc:bcol + kc + 1]),
            reads=[("ps", bank), "vf", *extra_reads], writes=[("aT", kc)])


def emit_resid_prep(S, T, gi, bi):
    X, bc = T["X"], T["bc"]
    for m in range(4):
        S.op("pool", lambda e, m=m: e.tensor_tensor(out=X[:, m, :], in0=X[:, m, :], in1=bc[:, gi, :], op=ALU.mult),
             reads=[("bc", gi)], writes=["X"])
        S.op("pool", lambda e, m=m: e.tensor_tensor(out=X[:, m, :], in0=X[:, m, :], in1=bc[:, bi, :], op=ALU.add),
             reads=[("bc", bi)], writes=["X"])


def emit_epilogue_block(S, T, m, n, gy):
    X, bc, ps, tmp = T["X"], T["bc"], T["ps"], T["tmp"]
    pm = _bank(ps, m)
    tq = tmp[:, (m % 2) * 512:(m % 2 + 1) * 512]
    S.op("dve", lambda e: e.tensor_tensor(out=tq, in0=pm, in1=bc[:, gy, n * 512:(n + 1) * 512], op=ALU.mult),
         reads=[("ps", m), ("bc", gy)], writes=[("tmp", m % 2)])
    S.op("pool", lambda e: e.tensor_tensor(out=X[:, m, n * 512:(n + 1) * 512],
                                           in0=X[:, m, n * 512:(n + 1) * 512], in1=tq, op=ALU.add),
         reads=[("tmp", m % 2)], writes=["X"])


def emit_ln_inplace(S, T):
    X, st, mv, epsb = T["X"], T["st"], T["mv"], T["epsb"]
    for m in range(4):
        for i in range(4):
            S.op("dve", lambda e, m=m, i=i: e.bn_stats(out=st[:, m, i, :], in_=X[:, m, i * 512:(i + 1) * 512]),
                 reads=["X"], writes=[("st", m)])
        S.op("dve", lambda e, m=m: e.bn_aggr(out=mv[:, m, 0:2], in_=st[:, m, :, :].rearrange("p a b -> p (a b)")),
             reads=[("st", m)], writes=[("mv", m)])
        S.op("act", lambda e, m=m: e.activation(out=mv[:, m, 2:3], in_=mv[:, m, 1:2], func=AF.Sqrt, bias=epsb[:, 0:1]),
             reads=[("mv", m), "epsb"], writes=[("mv", m)])
        S.op("dve", lambda e, m=m: e.reciprocal(out=mv[:, m, 3:4], in_=mv[:, m, 2:3]),
             reads=[("mv", m)], writes=[("mv", m)])
        S.op("dve", lambda e, m=m: e.scalar_tensor_tensor(
            out=mv[:, m, 4:5], in0=mv[:, m, 0:1], scalar=-1.0, in1=mv[:, m, 3:4], op0=ALU.mult, op1=ALU.mult),
            reads=[("mv", m)], writes=[("mv", m)])
        S.op("act", lambda e, m=m: e.activation(out=X[:, m, :], in_=X[:, m, :], func=AF.Identity,
                                                scale=mv[:, m, 3:4], bias=mv[:, m, 4:5]),
             reads=[("mv", m)], writes=["X"])


def emit_ffn(S, ws, T, wgu, wd, gy):
    aT, hT, ps, sg = T["aT"], T["hT"], T["ps"], T["sg"]
    for j2 in range(FC // 2):
        wt, wkey = ws.get(wgu[j2], 2 * 2 * KC * 128)
        wv = wt.rearrange("p (b g k c) -> p b g k c", b=2, g=2, k=KC)
        for b in range(2):
            j = j2 * 2 + b
            bg = 4 + (j % 2) * 2
            bu = bg + 1
            pg, pu = _bank(ps, bg), _bank(ps, bu)
            for g, pp, bk in ((0, pg, bg), (1, pu, bu)):
                for kc in range(KC):
                    S.op("pe", lambda e, pp=pp, wv=wv, b=b, g=g, kc=kc: e.matmul(
                        pp, wv[:, b, g, kc, :], aT[:, kc, :], start=(kc == 0), stop=(kc == KC - 1)),
                        reads=[wkey, ("aT", kc)], writes=[("ps", bk)])
            sgt = sg[:, (j % 2) * 512:(j % 2 + 1) * 512]
            S.op("act", lambda e, sgt=sgt, pg=pg: e.activation(out=sgt, in_=pg, func=AF.Silu),
                 reads=[("ps", bg)], writes=[("sg", j % 2)])
            S.op("dve", lambda e, sgt=sgt, pu=pu, j=j: e.tensor_tensor(out=hT[:, j, :], in0=pu, in1=sgt, op=ALU.mult),
                 reads=[("ps", bu), ("sg", j % 2)], writes=[("hT", j)])
    emit_resid_prep(S, T, 3, 4)
    for n in range(4):
        for jg in range(4):
            wt, wkey = ws.get(wd[n, jg], 11 * 512)
            wv = wt.rearrange("p (j c) -> p j c", j=11)
            for m in range(4):
                pm = _bank(ps, m)
                for jj in range(11):
                    j = jg * 11 + jj
                    S.op("pe", lambda e, pm=pm, wv=wv, jj=jj, j=j, m=m: e.matmul(
                        pm, hT[:, j, m * 128:(m + 1) * 128], wv[:, jj, :], start=(j == 0), stop=(j == FC - 1)),
                        reads=[wkey, ("hT", j)], writes=[("ps", m)])
        for m in range(4):
            emit_epilogue_block(S, T, m, n, gy)
    emit_ln_inplace(S, T)


V_BMOD = 0
V_LNPG = 96
V_LNPB = 112
V_LN1G = 128
V_LN1B = 144
V_CONVW = 160
V_CONVB = 192
V_LBL = 200
V_NRMW = 232
V_IN = 248
V_MOD = 256
V_GU1 = 352
V_BU1 = 368
V_GU2 = 384
V_BU2 = 400
V_ROWS = 416
V_LB = 512
V_OML = 520
V_TOT = 544


def _emit_prologue(S, T, g):
    ident, identb, tri, utri, ones, m0 = g["ident"], g["identb"], g["tri"], g["utri"], g["ones"], g["m0"]

    def dma_in(dst, src, key, eng="sp"):
        S.op(eng, lambda e: e.dma_start(out=dst, in_=src), writes=[key], dma=True)
    dma_in(ident[:], g["identd"], "ident")
    dma_in(identb[:], g["identd"], "identb", eng="pool")
    dma_in(tri[:], g["trid"], "tri")
    dma_in(g["cact"][:], g["c_fm"], "cact")
    dma_in(g["hfl"][:], g["hflag"], "hfl")
    dma_in(g["selt"][:], g["sel"], "selt")
    dma_in(g["selp"][:], g["selprev"], "selp")
    S.op("dve", lambda e: e.tensor_copy(out=utri[:], in_=tri[:]), reads=["tri"], writes=["utri"])
    S.op("dve", lambda e: e.memset(ones[:], 1.0), writes=["ones"])
    S.op("dve", lambda e: e.memset(g["lnsc"][:], LNSC), writes=["lnsc"])
    S.op("dve", lambda e: e.memset(g["epsb"][:], EPS), writes=["epsb"])
    S.op("dve", lambda e: e.memset(m0[:], 1.0), writes=["m0"])
    S.op("dve", lambda e: e.memset(m0[:].rearrange("p (c t) -> p c t", t=64)[:, :, 0:1], 0.0), writes=["m0"])
    S.op("dve", lambda e: e.memset(g["A_sb"][:], 0.0), writes=[("A_sb", 0), ("A_sb", 1)])
    S.op("dve", lambda e: e.memset(g["A_h"][:], 0.0), writes=[("A_h", 0), ("A_h", 1)])
    S.op("act", lambda e: e.activation(out=g["cactb"][:], in_=g["cact"][:], func=AF.Silu), reads=["cact"], writes=["cactb"])


def _emit_p1(S, ws, nc, T, g, NT, l):
    X, aT, ps, vf = T["X"], T["aT"], T["ps"], T["vf"]
    ident, identb, tri, utri, ones, m0 = g["ident"], g["identb"], g["tri"], g["utri"], g["ones"], g["m0"]

    def dma_in(dst, src, key, eng="sp"):
        S.op(eng, lambda e: e.dma_start(out=dst, in_=src), writes=[key], dma=True)
    dma_in(vf[:, 0:V_IN], g["vecs"], "vf")
    dma_in(g["gbias"][:], g["grow"].partition_broadcast(128), "gbias")
    dma_in(g["wifg"][:], g["w1ifg"], "wifg", eng="pool")
    S.op("dve", lambda e: e.memset(g["C32"][:], 0.0), writes=[("C32", h) for h in range(4)])
    S.op("dve", lambda e: e.memset(g["Cbf"][:], 0.0), writes=[("Cbf", h) for h in range(4)])
    S.op("dve", lambda e: e.memset(g["S32"][:], 0.0), writes=[("S32", h) for h in range(8)])
    S.op("dve", lambda e: e.memset(g["carry"][:], 0.0), writes=["carry"])
    S.op("dve", lambda e: e.memset(g["hcarry"][:], 0.0), writes=["hcarry"])
    S.op("dve", lambda e: e.memset(g["vaug"][:], 0.0), writes=[("vaug", m) for m in range(4)])
    cact, cactb = g["cact"], g["cactb"]
    psm = _bank(ps, 7)
    for t in range(24):
        wt, wkey = ws.get(g["wmod"][t], WSLOT)
        wv = wt.rearrange("p (b k c) -> p b k c", b=4, k=KC)
        for b in range(4):
            j = t * 4 + b
            for kc in range(KC):
                S.op("pe", lambda e, wv=wv, b=b, kc=kc, j=j: e.matmul(
                    psm[:, j:j + 1], wv[:, b, kc, :], cactb[:, kc:kc + 1], start=(kc == 0), stop=(kc == KC - 1)),
                    reads=[wkey, "cactb"], writes=[("ps", 7)])
    S.op("dve", lambda e: e.tensor_tensor(out=vf[:, V_MOD:V_MOD + 96], in0=psm[:, 0:96], in1=vf[:, V_BMOD:V_BMOD + 96], op=ALU.add),
         reads=[("ps", 7), "vf"], writes=["vf"])

    def vop(fn):
        S.op("dve", fn, writes=["vf"])
    SH1, SC1, G1, SH2, SC2, G2 = (V_MOD + 16 * i for i in range(6))

    def c(off):
        return vf[:, off:off + 16]
    for (lg, lb_, sc, sh, gu, bu) in ((V_LNPG, V_LNPB, SC1, SH1, V_GU1, V_BU1), (V_LN1G, V_LN1B, SC2, SH2, V_GU2, V_BU2)):
        vop(lambda e, lg=lg, sc=sc, gu=gu: e.scalar_tensor_tensor(out=c(gu), in0=c(sc), scalar=1.0, in1=c(lg), op0=ALU.add, op1=ALU.mult))
        vop(lambda e, lb_=lb_, sc=sc, bu=bu: e.scalar_tensor_tensor(out=c(bu), in0=c(sc), scalar=1.0, in1=c(lb_), op0=ALU.add, op1=ALU.mult))
        vop(lambda e, sh=sh, bu=bu: e.tensor_tensor(out=c(bu), in0=c(bu), in1=c(sh), op=ALU.add))
    for i, (lg, lb_, gg) in enumerate(((V_LNPG, V_LNPB, G1), (V_LN1G, V_LN1B, G2))):
        r0 = V_ROWS + 48 * i
        vop(lambda e, lg=lg, r0=r0: e.tensor_scalar(out=c(r0), in0=c(lg), scalar1=ALPHA, scalar2=None, op0=ALU.mult))
        vop(lambda e, lb_=lb_, r0=r0: e.tensor_scalar(out=c(r0 + 16), in0=c(lb_), scalar1=ALPHA, scalar2=None, op0=ALU.mult))
        vop(lambda e, gg=gg, r0=r0: e.tensor_scalar(out=c(r0 + 32), in0=c(gg), scalar1=1.0, scalar2=None, op0=ALU.add))
    lbl = vf[:, V_LBL:V_LBL + 8 * DEPTH].rearrange("p (h l) -> p h l", l=DEPTH)
    sm = g["hsc"]
    vop(lambda e: e.tensor_reduce(out=sm[:, :, 0], in_=lbl, axis=AX.X, op=ALU.max))
    vop(lambda e: e.tensor_tensor(out=lbl, in0=lbl, in1=sm[:, :, 0:1].broadcast_to([128, 8, DEPTH]), op=ALU.subtract))
    S.op("act", lambda e: e.activation(out=lbl, in_=lbl, func=AF.Exp), reads=["vf"], writes=["vf"])
    vop(lambda e: e.tensor_reduce(out=sm[:, :, 0], in_=lbl, axis=AX.X, op=ALU.add))
    vop(lambda e: e.reciprocal(out=sm[:, :, 1], in_=sm[:, :, 0]))
    S.op("dve", lambda e: e.tensor_tensor(out=lbl, in0=lbl, in1=g["gbias"][:, 8:8 + DEPTH].unsqueeze(1).broadcast_to([128, 8, DEPTH]), op=ALU.mult),
         reads=["gbias"], writes=["vf"])
    vop(lambda e: e.tensor_reduce(out=sm[:, :, 0], in_=lbl, axis=AX.X, op=ALU.add))
    vop(lambda e: e.tensor_tensor(out=vf[:, V_LB:V_LB + 8], in0=sm[:, :, 0], in1=sm[:, :, 1], op=ALU.mult))
    vop(lambda e: e.tensor_scalar(out=vf[:, V_OML:V_OML + 8], in0=vf[:, V_LB:V_LB + 8], scalar1=-1.0, scalar2=1.0, op0=ALU.mult, op1=ALU.add))
    prw = _bank(ps, 5)
    rowsb = g["rowsb"]
    for k in range(6):
        bk = 5 if k < 4 else 4
        dst = _bank(ps, bk)[0:16, (k % 4) * 128:(k % 4 + 1) * 128]
        S.op("pe", lambda e, k=k, dst=dst: e.transpose(dst, vf[:, V_ROWS + 16 * k:V_ROWS + 16 * k + 16], ident[:]),
             reads=["vf", "ident"], writes=[("ps", bk)])
    S.op("act", lambda e: e.activation(out=rowsb[0:16, 0:512], in_=prw[0:16, 0:512], func=AF.Copy), reads=[("ps", 5)], writes=["rowsb"])
    S.op("act", lambda e: e.activation(out=rowsb[0:16, 512:768], in_=_bank(ps, 4)[0:16, 0:256], func=AF.Copy), reads=[("ps", 4)], writes=["rowsb"])
    S.op("sp", lambda e: e.dma_start(out=g["ROWS"].rearrange("k (c p) -> c k p", p=128), in_=rowsb[0:16, :].rearrange("c (k p) -> c k p", p=128)),
         reads=["rowsb"], writes=["ROWS"], dma=True)

    hx, aTh, hfl = g["hx"], g["aTh"], g["hfl"]
    if l == 0:
        dma_in(hx, g["halo0"], "X")
    else:
        hx2, selp = g["hx2"], g["selp"]
        for k in range(4):
            S.op("sp", lambda e, k=k: e.dma_start(out=hx2, in_=g["HALO_DST"][k * 128:k * 128 + 24, :].rearrange("(r c) w -> r (c w)", r=3)), reads=["HALO_DST"], writes=["hx2"], dma=True)
            if k == 0:
                S.op("dve", lambda e: e.tensor_scalar(out=hx, in0=hx2, scalar1=selp[0:3, 0:1], scalar2=None, op0=ALU.mult),
                     reads=["hx2", "selp"], writes=["X"])
            else:
                S.op("dve", lambda e, k=k: e.scalar_tensor_tensor(out=hx, in0=hx2, scalar=selp[0:3, k:k + 1], in1=hx, op0=ALU.mult, op1=ALU.add),
                     reads=["hx2", "selp"], writes=["X"])
    psh = _bank(ps, 6)
    for kc in range(KC):
        S.op("pe", lambda e, kc=kc: e.transpose(psh[:, kc * 4:kc * 4 + 3], hx[:, kc * 128:(kc + 1) * 128], ident[0:3, 0:3]),
             reads=["X", "ident"], writes=[("ps", 6)])
    for kc in range(KC):
        S.op("act", lambda e, kc=kc: e.activation(out=aTh[:, kc, :], in_=psh[:, kc * 4:kc * 4 + 3], func=AF.Identity,
                                                  scale=vf[:, V_GU1 + kc:V_GU1 + kc + 1], bias=vf[:, V_BU1 + kc:V_BU1 + kc + 1]),
             reads=[("ps", 6), "vf"], writes=["aTh"])

    hist, ext, t1, qkT = g["hist"], g["ext"], g["t1"], g["qkT"]
    gsb, l1, nb, nbL, wv_, eo, ec, eL, carry, ctmp = (g[k] for k in ("gsb", "l1", "nb", "nbL", "wv_", "eo", "ec", "eL", "carry", "ctmp"))
    vaug, ktok, A_sb, C32, Cbf, vH = (g[k] for k in ("vaug", "ktok", "A_sb", "C32", "Cbf", "vH"))
    sig, lf, kk, bcs, qs, ebuf, q_in, k_in, qdec = (g[k] for k in ("sig", "lf", "kk", "bcs", "qs", "ebuf", "q_in", "k_in", "qdec"))
    hsc, hcarry, S32, Sp, gts, wifg, gbias = (g[k] for k in ("hsc", "hcarry", "S32", "Sp", "gts", "wifg", "gbias"))
    OHs = X
    xsrc = g["xsrc"]
    ppi = [0]

    def pp_next():
        b = ppi[0] % 2
        ppi[0] += 1
        return b

    for t in range(NT):
        tok0 = t * TT
        S.op("sp", lambda e, tok0=tok0, xsrc=xsrc: e.dma_start(
            out=X[:], in_=xsrc[tok0:tok0 + TT, :].rearrange("(m p) d -> p m d", p=128)),
            reads=["XN"], writes=["X"], dma=True)
        emit_transpose_affine(S, T, "X", V_GU1, V_BU1)
        pg = _bank(ps, 2)
        wg3 = wifg[:].rearrange("p (k c) -> p k c", c=8)
        for m in range(4):
            for kc in range(KC):
                S.op("pe", lambda e, m=m, kc=kc: e.matmul(pg[:, m * 8:(m + 1) * 8], aT[:, kc, m * 128:(m + 1) * 128], wg3[:, kc, :],
                                                          start=(kc == 0), stop=(kc == KC - 1)),
                     reads=[("aT", kc), "wifg"], writes=[("ps", 2)])
        S.op("dve", lambda e: e.tensor_tensor(out=gsb[:], in0=pg[:, 0:32].rearrange("p (m c) -> p m c", c=8),
                                              in1=gbias[:, 0:8].unsqueeze(1).broadcast_to([128, 4, 8]), op=ALU.add),
             reads=[("ps", 2), "gbias"], writes=["gsb"])
        S.op("act", lambda e: e.activation(out=l1[:], in_=gsb[:, :, 4:8], func=AF.Exp, scale=-1.0), reads=["gsb"], writes=["l1"])
        S.op("act", lambda e: e.activation(out=l1[:], in_=l1[:], func=AF.Ln, bias=1.0), writes=["l1"])
        pc = _bank(ps, 2)
        for m in range(4):
            S.op("pe", lambda e, m=m: e.matmul(pc[:, 64 + m * 4:64 + m * 4 + 4], utri[:], l1[:, m, :], start=True, stop=True),
                 reads=["utri", "l1"], writes=[("ps", 2)])
            S.op("pe", lambda e, m=m: e.matmul(pc[:, 96 + m * 4:96 + m * 4 + 4], ones[:], l1[:, m, :], start=True, stop=True),
                 reads=["ones", "l1"], writes=[("ps", 2)])
        S.op("dve", lambda e: e.tensor_copy(out=nb[:], in_=pc[:, 64:80].rearrange("p (m c) -> p m c", c=4)), reads=[("ps", 2)], writes=["nb"])
        S.op("dve", lambda e: e.tensor_copy(out=nbL[:], in_=pc[:, 96:112].rearrange("p (m c) -> p m c", c=4)), reads=[("ps", 2)], writes=["nbL"])
        S.op("dve", lambda e: e.tensor_tensor(out=ctmp[:], in0=gsb[:, :, 0:4], in1=nb[:], op=ALU.add), reads=["gsb", "nb"], writes=["ctmp"])
        S.op("act", lambda e: e.activation(out=wv_[:], in_=ctmp[:], func=AF.Exp), reads=["ctmp"], writes=["wv"])
        S.op("act", lambda e: e.activation(out=eo[:], in_=nb[:], func=AF.Exp, scale=-1.0, bias=g["lnsc"][:, 0:1]), reads=["nb", "lnsc"], writes=["eo"])
        S.op("act", lambda e: e.activation(out=eL[:], in_=nbL[:], func=AF.Exp, scale=-1.0), reads=["nbL"], writes=["eL"])
        for m in range(4):
            S.op("act", lambda e, m=m: e.activation(out=ctmp[:, m, :], in_=carry[:], func=AF.Exp), reads=["carry"], writes=["ctmp"])
            S.op("dve", lambda e, m=m: e.tensor_tensor(out=ec[:, m, :], in0=eo[:, m, :], in1=ctmp[:, m, :], op=ALU.mult),
                 reads=["eo", "ctmp"], writes=["ec"])
            S.op("dve", lambda e, m=m: e.tensor_tensor(out=carry[:], in0=carry[:], in1=nbL[:, m, :], op=ALU.subtract),
                 reads=["nbL"], writes=["carry"])
        S.op("sp", lambda e, tok0=tok0: e.dma_start(out=g["EM"][tok0:tok0 + TT, :].rearrange("(m p) c -> p m c", p=128), in_=ec[:]),
             reads=["ec"], writes=[("EM", t)], dma=True)
        for qk in range(2):
            wt, wkey = ws.get(g["w1fm"][qk], WSLOT)
            wv4 = wt.rearrange("p (b k c) -> p b k c", b=4, k=KC)
            for b in range(4):
                blk = qk * 4 + b
                bk = pp_next()
                pq = _bank(ps, bk)
                for kc in range(KC):
                    S.op("pe", lambda e, pq=pq, wv4=wv4, b=b, kc=kc: e.matmul(pq, wv4[:, b, kc, :], aT[:, kc, :], start=(kc == 0), stop=(kc == KC - 1)),
                         reads=[wkey, ("aT", kc)], writes=[("ps", bk)])
                if t == 0:
                    ph = _bank(ps, 6)
                    for kc in range(KC):
                        S.op("pe", lambda e, ph=ph, wv4=wv4, b=b, kc=kc, blk=blk: e.matmul(ph[:, 64 + blk * 4:64 + blk * 4 + 3], wv4[:, b, kc, :], aTh[:, kc, :],
                                                                                       start=(kc == 0), stop=(kc == KC - 1)),
                             reads=[wkey, "aTh"], writes=[("ps", 6)])
                    S.op("act", lambda e, ph=ph, blk=blk: e.activation(out=ext[:, 0:3], in_=ph[:, 64 + blk * 4:64 + blk * 4 + 3], func=AF.Copy, scale=g["hfl"][:, 0:1]),
                         reads=[("ps", 6), "hfl"], writes=["ext"])
                else:
                    S.op("act", lambda e, blk=blk: e.activation(out=ext[:, 0:3], in_=hist[:, blk, :], func=AF.Copy),
                         reads=["hist"], writes=["ext"])
                S.op("act", lambda e, pq=pq: e.activation(out=ext[:, 3:TT + 3], in_=pq, func=AF.Copy), reads=[("ps", bk)], writes=["ext"])
                S.op("act", lambda e, blk=blk: e.activation(out=hist[:, blk, :], in_=ext[:, TT:TT + 3], func=AF.Copy), reads=["ext"], writes=["hist"])
                cw = V_CONVW + blk * 4
                S.op("dve", lambda e, cw=cw, blk=blk: e.tensor_scalar(out=t1[:], in0=ext[:, 0:TT], scalar1=vf[:, cw:cw + 1], scalar2=vf[:, V_CONVB + blk:V_CONVB + blk + 1],
                                                                   op0=ALU.mult, op1=ALU.add), reads=["ext", "vf"], writes=["t1"])
                for j in range(1, 4):
                    S.op("dve", lambda e, cw=cw, j=j: e.scalar_tensor_tensor(out=t1[:], in0=ext[:, j:TT + j], scalar=vf[:, cw + j:cw + j + 1], in1=t1[:],
                                                                           op0=ALU.mult, op1=ALU.add), reads=["ext", "vf"], writes=["t1"])
                S.op("act", lambda e, blk=blk: e.activation(out=qkT[:, blk, :], in_=t1[:], func=AF.Silu), reads=["t1"], writes=[("qkT", blk)])
        S.op("sp", lambda e, tok0=tok0: e.dma_start(out=g["QM"][:, tok0:tok0 + TT].rearrange("(h p) t -> p h t", p=128), in_=qkT[:, 0:4, :]),
             reads=[("qkT", b) for b in range(4)], writes=[("QM", t)], dma=True)
        for half in range(2):
            wt, wkey = ws.get(g["w1tm"][half], WSLOT)
            wv3 = wt.rearrange("p (k c) -> p k c", k=KC)
            for m in range(4):
                bk = pp_next()
                pv = _bank(ps, bk)
                for kc in range(KC):
                    S.op("pe", lambda e, pv=pv, wv3=wv3, m=m, kc=kc: e.matmul(pv, aT[:, kc, m * 128:(m + 1) * 128], wv3[:, kc, :], start=(kc == 0), stop=(kc == KC - 1)),
                         reads=[wkey, ("aT", kc)], writes=[("ps", bk)])
                for hh in range(2):
                    h = half * 2 + hh
                    S.op("act", lambda e, pv=pv, m=m, h=h, hh=hh: e.activation(out=vaug[:, m, h, 0:256], in_=pv[:, hh * 256:(hh + 1) * 256], func=AF.Copy,
                                                                            scale=wv_[:, m, h:h + 1]),
                         reads=[("ps", bk), "wv"], writes=[("vaug", m)])
        for m in range(4):
            S.op("dve", lambda e, m=m: e.tensor_copy(out=vaug[:, m, :, 256], in_=wv_[:, m, :]), reads=["wv"], writes=[("vaug", m)])
        for m in range(4):
            for h in range(4):
                sl = slice(m * 128, (m + 1) * 128)
                pa = _bank(ps, 3)
                a2 = (m * 4 + h) % 2
                pkt = pa.bitcast(BF16)[:, 512 + a2 * 128:512 + (a2 + 1) * 128]
                S.op("pe", lambda e, pkt=pkt, h=h, sl=sl: e.transpose(pkt, qkT[:, 4 + h, sl], identb[:]),
                     reads=[("qkT", 4 + h), "identb"], writes=[("ps", 3, "t", a2)])
                S.op("act", lambda e, pkt=pkt, a2=a2: e.activation(out=ktok[:, a2, :], in_=pkt, func=AF.Copy),
                     reads=[("ps", 3, "t", a2)], writes=[("ktok", a2)])
                pA = pa[:, a2 * 128:(a2 + 1) * 128]
                S.op("pe", lambda e, pA=pA, h=h, sl=sl: e.matmul(pA, qkT[:, 4 + h, sl], qkT[:, h, sl], start=True, stop=True),
                     reads=[("qkT", 4 + h), ("qkT", h)], writes=[("ps", 3, "a", a2)])
                S.op("dve", lambda e, pA=pA, a2=a2: e.copy_predicated(out=A_sb[:, a2, :], mask=tri[:], data=pA),
                     reads=[("ps", 3, "a", a2), "tri"], writes=[("A_sb", a2)])
                bo = 4 + a2
                pP = _bank(ps, bo)
                S.op("pe", lambda e, pP=pP, a2=a2, m=m, h=h: e.matmul(pP[:, 0:257], A_sb[:, a2, :], vaug[:, m, h, 0:257], start=True, stop=False),
                     reads=[("A_sb", a2), ("vaug", m)], writes=[("ps", bo)])
                S.op("pe", lambda e, pP=pP, h=h, sl=sl: e.matmul(pP[:, 0:257], qkT[:, h, sl], Cbf[:, h, 0:257], start=False, stop=True),
                     reads=[("qkT", h), ("Cbf", h)], writes=[("ps", bo)])
                S.op("act", lambda e, pP=pP, m=m, h=h: e.activation(out=g["OMs"][:, m, h * MW:h * MW + 257], in_=pP[:, 0:257], func=AF.Copy, scale=eo[:, m, h:h + 1]),
                     reads=[("ps", bo), "eo"], writes=[("OMs", m)])
                pC = _bank(ps, 6)
                S.op("pe", lambda e, pC=pC, a2=a2, m=m, h=h: e.matmul(pC[:, 0:257], ktok[:, a2, :], vaug[:, m, h, 0:257], start=True, stop=True),
                     reads=[("ktok", a2), ("vaug", m)], writes=[("ps", 6)])
                S.op("dve", lambda e, m=m, h=h: e.tensor_scalar(out=C32[:, h, 0:257], in0=C32[:, h, 0:257], scalar1=eL[:, m, h:h + 1], scalar2=None, op0=ALU.mult),
                     reads=["eL"], writes=[("C32", h)])
                S.op("dve", lambda e, pC=pC, m=m, h=h: e.scalar_tensor_tensor(out=C32[:, h, 0:257], in0=pC[:, 0:257], scalar=eL[:, m, h:h + 1], in1=C32[:, h, 0:257],
                                                                             op0=ALU.mult, op1=ALU.add), reads=[("ps", 6), "eL"], writes=[("C32", h)])
                S.op("act", lambda e, h=h: e.activation(out=Cbf[:, h, :], in_=C32[:, h, :], func=AF.Copy), reads=[("C32", h)], writes=[("Cbf", h)])
        S.op("sp", lambda e, tok0=tok0: e.dma_start(out=g["OM"][tok0:tok0 + TT, :].rearrange("(m p) c -> p m c", p=128), in_=g["OMs"]),
             reads=[("OMs", m) for m in range(4)], writes=[("OM", t)], dma=True)
        for half in range(2):
            wt, wkey = ws.get(g["w1tm"][2 + half], WSLOT)
            wv3 = wt.rearrange("p (k c) -> p k c", k=KC)
            for m in range(4):
                bk = pp_next()
                pv = _bank(ps, bk)
                for kc in range(KC):
                    S.op("pe", lambda e, pv=pv, wv3=wv3, m=m, kc=kc: e.matmul(pv, aT[:, kc, m * 128:(m + 1) * 128], wv3[:, kc, :], start=(kc == 0), stop=(kc == KC - 1)),
                         reads=[wkey, ("aT", kc)], writes=[("ps", bk)])
                S.op("act", lambda e, pv=pv, m=m, half=half: e.activation(out=vH[:, m, half * 512:(half + 1) * 512], in_=pv, func=AF.Copy),
                     reads=[("ps", bk)], writes=[("vH", m)])
        for hp in range(4):
            wt, wkey = ws.get(g["w1fm"][2 + hp], WSLOT)
            wv4 = wt.rearrange("p (b k c) -> p b k c", b=4, k=KC)
            for hh in range(2):
                h = hp * 2 + hh
                bq = pp_next()
                pq = _bank(ps, bq)
                for kc in range(KC):
                    S.op("pe", lambda e, pq=pq, wv4=wv4, hh=hh, kc=kc: e.matmul(pq, wv4[:, hh * 2, kc, :], aT[:, kc, :], start=(kc == 0), stop=(kc == KC - 1)),
                         reads=[wkey, ("aT", kc)], writes=[("ps", bq)])
                bf = pp_next()
                pf = _bank(ps, bf)
                for kc in range(KC):
                    S.op("pe", lambda e, pf=pf, wv4=wv4, hh=hh, kc=kc: e.matmul(pf, wv4[:, hh * 2 + 1, kc, :], aT[:, kc, :], start=(kc == 0), stop=(kc == KC - 1)),
                         reads=[wkey, ("aT", kc)], writes=[("ps", bf)])
                S.op("act", lambda e, pq=pq: e.activation(out=qs[:], in_=pq, func=AF.Silu), reads=[("ps", bq)], writes=["qs"])
                S.op("act", lambda e, pf=pf: e.activation(out=sig[:], in_=pf, func=AF.Sigmoid), reads=[("ps", bf)], writes=["sig"])
                S.op("dve", lambda e, h=h: e.tensor_scalar(out=sig[:], in0=sig[:], scalar1=vf[:, V_OML + h:V_OML + h + 1], scalar2=vf[:, V_LB + h:V_LB + h + 1],
                                                          op0=ALU.mult, op1=ALU.add), reads=["vf"], writes=["sig"])
                S.op("act", lambda e: e.activation(out=lf[:], in_=sig[:], func=AF.Ln), reads=["sig"], writes=["lf"])
                S.op("dve", lambda e: e.tensor_scalar(out=kk[:], in0=sig[:], scalar1=-1.0, scalar2=1.0, op0=ALU.mult, op1=ALU.add), reads=["sig"], writes=["kk"])
                S.op("dve", lambda e: e.tensor_tensor_scan(out=bcs[:], data0=m0[:], data1=lf[:], initial=0.0, op0=ALU.mult, op1=ALU.add),
                     reads=["m0", "lf"], writes=["bcs"])
                b3 = bcs[:].rearrange("p (c t) -> p c t", t=64)
                S.op("dve", lambda e: e.tensor_copy(out=hsc[:, :, 0], in_=b3[:, :, 31]), reads=["bcs"], writes=["hsc"])
                S.op("dve", lambda e: e.tensor_copy(out=hsc[:, :, 1], in_=b3[:, :, 63]), reads=["bcs"], writes=["hsc"])
                S.op("dve", lambda e: e.tensor_tensor(out=hsc[:, :, 7], in0=hsc[:, :, 1], in1=hsc[:, :, 0], op=ALU.subtract), writes=["hsc"])
                S.op("act", lambda e: e.activation(out=hsc[:, :, 2], in_=hsc[:, :, 0], func=AF.Exp), writes=["hsc"])
                S.op("act", lambda e: e.activation(out=hsc[:, :, 3], in_=hsc[:, :, 1], func=AF.Exp), writes=["hsc"])
                S.op("act", lambda e: e.activation(out=hsc[:, :, 4], in_=hsc[:, :, 7], func=AF.Exp), writes=["hsc"])
                S.op("dve", lambda e, h=h: e.tensor_tensor_scan(out=hsc[:, :, 7], data0=ones[:, 0:8], data1=hsc[:, :, 1], initial=hcarry[:, h:h + 1],
                                                               op0=ALU.mult, op1=ALU.add), reads=["ones", "hcarry"], writes=["hsc"])
                S.op("dve", lambda e: e.tensor_tensor(out=hsc[:, :, 5], in0=hsc[:, :, 7], in1=hsc[:, :, 1], op=ALU.subtract), writes=["hsc"])
                S.op("dve", lambda e, h=h: e.tensor_copy(out=hcarry[:, h:h + 1], in_=hsc[:, 7, 7:8]), reads=["hsc"], writes=["hcarry"])
                S.op("dve", lambda e: e.tensor_tensor(out=hsc[:, :, 6], in0=hsc[:, :, 5], in1=hsc[:, :, 0], op=ALU.add), writes=["hsc"])
                S.op("act", lambda e: e.activation(out=hsc[:, :, 6], in_=hsc[:, :, 6], func=AF.Exp), writes=["hsc"])
                S.op("dve", lambda e: e.tensor_tensor(out=b3, in0=b3, in1=hsc[:, :, 0:1].broadcast_to([128, 8, 64]), op=ALU.subtract),
                     reads=["hsc"], writes=["bcs"])
                S.op("act", lambda e: e.activation(out=ebuf[:], in_=bcs[:], func=AF.Exp), reads=["bcs"], writes=["ebuf"])
                S.op("dve", lambda e: e.tensor_tensor(out=q_in[:], in0=qs[:], in1=ebuf[:], op=ALU.mult), reads=["qs", "ebuf"], writes=["q_in"])
                S.op("pool", lambda e: e.tensor_tensor(out=qs[:], in0=qs[:], in1=ebuf[:], op=ALU.mult), reads=["ebuf"], writes=["qs"])
                S.op("pool", lambda e: e.tensor_tensor(out=qdec[:].rearrange("p (c t) -> p c t", t=64), in0=qs[:].rearrange("p (c t) -> p c t", t=64),
                                                       in1=hsc[:, :, 6:7].broadcast_to([128, 8, 64]), op=ALU.mult), reads=["qs", "hsc"], writes=["qdec"])
                S.op("act", lambda e: e.activation(out=ebuf[:], in_=bcs[:], func=AF.Exp, scale=-1.0), reads=["bcs"], writes=["ebuf"])
                S.op("dve", lambda e: e.tensor_tensor(out=k_in[:], in0=kk[:], in1=ebuf[:], op=ALU.mult), reads=["kk", "ebuf"], writes=["k_in"])
                S.op("sp", lambda e, h=h, tok0=tok0: e.dma_start(out=g["QH"][h * 128:(h + 1) * 128, tok0:tok0 + TT], in_=qdec[:]),
                     reads=["qdec"], writes=[("QH", t, h)], dma=True)
                for cc in range(8):
                    m, hf = cc // 2, cc % 2
                    c0 = cc * 64
                    p0 = hf * 64
                    pa = _bank(ps, 3)
                    a2 = cc % 2
                    if hf == 0:
                        pkt = pa.bitcast(BF16)[:, 512:640]
                        S.op("pe", lambda e, pkt=pkt, m=m: e.transpose(pkt, k_in[:, m * 128:(m + 1) * 128], identb[:]),
                             reads=["k_in", "identb"], writes=[("ps", 3, "t", 0)])
                        S.op("act", lambda e, pkt=pkt: e.activation(out=ktok[:, 0, :], in_=pkt, func=AF.Copy),
                             reads=[("ps", 3, "t", 0)], writes=[("ktok", 0)])
                    S.op("dve", lambda e, cc=cc, h=h, a2=a2: e.tensor_scalar(out=Sp[:, a2, :], in0=S32[:, h, :], scalar1=hsc[:, cc, 2:3], scalar2=None, op0=ALU.mult),
                         reads=[("S32", h), "hsc"], writes=[("Sp", a2)])
                    pA = pa[p0:p0 + 64, a2 * 128:a2 * 128 + 64]
                    S.op("pe", lambda e, pA=pA, c0=c0: e.matmul(pA, k_in[:, c0:c0 + 64], q_in[:, c0:c0 + 64], start=True, stop=True),
                         reads=["k_in", "q_in"], writes=[("ps", 3, "a", a2)])
                    S.op("dve", lambda e, pA=pA, p0=p0, a2=a2: e.copy_predicated(out=g["A_h"][p0:p0 + 64, a2, 0:64], mask=tri[p0:p0 + 64, p0:p0 + 64], data=pA),
                         reads=[("ps", 3, "a", a2), "tri"], writes=[("A_h", a2)])
                    bo = 4 + a2
                    pO = _bank(ps, bo)[p0:p0 + 64, 0:128]
                    S.op("pe", lambda e, pO=pO, p0=p0, a2=a2, m=m, h=h: e.matmul(pO, g["A_h"][p0:p0 + 64, a2, 0:64], vH[p0:p0 + 64, m, h * 128:(h + 1) * 128], start=True, stop=False),
                         reads=[("A_h", a2), ("vH", m)], writes=[("ps", bo)])
                    S.op("pe", lambda e, pO=pO, c0=c0, a2=a2: e.matmul(pO, q_in[:, c0:c0 + 64], Sp[:, a2, :], start=False, stop=True),
                         reads=["q_in", ("Sp", a2)], writes=[("ps", bo)])
                    S.op("act", lambda e, pO=pO, p0=p0, m=m, h=h: e.activation(out=OHs[p0:p0 + 64, m, h * 128:(h + 1) * 128], in_=pO, func=AF.Copy),
                         reads=[("ps", bo)], writes=["X"])
                    pS = _bank(ps, 6)[:, 0:128]
                    S.op("pe", lambda e, pS=pS, p0=p0, m=m, h=h: e.matmul(pS, ktok[p0:p0 + 64, 0, :], vH[p0:p0 + 64, m, h * 128:(h + 1) * 128], start=True, stop=True),
                         reads=[("ktok", 0), ("vH", m)], writes=[("ps", 6)])
                    S.op("dve", lambda e, cc=cc, h=h: e.tensor_scalar(out=S32[:, h, :], in0=S32[:, h, :], scalar1=hsc[:, cc, 3:4], scalar2=None, op0=ALU.mult),
                         reads=["hsc", ("Sp", a2)], writes=[("S32", h)])
                    S.op("dve", lambda e, pS=pS, cc=cc, h=h: e.scalar_tensor_tensor(out=S32[:, h, :], in0=pS, scalar=hsc[:, cc, 4:5], in1=S32[:, h, :], op0=ALU.mult, op1=ALU.add),
                         reads=[("ps", 6), "hsc"], writes=[("S32", h)])
        S.op("sp", lambda e, tok0=tok0: e.dma_start(out=g["OH"][tok0:tok0 + TT, :].rearrange("(m p) c -> p m c", p=128), in_=OHs[:, :, 0:1024]),
             reads=["X"], writes=[("OH", t)], dma=True)
        for gi in range(4):
            wt, wkey = ws.get(g["w1tm"][4 + gi], WSLOT)
            wv3 = wt.rearrange("p (k c) -> p k c", k=KC)
            fn = AF.Sigmoid if gi < 2 else AF.Silu
            for m in range(4):
                bk = pp_next()
                pv = _bank(ps, bk)
                for kc in range(KC):
                    S.op("pe", lambda e, pv=pv, wv3=wv3, m=m, kc=kc: e.matmul(pv, aT[:, kc, m * 128:(m + 1) * 128], wv3[:, kc, :], start=(kc == 0), stop=(kc == KC - 1)),
                         reads=[wkey, ("aT", kc)], writes=[("ps", bk)])
                g2 = (gi * 4 + m) % 2
                S.op("act", lambda e, pv=pv, g2=g2, fn=fn: e.activation(out=gts[:, g2, :], in_=pv, func=fn), reads=[("ps", bk)], writes=[("gts", g2)])
                S.op("sp", lambda e, g2=g2, gi=gi, m=m, tok0=tok0: e.dma_start(out=g["GT"][tok0 + m * 128:tok0 + (m + 1) * 128, gi * 512:(gi + 1) * 512], in_=gts[:, g2, :]),
                     reads=[("gts", g2)], writes=[("GT", t, gi, m)], dma=True)
    stsb = X[:].rearrange("p m d -> p (m d)")[:, 0:NST]
    S.op("dve", lambda e: e.memset(stsb[:, 2064:NST], 0.0), writes=["X"])
    S.op("dve", lambda e: e.tensor_copy(out=stsb[:, 0:1024], in_=S32[:].rearrange("p h c -> p (h c)")), reads=[("S32", h) for h in range(8)], writes=["X"])
    S.op("dve", lambda e: e.tensor_copy(out=stsb[:, 1024:1024 + 4 * MW], in_=C32[:].rearrange("p h c -> p (h c)")), reads=[("C32", h) for h in range(4)], writes=["X"])
    S.op("act", lambda e: e.activation(out=stsb[:, 2064:2072], in_=hcarry[:], func=AF.Exp), reads=["hcarry"], writes=["X"])
    S.op("act", lambda e: e.activation(out=stsb[:, 2072:2076], in_=carry[:], func=AF.Exp), reads=["carry"], writes=["X"])
    S.op("sp", lambda e: e.dma_start(out=g["STATE_SRC"].rearrange("k p c -> p k c"), in_=stsb.rearrange("p (k c) -> p k c", c=256)),
         reads=["X"], writes=["STATE_SRC"], dma=True)
    for k in range(NPC):
        S.op("pool", lambda e, k=k: e.collective_compute("AllGather", ALU.bypass, replica_groups=[[0, 1, 2, 3], [4, 5, 6, 7]],
                                                         ins=[g["STATE_SRC"][k].opt()], outs=[g["STATES"][k].opt()]),
             reads=["STATE_SRC"], writes=[("STATES", k)], dma="cc")


def _emit_p2(S, ws, nc, T, g, NT, l, last):
    X, aT, ps, vf, bc, arena = T["X"], T["aT"], T["ps"], T["vf"], T["bc"], g["arena"]
    ident, identb, selt, Sst, Cst, EMt, hst, mst = (g[k] for k in ("ident", "identb", "selt", "Sst", "Cst", "EMt", "hst", "mst"))
    xsrc = g["xsrc"]

    def hk(a, b):
        return [("hT", j) for j in range(a // TT, (b + TT - 1) // TT)]

    def f32v(off, n):
        return arena[:, off:off + 2 * n].bitcast(F32)

    def dma_in(dst, src, keys, eng="sp", reads=()):
        S.op(eng, lambda e: e.dma_start(out=dst, in_=src), reads=list(reads), writes=list(keys), dma=True)
    blobs = X[:].rearrange("p m d -> p (m d)")
    for k in range(3):
        dma_in(blobs[:, k * NST:(k + 1) * NST].rearrange("p (j c) -> p j c", c=256), g["STATES"][:, k * 128:(k + 1) * 128, :].rearrange("j p c -> p j c"),
               ["X"], reads=[("STATES", j) for j in range(NPC)])
    KS = hk(0, 4 * NST)
    Scur = f32v(0, NST)
    Sacc = f32v(2 * NST, NST)
    S.op("dve", lambda e: e.tensor_copy(out=Scur, in_=blobs[:, 0:NST]), reads=["X"], writes=KS)
    S.op("dve", lambda e: e.tensor_scalar(out=Sacc, in0=Scur, scalar1=selt[:, 1:2], scalar2=None, op0=ALU.mult), reads=["selt"], writes=KS)
    for k in (1, 2):
        bk = blobs[:, k * NST:(k + 1) * NST]
        S.op("dve", lambda e, bk=bk: e.tensor_tensor(out=Scur[:, 0:1024].rearrange("p (h c) -> p h c", c=128), in0=Scur[:, 0:1024].rearrange("p (h c) -> p h c", c=128),
                                                     in1=bk[:, 2064:2072].unsqueeze(2).broadcast_to([128, 8, 128]), op=ALU.mult), reads=["X"], writes=KS)
        S.op("dve", lambda e, bk=bk: e.tensor_tensor(out=Scur[:, 1024:1024 + 4 * MW].rearrange("p (h c) -> p h c", c=MW), in0=Scur[:, 1024:1024 + 4 * MW].rearrange("p (h c) -> p h c", c=MW),
                                                     in1=bk[:, 2072:2076].unsqueeze(2).broadcast_to([128, 4, MW]), op=ALU.mult), reads=["X"], writes=KS)
        S.op("dve", lambda e, bk=bk: e.tensor_tensor(out=Scur[:, 0:2064], in0=Scur[:, 0:2064], in1=bk[:, 0:2064], op=ALU.add), reads=["X"], writes=KS)
        S.op("dve", lambda e, k=k: e.scalar_tensor_tensor(out=Sacc[:, 0:2064], in0=Scur[:, 0:2064], scalar=selt[:, k + 1:k + 2], in1=Sacc[:, 0:2064], op0=ALU.mult, op1=ALU.add),
             reads=["selt"], writes=KS)
    S.op("dve", lambda e: e.tensor_copy(out=Sst[:].rearrange("p h c -> p (h c)"), in_=Sacc[:, 0:1024]), reads=KS, writes=["Sst"])
    S.op("dve", lambda e: e.tensor_copy(out=Cst[:].rearrange("p h c -> p (h c)"), in_=Sacc[:, 1024:1024 + 4 * MW]), reads=KS, writes=["Cst"])

    O_OH, O_OM, O_GT, O_QH, O_QM, O_YC, O_W1, O_W2, O_WM, O_WH = 0, 2048, 4608, 6656, 10752, 12800, 14848, 16896, 18944, 19968
    OHm, K_OH = f32v(O_OH, 1024), hk(O_OH, O_OH + 2048)
    OMm, K_OM = f32v(O_OM, 4 * MW), hk(O_OM, O_OM + 8 * MW)
    GTm, K_GT = arena[:, O_GT:O_GT + 2048], hk(O_GT, O_GT + 2048)
    QHt, K_QH = arena[:, O_QH:O_QH + 4096].rearrange("p (h t) -> p h t", t=TT), hk(O_QH, O_QH + 4096)
    QMt, K_QM = arena[:, O_QM:O_QM + 2048].rearrange("p (h t) -> p h t", t=TT), hk(O_QM, O_QM + 2048)
    ycat, K_YC = arena[:, O_YC:O_YC + 2048], hk(O_YC, O_YC + 2048)
    wk1, K_W1 = f32v(O_W1, 1024), hk(O_W1, O_W1 + 2048)
    wk2, K_W2 = f32v(O_W2, 1024), hk(O_W2, O_W2 + 2048)
    wkm, K_WM = f32v(O_WM, MW), hk(O_WM, O_WM + 2 * MW)
    wkh, K_WH = f32v(O_WH, 256), hk(O_WH, O_WH + 512)

    for t in range(NT):
        tok0 = t * TT
        dma_in(QHt, g["QH"][:, tok0:tok0 + TT].rearrange("(h p) t -> p h t", p=128), K_QH)
        dma_in(QMt, g["QM"][:, tok0:tok0 + TT].rearrange("(h p) t -> p h t", p=128), K_QM)
        dma_in(EMt[:], g["EM"][tok0:tok0 + TT, :].rearrange("(m p) c -> p m c", p=128), ["EMt"])
        for i in range(3):
            dma_in(bc[:, i, :], g["ROWS"][i:i + 1, :].partition_broadcast(128), [("bc", i)], reads=["ROWS"])
        for m in range(4):
            r0 = tok0 + m * 128
            msl = slice(m * 128, (m + 1) * 128)
            dma_in(OHm, g["OH"][r0:r0 + 128, :], K_OH)
            dma_in(OMm, g["OM"][r0:r0 + 128, :], K_OM)
            dma_in(GTm, g["GT"][r0:r0 + 128, :], K_GT)
            for h in range(8):
                bk = 4 + h // 4
                pcor = _bank(ps, bk)[:, (h % 4) * 128:(h % 4 + 1) * 128]
                S.op("pe", lambda e, pcor=pcor, h=h, msl=msl: e.matmul(pcor, QHt[:, h, msl], Sst[:, h, :], start=True, stop=True),
                     reads=K_QH + ["Sst"], writes=[("ps", bk)])
            for hb in range(2):
                S.op("dve", lambda e, hb=hb: e.tensor_tensor(out=wk1[:, hb * 512:(hb + 1) * 512], in0=_bank(ps, 4 + hb), in1=OHm[:, hb * 512:(hb + 1) * 512], op=ALU.add),
                     reads=[("ps", 4 + hb)] + K_OH, writes=K_W1)
            S.op("pool", lambda e: e.tensor_tensor(out=wk2, in0=wk1, in1=wk1, op=ALU.mult), reads=K_W1, writes=K_W2)
            S.op("dve", lambda e: e.tensor_reduce(out=hst[:, :, 0], in_=wk2.rearrange("p (h c) -> p h c", c=128), axis=AX.X, op=ALU.add), reads=K_W2, writes=["hst"])
            S.op("act", lambda e: e.activation(out=hst[:, :, 1], in_=hst[:, :, 0], func=AF.Sqrt, scale=1.0 / 128.0, bias=g["epsb"][:, 0:1]), reads=["hst", "epsb"], writes=["hst"])
            S.op("dve", lambda e: e.reciprocal(out=hst[:, :, 2], in_=hst[:, :, 1]), writes=["hst"])
            S.op("dve", lambda e: e.tensor_tensor(out=wk1.rearrange("p (h c) -> p h c", c=128), in0=wk1.rearrange("p (h c) -> p h c", c=128),
                                                  in1=hst[:, :, 2:3].broadcast_to([128, 8, 128]), op=ALU.mult), reads=["hst"], writes=K_W1)
            S.op("pool", lambda e: e.tensor_tensor(out=ycat[:, 1024:2048], in0=wk1, in1=GTm[:, 1024:2048], op=ALU.mult), reads=K_W1 + K_GT, writes=K_YC)
            for h in range(4):
                bk = 6 + h % 2
                pP = _bank(ps, bk)
                S.op("pe", lambda e, pP=pP, h=h, msl=msl: e.matmul(pP[:, 0:257], QMt[:, h, msl], Cst[:, h, 0:257], start=True, stop=True),
                     reads=K_QM + ["Cst"], writes=[("ps", bk)])
                S.op("dve", lambda e, pP=pP, h=h, m=m: e.scalar_tensor_tensor(out=wkm[:, 0:257], in0=pP[:, 0:257], scalar=EMt[:, m, h:h + 1], in1=OMm[:, h * MW:h * MW + 257],
                                                                             op0=ALU.mult, op1=ALU.add), reads=[("ps", bk), "EMt"] + K_OM, writes=K_WM)
                ms = mst[:, h, :]
                S.op("dve", lambda e, ms=ms: e.tensor_scalar(out=ms[:, 0:1], in0=wkm[:, 256:257], scalar1=-1.0, scalar2=None, op0=ALU.mult), reads=K_WM, writes=[("mst", h)])
                S.op("dve", lambda e, ms=ms: e.tensor_tensor(out=ms[:, 0:1], in0=ms[:, 0:1], in1=wkm[:, 256:257], op=ALU.max), reads=K_WM, writes=[("mst", h)])
                S.op("dve", lambda e, ms=ms: e.tensor_scalar(out=ms[:, 0:1], in0=ms[:, 0:1], scalar1=1.0, scalar2=None, op0=ALU.max), writes=[("mst", h)])
                S.op("dve", lambda e, ms=ms: e.reciprocal(out=ms[:, 1:2], in_=ms[:, 0:1]), writes=[("mst", h)])
                S.op("act", lambda e, ms=ms: e.activation(out=wkh, in_=wkm[:, 0:256], func=AF.Copy, scale=ms[:, 1:2]), reads=K_WM + [("mst", h)], writes=K_WH)
                S.op("dve", lambda e, ms=ms: e.bn_stats(out=ms[:, 2:8], in_=wkh), reads=K_WH, writes=[("mst", h)])
                S.op("dve", lambda e, ms=ms: e.bn_aggr(out=ms[:, 8:10], in_=ms[:, 2:8]), writes=[("mst", h)])
                S.op("act", lambda e, ms=ms: e.activation(out=ms[:, 10:11], in_=ms[:, 9:10], func=AF.Sqrt, bias=g["epsb"][:, 0:1]), reads=["epsb"], writes=[("mst", h)])
                S.op("dve", lambda e, ms=ms: e.reciprocal(out=ms[:, 11:12], in_=ms[:, 10:11]), writes=[("mst", h)])
                S.op("dve", lambda e, ms=ms: e.scalar_tensor_tensor(out=ms[:, 12:13], in0=ms[:, 8:9], scalar=-1.0, in1=ms[:, 11:12], op0=ALU.mult, op1=ALU.mult), writes=[("mst", h)])
                S.op("act", lambda e, ms=ms: e.activation(out=wkh, in_=wkh, func=AF.Identity, scale=ms[:, 11:12], bias=ms[:, 12:13]), reads=[("mst", h)], writes=K_WH)
                S.op("dve", lambda e, h=h: e.tensor_tensor(out=ycat[:, h * 256:(h + 1) * 256], in0=wkh, in1=GTm[:, h * 256:(h + 1) * 256], op=ALU.mult),
                     reads=K_WH + K_GT, writes=K_YC)
            for kc in range(KC):
                t2 = kc % 2
                pt = _bank(ps, 4 + t2).bitcast(BF16)[:, 0:128]
                S.op("pe", lambda e, pt=pt, kc=kc: e.transpose(pt, ycat[:, kc * 128:(kc + 1) * 128], identb[:]), reads=K_YC + ["identb"], writes=[("ps", 4 + t2)])
                S.op("act", lambda e, pt=pt, kc=kc, msl=msl: e.activation(out=aT[:, kc, msl], in_=pt, func=AF.Copy, scale=vf[:, V_NRMW + kc:V_NRMW + kc + 1]),
                     reads=[("ps", 4 + t2), "vf"], writes=[("aT", kc)])
        S.op("sp", lambda e, tok0=tok0, xsrc=xsrc: e.dma_start(out=X[:], in_=xsrc[tok0:tok0 + TT, :].rearrange("(m p) d -> p m d", p=128)),
             reads=["XN"], writes=["X"], dma=True)
        emit_resid_prep(S, T, 0, 1)
        for n in range(4):
            wt, wkey = ws.get(g["wout"][n], WSLOT)
            wv3 = wt.rearrange("p (k c) -> p k c", k=KC)
            for m in range(4):
                pm = _bank(ps, m)
                for kc in range(KC):
                    S.op("pe", lambda e, pm=pm, wv3=wv3, m=m, kc=kc: e.matmul(pm, aT[:, kc, m * 128:(m + 1) * 128], wv3[:, kc, :], start=(kc == 0), stop=(kc == KC - 1)),
                         reads=[wkey, ("aT", kc)], writes=[("ps", m)])
            for m in range(4):
                emit_epilogue_block(S, T, m, n, 2)
        emit_ln_inplace(S, T)
        for i in range(3):
            dma_in(bc[:, i, :], g["ROWS"][3 + i:4 + i, :].partition_broadcast(128), [("bc", i)], reads=["ROWS"])
        emit_transpose_affine(S, T, "X", V_GU2, V_BU2)
        emit_ffn2(S, ws, T, g["wgu"], g["wd"])
        if not last:
            S.op("sp", lambda e, tok0=tok0: e.dma_start(out=g["XN"][tok0:tok0 + TT, :].rearrange("(m p) d -> p m d", p=128), in_=X[:]),
                 reads=["X"], writes=["XN"], dma=True)
            if t == NT - 1:
                S.op("sp", lambda e: e.dma_start(out=g["HALO_SRC"][0:24, :].rearrange("(r c) w -> r (c w)", r=3), in_=X[125:128, 3, :]), reads=["X"], writes=["HALO_SRC"], dma=True)
                S.op("pool", lambda e: e.collective_compute("AllGather", ALU.bypass, replica_groups=[[0, 1, 2, 3], [4, 5, 6, 7]],
                                                            ins=[g["HALO_SRC"].opt()], outs=[g["HALO_DST"].opt()]),
                     reads=["HALO_SRC"], writes=["HALO_DST"], dma="cc")
        else:
            for i in range(2):
                dma_in(bc[:, i, :], g["fin"][i:i + 1, :].partition_broadcast(128), [("bc", i)])
            emit_resid_prep(S, T, 0, 1)
            S.op("sp", lambda e, tok0=tok0: e.dma_start(out=g["OUT"][tok0:tok0 + TT, :].rearrange("(m p) d -> p m d", p=128), in_=X[:]),
                 reads=["X"], writes=[("OUT", t)], dma=True)


def emit_ffn2(S, ws, T, wgu, wd):
    aT, hT, ps, sg = T["aT"], T["hT"], T["ps"], T["sg"]
    for j2 in range(FC // 2):
        wt, wkey = ws.get(wgu[j2], 2 * 2 * KC * 128)
        wv = wt.rearrange("p (b g k c) -> p b g k c", b=2, g=2, k=KC)
        for b in range(2):
            j = j2 * 2 + b
            bg = 4 + (j % 2) * 2
            bu = bg + 1
            pg, pu = _bank(ps, bg), _bank(ps, bu)
            for gi, pp, bk in ((0, pg, bg), (1, pu, bu)):
                for kc in range(KC):
                    S.op("pe", lambda e, pp=pp, wv=wv, b=b, gi=gi, kc=kc: e.matmul(
                        pp, wv[:, b, gi, kc, :], aT[:, kc, :], start=(kc == 0), stop=(kc == KC - 1)),
                        reads=[wkey, ("aT", kc)], writes=[("ps", bk)])
            sgt = sg[:, (j % 2) * 512:(j % 2 + 1) * 512]
            S.op("act", lambda e, sgt=sgt, pg=pg: e.activation(out=sgt, in_=pg, func=AF.Silu),
                 reads=[("ps", bg)], writes=[("sg", j % 2)])
            S.op("dve", lambda e, sgt=sgt, pu=pu, j=j: e.tensor_tensor(out=hT[:, j, :], in0=pu, in1=sgt, op=ALU.mult),
                 reads=[("ps", bu), ("sg", j % 2)], writes=[("hT", j)])
    emit_resid_prep(S, T, 0, 1)
    for n in range(4):
        for jg in range(4):
            wt, wkey = ws.get(wd[n, jg], 11 * 512)
            wv = wt.rearrange("p (j c) -> p j c", j=11)
            for m in range(4):
                pm = _bank(ps, m)
                for jj in range(11):
                    j = jg * 11 + jj
                    S.op("pe", lambda e, pm=pm, wv=wv, jj=jj, j=j, m=m: e.matmul(
                        pm, hT[:, j, m * 128:(m + 1) * 128], wv[:, jj, :], start=(j == 0), stop=(j == FC - 1)),
                        reads=[wkey, ("hT", j)], writes=[("ps", m)])
        for m in range(4):
            emit_epilogue_block(S, T, m, n, 2)
    emit_ln_inplace(S, T)


def _fm_block(W, col, width=128):
    return W[:, col:col + width].reshape(KC, 128, width).transpose(1, 0, 2)


def _fm16(v):
    return np.ascontiguousarray(v.reshape(-1, 128).T)


def _layer_arrays(inp, l, depth):
    W = inp["w_in"][l]
    fm_tiles = []
    fm_tiles.append(np.stack([_fm_block(W, h * 128) for h in range(4)], axis=1))
    fm_tiles.append(np.stack([_fm_block(W, 512 + h * 128) for h in range(4)], axis=1))
    for hp in range(4):
        blks = []
        for hh in range(2):
            h = hp * 2 + hh
            blks += [_fm_block(W, 3080 + h * 128), _fm_block(W, 4104 + h * 128)]
        fm_tiles.append(np.stack(blks, axis=1))
    w1fm = np.ascontiguousarray(np.stack(fm_tiles)).reshape(6, 128, WSLOT)
    tm_cols = [1024, 1536, 5128, 5640, 2048, 2560, 6152, 6664]
    w1tm = np.ascontiguousarray(np.stack([_fm_block(W, c, 512) for c in tm_cols])).reshape(8, 128, WSLOT)
    w1ifg = np.ascontiguousarray(_fm_block(W, 3072, 8)).reshape(128, KC * 8)
    Wm = inp["w_mod"][l]
    wmod = np.ascontiguousarray(np.stack([np.stack([_fm_block(Wm, t * 512 + b * 128) for b in range(4)], axis=1)
                                          for t in range(24)])).reshape(24, 128, WSLOT)
    Wo = inp["w_out"][l]
    wout = np.ascontiguousarray(np.stack([_fm_block(Wo, n * 512, 512) for n in range(4)])).reshape(4, 128, WSLOT)
    g4 = inp["w_gate"][l].reshape(KC, 128, FC // 2, 2, 128)
    u4 = inp["w_up"][l].reshape(KC, 128, FC // 2, 2, 128)
    gu = np.stack([g4, u4], axis=0)
    wgu = np.ascontiguousarray(gu.transpose(3, 2, 4, 0, 1, 5)).reshape(FC // 2, 128, WSLOT)
    wd = np.ascontiguousarray(inp["w_down"][l].reshape(4, 11, 128, 4, 512).transpose(3, 0, 2, 1, 4)).reshape(4, 4, 128, 11 * 512)
    vecs = np.zeros((128, V_IN), np.float32)
    vecs[:, V_BMOD:V_BMOD + 96] = _fm16(inp["b_mod"][l])
    if l == 0:
        vecs[:, V_LNPG:V_LNPG + 16] = np.ones((128, 16), np.float32)
    else:
        vecs[:, V_LNPG:V_LNPG + 16] = _fm16(inp["ln2_g"][l - 1])
        vecs[:, V_LNPB:V_LNPB + 16] = _fm16(inp["ln2_b"][l - 1])
    vecs[:, V_LN1G:V_LN1G + 16] = _fm16(inp["ln1_g"][l])
    vecs[:, V_LN1B:V_LN1B + 16] = _fm16(inp["ln1_b"][l])
    vecs[:, V_CONVW:V_CONVW + 32] = inp["conv_w"][l].reshape(4, 8, 128).transpose(2, 1, 0).reshape(128, 32)
    vecs[:, V_CONVB:V_CONVB + 8] = _fm16(inp["conv_b"][l])
    vecs[:, V_LBL:V_LBL + 8 * depth] = inp["lb_logits"].reshape(depth, 8, 128).transpose(2, 1, 0).reshape(128, 8 * depth)
    nrmw = _fm16(np.concatenate([inp["mlstm_norm_w"][l], inp["hgrn_norm_w"][l]]))
    vecs[:, V_NRMW:V_NRMW + 16] = nrmw
    grow = np.zeros((1, 8 + DEPTH), np.float32)
    grow[0, 0:4] = inp["b_igate"][l]
    grow[0, 4:8] = inp["b_fgate"][l]
    for i in range(1, l + 1):
        grow[0, 8 + i] = np.float32(1.0)
    return dict(w1fm=w1fm, w1tm=w1tm, w1ifg=w1ifg, wmod=wmod, wout=wout, wgu=wgu, wd=wd, vecs=vecs, grow=grow, nrmw=nrmw)


def build_fused(TOK, depth):
    NT = TOK // TT
    nc = bass.Bass("TRN2", target_bir_lowering=False)

    def din(name, shape, dt=F32):
        return nc.dram_tensor(name, shape, dt, kind="ExternalInput").ap()

    def dint(name, shape, dt=F32):
        return nc.dram_tensor(name, shape, dt, kind="Internal").ap()
    g = {}
    x_in = din("x_in", [TOK, D])
    g["halo0"] = din("halo0", [3, D])
    g["hflag"] = din("hflag", [128, 1])
    g["c_fm"] = din("c_fm", [128, KC])
    g["sel"] = din("sel", [128, 4])
    g["selprev"] = din("selprev", [128, 4])
    g["fin"] = din("fin", [2, D])
    g["identd"] = din("identd", [128, 128])
    g["trid"] = din("trid", [128, 128], U8)
    wmod = din("wmod", [depth * 24, 128, WSLOT])
    w1fm = din("w1fm", [depth * 6, 128, WSLOT])
    w1tm = din("w1tm", [depth * 8, 128, WSLOT])
    w1ifg = din("w1ifg", [depth * 128, KC * 8])
    wout = din("wout", [depth * 4, 128, WSLOT])
    wgu = din("wgu", [depth * (FC // 2), 128, WSLOT])
    wd = din("wd", [depth * 4, 4, 128, 11 * 512])
    vecs = din("vecs", [depth * 128, V_IN])
    grow = din("grow", [depth, 8 + DEPTH])
    g["OUT"] = nc.dram_tensor("OUT", [TOK, D], F32, kind="ExternalOutput").ap()
    g["XN"] = dint("XN", [TOK, D])
    g["OH"] = dint("OH", [TOK, 1024])
    g["OM"] = dint("OM", [TOK, 4 * MW])
    g["QH"] = dint("QH", [1024, TOK], BF16)
    g["QM"] = dint("QM", [512, TOK], BF16)
    g["EM"] = dint("EM", [TOK, 4])
    g["GT"] = dint("GT", [TOK, 2048], BF16)
    g["ROWS"] = dint("ROWS", [6, D])
    g["STATE_SRC"] = dint("STATE_SRC", [NPC, 128, 256])
    g["STATES"] = dint("STATES", [NPC, 4 * 128, 256])
    g["HALO_SRC"] = dint("HALO_SRC", [128, 256])
    g["HALO_DST"] = dint("HALO_DST", [4 * 128, 256])

    with contextlib.ExitStack() as es:
        def sb(name, shape, dt=F32):
            return es.enter_context(nc.sbuf_tensor(name, shape, dt))
        T = {}
        X = T["X"] = sb("X", [128, 4, D])
        T["aT"] = sb("aT", [128, KC, TT], BF16)
        wbuf = sb("wbuf", [128, NWS * WSLOT], BF16)
        T["ident"] = g["ident"] = sb("ident", [128, 128])
        g["identb"] = sb("identb", [128, 128], BF16)
        g["tri"] = sb("tri", [128, 128], U8)
        g["utri"] = sb("utri", [128, 128])
        g["ones"] = sb("ones", [128, 128])
        g["m0"] = sb("m0", [128, TT])
        T["vf"] = sb("vf", [128, V_TOT])
        g["cact"] = sb("cact", [128, KC])
        g["cactb"] = sb("cactb", [128, KC], BF16)
        g["hfl"] = sb("hfl", [128, 1])
        g["selt"] = sb("selt", [128, 4])
        g["selp"] = sb("selp", [128, 4])
        g["lnsc"] = sb("lnsc", [128, 1])
        T["epsb"] = g["epsb"] = sb("epsb", [128, 1])
        g["A_sb"] = sb("A_sb", [128, 2, 128], BF16)
        g["A_h"] = sb("A_h", [128, 2, 64], BF16)
        T["st"] = sb("st", [128, 4, 4, 6])
        T["mv"] = sb("mv", [128, 4, 8])
        fsc = sb("fsc", [128, 1])
        PAR = 42000
        par = sb("par", [128, PAR], BF16)
        T["ps"] = es.enter_context(nc.psum_tensor("ps", [128, 8 * 512], F32))
        g["hx"] = X[0:3, 0, :]
        g["hx2"] = X[0:3, 1, :]
        off = [0]

        def cv(n_bf16, dt, shape_str=None, **kw):
            a = par[:, off[0]:off[0] + n_bf16]
            off[0] += (n_bf16 + 15) // 16 * 16
            if dt == F32:
                a = a.bitcast(F32)
            if shape_str:
                a = a.rearrange(shape_str, **kw)
            return a
        g["aTh"] = cv(48, BF16, "p (k t) -> p k t", t=3)
        g["hist"] = cv(48, F32, "p (k t) -> p k t", t=3)
        g["ext"] = cv(2 * (TT + 4), F32)
        g["t1"] = cv(2 * TT, F32)
        g["qkT"] = cv(8 * TT, BF16, "p (k t) -> p k t", t=TT)
        g["wifg"] = cv(KC * 8, BF16)
        g["gsb"] = cv(64, F32, "p (m c) -> p m c", c=8)
        for nm in ("l1", "nb", "nbL", "wv_", "eo", "ec", "eL", "ctmp"):
            g[nm] = cv(32, F32, "p (m c) -> p m c", c=4)
        g["carry"] = cv(8, F32)
        g["gbias"] = cv(2 * (8 + DEPTH), F32)
        g["vaug"] = cv(16 * MW, BF16, "p (m h c) -> p m h c", m=4, h=4)
        g["ktok"] = cv(256, BF16, "p (a c) -> p a c", a=2)
        g["C32"] = cv(8 * MW, F32, "p (h c) -> p h c", h=4)
        g["Cbf"] = cv(4 * MW, BF16, "p (h c) -> p h c", h=4)
        g["vH"] = cv(4096, BF16, "p (m c) -> p m c", m=4)
        for nm in ("sig", "lf", "kk", "bcs", "qs", "ebuf"):
            g[nm] = cv(2 * TT, F32)
        for nm in ("q_in", "k_in", "qdec"):
            g[nm] = cv(TT, BF16)
        g["hsc"] = cv(128, F32, "p (c s) -> p c s", s=8)
        g["hcarry"] = cv(16, F32)
        g["S32"] = cv(2048, F32, "p (h c) -> p h c", h=8)
        g["Sp"] = cv(256, BF16, "p (a c) -> p a c", a=2)
        g["gts"] = cv(2 * TT, BF16, "p (a c) -> p a c", a=2)
        g["rowsb"] = cv(2 * 768, F32)
        g["OMs"] = cv(2 * 16 * MW, F32, "p (m c) -> p m c", m=4)
        print("P1 arena", off[0])
        assert off[0] <= PAR, off[0]
        off[0] = 0
        g["arena"] = cv(FC * TT, BF16)
        T["hT"] = g["arena"].rearrange("p (j t) -> p j t", t=TT)
        T["bc"] = cv(2 * 3 * D, F32, "p (i d) -> p i d", i=3)
        T["sg"] = cv(2 * 1024, F32)
        T["tmp"] = cv(2 * 1024, F32)
        g["Sst"] = cv(1024, BF16, "p (h c) -> p h c", h=8)
        g["Cst"] = cv(4 * MW, BF16, "p (h c) -> p h c", h=4)
        g["EMt"] = cv(32, F32, "p (m c) -> p m c", c=4)
        g["hst"] = cv(64, F32, "p (h c) -> p h c", c=4)
        g["mst"] = cv(128, F32, "p (h c) -> p h c", c=16)
        print("P2 arena", off[0])
        assert off[0] <= PAR, off[0]
        rec = None
        for pas in range(2):
            S = Sched(nc)
            ws = WStream(S, wbuf, record=rec)
            _emit_prologue(S, T, g)
            for l in range(depth):
                g["xsrc"] = x_in if l == 0 else g["XN"]
                g["wmod"] = wmod[l * 24:(l + 1) * 24]
                g["w1fm"] = w1fm[l * 6:(l + 1) * 6]
                g["w1tm"] = w1tm[l * 8:(l + 1) * 8]
                g["w1ifg"] = w1ifg[l * 128:(l + 1) * 128, :]
                g["wout"] = wout[l * 4:(l + 1) * 4]
                g["wgu"] = wgu[l * (FC // 2):(l + 1) * (FC // 2)]
                g["wd"] = wd[l * 4:(l + 1) * 4]
                g["vecs"] = vecs[l * 128:(l + 1) * 128, :]
                g["grow"] = grow[l:l + 1, :]
                _emit_p1(S, ws, nc, T, g, NT, l)
                S.fence(fsc[:])
                _emit_p2(S, ws, nc, T, g, NT, l, l == depth - 1)
                S.fence(fsc[:])
            S.barrier_all("sp")
            rec = ws.seq
        S.run()
    return nc


def _fm_block(W, col, width=128):
    return W[:, col:col + width].reshape(KC, 128, width).transpose(1, 0, 2)


def _fm16(v):
    return np.ascontiguousarray(v.reshape(-1, 128).T)


def _layer_arrays(inp, l, depth):
    W = inp["w_in"][l]
    fm_tiles = []
    fm_tiles.append(np.stack([_fm_block(W, h * 128) for h in range(4)], axis=1))
    fm_tiles.append(np.stack([_fm_block(W, 512 + h * 128) for h in range(4)], axis=1))
    for hp in range(4):
        blks = []
        for hh in range(2):
            h = hp * 2 + hh
            blks += [_fm_block(W, 3080 + h * 128), _fm_block(W, 4104 + h * 128)]
        fm_tiles.append(np.stack(blks, axis=1))
    w1fm = np.ascontiguousarray(np.stack(fm_tiles)).reshape(6, 128, WSLOT)
    tm_cols = [1024, 1536, 5128, 5640, 2048, 2560, 6152, 6664]
    w1tm = np.ascontiguousarray(np.stack([_fm_block(W, c, 512) for c in tm_cols])).reshape(8, 128, WSLOT)
    w1ifg = np.ascontiguousarray(_fm_block(W, 3072, 8)).reshape(128, KC * 8)
    Wm = inp["w_mod"][l]
    wmod = np.ascontiguousarray(np.stack([np.stack([_fm_block(Wm, t * 512 + b * 128) for b in range(4)], axis=1)
                                          for t in range(24)])).reshape(24, 128, WSLOT)
    Wo = inp["w_out"][l]
    wout = np.ascontiguousarray(np.stack([_fm_block(Wo, n * 512, 512) for n in range(4)])).reshape(4, 128, WSLOT)
    g4 = inp["w_gate"][l].reshape(KC, 128, FC // 2, 2, 128)
    u4 = inp["w_up"][l].reshape(KC, 128, FC // 2, 2, 128)
    gu = np.stack([g4, u4], axis=0)
    wgu = np.ascontiguousarray(gu.transpose(3, 2, 4, 0, 1, 5)).reshape(FC // 2, 128, WSLOT)
    wd = np.ascontiguousarray(inp["w_down"][l].reshape(4, 11, 128, 4, 512).transpose(3, 0, 2, 1, 4)).reshape(4, 4, 128, 11 * 512)
    vecs = np.zeros((128, V_IN), np.float32)
    vecs[:, V_BMOD:V_BMOD + 96] = _fm16(inp["b_mod"][l])
    if l == 0:
        vecs[:, V_LNPG:V_LNPG + 16] = np.ones((128, 16), np.float32)
    else:
        vecs[:, V_LNPG:V_LNPG + 16] = _fm16(inp["ln2_g"][l - 1])
        vecs[:, V_LNPB:V_LNPB + 16] = _fm16(inp["ln2_b"][l - 1])
    vecs[:, V_LN1G:V_LN1G + 16] = _fm16(inp["ln1_g"][l])
    vecs[:, V_LN1B:V_LN1B + 16] = _fm16(inp["ln1_b"][l])
    vecs[:, V_CONVW:V_CONVW + 32] = inp["conv_w"][l].reshape(4, 8, 128).transpose(2, 1, 0).reshape(128, 32)
    vecs[:, V_CONVB:V_CONVB + 8] = _fm16(inp["conv_b"][l])
    vecs[:, V_LBL:V_LBL + 8 * depth] = inp["lb_logits"].reshape(depth, 8, 128).transpose(2, 1, 0).reshape(128, 8 * depth)
    vecs[:, V_NRMW:V_NRMW + 16] = _fm16(np.concatenate([inp["mlstm_norm_w"][l], inp["hgrn_norm_w"][l]]))
    grow = np.zeros((1, 8 + DEPTH), np.float32)
    grow[0, 0:4] = inp["b_igate"][l]
    grow[0, 4:8] = inp["b_fgate"][l]
    for i in range(1, l + 1):
        grow[0, 8 + i] = np.float32(1.0)
    return dict(w1fm=w1fm, w1tm=w1tm, w1ifg=w1ifg, wmod=wmod, wout=wout, wgu=wgu, wd=wd, vecs=vecs, grow=grow)


_PROGS = {}


def kernel(**inputs):
    inp = {k: np.asarray(v) for k, v in inputs.items()}
    x = inp["x"]
    B, T_, _ = x.shape
    depth = inp["w_in"].shape[0]
    SEG = NCORE // B
    TOK = T_ // SEG
    key = (TOK, depth, DEPTH)
    if key not in _PROGS:
        _PROGS[key] = build_fused(TOK, depth)
    nc = _PROGS[key]
    LA = [_layer_arrays(inp, l, depth) for l in range(depth)]
    shared = {k: np.ascontiguousarray(np.concatenate([LA[l][k] for l in range(depth)], axis=0))
              for k in ("wmod", "w1fm", "w1tm", "w1ifg", "wout", "wgu", "wd", "vecs", "grow")}
    del LA
    shared["identd"] = np.eye(128, dtype=np.float32)
    shared["trid"] = np.triu(np.ones((128, 128), np.uint8))
    shared["fin"] = np.ascontiguousarray(np.stack([inp["ln2_g"][depth - 1], inp["ln2_b"][depth - 1]]).astype(np.float32))
    in_maps = []
    for i in range(NCORE):
        b, sgi = i // SEG, i % SEG
        sel = np.zeros((128, 4), np.float32)
        sel[:, sgi] = 1.0
        selp = np.zeros((128, 4), np.float32)
        if sgi > 0:
            selp[:, sgi - 1] = 1.0
        halo = np.zeros((3, D), np.float32) if sgi == 0 else np.ascontiguousarray(x[b, sgi * TOK - 3:sgi * TOK, :])
        m = dict(shared)
        m.update(x_in=np.ascontiguousarray(x[b, sgi * TOK:(sgi + 1) * TOK, :]), halo0=halo,
                 hflag=np.full((128, 1), 0.0 if sgi == 0 else 1.0, np.float32), c_fm=_fm16(inp["c"][b]), sel=sel, selprev=selp)
        in_maps.append(m)
    res = run_bass_kernel_spmd(nc, in_maps, core_ids=list(range(NCORE))).results
    y = np.empty((B, T_, D), np.float32)
    for i in range(NCORE):
        y[i // SEG, (i % SEG) * TOK:(i % SEG + 1) * TOK, :] = res[i]["OUT"]
    return y


_DEBUG = None
```

```python
import contextlib
import math
import numpy as np
import ml_dtypes
import concourse.bass as bass
import concourse.mybir as mybir
from concourse.bass_utils import run_bass_kernel_spmd

F32 = mybir.dt.float32
BF16 = mybir.dt.bfloat16
U8 = mybir.dt.uint8
AF = mybir.ActivationFunctionType
ALU = mybir.AluOpType
AX = mybir.AxisListType

D = 2048
DFF = 5632
KC = D // 128
FC = DFF // 128
TT = 512
EPS = 1e-5
DEPTH = 4
NCORE = 8
ALPHA = (2 * DEPTH) ** 0.25
WSLOT = 8192
NWS = 3
MW = 260
NST = 2304
NPC = NST // 256
LNSC = math.log(128.0 ** -0.5)

NO_CC = False
PIPE_ML = True
SKIP_ML = False
SKIP_HG = False
PIPE_HG = True
ENGS = ("pe", "act", "dve", "pool", "sp")
SEM_EPOCH = 30000
NDS = 20


class _Op:
    __slots__ = ("eng", "fn", "waits", "is_dma", "needed", "sem", "val")

    def __init__(self, eng, fn, is_dma):
        self.eng = eng
        self.fn = fn
        self.is_dma = is_dma
        self.waits = []
        self.needed = False
        self.sem = None
        self.val = None


class Sched:
    def __init__(self, nc):
        self.nc = nc
        self.q = {e: [] for e in ENGS}
        self.last_w = {}
        self.readers = {}
        self.all_ops = []

    def op(self, eng, fn, reads=(), writes=(), dma=False, nofence=False):
        o = _Op(eng, fn, dma)
        deps = []
        seen = set()
        if not nofence:
            reads = list(reads) + ["__fence__"]

        def add(d):
            if d is None or id(d) in seen:
                return
            seen.add(id(d))
            deps.append(d)
        for k in reads:
            add(self.last_w.get(k))
        for k in writes:
            add(self.last_w.get(k))
            for r in self.readers.get(k, ()):
                add(r)
        for d in deps:
            if d.eng == "pe" and eng == "pe" and not d.is_dma and not dma:
                continue
            d.needed = True
            o.waits.append(d)
        for k in reads:
            self.readers.setdefault(k, []).append(o)
        for k in writes:
            self.last_w[k] = o
            self.readers[k] = []
        self.q[eng].append(o)
        self.all_ops.append(o)
        return o

    def fence(self, scratch):
        self.op("dve", lambda e: e.memset(scratch, 0.0), writes=["__fence__"], nofence=True)

    def barrier_all(self, eng="sp"):
        o = _Op(eng, None, False)
        for d in self.all_ops:
            if d.is_dma:
                d.needed = True
                o.waits.append(d)
        self.q[eng].append(o)
        self.all_ops.append(o)

    def run(self):
        nc = self.nc
        cnt = {e: 0 for e in ENGS}
        di = {e: 0 for e in ENGS}
        dlast = {}
        dcount = {}
        for o in self.all_ops:
            if not o.needed:
                continue
            e = o.eng
            if o.is_dma == "cc":
                o.sem, o.val = f"cc_{di[e]}", 1
                di[e] += 1
            elif o.is_dma:
                name = f"d_{e}_{di[e] % NDS}"
                di[e] += 1
                prev = dlast.get(name)
                if prev is not None and prev not in o.waits:
                    o.waits.append(prev)
                dcount[name] = dcount.get(name, 0) + 16
                o.sem, o.val = name, dcount[name]
                dlast[name] = o
            else:
                n = cnt[e]
                o.sem = f"c_{e}_{n // SEM_EPOCH}"
                o.val = n % SEM_EPOCH + 1
                cnt[e] = n + 1
        names = sorted({o.sem for o in self.all_ops if o.sem is not None})
        for o in self.all_ops:
            best = {}
            for d in o.waits:
                if d.sem not in best or d.val > best[d.sem].val:
                    best[d.sem] = d
            o.waits = list(best.values())
        with contextlib.ExitStack() as es:
            S = {n: es.enter_context(nc.semaphore(n)) for n in names}
            block = es.enter_context(nc.Block())

            def body(eng_name):
                def _f(eng):
                    known = {}
                    for o in self.q[eng_name]:
                        for d in o.waits:
                            if known.get(d.sem, 0) >= d.val:
                                continue
                            eng.wait_ge(S[d.sem], d.val)
                            known[d.sem] = d.val
                        if o.fn is None:
                            continue
                        ins = o.fn(eng)
                        if o.needed:
                            ins.then_inc(S[o.sem], 16 if (o.is_dma and o.is_dma != "cc") else 1)
                return _f
            block.tensor(body("pe"))
            block.scalar(body("act"))
            block.vector(body("dve"))
            block.gpsimd(body("pool"))
            block.sync(body("sp"))
        return len(names)


class WStream:
    def __init__(self, S, buf, record=None):
        self.S = S
        self.buf = buf
        self.seq = [] if record is None else record
        self.recording = record is None
        self.k = 0
        self.issued = 0

    def _issue(self, j):
        ap, n = self.seq[j]
        slot = j % NWS
        dst = self.buf[:, slot * WSLOT: slot * WSLOT + n]
        self.S.op("pool", lambda e, dst=dst, ap=ap: e.dma_start(out=dst, in_=ap),
                  writes=[("w", slot)], dma=True, nofence=True)

    def get(self, ap, n):
        if self.recording:
            self.seq.append((ap, n))
            return self.buf[:, 0:n], ("w", 0)
        k = self.k
        while self.issued < min(len(self.seq), k + NWS):
            self._issue(self.issued)
            self.issued += 1
        self.k += 1
        slot = k % NWS
        return self.buf[:, slot * WSLOT: slot * WSLOT + n], ("w", slot)


def _bank(ps, b):
    return ps[:, b * 512:(b + 1) * 512]


def emit_transpose_affine(S, T, src_key, gcol, bcol, extra_reads=()):
    X, aT, ps, ident, vf = T["X"], T["aT"], T["ps"], T["ident"], T["vf"]
    for kc in range(KC):
        bank = 4 + (kc % 2)
        pst = _bank(ps, bank)
        for m in range(4):
            S.op("pe", lambda e, pst=pst, m=m, kc=kc: e.transpose(
                pst[:, m * 128:(m + 1) * 128], X[:, m, kc * 128:(kc + 1) * 128], ident[:]),
                reads=[src_key, "ident"], writes=[("ps", bank)])
        S.op("act", lambda e, pst=pst, kc=kc: e.activation(
            out=aT[:, kc, :], in_=pst, func=AF.Identity,
            scale=vf[:, gcol + kc:gcol + kc + 1], bias=vf[:, bcol + kc:bcol + kc + 1]),
            reads=[("ps", bank), "vf", *extra_reads], writes=[("aT", kc)])


def emit_resid_prep(S, T, gi, bi):
    X, bc = T["X"], T["bc"]
    for m in range(4):
        S.op("pool", lambda e, m=m: e.tensor_tensor(out=X[:, m, :], in0=X[:, m, :], in1=bc[:, gi, :], op=ALU.mult),
             reads=[("bc", gi)], writes=["X"])
        S.op("pool", lambda e, m=m: e.tensor_tensor(out=X[:, m, :], in0=X[:, m, :], in1=bc[:, bi, :], op=ALU.add),
             reads=[("bc", bi)], writes=["X"])


def emit_epilogue_block(S, T, m, n, gy):
    X, bc, ps, tmp = T["X"], T["bc"], T["ps"], T["tmp"]
    pm = _bank(ps, m)
    tq = tmp[:, (m % 2) * 512:(m % 2 + 1) * 512]
    S.op("dve", lambda e: e.tensor_tensor(out=tq, in0=pm, in1=bc[:, gy, n * 512:(n + 1) * 512], op=ALU.mult),
         reads=[("ps", m), ("bc", gy)], writes=[("tmp", m % 2)])
    S.op("pool", lambda e: e.tensor_tensor(out=X[:, m, n * 512:(n + 1) * 512],
                                           in0=X[:, m, n * 512:(n + 1) * 512], in1=tq, op=ALU.add),
         reads=[("tmp", m % 2)], writes=["X"])


def emit_ln_inplace(S, T):
    X, st, mv, epsb = T["X"], T["st"], T["mv"], T["epsb"]
    for m in range(4):
        for i in range(4):
            S.op("dve", lambda e, m=m, i=i: e.bn_stats(out=st[:, m, i, :], in_=X[:, m, i * 512:(i + 1) * 512]),
                 reads=["X"], writes=[("st", m)])
        S.op("dve", lambda e, m=m: e.bn_aggr(out=mv[:, m, 0:2], in_=st[:, m, :, :].rearrange("p a b -> p (a b)")),
             reads=[("st", m)], writes=[("mv", m)])
        S.op("act", lambda e, m=m: e.activation(out=mv[:, m, 2:3], in_=mv[:, m, 1:2], func=AF.Sqrt, bias=epsb[:, 0:1]),
             reads=[("mv", m), "epsb"], writes=[("mv", m)])
        S.op("dve", lambda e, m=m: e.reciprocal(out=mv[:, m, 3:4], in_=mv[:, m, 2:3]),
             reads=[("mv", m)], writes=[("mv", m)])
        S.op("dve", lambda e, m=m: e.scalar_tensor_tensor(
            out=mv[:, m, 4:5], in0=mv[:, m, 0:1], scalar=-1.0, in1=mv[:, m, 3:4], op0=ALU.mult, op1=ALU.mult),
            reads=[("mv", m)], writes=[("mv", m)])
        S.op("act", lambda e, m=m: e.activation(out=X[:, m, :], in_=X[:, m, :], func=AF.Identity,
                                                scale=mv[:, m, 3:4], bias=mv[:, m, 4:5]),
             reads=[("mv", m)], writes=["X"])


def emit_ffn(S, ws, T, wgu, wd, gy):
    aT, hT, ps, sg = T["aT"], T["hT"], T["ps"], T["sg"]
    for j2 in range(FC // 2):
        wt, wkey = ws.get(wgu[j2], 2 * 2 * KC * 128)
        wv = wt.rearrange("p (b g k c) -> p b g k c", b=2, g=2, k=KC)
        for b in range(2):
            j = j2 * 2 + b
            bg = 4 + (j % 2) * 2
            bu = bg + 1
            pg, pu = _bank(ps, bg), _bank(ps, bu)
            for g, pp, bk in ((0, pg, bg), (1, pu, bu)):
                for kc in range(KC):
                    S.op("pe", lambda e, pp=pp, wv=wv, b=b, g=g, kc=kc: e.matmul(
                        pp, wv[:, b, g, kc, :], aT[:, kc, :], start=(kc == 0), stop=(kc == KC - 1)),
                        reads=[wkey, ("aT", kc)], writes=[("ps", bk)])
            sgt = sg[:, (j % 2) * 512:(j % 2 + 1) * 512]
            S.op("act", lambda e, sgt=sgt, pg=pg: e.activation(out=sgt, in_=pg, func=AF.Silu),
                 reads=[("ps", bg)], writes=[("sg", j % 2)])
            S.op("dve", lambda e, sgt=sgt, pu=pu, j=j: e.tensor_tensor(out=hT[:, j, :], in0=pu, in1=sgt, op=ALU.mult),
                 reads=[("ps", bu), ("sg", j % 2)], writes=[("hT", j)])
    emit_resid_prep(S, T, 3, 4)
    for n in range(4):
        for jg in range(4):
            wt, wkey = ws.get(wd[n, jg], 11 * 512)
            wv = wt.rearrange("p (j c) -> p j c", j=11)
            for m in range(4):
                pm = _bank(ps, m)
                for jj in range(11):
                    j = jg * 11 + jj
                    S.op("pe", lambda e, pm=pm, wv=wv, jj=jj, j=j, m=m: e.matmul(
                        pm, hT[:, j, m * 128:(m + 1) * 128], wv[:, jj, :], start=(j == 0), stop=(j == FC - 1)),
                        reads=[wkey, ("hT", j)], writes=[("ps", m)])
        for m in range(4):
            emit_epilogue_block(S, T, m, n, gy)
    emit_ln_inplace(S, T)


V_BMOD = 0
V_LNPG = 96
V_LNPB = 112
V_LN1G = 128
V_LN1B = 144
V_CONVW = 160
V_CONVB = 192
V_LBL = 200
V_NRMW = 232
V_IN = 248
V_MOD = 256
V_GU1 = 352
V_BU1 = 368
V_GU2 = 384
V_BU2 = 400
V_ROWS = 416
V_LB = 512
V_OML = 520
V_TOT = 544


def _emit_prologue(S, T, g):
    ident, identb, tri, utri, ones, m0 = g["ident"], g["identb"], g["tri"], g["utri"], g["ones"], g["m0"]

    def dma_in(dst, src, key, eng="sp"):
        S.op(eng, lambda e: e.dma_start(out=dst, in_=src), writes=[key], dma=True)
    dma_in(ident[:], g["identd"], "ident")
    dma_in(identb[:], g["identd"], "identb", eng="pool")
    dma_in(tri[:], g["trid"], "tri")
    dma_in(g["cact"][:], g["c_fm"], "cact")
    dma_in(g["hfl"][:], g["hflag"], "hfl")
    dma_in(g["selt"][:], g["sel"], "selt")
    dma_in(g["selp"][:], g["selprev"], "selp")
    S.op("dve", lambda e: e.tensor_copy(out=utri[:], in_=tri[:]), reads=["tri"], writes=["utri"])
    S.op("dve", lambda e: e.memset(ones[:], 1.0), writes=["ones"])
    S.op("dve", lambda e: e.memset(g["lnsc"][:], LNSC), writes=["lnsc"])
    S.op("dve", lambda e: e.memset(g["epsb"][:], EPS), writes=["epsb"])
    S.op("dve", lambda e: e.memset(m0[:], 1.0), writes=["m0"])
    S.op("dve", lambda e: e.memset(m0[:].rearrange("p (c t) -> p c t", t=64)[:, :, 0:1], 0.0), writes=["m0"])
    S.op("dve", lambda e: e.memset(g["A_sb4"][:], 0.0), writes=[("A_sb4", p, m) for p in range(2) for m in range(4)])
    S.op("dve", lambda e: e.memset(g["A_h8"][:], 0.0), writes=[("A_h8", p, c) for p in range(2) for c in range(8)])
    S.op("act", lambda e: e.activation(out=g["cactb"][:], in_=g["cact"][:], func=AF.Silu), reads=["cact"], writes=["cactb"])


def _emit_p1(S, ws, nc, T, g, NT, l):
    X, aT, ps, vf = T["X"], T["aT"], T["ps"], T["vf"]
    ident, identb, tri, utri, ones, m0 = g["ident"], g["identb"], g["tri"], g["utri"], g["ones"], g["m0"]

    def dma_in(dst, src, key, eng="sp"):
        S.op(eng, lambda e: e.dma_start(out=dst, in_=src), writes=[key], dma=True)
    dma_in(vf[:, 0:V_IN], g["vecs"], "vf")
    dma_in(g["gbias"][:], g["grow"].partition_broadcast(128), "gbias")
    dma_in(g["wifg"][:], g["w1ifg"], "wifg", eng="pool")
    S.op("dve", lambda e: e.memset(g["C32"][:], 0.0), writes=[("C32", h) for h in range(4)])
    S.op("dve", lambda e: e.memset(g["S32"][:], 0.0), writes=[("S32", h) for h in range(8)])
    S.op("dve", lambda e: e.memset(g["carry"][:], 0.0), writes=["carry"])
    S.op("dve", lambda e: e.memset(g["hcarry"][:], 0.0), writes=["hcarry"])
    S.op("dve", lambda e: e.memset(g["vaug"][:], 0.0), writes=[("vaug", m) for m in range(4)])
    cact, cactb = g["cact"], g["cactb"]
    psm = _bank(ps, 7)
    for t in range(24):
        wt, wkey = ws.get(g["wmod"][t], WSLOT)
        wv = wt.rearrange("p (b k c) -> p b k c", b=4, k=KC)
        for b in range(4):
            j = t * 4 + b
            for kc in range(KC):
                S.op("pe", lambda e, wv=wv, b=b, kc=kc, j=j: e.matmul(
                    psm[:, j:j + 1], wv[:, b, kc, :], cactb[:, kc:kc + 1], start=(kc == 0), stop=(kc == KC - 1)),
                    reads=[wkey, "cactb"], writes=[("ps", 7)])
    S.op("dve", lambda e: e.tensor_tensor(out=vf[:, V_MOD:V_MOD + 96], in0=psm[:, 0:96], in1=vf[:, V_BMOD:V_BMOD + 96], op=ALU.add),
         reads=[("ps", 7), "vf"], writes=["vf"])

    def vop(fn):
        S.op("dve", fn, writes=["vf"])
    SH1, SC1, G1, SH2, SC2, G2 = (V_MOD + 16 * i for i in range(6))

    def c(off):
        return vf[:, off:off + 16]
    for (lg, lb_, sc, sh, gu, bu) in ((V_LNPG, V_LNPB, SC1, SH1, V_GU1, V_BU1), (V_LN1G, V_LN1B, SC2, SH2, V_GU2, V_BU2)):
        vop(lambda e, lg=lg, sc=sc, gu=gu: e.scalar_tensor_tensor(out=c(gu), in0=c(sc), scalar=1.0, in1=c(lg), op0=ALU.add, op1=ALU.mult))
        vop(lambda e, lb_=lb_, sc=sc, bu=bu: e.scalar_tensor_tensor(out=c(bu), in0=c(sc), scalar=1.0, in1=c(lb_), op0=ALU.add, op1=ALU.mult))
        vop(lambda e, sh=sh, bu=bu: e.tensor_tensor(out=c(bu), in0=c(bu), in1=c(sh), op=ALU.add))
    for i, (lg, lb_, gg) in enumerate(((V_LNPG, V_LNPB, G1), (V_LN1G, V_LN1B, G2))):
        r0 = V_ROWS + 48 * i
        vop(lambda e, lg=lg, r0=r0: e.tensor_scalar(out=c(r0), in0=c(lg), scalar1=ALPHA, scalar2=None, op0=ALU.mult))
        vop(lambda e, lb_=lb_, r0=r0: e.tensor_scalar(out=c(r0 + 16), in0=c(lb_), scalar1=ALPHA, scalar2=None, op0=ALU.mult))
        vop(lambda e, gg=gg, r0=r0: e.tensor_scalar(out=c(r0 + 32), in0=c(gg), scalar1=1.0, scalar2=None, op0=ALU.add))
    lbl = vf[:, V_LBL:V_LBL + 8 * DEPTH].rearrange("p (h l) -> p h l", l=DEPTH)
    sm = g["hsc2"][:, 0]
    vop(lambda e: e.tensor_reduce(out=sm[:, :, 0], in_=lbl, axis=AX.X, op=ALU.max))
    vop(lambda e: e.tensor_tensor(out=lbl, in0=lbl, in1=sm[:, :, 0:1].broadcast_to([128, 8, DEPTH]), op=ALU.subtract))
    S.op("act", lambda e: e.activation(out=lbl, in_=lbl, func=AF.Exp), reads=["vf"], writes=["vf"])
    vop(lambda e: e.tensor_reduce(out=sm[:, :, 0], in_=lbl, axis=AX.X, op=ALU.add))
    vop(lambda e: e.reciprocal(out=sm[:, :, 1], in_=sm[:, :, 0]))
    S.op("dve", lambda e: e.tensor_tensor(out=lbl, in0=lbl, in1=g["gbias"][:, 8:8 + DEPTH].unsqueeze(1).broadcast_to([128, 8, DEPTH]), op=ALU.mult),
         reads=["gbias"], writes=["vf"])
    vop(lambda e: e.tensor_reduce(out=sm[:, :, 0], in_=lbl, axis=AX.X, op=ALU.add))
    vop(lambda e: e.tensor_tensor(out=vf[:, V_LB:V_LB + 8], in0=sm[:, :, 0], in1=sm[:, :, 1], op=ALU.mult))
    vop(lambda e: e.tensor_scalar(out=vf[:, V_OML:V_OML + 8], in0=vf[:, V_LB:V_LB + 8], scalar1=-1.0, scalar2=1.0, op0=ALU.mult, op1=ALU.add))
    prw = _bank(ps, 5)
    rowsb = g["rowsb"]
    for k in range(6):
        bk = 5 if k < 4 else 4
        dst = _bank(ps, bk)[0:16, (k % 4) * 128:(k % 4 + 1) * 128]
        S.op("pe", lambda e, k=k, dst=dst: e.transpose(dst, vf[:, V_ROWS + 16 * k:V_ROWS + 16 * k + 16], ident[:]),
             reads=["vf", "ident"], writes=[("ps", bk)])
    S.op("act", lambda e: e.activation(out=rowsb[0:16, 0:512], in_=prw[0:16, 0:512], func=AF.Copy), reads=[("ps", 5)], writes=["rowsb"])
    S.op("act", lambda e: e.activation(out=rowsb[0:16, 512:768], in_=_bank(ps, 4)[0:16, 0:256], func=AF.Copy), reads=[("ps", 4)], writes=["rowsb"])
    S.op("sp", lambda e: e.dma_start(out=g["ROWS"].rearrange("k (c p) -> c k p", p=128), in_=rowsb[0:16, :].rearrange("c (k p) -> c k p", p=128)),
         reads=["rowsb"], writes=["ROWS"], dma=True)

    hx, aTh, hfl = g["hx"], g["aTh"], g["hfl"]
    if l == 0:
        dma_in(hx, g["halo0"], "X")
    else:
        hx2, selp = g["hx2"], g["selp"]
        for k in range(4):
            S.op("sp", lambda e, k=k: e.dma_start(out=hx2, in_=g["HALO_DST"][k * 128:k * 128 + 24, :].rearrange("(r c) w -> r (c w)", r=3)), reads=["HALO_DST"], writes=["hx2"], dma=True)
            if k == 0:
                S.op("dve", lambda e: e.tensor_scalar(out=hx, in0=hx2, scalar1=selp[0:3, 0:1], scalar2=None, op0=ALU.mult),
                     reads=["hx2", "selp"], writes=["X"])
            else:
                S.op("dve", lambda e, k=k: e.scalar_tensor_tensor(out=hx, in0=hx2, scalar=selp[0:3, k:k + 1], in1=hx, op0=ALU.mult, op1=ALU.add),
                     reads=["hx2", "selp"], writes=["X"])
    psh = _bank(ps, 6)
    for kc in range(KC):
        S.op("pe", lambda e, kc=kc: e.transpose(psh[:, kc * 4:kc * 4 + 3], hx[:, kc * 128:(kc + 1) * 128], ident[0:3, 0:3]),
             reads=["X", "ident"], writes=[("ps", 6)])
    for kc in range(KC):
        S.op("act", lambda e, kc=kc: e.activation(out=aTh[:, kc, :], in_=psh[:, kc * 4:kc * 4 + 3], func=AF.Identity,
                                                  scale=vf[:, V_GU1 + kc:V_GU1 + kc + 1], bias=vf[:, V_BU1 + kc:V_BU1 + kc + 1]),
             reads=[("ps", 6), "vf"], writes=["aTh"])

    hist, ext, t1, qkT = g["hist"], g["ext"], g["t1"], g["qkT"]
    gsb, l1, nb, nbL, wv_, eo, ec, eL, carry, ctmp = (g[k] for k in ("gsb", "l1", "nb", "nbL", "wv_", "eo", "ec", "eL", "carry", "ctmp"))
    vaug, C32, vH = (g[k] for k in ("vaug", "C32", "vH"))
    sig, kk, bcs, qs, ebuf, qdec = (g[k] for k in ("sig", "kk", "bcs", "qs", "ebuf", "qdec"))
    hcarry, S32, gts, wifg, gbias = (g[k] for k in ("hcarry", "S32", "gts", "wifg", "gbias"))
    OHs = X
    xsrc = g["xsrc"]
    ppi = [0]

    def pp_next():
        b = ppi[0] % 2
        ppi[0] += 1
        return b

    for t in range(NT):
        tok0 = t * TT
        S.op("sp", lambda e, tok0=tok0, xsrc=xsrc: e.dma_start(
            out=X[:], in_=xsrc[tok0:tok0 + TT, :].rearrange("(m p) d -> p m d", p=128)),
            reads=["XN"], writes=["X"], dma=True)
        emit_transpose_affine(S, T, "X", V_GU1, V_BU1)
        pg = _bank(ps, 2)
        wg3 = wifg[:].rearrange("p (k c) -> p k c", c=8)
        for m in range(4):
            for kc in range(KC):
                S.op("pe", lambda e, m=m, kc=kc: e.matmul(pg[:, m * 8:(m + 1) * 8], aT[:, kc, m * 128:(m + 1) * 128], wg3[:, kc, :],
                                                          start=(kc == 0), stop=(kc == KC - 1)),
                     reads=[("aT", kc), "wifg"], writes=[("ps", 2)])
        S.op("dve", lambda e: e.tensor_tensor(out=gsb[:], in0=pg[:, 0:32].rearrange("p (m c) -> p m c", c=8),
                                              in1=gbias[:, 0:8].unsqueeze(1).broadcast_to([128, 4, 8]), op=ALU.add),
             reads=[("ps", 2), "gbias"], writes=["gsb"])
        S.op("act", lambda e: e.activation(out=l1[:], in_=gsb[:, :, 4:8], func=AF.Exp, scale=-1.0), reads=["gsb"], writes=["l1"])
        S.op("act", lambda e: e.activation(out=l1[:], in_=l1[:], func=AF.Ln, bias=1.0), writes=["l1"])
        pc = _bank(ps, 2)
        for m in range(4):
            S.op("pe", lambda e, m=m: e.matmul(pc[:, 64 + m * 4:64 + m * 4 + 4], utri[:], l1[:, m, :], start=True, stop=True),
                 reads=["utri", "l1"], writes=[("ps", 2)])
            S.op("pe", lambda e, m=m: e.matmul(pc[:, 96 + m * 4:96 + m * 4 + 4], ones[:], l1[:, m, :], start=True, stop=True),
                 reads=["ones", "l1"], writes=[("ps", 2)])
        S.op("dve", lambda e: e.tensor_copy(out=nb[:], in_=pc[:, 64:80].rearrange("p (m c) -> p m c", c=4)), reads=[("ps", 2)], writes=["nb"])
        S.op("dve", lambda e: e.tensor_copy(out=nbL[:], in_=pc[:, 96:112].rearrange("p (m c) -> p m c", c=4)), reads=[("ps", 2)], writes=["nbL"])
        S.op("dve", lambda e: e.tensor_tensor(out=ctmp[:], in0=gsb[:, :, 0:4], in1=nb[:], op=ALU.add), reads=["gsb", "nb"], writes=["ctmp"])
        S.op("act", lambda e: e.activation(out=wv_[:], in_=ctmp[:], func=AF.Exp), reads=["ctmp"], writes=["wv"])
        S.op("act", lambda e: e.activation(out=eo[:], in_=nb[:], func=AF.Exp, scale=-1.0, bias=g["lnsc"][:, 0:1]), reads=["nb", "lnsc"], writes=["eo"])
        S.op("act", lambda e: e.activation(out=eL[:], in_=nbL[:], func=AF.Exp, scale=-1.0), reads=["nbL"], writes=["eL"])
        for m in range(4):
            S.op("act", lambda e, m=m: e.activation(out=ctmp[:, m, :], in_=carry[:], func=AF.Exp), reads=["carry"], writes=["ctmp"])
            S.op("dve", lambda e, m=m: e.tensor_tensor(out=ec[:, m, :], in0=eo[:, m, :], in1=ctmp[:, m, :], op=ALU.mult),
                 reads=["eo", "ctmp"], writes=["ec"])
            S.op("dve", lambda e, m=m: e.tensor_tensor(out=carry[:], in0=carry[:], in1=nbL[:, m, :], op=ALU.subtract),
                 reads=["nbL"], writes=["carry"])
        S.op("sp", lambda e, tok0=tok0: e.dma_start(out=g["EM"][tok0:tok0 + TT, :].rearrange("(m p) c -> p m c", p=128), in_=ec[:]),
             reads=["ec"], writes=[("EM", t)], dma=True)
        for qk in range(2):
            wt, wkey = ws.get(g["w1fm"][qk], WSLOT)
            wv4 = wt.rearrange("p (b k c) -> p b k c", b=4, k=KC)
            for b in range(4):
                blk = qk * 4 + b
                bk = pp_next()
                pq = _bank(ps, bk)
                for kc in range(KC):
                    S.op("pe", lambda e, pq=pq, wv4=wv4, b=b, kc=kc: e.matmul(pq, wv4[:, b, kc, :], aT[:, kc, :], start=(kc == 0), stop=(kc == KC - 1)),
                         reads=[wkey, ("aT", kc)], writes=[("ps", bk)])
                if t == 0:
                    ph = _bank(ps, 6)
                    for kc in range(KC):
                        S.op("pe", lambda e, ph=ph, wv4=wv4, b=b, kc=kc, blk=blk: e.matmul(ph[:, 64 + blk * 4:64 + blk * 4 + 3], wv4[:, b, kc, :], aTh[:, kc, :],
                                                                                       start=(kc == 0), stop=(kc == KC - 1)),
                             reads=[wkey, "aTh"], writes=[("ps", 6)])
                    S.op("act", lambda e, ph=ph, blk=blk: e.activation(out=ext[:, 0:3], in_=ph[:, 64 + blk * 4:64 + blk * 4 + 3], func=AF.Copy, scale=g["hfl"][:, 0:1]),
                         reads=[("ps", 6), "hfl"], writes=["ext"])
                else:
                    S.op("act", lambda e, blk=blk: e.activation(out=ext[:, 0:3], in_=hist[:, blk, :], func=AF.Copy),
                         reads=["hist"], writes=["ext"])
                S.op("act", lambda e, pq=pq: e.activation(out=ext[:, 3:TT + 3], in_=pq, func=AF.Copy), reads=[("ps", bk)], writes=["ext"])
                S.op("act", lambda e, blk=blk: e.activation(out=hist[:, blk, :], in_=ext[:, TT:TT + 3], func=AF.Copy), reads=["ext"], writes=["hist"])
                cw = V_CONVW + blk * 4
                S.op("dve", lambda e, cw=cw, blk=blk: e.tensor_scalar(out=t1[:], in0=ext[:, 0:TT], scalar1=vf[:, cw:cw + 1], scalar2=vf[:, V_CONVB + blk:V_CONVB + blk + 1],
                                                                   op0=ALU.mult, op1=ALU.add), reads=["ext", "vf"], writes=["t1"])
                for j in range(1, 4):
                    S.op("dve", lambda e, cw=cw, j=j: e.scalar_tensor_tensor(out=t1[:], in0=ext[:, j:TT + j], scalar=vf[:, cw + j:cw + j + 1], in1=t1[:],
                                                                           op0=ALU.mult, op1=ALU.add), reads=["ext", "vf"], writes=["t1"])
                S.op("act", lambda e, blk=blk: e.activation(out=qkT[:, blk, :], in_=t1[:], func=AF.Silu), reads=["t1"], writes=[("qkT", blk)])
        S.op("sp", lambda e, tok0=tok0: e.dma_start(out=g["QM"][:, tok0:tok0 + TT].rearrange("(h p) t -> p h t", p=128), in_=qkT[:, 0:4, :]),
             reads=[("qkT", b) for b in range(4)], writes=[("QM", t)], dma=True)
        for half in range(2):
            wt, wkey = ws.get(g["w1tm"][half], WSLOT)
            wv3 = wt.rearrange("p (k c) -> p k c", k=KC)
            for m in range(4):
                bk = pp_next()
                pv = _bank(ps, bk)
                for kc in range(KC):
                    S.op("pe", lambda e, pv=pv, wv3=wv3, m=m, kc=kc: e.matmul(pv, aT[:, kc, m * 128:(m + 1) * 128], wv3[:, kc, :], start=(kc == 0), stop=(kc == KC - 1)),
                         reads=[wkey, ("aT", kc)], writes=[("ps", bk)])
                for hh in range(2):
                    h = half * 2 + hh
                    S.op("act", lambda e, pv=pv, m=m, h=h, hh=hh: e.activation(out=vaug[:, m, h, 0:256], in_=pv[:, hh * 256:(hh + 1) * 256], func=AF.Copy,
                                                                            scale=wv_[:, m, h:h + 1]),
                         reads=[("ps", bk), "wv"], writes=[("vaug", m)])
        for m in range(4):
            S.op("dve", lambda e, m=m: e.tensor_copy(out=vaug[:, m, :, 256], in_=wv_[:, m, :]), reads=["wv"], writes=[("vaug", m)])
        pa = _bank(ps, 3)
        ktok4, A_sb4, dCs, Cbf4 = g["ktok4"], g["A_sb4"], g["dCs"], g["Cbf4"]

        def ml_abc(h):
            par = h % 2
            p2 = _bank(ps, 2).bitcast(BF16)
            for m in range(4):
                sl = slice(m * 128, (m + 1) * 128)
                S.op("pe", lambda e, m=m, sl=sl: e.transpose(p2[:, m * 128:(m + 1) * 128], qkT[:, 4 + h, sl], identb[:]),
                     reads=[("qkT", 4 + h), "identb"], writes=[("ps", 2)])
            for m in range(4):
                S.op("act", lambda e, m=m: e.activation(out=ktok4[:, par, m, :], in_=p2[:, m * 128:(m + 1) * 128], func=AF.Copy),
                     reads=[("ps", 2)], writes=[("ktok4", par, m)])
            for m in range(4):
                sl = slice(m * 128, (m + 1) * 128)
                S.op("pe", lambda e, m=m, sl=sl: e.matmul(pa[:, m * 128:(m + 1) * 128], qkT[:, 4 + h, sl], qkT[:, h, sl], start=True, stop=True),
                     reads=[("qkT", 4 + h), ("qkT", h)], writes=[("ps", 3)])
            for m in range(4):
                S.op("dve", lambda e, m=m: e.copy_predicated(out=A_sb4[:, par, m, :], mask=tri[:], data=pa[:, m * 128:(m + 1) * 128]),
                     reads=[("ps", 3), "tri"], writes=[("A_sb4", par, m)])
            for m in range(4):
                bk = 6 + m % 2
                pC = _bank(ps, bk)
                S.op("pe", lambda e, pC=pC, m=m: e.matmul(pC[:, 0:257], ktok4[:, par, m, :], vaug[:, m, h, 0:257], start=True, stop=True),
                     reads=[("ktok4", par, m), ("vaug", m)], writes=[("ps", bk)])
                S.op("act", lambda e, pC=pC, m=m: e.activation(out=dCs[:, m, 0:257], in_=pC[:, 0:257], func=AF.Copy),
                     reads=[("ps", bk)], writes=[("dCs", m)])
            for m in range(4):
                S.op("dve", lambda e, m=m: e.tensor_copy(out=Cbf4[:, par, m, :], in_=C32[:, h, :]), reads=[("C32", h)], writes=[("Cbf4", par, m)])
                S.op("dve", lambda e, m=m: e.tensor_scalar(out=C32[:, h, 0:257], in0=C32[:, h, 0:257], scalar1=eL[:, m, h:h + 1], scalar2=None, op0=ALU.mult),
                     reads=["eL"], writes=[("C32", h)])
                S.op("dve", lambda e, m=m: e.scalar_tensor_tensor(out=C32[:, h, 0:257], in0=dCs[:, m, 0:257], scalar=eL[:, m, h:h + 1], in1=C32[:, h, 0:257],
                                                                 op0=ALU.mult, op1=ALU.add), reads=[("dCs", m), "eL"], writes=[("C32", h)])

        def ml_d(h):
            par = h % 2
            for m in range(4):
                sl = slice(m * 128, (m + 1) * 128)
                bo = 4 + m % 2
                pP = _bank(ps, bo)
                S.op("pe", lambda e, pP=pP, m=m: e.matmul(pP[:, 0:257], A_sb4[:, par, m, :], vaug[:, m, h, 0:257], start=True, stop=False),
                     reads=[("A_sb4", par, m), ("vaug", m)], writes=[("ps", bo)])
                S.op("pe", lambda e, pP=pP, m=m, sl=sl: e.matmul(pP[:, 0:257], qkT[:, h, sl], Cbf4[:, par, m, 0:257], start=False, stop=True),
                     reads=[("qkT", h), ("Cbf4", par, m)], writes=[("ps", bo)])
                r = m % 2
                S.op("act", lambda e, pP=pP, m=m, r=r: e.activation(out=g["OMs"][:, r, 0:257], in_=pP[:, 0:257], func=AF.Copy, scale=eo[:, m, h:h + 1]),
                     reads=[("ps", bo), "eo"], writes=[("OMs", r)])
                S.op("sp", lambda e, m=m, r=r, tok0=tok0: e.dma_start(out=g["OM"][tok0 + m * 128:tok0 + (m + 1) * 128, h * MW:h * MW + 257], in_=g["OMs"][:, r, 0:257]),
                     reads=[("OMs", r)], writes=[("OM", t, m, h)], dma=True)
        for h in range(4):
            if SKIP_ML:
                break
            ml_abc(h)
            if PIPE_ML:
                if h > 0:
                    ml_d(h - 1)
            else:
                ml_d(h)
        if PIPE_ML and not SKIP_ML:
            ml_d(3)
        for half in range(2):
            wt, wkey = ws.get(g["w1tm"][2 + half], WSLOT)
            wv3 = wt.rearrange("p (k c) -> p k c", k=KC)
            for m in range(4):
                bk = pp_next()
                pv = _bank(ps, bk)
                for kc in range(KC):
                    S.op("pe", lambda e, pv=pv, wv3=wv3, m=m, kc=kc: e.matmul(pv, aT[:, kc, m * 128:(m + 1) * 128], wv3[:, kc, :], start=(kc == 0), stop=(kc == KC - 1)),
                         reads=[wkey, ("aT", kc)], writes=[("ps", bk)])
                S.op("act", lambda e, pv=pv, m=m, half=half: e.activation(out=vH[:, m, half * 512:(half + 1) * 512], in_=pv, func=AF.Copy),
                     reads=[("ps", bk)], writes=[("vH", m)])
        A_h8, Sp8 = g["A_h8"], g["Sp8"]
        K6, K7 = ("ps", 6), ("ps", 7)

        def hg_abc(h, par, first):
            hsc = g["hsc2"][:, par]
            q_in = g["q_in2"][:, par, :]
            k_in = g["k_in2"][:, par, :]
            p2 = _bank(ps, 2).bitcast(BF16)
            for m in range(4):
                S.op("pe", lambda e, m=m: e.transpose(p2[:, m * 128:(m + 1) * 128], k_in[:, m * 128:(m + 1) * 128], identb[:]),
                     reads=[("k_in", par), "identb"], writes=[("ps", 2)])
            for m in range(4):
                S.op("act", lambda e, m=m: e.activation(out=ktok4[:, par, m, :], in_=p2[:, m * 128:(m + 1) * 128], func=AF.Copy),
                     reads=[("ps", 2)], writes=[("ktok4", par, m)])
            for cc in range(8):
                m, hf = cc // 2, cc % 2
                c0, p0 = cc * 64, hf * 64
                pA = pa[p0:p0 + 64, m * 64:(m + 1) * 64]
                S.op("pe", lambda e, pA=pA, c0=c0: e.matmul(pA, k_in[:, c0:c0 + 64], q_in[:, c0:c0 + 64], start=True, stop=True),
                     reads=[("k_in", par), ("q_in", par)], writes=[("ps", 3)])
            for cc in range(8):
                m, hf = cc // 2, cc % 2
                p0 = hf * 64
                pA = pa[p0:p0 + 64, m * 64:(m + 1) * 64]
                S.op("dve", lambda e, pA=pA, p0=p0, m=m: e.copy_predicated(out=A_h8[p0:p0 + 64, par, m, :], mask=tri[p0:p0 + 64, p0:p0 + 64], data=pA),
                     reads=[("ps", 3), "tri"], writes=[("A_h8", par, cc)])
            for cc in range(8):
                m, hf = cc // 2, cc % 2
                p0 = hf * 64
                bk = 6 + hf
                pS = _bank(ps, bk)[:, m * 128:(m + 1) * 128]
                S.op("pe", lambda e, pS=pS, p0=p0, m=m: e.matmul(pS, ktok4[p0:p0 + 64, par, m, :], vH[p0:p0 + 64, m, h * 128:(h + 1) * 128], start=True, stop=True),
                     reads=[("ktok4", par, m), ("vH", m)], writes=[("ps", bk)])
            for cc in range(8):
                bk = 6 + cc % 2
                pS = _bank(ps, bk)[:, (cc // 2) * 128:(cc // 2 + 1) * 128]
                S.op("dve", lambda e, cc=cc: e.tensor_scalar(out=Sp8[:, par, cc, :], in0=S32[:, h, :], scalar1=hsc[:, cc, 2:3], scalar2=None, op0=ALU.mult),
                     reads=[("S32", h), ("hsc", par)], writes=[("Sp8", par, cc)])
                S.op("dve", lambda e, cc=cc: e.tensor_scalar(out=S32[:, h, :], in0=S32[:, h, :], scalar1=hsc[:, cc, 3:4], scalar2=None, op0=ALU.mult),
                     reads=[("hsc", par)], writes=[("S32", h)])
                S.op("dve", lambda e, pS=pS, cc=cc: e.scalar_tensor_tensor(out=S32[:, h, :], in0=pS, scalar=hsc[:, cc, 4:5], in1=S32[:, h, :], op0=ALU.mult, op1=ALU.add),
                     reads=[("ps", bk), ("hsc", par)], writes=[("S32", h)])

        def hg_d(h, par):
            q_in = g["q_in2"][:, par, :]
            for cc in range(8):
                m, hf = cc // 2, cc % 2
                c0, p0 = cc * 64, hf * 64
                bo = 4 + cc % 2
                pO = _bank(ps, bo)[p0:p0 + 64, 0:128]
                S.op("pe", lambda e, pO=pO, p0=p0, m=m: e.matmul(pO, A_h8[p0:p0 + 64, par, m, :], vH[p0:p0 + 64, m, h * 128:(h + 1) * 128], start=True, stop=False),
                     reads=[("A_h8", par, cc), ("vH", m)], writes=[("ps", bo)])
                S.op("pe", lambda e, pO=pO, c0=c0, cc=cc: e.matmul(pO, q_in[:, c0:c0 + 64], Sp8[:, par, cc, :], start=False, stop=True),
                     reads=[("q_in", par), ("Sp8", par, cc)], writes=[("ps", bo)])
                S.op("act", lambda e, pO=pO, p0=p0, m=m: e.activation(out=OHs[p0:p0 + 64, m, h * 128:(h + 1) * 128], in_=pO, func=AF.Copy),
                     reads=[("ps", bo)], writes=["X"])
        for hp in range(4):
            wt, wkey = ws.get(g["w1fm"][2 + hp], WSLOT)
            wv4 = wt.rearrange("p (b k c) -> p b k c", b=4, k=KC)
            for hh in range(2):
                def _prep(hh=hh, wv4=wv4, wkey=wkey):
                    h = hp * 2 + hh
                    par = h % 2
                    hsc = g["hsc2"][:, par]
                    q_in = g["q_in2"][:, par, :]
                    k_in = g["k_in2"][:, par, :]
                    bq = pp_next()
                    pq = _bank(ps, bq)
                    for kc in range(KC):
                        S.op("pe", lambda e, pq=pq, wv4=wv4, hh=hh, kc=kc: e.matmul(pq, wv4[:, hh * 2, kc, :], aT[:, kc, :], start=(kc == 0), stop=(kc == KC - 1)),
                             reads=[wkey, ("aT", kc)], writes=[("ps", bq)])
                    bf = pp_next()
                    pf = _bank(ps, bf)
                    for kc in range(KC):
                        S.op("pe", lambda e, pf=pf, wv4=wv4, hh=hh, kc=kc: e.matmul(pf, wv4[:, hh * 2 + 1, kc, :], aT[:, kc, :], start=(kc == 0), stop=(kc == KC - 1)),
                             reads=[wkey, ("aT", kc)], writes=[("ps", bf)])
                    S.op("act", lambda e, pq=pq: e.activation(out=qs[:], in_=pq, func=AF.Silu), reads=[("ps", bq)], writes=["qs"])
                    S.op("act", lambda e, pf=pf: e.activation(out=sig[:], in_=pf, func=AF.Sigmoid), reads=[("ps", bf)], writes=["sig"])
                    S.op("dve", lambda e, h=h: e.tensor_scalar(out=sig[:], in0=sig[:], scalar1=vf[:, V_OML + h:V_OML + h + 1], scalar2=vf[:, V_LB + h:V_LB + h + 1],
                                                              op0=ALU.mult, op1=ALU.add), reads=["vf"], writes=["sig"])
                    S.op("dve", lambda e: e.tensor_scalar(out=kk[:], in0=sig[:], scalar1=-1.0, scalar2=1.0, op0=ALU.mult, op1=ALU.add), reads=["sig"], writes=["kk"])
                    S.op("act", lambda e: e.activation(out=sig[:], in_=sig[:], func=AF.Ln), writes=["sig"])
                    S.op("dve", lambda e: e.tensor_tensor_scan(out=bcs[:], data0=m0[:], data1=sig[:], initial=0.0, op0=ALU.mult, op1=ALU.add),
                         reads=["m0", "sig"], writes=["bcs"])
                    b3 = bcs[:].rearrange("p (c t) -> p c t", t=64)
                    S.op("dve", lambda e: e.tensor_copy(out=hsc[:, :, 0], in_=b3[:, :, 31]), reads=["bcs"], writes=[("hsc", par)])
                    S.op("dve", lambda e: e.tensor_copy(out=hsc[:, :, 1], in_=b3[:, :, 63]), reads=["bcs"], writes=[("hsc", par)])
                    S.op("dve", lambda e: e.tensor_tensor(out=hsc[:, :, 7], in0=hsc[:, :, 1], in1=hsc[:, :, 0], op=ALU.subtract), writes=[("hsc", par)])
                    S.op("act", lambda e: e.activation(out=hsc[:, :, 2], in_=hsc[:, :, 0], func=AF.Exp), writes=[("hsc", par)])
                    S.op("act", lambda e: e.activation(out=hsc[:, :, 3], in_=hsc[:, :, 1], func=AF.Exp), writes=[("hsc", par)])
                    S.op("act", lambda e: e.activation(out=hsc[:, :, 4], in_=hsc[:, :, 7], func=AF.Exp), writes=[("hsc", par)])
                    S.op("dve", lambda e, h=h: e.tensor_tensor_scan(out=hsc[:, :, 7], data0=ones[:, 0:8], data1=hsc[:, :, 1], initial=hcarry[:, h:h + 1],
                                                                   op0=ALU.mult, op1=ALU.add), reads=["ones", "hcarry"], writes=[("hsc", par)])
                    S.op("dve", lambda e: e.tensor_tensor(out=hsc[:, :, 5], in0=hsc[:, :, 7], in1=hsc[:, :, 1], op=ALU.subtract), writes=[("hsc", par)])
                    S.op("dve", lambda e, h=h: e.tensor_copy(out=hcarry[:, h:h + 1], in_=hsc[:, 7, 7:8]), reads=[("hsc", par)], writes=["hcarry"])
                    S.op("dve", lambda e: e.tensor_tensor(out=hsc[:, :, 6], in0=hsc[:, :, 5], in1=hsc[:, :, 0], op=ALU.add), writes=[("hsc", par)])
                    S.op("act", lambda e: e.activation(out=hsc[:, :, 6], in_=hsc[:, :, 6], func=AF.Exp), writes=[("hsc", par)])
                    S.op("dve", lambda e: e.tensor_tensor(out=b3, in0=b3, in1=hsc[:, :, 0:1].broadcast_to([128, 8, 64]), op=ALU.subtract),
                         reads=[("hsc", par)], writes=["bcs"])
                    S.op("act", lambda e: e.activation(out=ebuf[:], in_=bcs[:], func=AF.Exp), reads=["bcs"], writes=["ebuf"])
                    S.op("dve", lambda e: e.tensor_tensor(out=q_in, in0=qs[:], in1=ebuf[:], op=ALU.mult), reads=["qs", "ebuf"], writes=[("q_in", par)])
                    S.op("pool", lambda e: e.tensor_tensor(out=qs[:], in0=qs[:], in1=ebuf[:], op=ALU.mult), reads=["ebuf"], writes=["qs"])
                    S.op("pool", lambda e: e.tensor_tensor(out=qdec[:].rearrange("p (c t) -> p c t", t=64), in0=qs[:].rearrange("p (c t) -> p c t", t=64),
                                                           in1=hsc[:, :, 6:7].broadcast_to([128, 8, 64]), op=ALU.mult), reads=["qs", ("hsc", par)], writes=["qdec"])
                    S.op("act", lambda e: e.activation(out=ebuf[:], in_=bcs[:], func=AF.Exp, scale=-1.0), reads=["bcs"], writes=["ebuf"])
                    S.op("dve", lambda e: e.tensor_tensor(out=k_in, in0=kk[:], in1=ebuf[:], op=ALU.mult), reads=["kk", "ebuf"], writes=[("k_in", par)])
                    S.op("sp", lambda e, h=h, tok0=tok0: e.dma_start(out=g["QH"][h * 128:(h + 1) * 128, tok0:tok0 + TT], in_=qdec[:]),
                         reads=["qdec"], writes=[("QH", t, h)], dma=True)
                    return h, par
                h, par = _prep()
                if SKIP_HG:
                    continue
                hg_abc(h, par, first=(h == 0))
                if PIPE_HG:
                    if h > 0:
                        hg_d(h - 1, (h - 1) % 2)
                else:
                    hg_d(h, par)
        if PIPE_HG and not SKIP_HG:
            hg_d(7, 1)
        S.op("sp", lambda e, tok0=tok0: e.dma_start(out=g["OH"][tok0:tok0 + TT, :].rearrange("(m p) c -> p m c", p=128), in_=OHs[:, :, 0:1024]),
             reads=["X"], writes=[("OH", t)], dma=True)
        for gi in range(4):
            wt, wkey = ws.get(g["w1tm"][4 + gi], WSLOT)
            wv3 = wt.rearrange("p (k c) -> p k c", k=KC)
            fn = AF.Sigmoid if gi < 2 else AF.Silu
            for m in range(4):
                bk = pp_next()
                pv = _bank(ps, bk)
                for kc in range(KC):
                    S.op("pe", lambda e, pv=pv, wv3=wv3, m=m, kc=kc: e.matmul(pv, aT[:, kc, m * 128:(m + 1) * 128], wv3[:, kc, :], start=(kc == 0), stop=(kc == KC - 1)),
                         reads=[wkey, ("aT", kc)], writes=[("ps", bk)])
                g2 = (gi * 4 + m) % 2
                S.op("act", lambda e, pv=pv, g2=g2, fn=fn: e.activation(out=gts[:, g2, :], in_=pv, func=fn), reads=[("ps", bk)], writes=[("gts", g2)])
                S.op("sp", lambda e, g2=g2, gi=gi, m=m, tok0=tok0: e.dma_start(out=g["GT"][tok0 + m * 128:tok0 + (m + 1) * 128, gi * 512:(gi + 1) * 512], in_=gts[:, g2, :]),
                     reads=[("gts", g2)], writes=[("GT", t, gi, m)], dma=True)
    stsb = X[:].rearrange("p m d -> p (m d)")[:, 0:NST]
    S.op("dve", lambda e: e.memset(stsb[:, 2064:NST], 0.0), writes=["X"])
    S.op("dve", lambda e: e.tensor_copy(out=stsb[:, 0:1024], in_=S32[:].rearrange("p h c -> p (h c)")), reads=[("S32", h) for h in range(8)], writes=["X"])
    S.op("dve", lambda e: e.tensor_copy(out=stsb[:, 1024:1024 + 4 * MW], in_=C32[:].rearrange("p h c -> p (h c)")), reads=[("C32", h) for h in range(4)], writes=["X"])
    S.op("act", lambda e: e.activation(out=stsb[:, 2064:2072], in_=hcarry[:], func=AF.Exp), reads=["hcarry"], writes=["X"])
    S.op("act", lambda e: e.activation(out=stsb[:, 2072:2076], in_=carry[:], func=AF.Exp), reads=["carry"], writes=["X"])
    S.op("sp", lambda e: e.dma_start(out=g["STATE_SRC"].rearrange("k p c -> p k c"), in_=stsb.rearrange("p (k c) -> p k c", c=256)),
         reads=["X"], writes=["STATE_SRC"], dma=True)
    for k in range(NPC):
        S.op("pool", lambda e, k=k: e.collective_compute("AllGather", ALU.bypass, replica_groups=[[0, 1, 2, 3], [4, 5, 6, 7]],
                                                         ins=[g["STATE_SRC"][k].opt()], outs=[g["STATES"][k].opt()]),
             reads=["STATE_SRC"], writes=[("STATES", k)], dma="cc")


def _emit_p2(S, ws, nc, T, g, NT, l, last):
    X, aT, ps, vf, bc, arena = T["X"], T["aT"], T["ps"], T["vf"], T["bc"], g["arena"]
    ident, identb, selt, Sst, Cst, EMt, hst, mst = (g[k] for k in ("ident", "identb", "selt", "Sst", "Cst", "EMt", "hst", "mst"))
    xsrc = g["xsrc"]

    def hk(a, b):
        return [("hT", j) for j in range(a // TT, (b + TT - 1) // TT)]

    def f32v(off, n):
        return arena[:, off:off + 2 * n].bitcast(F32)

    def dma_in(dst, src, keys, eng="sp", reads=()):
        S.op(eng, lambda e: e.dma_start(out=dst, in_=src), reads=list(reads), writes=list(keys), dma=True)
    blobs = X[:].rearrange("p m d -> p (m d)")
    for k in range(3):
        dma_in(blobs[:, k * NST:(k + 1) * NST].rearrange("p (j c) -> p j c", c=256), g["STATES"][:, k * 128:(k + 1) * 128, :].rearrange("j p c -> p j c"),
               ["X"], reads=[("STATES", j) for j in range(NPC)])
    KS = hk(0, 4 * NST)
    Scur = f32v(0, NST)
    Sacc = f32v(2 * NST, NST)
    S.op("dve", lambda e: e.tensor_copy(out=Scur, in_=blobs[:, 0:NST]), reads=["X"], writes=KS)
    S.op("dve", lambda e: e.tensor_scalar(out=Sacc, in0=Scur, scalar1=selt[:, 1:2], scalar2=None, op0=ALU.mult), reads=["selt"], writes=KS)
    for k in (1, 2):
        bk = blobs[:, k * NST:(k + 1) * NST]
        S.op("dve", lambda e, bk=bk: e.tensor_tensor(out=Scur[:, 0:1024].rearrange("p (h c) -> p h c", c=128), in0=Scur[:, 0:1024].rearrange("p (h c) -> p h c", c=128),
                                                     in1=bk[:, 2064:2072].unsqueeze(2).broadcast_to([128, 8, 128]), op=ALU.mult), reads=["X"], writes=KS)
        S.op("dve", lambda e, bk=bk: e.tensor_tensor(out=Scur[:, 1024:1024 + 4 * MW].rearrange("p (h c) -> p h c", c=MW), in0=Scur[:, 1024:1024 + 4 * MW].rearrange("p (h c) -> p h c", c=MW),
                                                     in1=bk[:, 2072:2076].unsqueeze(2).broadcast_to([128, 4, MW]), op=ALU.mult), reads=["X"], writes=KS)
        S.op("dve", lambda e, bk=bk: e.tensor_tensor(out=Scur[:, 0:2064], in0=Scur[:, 0:2064], in1=bk[:, 0:2064], op=ALU.add), reads=["X"], writes=KS)
        S.op("dve", lambda e, k=k: e.scalar_tensor_tensor(out=Sacc[:, 0:2064], in0=Scur[:, 0:2064], scalar=selt[:, k + 1:k + 2], in1=Sacc[:, 0:2064], op0=ALU.mult, op1=ALU.add),
             reads=["selt"], writes=KS)
    S.op("dve", lambda e: e.tensor_copy(out=Sst[:].rearrange("p h c -> p (h c)"), in_=Sacc[:, 0:1024]), reads=KS, writes=["Sst"])
    S.op("dve", lambda e: e.tensor_copy(out=Cst[:].rearrange("p h c -> p (h c)"), in_=Sacc[:, 1024:1024 + 4 * MW]), reads=KS, writes=["Cst"])

    O_OH, O_OM, O_GT, O_QH, O_QM, O_YC, O_W1, O_W2, O_WM, O_WH = 0, 2048, 4608, 6656, 10752, 12800, 14848, 16896, 18944, 19968
    OHm, K_OH = f32v(O_OH, 1024), hk(O_OH, O_OH + 2048)
    OMm, K_OM = f32v(O_OM, 4 * MW), hk(O_OM, O_OM + 8 * MW)
    GTm, K_GT = arena[:, O_GT:O_GT + 2048], hk(O_GT, O_GT + 2048)
    QHt, K_QH = arena[:, O_QH:O_QH + 4096].rearrange("p (h t) -> p h t", t=TT), hk(O_QH, O_QH + 4096)
    QMt, K_QM = arena[:, O_QM:O_QM + 2048].rearrange("p (h t) -> p h t", t=TT), hk(O_QM, O_QM + 2048)
    ycat, K_YC = arena[:, O_YC:O_YC + 2048], hk(O_YC, O_YC + 2048)
    wk1, K_W1 = f32v(O_W1, 1024), hk(O_W1, O_W1 + 2048)
    wk2, K_W2 = f32v(O_W2, 1024), hk(O_W2, O_W2 + 2048)
    wkm, K_WM = f32v(O_WM, MW), hk(O_WM, O_WM + 2 * MW)
    wkh, K_WH = f32v(O_WH, 256), hk(O_WH, O_WH + 512)

    for t in range(NT):
        tok0 = t * TT
        dma_in(QHt, g["QH"][:, tok0:tok0 + TT].rearrange("(h p) t -> p h t", p=128), K_QH)
        dma_in(QMt, g["QM"][:, tok0:tok0 + TT].rearrange("(h p) t -> p h t", p=128), K_QM)
        dma_in(EMt[:], g["EM"][tok0:tok0 + TT, :].rearrange("(m p) c -> p m c", p=128), ["EMt"])
        for i in range(3):
            dma_in(bc[:, i, :], g["ROWS"][i:i + 1, :].partition_broadcast(128), [("bc", i)], reads=["ROWS"])
        for m in range(4):
            r0 = tok0 + m * 128
            msl = slice(m * 128, (m + 1) * 128)
            dma_in(OHm, g["OH"][r0:r0 + 128, :], K_OH)
            dma_in(OMm, g["OM"][r0:r0 + 128, :], K_OM)
            dma_in(GTm, g["GT"][r0:r0 + 128, :], K_GT)
            for h in range(8):
                bk = 4 + h // 4
                pcor = _bank(ps, bk)[:, (h % 4) * 128:(h % 4 + 1) * 128]
                S.op("pe", lambda e, pcor=pcor, h=h, msl=msl: e.matmul(pcor, QHt[:, h, msl], Sst[:, h, :], start=True, stop=True),
                     reads=K_QH + ["Sst"], writes=[("ps", bk)])
            for hb in range(2):
                S.op("dve", lambda e, hb=hb: e.tensor_tensor(out=wk1[:, hb * 512:(hb + 1) * 512], in0=_bank(ps, 4 + hb), in1=OHm[:, hb * 512:(hb + 1) * 512], op=ALU.add),
                     reads=[("ps", 4 + hb)] + K_OH, writes=K_W1)
            S.op("pool", lambda e: e.tensor_tensor(out=wk2, in0=wk1, in1=wk1, op=ALU.mult), reads=K_W1, writes=K_W2)
            S.op("dve", lambda e: e.tensor_reduce(out=hst[:, :, 0], in_=wk2.rearrange("p (h c) -> p h c", c=128), axis=AX.X, op=ALU.add), reads=K_W2, writes=["hst"])
            S.op("act", lambda e: e.activation(out=hst[:, :, 1], in_=hst[:, :, 0], func=AF.Sqrt, scale=1.0 / 128.0, bias=g["epsb"][:, 0:1]), reads=["hst", "epsb"], writes=["hst"])
            S.op("dve", lambda e: e.reciprocal(out=hst[:, :, 2], in_=hst[:, :, 1]), writes=["hst"])
            S.op("dve", lambda e: e.tensor_tensor(out=wk1.rearrange("p (h c) -> p h c", c=128), in0=wk1.rearrange("p (h c) -> p h c", c=128),
                                                  in1=hst[:, :, 2:3].broadcast_to([128, 8, 128]), op=ALU.mult), reads=["hst"], writes=K_W1)
            S.op("pool", lambda e: e.tensor_tensor(out=ycat[:, 1024:2048], in0=wk1, in1=GTm[:, 1024:2048], op=ALU.mult), reads=K_W1 + K_GT, writes=K_YC)
            for h in range(4):
                bk = 6 + h % 2
                pP = _bank(ps, bk)
                S.op("pe", lambda e, pP=pP, h=h, msl=msl: e.matmul(pP[:, 0:257], QMt[:, h, msl], Cst[:, h, 0:257], start=True, stop=True),
                     reads=K_QM + ["Cst"], writes=[("ps", bk)])
                S.op("dve", lambda e, pP=pP, h=h, m=m: e.scalar_tensor_tensor(out=wkm[:, 0:257], in0=pP[:, 0:257], scalar=EMt[:, m, h:h + 1], in1=OMm[:, h * MW:h * MW + 257],
                                                                             op0=ALU.mult, op1=ALU.add), reads=[("ps", bk), "EMt"] + K_OM, writes=K_WM)
                ms = mst[:, h, :]
                S.op("dve", lambda e, ms=ms: e.tensor_scalar(out=ms[:, 0:1], in0=wkm[:, 256:257], scalar1=-1.0, scalar2=None, op0=ALU.mult), reads=K_WM, writes=[("mst", h)])
                S.op("dve", lambda e, ms=ms: e.tensor_tensor(out=ms[:, 0:1], in0=ms[:, 0:1], in1=wkm[:, 256:257], op=ALU.max), reads=K_WM, writes=[("mst", h)])
                S.op("dve", lambda e, ms=ms: e.tensor_scalar(out=ms[:, 0:1], in0=ms[:, 0:1], scalar1=1.0, scalar2=None, op0=ALU.max), writes=[("mst", h)])
                S.op("dve", lambda e, ms=ms: e.reciprocal(out=ms[:, 1:2], in_=ms[:, 0:1]), writes=[("mst", h)])
                S.op("act", lambda e, ms=ms: e.activation(out=wkh, in_=wkm[:, 0:256], func=AF.Copy, scale=ms[:, 1:2]), reads=K_WM + [("mst", h)], writes=K_WH)
                S.op("dve", lambda e, ms=ms: e.bn_stats(out=ms[:, 2:8], in_=wkh), reads=K_WH, writes=[("mst", h)])
                S.op("dve", lambda e, ms=ms: e.bn_aggr(out=ms[:, 8:10], in_=ms[:, 2:8]), writes=[("mst", h)])
                S.op("act", lambda e, ms=ms: e.activation(out=ms[:, 10:11], in_=ms[:, 9:10], func=AF.Sqrt, bias=g["epsb"][:, 0:1]), reads=["epsb"], writes=[("mst", h)])
                S.op("dve", lambda e, ms=ms: e.reciprocal(out=ms[:, 11:12], in_=ms[:, 10:11]), writes=[("mst", h)])
                S.op("dve", lambda e, ms=ms: e.scalar_tensor_tensor(out=ms[:, 12:13], in0=ms[:, 8:9], scalar=-1.0, in1=ms[:, 11:12], op0=ALU.mult, op1=ALU.mult), writes=[("mst", h)])
                S.op("act", lambda e, ms=ms: e.activation(out=wkh, in_=wkh, func=AF.Identity, scale=ms[:, 11:12], bias=ms[:, 12:13]), reads=[("mst", h)], writes=K_WH)
                S.op("dve", lambda e, h=h: e.tensor_tensor(out=ycat[:, h * 256:(h + 1) * 256], in0=wkh, in1=GTm[:, h * 256:(h + 1) * 256], op=ALU.mult),
                     reads=K_WH + K_GT, writes=K_YC)
            for kc in range(KC):
                t2 = kc % 2
                pt = _bank(ps, 4 + t2).bitcast(BF16)[:, 0:128]
                S.op("pe", lambda e, pt=pt, kc=kc: e.transpose(pt, ycat[:, kc * 128:(kc + 1) * 128], identb[:]), reads=K_YC + ["identb"], writes=[("ps", 4 + t2)])
                S.op("act", lambda e, pt=pt, kc=kc, msl=msl: e.activation(out=aT[:, kc, msl], in_=pt, func=AF.Copy, scale=vf[:, V_NRMW + kc:V_NRMW + kc + 1]),
                     reads=[("ps", 4 + t2), "vf"], writes=[("aT", kc)])
        S.op("sp", lambda e, tok0=tok0, xsrc=xsrc: e.dma_start(out=X[:], in_=xsrc[tok0:tok0 + TT, :].rearrange("(m p) d -> p m d", p=128)),
             reads=["XN"], writes=["X"], dma=True)
        emit_resid_prep(S, T, 0, 1)
        for n in range(4):
            wt, wkey = ws.get(g["wout"][n], WSLOT)
            wv3 = wt.rearrange("p (k c) -> p k c", k=KC)
            for m in range(4):
                pm = _bank(ps, m)
                for kc in range(KC):
                    S.op("pe", lambda e, pm=pm, wv3=wv3, m=m, kc=kc: e.matmul(pm, aT[:, kc, m * 128:(m + 1) * 128], wv3[:, kc, :], start=(kc == 0), stop=(kc == KC - 1)),
                         reads=[wkey, ("aT", kc)], writes=[("ps", m)])
            for m in range(4):
                emit_epilogue_block(S, T, m, n, 2)
        emit_ln_inplace(S, T)
        for i in range(3):
            dma_in(bc[:, i, :], g["ROWS"][3 + i:4 + i, :].partition_broadcast(128), [("bc", i)], reads=["ROWS"])
        emit_transpose_affine(S, T, "X", V_GU2, V_BU2)
        emit_ffn2(S, ws, T, g["wgu"], g["wd"])
        if not last:
            S.op("sp", lambda e, tok0=tok0: e.dma_start(out=g["XN"][tok0:tok0 + TT, :].rearrange("(m p) d -> p m d", p=128), in_=X[:]),
                 reads=["X"], writes=["XN"], dma=True)
            if t == NT - 1:
                S.op("sp", lambda e: e.dma_start(out=g["HALO_SRC"][0:24, :].rearrange("(r c) w -> r (c w)", r=3), in_=X[125:128, 3, :]), reads=["X"], writes=["HALO_SRC"], dma=True)
                S.op("pool", lambda e: e.collective_compute("AllGather", ALU.bypass, replica_groups=[[0, 1, 2, 3], [4, 5, 6, 7]],
                                                            ins=[g["HALO_SRC"].opt()], outs=[g["HALO_DST"].opt()]),
                     reads=["HALO_SRC"], writes=["HALO_DST"], dma="cc")
        else:
            for i in range(2):
                dma_in(bc[:, i, :], g["fin"][i:i + 1, :].partition_broadcast(128), [("bc", i)])
            emit_resid_prep(S, T, 0, 1)
            S.op("sp", lambda e, tok0=tok0: e.dma_start(out=g["OUT"][tok0:tok0 + TT, :].rearrange("(m p) d -> p m d", p=128), in_=X[:]),
                 reads=["X"], writes=[("OUT", t)], dma=True)


def emit_ffn2(S, ws, T, wgu, wd):
    aT, hT, ps, sg = T["aT"], T["hT"], T["ps"], T["sg"]
    for j2 in range(FC // 2):
        wt, wkey = ws.get(wgu[j2], 2 * 2 * KC * 128)
        wv = wt.rearrange("p (b g k c) -> p b g k c", b=2, g=2, k=KC)
        for b in range(2):
            j = j2 * 2 + b
            bg = 4 + (j % 2) * 2
            bu = bg + 1
            pg, pu = _bank(ps, bg), _bank(ps, bu)
            for gi, pp, bk in ((0, pg, bg), (1, pu, bu)):
                for kc in range(KC):
                    S.op("pe", lambda e, pp=pp, wv=wv, b=b, gi=gi, kc=kc: e.matmul(
                        pp, wv[:, b, gi, kc, :], aT[:, kc, :], start=(kc == 0), stop=(kc == KC - 1)),
                        reads=[wkey, ("aT", kc)], writes=[("ps", bk)])
            sgt = sg[:, (j % 2) * 512:(j % 2 + 1) * 512]
            S.op("act", lambda e, sgt=sgt, pg=pg: e.activation(out=sgt, in_=pg, func=AF.Silu),
                 reads=[("ps", bg)], writes=[("sg", j % 2)])
            S.op("dve", lambda e, sgt=sgt, pu=pu, j=j: e.tensor_tensor(out=hT[:, j, :], in0=pu, in1=sgt, op=ALU.mult),
                 reads=[("ps", bu), ("sg", j % 2)], writes=[("hT", j)])
    emit_resid_prep(S, T, 0, 1)
    for n in range(4):
        for jg in range(4):
            wt, wkey = ws.get(wd[n, jg], 11 * 512)
            wv = wt.rearrange("p (j c) -> p j c", j=11)
            for m in range(4):
                pm = _bank(ps, m)
                for jj in range(11):
                    j = jg * 11 + jj
                    S.op("pe", lambda e, pm=pm, wv=wv, jj=jj, j=j, m=m: e.matmul(
                        pm, hT[:, j, m * 128:(m + 1) * 128], wv[:, jj, :], start=(j == 0), stop=(j == FC - 1)),
                        reads=[wkey, ("hT", j)], writes=[("ps", m)])
        for m in range(4):
            emit_epilogue_block(S, T, m, n, 2)
    emit_ln_inplace(S, T)


def _fm_block(W, col, width=128):
    return W[:, col:col + width].reshape(KC, 128, width).transpose(1, 0, 2)


def _fm16(v):
    return np.ascontiguousarray(v.reshape(-1, 128).T)


def _layer_arrays(inp, l, depth):
    W = inp["w_in"][l]
    fm_tiles = []
    fm_tiles.append(np.stack([_fm_block(W, h * 128) for h in range(4)], axis=1))
    fm_tiles.append(np.stack([_fm_block(W, 512 + h * 128) for h in range(4)], axis=1))
    for hp in range(4):
        blks = []
        for hh in range(2):
            h = hp * 2 + hh
            blks += [_fm_block(W, 3080 + h * 128), _fm_block(W, 4104 + h * 128)]
        fm_tiles.append(np.stack(blks, axis=1))
    w1fm = np.ascontiguousarray(np.stack(fm_tiles)).reshape(6, 128, WSLOT)
    tm_cols = [1024, 1536, 5128, 5640, 2048, 2560, 6152, 6664]
    w1tm = np.ascontiguousarray(np.stack([_fm_block(W, c, 512) for c in tm_cols])).reshape(8, 128, WSLOT)
    w1ifg = np.ascontiguousarray(_fm_block(W, 3072, 8)).reshape(128, KC * 8)
    Wm = inp["w_mod"][l]
    wmod = np.ascontiguousarray(np.stack([np.stack([_fm_block(Wm, t * 512 + b * 128) for b in range(4)], axis=1)
                                          for t in range(24)])).reshape(24, 128, WSLOT)
    Wo = inp["w_out"][l]
    wout = np.ascontiguousarray(np.stack([_fm_block(Wo, n * 512, 512) for n in range(4)])).reshape(4, 128, WSLOT)
    g4 = inp["w_gate"][l].reshape(KC, 128, FC // 2, 2, 128)
    u4 = inp["w_up"][l].reshape(KC, 128, FC // 2, 2, 128)
    gu = np.stack([g4, u4], axis=0)
    wgu = np.ascontiguousarray(gu.transpose(3, 2, 4, 0, 1, 5)).reshape(FC // 2, 128, WSLOT)
    wd = np.ascontiguousarray(inp["w_down"][l].reshape(4, 11, 128, 4, 512).transpose(3, 0, 2, 1, 4)).reshape(4, 4, 128, 11 * 512)
    vecs = np.zeros((128, V_IN), np.float32)
    vecs[:, V_BMOD:V_BMOD + 96] = _fm16(inp["b_mod"][l])
    if l == 0:
        vecs[:, V_LNPG:V_LNPG + 16] = np.ones((128, 16), np.float32)
    else:
        vecs[:, V_LNPG:V_LNPG + 16] = _fm16(inp["ln2_g"][l - 1])
        vecs[:, V_LNPB:V_LNPB + 16] = _fm16(inp["ln2_b"][l - 1])
    vecs[:, V_LN1G:V_LN1G + 16] = _fm16(inp["ln1_g"][l])
    vecs[:, V_LN1B:V_LN1B + 16] = _fm16(inp["ln1_b"][l])
    vecs[:, V_CONVW:V_CONVW + 32] = inp["conv_w"][l].reshape(4, 8, 128).transpose(2, 1, 0).reshape(128, 32)
    vecs[:, V_CONVB:V_CONVB + 8] = _fm16(inp["conv_b"][l])
    vecs[:, V_LBL:V_LBL + 8 * depth] = inp["lb_logits"].reshape(depth, 8, 128).transpose(2, 1, 0).reshape(128, 8 * depth)
    nrmw = _fm16(np.concatenate([inp["mlstm_norm_w"][l], inp["hgrn_norm_w"][l]]))
    vecs[:, V_NRMW:V_NRMW + 16] = nrmw
    grow = np.zeros((1, 8 + DEPTH), np.float32)
    grow[0, 0:4] = inp["b_igate"][l]
    grow[0, 4:8] = inp["b_fgate"][l]
    for i in range(1, l + 1):
        grow[0, 8 + i] = np.float32(1.0)
    return dict(w1fm=w1fm, w1tm=w1tm, w1ifg=w1ifg, wmod=wmod, wout=wout, wgu=wgu, wd=wd, vecs=vecs, grow=grow, nrmw=nrmw)


def build_fused(TOK, depth):
    NT = TOK // TT
    nc = bass.Bass("TRN2", target_bir_lowering=False)

    def din(name, shape, dt=F32):
        return nc.dram_tensor(name, shape, dt, kind="ExternalInput").ap()

    def dint(name, shape, dt=F32):
        return nc.dram_tensor(name, shape, dt, kind="Internal").ap()
    g = {}
    x_in = din("x_in", [TOK, D])
    g["halo0"] = din("halo0", [3, D])
    g["hflag"] = din("hflag", [128, 1])
    g["c_fm"] = din("c_fm", [128, KC])
    g["sel"] = din("sel", [128, 4])
    g["selprev"] = din("selprev", [128, 4])
    g["fin"] = din("fin", [2, D])
    g["identd"] = din("identd", [128, 128])
    g["trid"] = din("trid", [128, 128], U8)
    wmod = din("wmod", [depth * 24, 128, WSLOT])
    w1fm = din("w1fm", [depth * 6, 128, WSLOT])
    w1tm = din("w1tm", [depth * 8, 128, WSLOT])
    w1ifg = din("w1ifg", [depth * 128, KC * 8])
    wout = din("wout", [depth * 4, 128, WSLOT])
    wgu = din("wgu", [depth * (FC // 2), 128, WSLOT])
    wd = din("wd", [depth * 4, 4, 128, 11 * 512])
    vecs = din("vecs", [depth * 128, V_IN])
    grow = din("grow", [depth, 8 + DEPTH])
    g["OUT"] = nc.dram_tensor("OUT", [TOK, D], F32, kind="ExternalOutput").ap()
    g["XN"] = dint("XN", [TOK, D])
    g["OH"] = dint("OH", [TOK, 1024])
    g["OM"] = dint("OM", [TOK, 4 * MW])
    g["QH"] = dint("QH", [1024, TOK], BF16)
    g["QM"] = dint("QM", [512, TOK], BF16)
    g["EM"] = dint("EM", [TOK, 4])
    g["GT"] = dint("GT", [TOK, 2048], BF16)
    g["ROWS"] = dint("ROWS", [6, D])
    g["STATE_SRC"] = dint("STATE_SRC", [NPC, 128, 256])
    g["STATES"] = dint("STATES", [NPC, 4 * 128, 256])
    g["HALO_SRC"] = dint("HALO_SRC", [128, 256])
    g["HALO_DST"] = dint("HALO_DST", [4 * 128, 256])

    with contextlib.ExitStack() as es:
        def sb(name, shape, dt=F32):
            return es.enter_context(nc.sbuf_tensor(name, shape, dt))
        T = {}
        X = T["X"] = sb("X", [128, 4, D])
        T["aT"] = sb("aT", [128, KC, TT], BF16)
        wbuf = sb("wbuf", [128, NWS * WSLOT], BF16)
        T["ident"] = g["ident"] = sb("ident", [128, 128])
        g["identb"] = sb("identb", [128, 128], BF16)
        g["tri"] = sb("tri", [128, 128], U8)
        g["utri"] = sb("utri", [128, 128])
        g["ones"] = sb("ones", [128, 128])
        g["m0"] = sb("m0", [128, TT])
        T["vf"] = sb("vf", [128, V_TOT])
        g["cact"] = sb("cact", [128, KC])
        g["cactb"] = sb("cactb", [128, KC], BF16)
        g["hfl"] = sb("hfl", [128, 1])
        g["selt"] = sb("selt", [128, 4])
        g["selp"] = sb("selp", [128, 4])
        g["lnsc"] = sb("lnsc", [128, 1])
        T["epsb"] = g["epsb"] = sb("epsb", [128, 1])
        g["A_sb4"] = sb("A_sb4", [128, 2, 4, 128], BF16)
        g["A_h8"] = sb("A_h8", [128, 2, 4, 64], BF16)
        T["st"] = sb("st", [128, 4, 4, 6])
        T["mv"] = sb("mv", [128, 4, 8])
        fsc = sb("fsc", [128, 1])
        PAR = 41216
        par = sb("par", [128, PAR], BF16)
        T["ps"] = es.enter_context(nc.psum_tensor("ps", [128, 8 * 512], F32))
        g["hx"] = X[0:3, 0, :]
        g["hx2"] = X[0:3, 1, :]
        off = [0]

        def cv(n_bf16, dt, shape_str=None, **kw):
            a = par[:, off[0]:off[0] + n_bf16]
            off[0] += (n_bf16 + 15) // 16 * 16
            if dt == F32:
                a = a.bitcast(F32)
            if shape_str:
                a = a.rearrange(shape_str, **kw)
            return a
        g["aTh"] = cv(48, BF16, "p (k t) -> p k t", t=3)
        g["hist"] = cv(48, F32, "p (k t) -> p k t", t=3)
        g["ext"] = cv(2 * (TT + 4), F32)
        g["t1"] = cv(2 * TT, F32)
        g["qkT"] = cv(8 * TT, BF16, "p (k t) -> p k t", t=TT)
        g["wifg"] = cv(KC * 8, BF16)
        g["gsb"] = cv(64, F32, "p (m c) -> p m c", c=8)
        for nm in ("l1", "nb", "nbL", "wv_", "eo", "ec", "eL", "ctmp"):
            g[nm] = cv(32, F32, "p (m c) -> p m c", c=4)
        g["carry"] = cv(8, F32)
        g["gbias"] = cv(2 * (8 + DEPTH), F32)
        g["vaug"] = cv(16 * MW, BF16, "p (m h c) -> p m h c", m=4, h=4)
        g["ktok4"] = cv(1024, BF16, "p (a m c) -> p a m c", a=2, m=4)
        g["C32"] = cv(8 * MW, F32, "p (h c) -> p h c", h=4)
        g["Cbf4"] = cv(8 * MW, BF16, "p (a m c) -> p a m c", a=2, m=4)
        g["dCs"] = cv(8 * MW, F32, "p (m c) -> p m c", m=4)
        g["vH"] = cv(4096, BF16, "p (m c) -> p m c", m=4)
        for nm in ("sig", "kk", "bcs", "qs", "ebuf"):
            g[nm] = cv(2 * TT, F32)
        g["qdec"] = cv(TT, BF16)
        g["q_in2"] = cv(2 * TT, BF16, "p (a t) -> p a t", a=2)
        g["k_in2"] = cv(2 * TT, BF16, "p (a t) -> p a t", a=2)
        g["hsc2"] = cv(256, F32, "p (a c s) -> p a c s", a=2, s=8)
        g["hcarry"] = cv(16, F32)
        g["S32"] = cv(2048, F32, "p (h c) -> p h c", h=8)
        g["Sp8"] = cv(2048, BF16, "p (a c d) -> p a c d", a=2, c=8)
        g["gts"] = cv(2 * TT, BF16, "p (a c) -> p a c", a=2)
        g["rowsb"] = cv(2 * 768, F32)
        g["OMs"] = cv(2 * 2 * MW, F32, "p (m c) -> p m c", m=2)
        print("P1 arena", off[0])
        assert off[0] <= PAR, off[0]
        off[0] = 0
        g["arena"] = cv(FC * TT, BF16)
        T["hT"] = g["arena"].rearrange("p (j t) -> p j t", t=TT)
        T["bc"] = cv(2 * 3 * D, F32, "p (i d) -> p i d", i=3)
        T["sg"] = cv(2 * 1024, F32)
        T["tmp"] = cv(2 * 1024, F32)
        g["Sst"] = cv(1024, BF16, "p (h c) -> p h c", h=8)
        g["Cst"] = cv(4 * MW, BF16, "p (h c) -> p h c", h=4)
        g["EMt"] = cv(32, F32, "p (m c) -> p m c", c=4)
        g["hst"] = cv(64, F32, "p (h c) -> p h c", c=4)
        g["mst"] = cv(128, F32, "p (h c) -> p h c", c=16)
        print("P2 arena", off[0])
        assert off[0] <= PAR, off[0]
        rec = None
        for pas in range(2):
            S = Sched(nc)
            ws = WStream(S, wbuf, record=rec)
            _emit_prologue(S, T, g)
            for l in range(depth):
                g["xsrc"] = x_in if l == 0 else g["XN"]
                g["wmod"] = wmod[l * 24:(l + 1) * 24]
                g["w1fm"] = w1fm[l * 6:(l + 1) * 6]
                g["w1tm"] = w1tm[l * 8:(l + 1) * 8]
                g["w1ifg"] = w1ifg[l * 128:(l + 1) * 128, :]
                g["wout"] = wout[l * 4:(l + 1) * 4]
                g["wgu"] = wgu[l * (FC // 2):(l + 1) * (FC // 2)]
                g["wd"] = wd[l * 4:(l + 1) * 4]
                g["vecs"] = vecs[l * 128:(l + 1) * 128, :]
                g["grow"] = grow[l:l + 1, :]
                _emit_p1(S, ws, nc, T, g, NT, l)
                S.fence(fsc[:])
                _emit_p2(S, ws, nc, T, g, NT, l, l == depth - 1)
                S.fence(fsc[:])
            S.barrier_all("sp")
            rec = ws.seq
        S.run()
    return nc


def _fm_block(W, col, width=128):
    return W[:, col:col + width].reshape(KC, 128, width).transpose(1, 0, 2)


def _fm16(v):
    return np.ascontiguousarray(v.reshape(-1, 128).T)


def _layer_arrays(inp, l, depth):
    W = inp["w_in"][l]
    fm_tiles = []
    fm_tiles.append(np.stack([_fm_block(W, h * 128) for h in range(4)], axis=1))
    fm_tiles.append(np.stack([_fm_block(W, 512 + h * 128) for h in range(4)], axis=1))
    for hp in range(4):
        blks = []
        for hh in range(2):
            h = hp * 2 + hh
            blks += [_fm_block(W, 3080 + h * 128), _fm_block(W, 4104 + h * 128)]
        fm_tiles.append(np.stack(blks, axis=1))
    w1fm = np.ascontiguousarray(np.stack(fm_tiles)).reshape(6, 128, WSLOT)
    tm_cols = [1024, 1536, 5128, 5640, 2048, 2560, 6152, 6664]
    w1tm = np.ascontiguousarray(np.stack([_fm_block(W, c, 512) for c in tm_cols])).reshape(8, 128, WSLOT)
    w1ifg = np.ascontiguousarray(_fm_block(W, 3072, 8)).reshape(128, KC * 8)
    Wm = inp["w_mod"][l]
    wmod = np.ascontiguousarray(np.stack([np.stack([_fm_block(Wm, t * 512 + b * 128) for b in range(4)], axis=1)
                                          for t in range(24)])).reshape(24, 128, WSLOT)
    Wo = inp["w_out"][l]
    wout = np.ascontiguousarray(np.stack([_fm_block(Wo, n * 512, 512) for n in range(4)])).reshape(4, 128, WSLOT)
    g4 = inp["w_gate"][l].reshape(KC, 128, FC // 2, 2, 128)
    u4 = inp["w_up"][l].reshape(KC, 128, FC // 2, 2, 128)
    gu = np.stack([g4, u4], axis=0)
    wgu = np.ascontiguousarray(gu.transpose(3, 2, 4, 0, 1, 5)).reshape(FC // 2, 128, WSLOT)
    wd = np.ascontiguousarray(inp["w_down"][l].reshape(4, 11, 128, 4, 512).transpose(3, 0, 2, 1, 4)).reshape(4, 4, 128, 11 * 512)
    vecs = np.zeros((128, V_IN), np.float32)
    vecs[:, V_BMOD:V_BMOD + 96] = _fm16(inp["b_mod"][l])
    if l == 0:
        vecs[:, V_LNPG:V_LNPG + 16] = np.ones((128, 16), np.float32)
    else:
        vecs[:, V_LNPG:V_LNPG + 16] = _fm16(inp["ln2_g"][l - 1])
        vecs[:, V_LNPB:V_LNPB + 16] = _fm16(inp["ln2_b"][l - 1])
    vecs[:, V_LN1G:V_LN1G + 16] = _fm16(inp["ln1_g"][l])
    vecs[:, V_LN1B:V_LN1B + 16] = _fm16(inp["ln1_b"][l])
    vecs[:, V_CONVW:V_CONVW + 32] = inp["conv_w"][l].reshape(4, 8, 128).transpose(2, 1, 0).reshape(128, 32)
    vecs[:, V_CONVB:V_CONVB + 8] = _fm16(inp["conv_b"][l])
    vecs[:, V_LBL:V_LBL + 8 * depth] = inp["lb_logits"].reshape(depth, 8, 128).transpose(2, 1, 0).reshape(128, 8 * depth)
    vecs[:, V_NRMW:V_NRMW + 16] = _fm16(np.concatenate([inp["mlstm_norm_w"][l], inp["hgrn_norm_w"][l]]))
    grow = np.zeros((1, 8 + DEPTH), np.float32)
    grow[0, 0:4] = inp["b_igate"][l]
    grow[0, 4:8] = inp["b_fgate"][l]
    for i in range(1, l + 1):
        grow[0, 8 + i] = np.float32(1.0)
    return dict(w1fm=w1fm, w1tm=w1tm, w1ifg=w1ifg, wmod=wmod, wout=wout, wgu=wgu, wd=wd, vecs=vecs, grow=grow)


_PROGS = {}


def kernel(**inputs):
    inp = {k: np.asarray(v) for k, v in inputs.items()}
    x = inp["x"]
    B, T_, _ = x.shape
    depth = inp["w_in"].shape[0]
    SEG = NCORE // B
    TOK = T_ // SEG
    key = (TOK, depth, DEPTH)
    if key not in _PROGS:
        _PROGS[key] = build_fused(TOK, depth)
    nc = _PROGS[key]
    LA = [_layer_arrays(inp, l, depth) for l in range(depth)]
    shared = {k: np.ascontiguousarray(np.concatenate([LA[l][k] for l in range(depth)], axis=0))
              for k in ("wmod", "w1fm", "w1tm", "w1ifg", "wout", "wgu", "wd", "vecs", "grow")}
    del LA
    shared["identd"] = np.eye(128, dtype=np.float32)
    shared["trid"] = np.triu(np.ones((128, 128), np.uint8))
    shared["fin"] = np.ascontiguousarray(np.stack([inp["ln2_g"][depth - 1], inp["ln2_b"][depth - 1]]).astype(np.float32))
    in_maps = []
    for i in range(NCORE):
        b, sgi = i // SEG, i % SEG
        sel = np.zeros((128, 4), np.float32)
        sel[:, sgi] = 1.0
        selp = np.zeros((128, 4), np.float32)
        if sgi > 0:
            selp[:, sgi - 1] = 1.0
        halo = np.zeros((3, D), np.float32) if sgi == 0 else np.ascontiguousarray(x[b, sgi * TOK - 3:sgi * TOK, :])
        m = dict(shared)
        m.update(x_in=np.ascontiguousarray(x[b, sgi * TOK:(sgi + 1) * TOK, :]), halo0=halo,
                 hflag=np.full((128, 1), 0.0 if sgi == 0 else 1.0, np.float32), c_fm=_fm16(inp["c"][b]), sel=sel, selprev=selp)
        in_maps.append(m)
    res = run_bass_kernel_spmd(nc, in_maps, core_ids=list(range(NCORE))).results
    y = np.empty((B, T_, D), np.float32)
    for i in range(NCORE):
        y[i // SEG, (i % SEG) * TOK:(i % SEG + 1) * TOK, :] = res[i]["OUT"]
    return y


_DEBUG = None
```

```python
import contextlib
import math
import numpy as np
import ml_dtypes
import concourse.bass as bass
import concourse.mybir as mybir
from concourse.bass_utils import run_bass_kernel_spmd

F32 = mybir.dt.float32
BF16 = mybir.dt.bfloat16
U8 = mybir.dt.uint8
AF = mybir.ActivationFunctionType
ALU = mybir.AluOpType
AX = mybir.AxisListType

D = 2048
DFF = 5632
KC = D // 128
FC = DFF // 128
TT = 512
EPS = 1e-5
DEPTH = 4
NCORE = 8
ALPHA = (2 * DEPTH) ** 0.25
WSLOT = 8192
NWS = 3
MW = 260
NST = 2304
NPC = NST // 256
LNSC = math.log(128.0 ** -0.5)

NO_CC = False
PIPE_ML = True
SKIP_ML = False
SKIP_HG = False
PIPE_HG = True
ENGS = ("pe", "act", "dve", "pool", "sp")
SEM_EPOCH = 30000
NDS = 20


class _Op:
    __slots__ = ("eng", "fn", "waits", "is_dma", "needed", "sem", "val")

    def __init__(self, eng, fn, is_dma):
        self.eng = eng
        self.fn = fn
        self.is_dma = is_dma
        self.waits = []
        self.needed = False
        self.sem = None
        self.val = None


class Sched:
    def __init__(self, nc):
        self.nc = nc
        self.q = {e: [] for e in ENGS}
        self.last_w = {}
        self.readers = {}
        self.all_ops = []

    def op(self, eng, fn, reads=(), writes=(), dma=False, nofence=False):
        o = _Op(eng, fn, dma)
        deps = []
        seen = set()
        if not nofence:
            reads = list(reads) + ["__fence__"]

        def add(d):
            if d is None or id(d) in seen:
                return
            seen.add(id(d))
            deps.append(d)
        for k in reads:
            add(self.last_w.get(k))
        for k in writes:
            add(self.last_w.get(k))
            for r in self.readers.get(k, ()):
                add(r)
        for d in deps:
            if d.eng == "pe" and eng == "pe" and not d.is_dma and not dma:
                continue
            d.needed = True
            o.waits.append(d)
        for k in reads:
            self.readers.setdefault(k, []).append(o)
        for k in writes:
            self.last_w[k] = o
            self.readers[k] = []
        self.q[eng].append(o)
        self.all_ops.append(o)
        return o

    def fence(self, scratch):
        self.op("dve", lambda e: e.memset(scratch, 0.0), writes=["__fence__"], nofence=True)

    def barrier_all(self, eng="sp"):
        o = _Op(eng, None, False)
        for d in self.all_ops:
            if d.is_dma:
                d.needed = True
                o.waits.append(d)
        self.q[eng].append(o)
        self.all_ops.append(o)

    def run(self):
        nc = self.nc
        cnt = {e: 0 for e in ENGS}
        di = {e: 0 for e in ENGS}
        dlast = {}
        dcount = {}
        for o in self.all_ops:
            if not o.needed:
                continue
            e = o.eng
            if o.is_dma == "cc":
                o.sem, o.val = f"cc_{di[e]}", 1
                di[e] += 1
            elif o.is_dma:
                name = f"d_{e}_{di[e] % NDS}"
                di[e] += 1
                prev = dlast.get(name)
                if prev is not None and prev not in o.waits:
                    o.waits.append(prev)
                dcount[name] = dcount.get(name, 0) + 16
                o.sem, o.val = name, dcount[name]
                dlast[name] = o
            else:
                n = cnt[e]
                o.sem = f"c_{e}_{n // SEM_EPOCH}"
                o.val = n % SEM_EPOCH + 1
                cnt[e] = n + 1
        names = sorted({o.sem for o in self.all_ops if o.sem is not None})
        for o in self.all_ops:
            best = {}
            for d in o.waits:
                if d.sem not in best or d.val > best[d.sem].val:
                    best[d.sem] = d
            o.waits = list(best.values())
        with contextlib.ExitStack() as es:
            S = {n: es.enter_context(nc.semaphore(n)) for n in names}
            block = es.enter_context(nc.Block())

            def body(eng_name):
                def _f(eng):
                    known = {}
                    for o in self.q[eng_name]:
                        for d in o.waits:
                            if known.get(d.sem, 0) >= d.val:
                                continue
                            eng.wait_ge(S[d.sem], d.val)
                            known[d.sem] = d.val
                        if o.fn is None:
                            continue
                        ins = o.fn(eng)
                        if o.needed:
                            ins.then_inc(S[o.sem], 16 if (o.is_dma and o.is_dma != "cc") else 1)
                return _f
            block.tensor(body("pe"))
            block.scalar(body("act"))
            block.vector(body("dve"))
            block.gpsimd(body("pool"))
            block.sync(body("sp"))
        return len(names)


class WStream:
    def __init__(self, S, buf, record=None):
        self.S = S
        self.buf = buf
        self.seq = [] if record is None else record
        self.recording = record is None
        self.k = 0
        self.issued = 0
        self.ahead = NWS

    def _issue(self, j):
        ap, n = self.seq[j]
        slot = j % NWS
        dst = self.buf[:, slot * WSLOT: slot * WSLOT + n]
        self.S.op("pool", lambda e, dst=dst, ap=ap: e.dma_start(out=dst, in_=ap),
                  writes=[("w", slot)], dma=True, nofence=True)

    def get(self, ap, n):
        if self.recording:
            self.seq.append((ap, n))
            return self.buf[:, 0:n], ("w", 0)
        k = self.k
        while self.issued < min(len(self.seq), k + self.ahead):
            self._issue(self.issued)
            self.issued += 1
        self.k += 1
        slot = k % NWS
        return self.buf[:, slot * WSLOT: slot * WSLOT + n], ("w", slot)


def _bank(ps, b):
    return ps[:, b * 512:(b + 1) * 512]


def emit_transpose_affine(S, T, src_key, gcol, bcol, extra_reads=()):
    X, aT, ps, ident, vf = T["X"], T["aT"], T["ps"], T["ident"], T["vf"]
    for kc in range(KC):
        bank = 4 + (kc % 2)
        pst = _bank(ps, bank)
        for m in range(4):
            S.op("pe", lambda e, pst=pst, m=m, kc=kc: e.transpose(
                pst[:, m * 128:(m + 1) * 128], X[:, m, kc * 128:(kc + 1) * 128], ident[:]),
                reads=[src_key, "ident"], writes=[("ps", bank)])
        S.op("act", lambda e, pst=pst, kc=kc: e.activation(
            out=aT[:, kc, :], in_=pst, func=AF.Identity,
            scale=vf[:, gcol + kc:gcol + kc + 1], bias=vf[:, bcol + kc:bcol + kc + 1]),
            reads=[("ps", bank), "vf", *extra_reads], writes=[("aT", kc)])


def emit_resid_prep(S, T, gi, bi):
    X, bc = T["X"], T["bc"]
    for m in range(4):
        S.op("pool", lambda e, m=m: e.tensor_tensor(out=X[:, m, :], in0=X[:, m, :], in1=bc[:, gi, :], op=ALU.mult),
             reads=[("bc", gi)], writes=["X"])
        S.op("pool", lambda e, m=m: e.tensor_tensor(out=X[:, m, :], in0=X[:, m, :], in1=bc[:, bi, :], op=ALU.add),
             reads=[("bc", bi)], writes=["X"])


def emit_epilogue_block(S, T, m, n, gy):
    X, bc, ps, tmp = T["X"], T["bc"], T["ps"], T["tmp"]
    pm = _bank(ps, m)
    tq = tmp[:, (m % 2) * 512:(m % 2 + 1) * 512]
    S.op("dve", lambda e: e.tensor_tensor(out=tq, in0=pm, in1=bc[:, gy, n * 512:(n + 1) * 512], op=ALU.mult),
         reads=[("ps", m), ("bc", gy)], writes=[("tmp", m % 2)])
    S.op("pool", lambda e: e.tensor_tensor(out=X[:, m, n * 512:(n + 1) * 512],
                                           in0=X[:, m, n * 512:(n + 1) * 512], in1=tq, op=ALU.add),
         reads=[("tmp", m % 2)], writes=["X"])


def emit_ln_inplace(S, T):
    X, st, mv, epsb = T["X"], T["st"], T["mv"], T["epsb"]
    for m in range(4):
        for i in range(4):
            S.op("dve", lambda e, m=m, i=i: e.bn_stats(out=st[:, m, i, :], in_=X[:, m, i * 512:(i + 1) * 512]),
                 reads=["X"], writes=[("st", m)])
        S.op("dve", lambda e, m=m: e.bn_aggr(out=mv[:, m, 0:2], in_=st[:, m, :, :].rearrange("p a b -> p (a b)")),
             reads=[("st", m)], writes=[("mv", m)])
        S.op("act", lambda e, m=m: e.activation(out=mv[:, m, 2:3], in_=mv[:, m, 1:2], func=AF.Sqrt, bias=epsb[:, 0:1]),
             reads=[("mv", m), "epsb"], writes=[("mv", m)])
        S.op("dve", lambda e, m=m: e.reciprocal(out=mv[:, m, 3:4], in_=mv[:, m, 2:3]),
             reads=[("mv", m)], writes=[("mv", m)])
        S.op("dve", lambda e, m=m: e.scalar_tensor_tensor(
            out=mv[:, m, 4:5], in0=mv[:, m, 0:1], scalar=-1.0, in1=mv[:, m, 3:4], op0=ALU.mult, op1=ALU.mult),
            reads=[("mv", m)], writes=[("mv", m)])
        S.op("act", lambda e, m=m: e.activation(out=X[:, m, :], in_=X[:, m, :], func=AF.Identity,
                                                scale=mv[:, m, 3:4], bias=mv[:, m, 4:5]),
             reads=[("mv", m)], writes=["X"])


def emit_ffn(S, ws, T, wgu, wd, gy):
    aT, hT, ps, sg = T["aT"], T["hT"], T["ps"], T["sg"]
    for j2 in range(FC // 2):
        wt, wkey = ws.get(wgu[j2], 2 * 2 * KC * 128)
        wv = wt.rearrange("p (b g k c) -> p b g k c", b=2, g=2, k=KC)
        for b in range(2):
            j = j2 * 2 + b
            bg = 4 + (j % 2) * 2
            bu = bg + 1
            pg, pu = _bank(ps, bg), _bank(ps, bu)
            for g, pp, bk in ((0, pg, bg), (1, pu, bu)):
                for kc in range(KC):
                    S.op("pe", lambda e, pp=pp, wv=wv, b=b, g=g, kc=kc: e.matmul(
                        pp, wv[:, b, g, kc, :], aT[:, kc, :], start=(kc == 0), stop=(kc == KC - 1)),
                        reads=[wkey, ("aT", kc)], writes=[("ps", bk)])
            sgt = sg[:, (j % 2) * 512:(j % 2 + 1) * 512]
            S.op("act", lambda e, sgt=sgt, pg=pg: e.activation(out=sgt, in_=pg, func=AF.Silu),
                 reads=[("ps", bg)], writes=[("sg", j % 2)])
            S.op("dve", lambda e, sgt=sgt, pu=pu, j=j: e.tensor_tensor(out=hT[:, j, :], in0=pu, in1=sgt, op=ALU.mult),
                 reads=[("ps", bu), ("sg", j % 2)], writes=[("hT", j)])
    emit_resid_prep(S, T, 3, 4)
    for n in range(4):
        for jg in range(4):
            wt, wkey = ws.get(wd[n, jg], 11 * 512)
            wv = wt.rearrange("p (j c) -> p j c", j=11)
            for m in range(4):
                pm = _bank(ps, m)
                for jj in range(11):
                    j = jg * 11 + jj
                    S.op("pe", lambda e, pm=pm, wv=wv, jj=jj, j=j, m=m: e.matmul(
                        pm, hT[:, j, m * 128:(m + 1) * 128], wv[:, jj, :], start=(j == 0), stop=(j == FC - 1)),
                        reads=[wkey, ("hT", j)], writes=[("ps", m)])
        for m in range(4):
            emit_epilogue_block(S, T, m, n, gy)
    emit_ln_inplace(S, T)


V_BMOD = 0
V_LNPG = 96
V_LNPB = 112
V_LN1G = 128
V_LN1B = 144
V_CONVW = 160
V_CONVB = 192
V_LBL = 200
V_NRMW = 232
V_IN = 248
V_MOD = 256
V_GU1 = 352
V_BU1 = 368
V_GU2 = 384
V_BU2 = 400
V_ROWS = 416
V_LB = 512
V_OML = 520
V_TOT = 544


def _emit_prologue(S, T, g):
    ident, identb, tri, utri, ones, m0 = g["ident"], g["identb"], g["tri"], g["utri"], g["ones"], g["m0"]

    def dma_in(dst, src, key, eng="sp"):
        S.op(eng, lambda e: e.dma_start(out=dst, in_=src), writes=[key], dma=True)
    dma_in(ident[:], g["identd"], "ident")
    dma_in(identb[:], g["identd"], "identb", eng="pool")
    dma_in(tri[:], g["trid"], "tri")
    dma_in(g["cact"][:], g["c_fm"], "cact")
    dma_in(g["hfl"][:], g["hflag"], "hfl")
    dma_in(g["selt"][:], g["sel"], "selt")
    dma_in(g["selp"][:], g["selprev"], "selp")
    S.op("dve", lambda e: e.tensor_copy(out=utri[:], in_=tri[:]), reads=["tri"], writes=["utri"])
    S.op("dve", lambda e: e.memset(ones[:], 1.0), writes=["ones"])
    S.op("dve", lambda e: e.memset(g["lnsc"][:], LNSC), writes=["lnsc"])
    S.op("dve", lambda e: e.memset(g["epsb"][:], EPS), writes=["epsb"])
    S.op("dve", lambda e: e.memset(m0[:], 1.0), writes=["m0"])
    S.op("dve", lambda e: e.memset(m0[:].rearrange("p (c t) -> p c t", t=64)[:, :, 0:1], 0.0), writes=["m0"])
    S.op("dve", lambda e: e.memset(g["A_sb4"][:], 0.0), writes=[("A_sb4", p, m) for p in range(2) for m in range(4)])
    S.op("dve", lambda e: e.memset(g["A_h8"][:], 0.0), writes=[("A_h8", p, c) for p in range(2) for c in range(8)])
    S.op("act", lambda e: e.activation(out=g["cactb"][:], in_=g["cact"][:], func=AF.Silu), reads=["cact"], writes=["cactb"])


def _emit_p1(S, ws, nc, T, g, NT, l):
    X, aT, ps, vf = T["X"], T["aT"], T["ps"], T["vf"]
    ident, identb, tri, utri, ones, m0 = g["ident"], g["identb"], g["tri"], g["utri"], g["ones"], g["m0"]

    def dma_in(dst, src, key, eng="sp"):
        S.op(eng, lambda e: e.dma_start(out=dst, in_=src), writes=[key], dma=True)
    dma_in(vf[:, 0:V_IN], g["vecs"], "vf")
    dma_in(g["gbias"][:], g["grow"].partition_broadcast(128), "gbias")
    dma_in(g["wifg"][:], g["w1ifg"], "wifg", eng="pool")
    S.op("dve", lambda e: e.memset(g["C32"][:], 0.0), writes=[("C32", h) for h in range(4)])
    S.op("dve", lambda e: e.memset(g["S32"][:], 0.0), writes=[("S32", h) for h in range(8)])
    S.op("dve", lambda e: e.memset(g["carry"][:], 0.0), writes=["carry"])
    S.op("dve", lambda e: e.memset(g["hcarry"][:], 0.0), writes=["hcarry"])
    S.op("dve", lambda e: e.memset(g["vaug"][:], 0.0), writes=[("vaug", m) for m in range(4)])
    cact, cactb = g["cact"], g["cactb"]
    psm = _bank(ps, 7)
    for t in range(24):
        wt, wkey = ws.get(g["wmod"][t], WSLOT)
        wv = wt.rearrange("p (b k c) -> p b k c", b=4, k=KC)
        for b in range(4):
            j = t * 4 + b
            for kc in range(KC):
                S.op("pe", lambda e, wv=wv, b=b, kc=kc, j=j: e.matmul(
                    psm[:, j:j + 1], wv[:, b, kc, :], cactb[:, kc:kc + 1], start=(kc == 0), stop=(kc == KC - 1)),
                    reads=[wkey, "cactb"], writes=[("ps", 7)])
    S.op("dve", lambda e: e.tensor_tensor(out=vf[:, V_MOD:V_MOD + 96], in0=psm[:, 0:96], in1=vf[:, V_BMOD:V_BMOD + 96], op=ALU.add),
         reads=[("ps", 7), "vf"], writes=["vf"])

    def vop(fn):
        S.op("dve", fn, writes=["vf"])
    SH1, SC1, G1, SH2, SC2, G2 = (V_MOD + 16 * i for i in range(6))

    def c(off):
        return vf[:, off:off + 16]
    for (lg, lb_, sc, sh, gu, bu) in ((V_LNPG, V_LNPB, SC1, SH1, V_GU1, V_BU1), (V_LN1G, V_LN1B, SC2, SH2, V_GU2, V_BU2)):
        vop(lambda e, lg=lg, sc=sc, gu=gu: e.scalar_tensor_tensor(out=c(gu), in0=c(sc), scalar=1.0, in1=c(lg), op0=ALU.add, op1=ALU.mult))
        vop(lambda e, lb_=lb_, sc=sc, bu=bu: e.scalar_tensor_tensor(out=c(bu), in0=c(sc), scalar=1.0, in1=c(lb_), op0=ALU.add, op1=ALU.mult))
        vop(lambda e, sh=sh, bu=bu: e.tensor_tensor(out=c(bu), in0=c(bu), in1=c(sh), op=ALU.add))
    for i, (lg, lb_, gg) in enumerate(((V_LNPG, V_LNPB, G1), (V_LN1G, V_LN1B, G2))):
        r0 = V_ROWS + 48 * i
        vop(lambda e, lg=lg, r0=r0: e.tensor_scalar(out=c(r0), in0=c(lg), scalar1=ALPHA, scalar2=None, op0=ALU.mult))
        vop(lambda e, lb_=lb_, r0=r0: e.tensor_scalar(out=c(r0 + 16), in0=c(lb_), scalar1=ALPHA, scalar2=None, op0=ALU.mult))
        vop(lambda e, gg=gg, r0=r0: e.tensor_scalar(out=c(r0 + 32), in0=c(gg), scalar1=1.0, scalar2=None, op0=ALU.add))
    lbl = vf[:, V_LBL:V_LBL + 8 * DEPTH].rearrange("p (h l) -> p h l", l=DEPTH)
    sm = g["hsc2"][:, 0]
    vop(lambda e: e.tensor_reduce(out=sm[:, :, 0], in_=lbl, axis=AX.X, op=ALU.max))
    vop(lambda e: e.tensor_tensor(out=lbl, in0=lbl, in1=sm[:, :, 0:1].broadcast_to([128, 8, DEPTH]), op=ALU.subtract))
    S.op("act", lambda e: e.activation(out=lbl, in_=lbl, func=AF.Exp), reads=["vf"], writes=["vf"])
    vop(lambda e: e.tensor_reduce(out=sm[:, :, 0], in_=lbl, axis=AX.X, op=ALU.add))
    vop(lambda e: e.reciprocal(out=sm[:, :, 1], in_=sm[:, :, 0]))
    S.op("dve", lambda e: e.tensor_tensor(out=lbl, in0=lbl, in1=g["gbias"][:, 8:8 + DEPTH].unsqueeze(1).broadcast_to([128, 8, DEPTH]), op=ALU.mult),
         reads=["gbias"], writes=["vf"])
    vop(lambda e: e.tensor_reduce(out=sm[:, :, 0], in_=lbl, axis=AX.X, op=ALU.add))
    vop(lambda e: e.tensor_tensor(out=vf[:, V_LB:V_LB + 8], in0=sm[:, :, 0], in1=sm[:, :, 1], op=ALU.mult))
    vop(lambda e: e.tensor_scalar(out=vf[:, V_OML:V_OML + 8], in0=vf[:, V_LB:V_LB + 8], scalar1=-1.0, scalar2=1.0, op0=ALU.mult, op1=ALU.add))
    prw = _bank(ps, 5)
    rowsb = g["rowsb"]
    for k in range(6):
        bk = 5 if k < 4 else 4
        dst = _bank(ps, bk)[0:16, (k % 4) * 128:(k % 4 + 1) * 128]
        S.op("pe", lambda e, k=k, dst=dst: e.transpose(dst, vf[:, V_ROWS + 16 * k:V_ROWS + 16 * k + 16], ident[:]),
             reads=["vf", "ident"], writes=[("ps", bk)])
    S.op("act", lambda e: e.activation(out=rowsb[0:16, 0:512], in_=prw[0:16, 0:512], func=AF.Copy), reads=[("ps", 5)], writes=["rowsb"])
    S.op("act", lambda e: e.activation(out=rowsb[0:16, 512:768], in_=_bank(ps, 4)[0:16, 0:256], func=AF.Copy), reads=[("ps", 4)], writes=["rowsb"])
    S.op("sp", lambda e: e.dma_start(out=g["ROWS"].rearrange("k (c p) -> c k p", p=128), in_=rowsb[0:16, :].rearrange("c (k p) -> c k p", p=128)),
         reads=["rowsb"], writes=["ROWS"], dma=True)

    hx, aTh, hfl = g["hx"], g["aTh"], g["hfl"]
    if l == 0:
        dma_in(hx, g["halo0"], "X")
    else:
        hx2, selp = g["hx2"], g["selp"]
        for k in range(4):
            S.op("sp", lambda e, k=k: e.dma_start(out=hx2, in_=g["HALO_DST"][k * 128:k * 128 + 24, :].rearrange("(r c) w -> r (c w)", r=3)), reads=["HALO_DST"], writes=["hx2"], dma=True)
            if k == 0:
                S.op("dve", lambda e: e.tensor_scalar(out=hx, in0=hx2, scalar1=selp[0:3, 0:1], scalar2=None, op0=ALU.mult),
                     reads=["hx2", "selp"], writes=["X"])
            else:
                S.op("dve", lambda e, k=k: e.scalar_tensor_tensor(out=hx, in0=hx2, scalar=selp[0:3, k:k + 1], in1=hx, op0=ALU.mult, op1=ALU.add),
                     reads=["hx2", "selp"], writes=["X"])
    psh = _bank(ps, 6)
    for kc in range(KC):
        S.op("pe", lambda e, kc=kc: e.transpose(psh[:, kc * 4:kc * 4 + 3], hx[:, kc * 128:(kc + 1) * 128], ident[0:3, 0:3]),
             reads=["X", "ident"], writes=[("ps", 6)])
    for kc in range(KC):
        S.op("act", lambda e, kc=kc: e.activation(out=aTh[:, kc, :], in_=psh[:, kc * 4:kc * 4 + 3], func=AF.Identity,
                                                  scale=vf[:, V_GU1 + kc:V_GU1 + kc + 1], bias=vf[:, V_BU1 + kc:V_BU1 + kc + 1]),
             reads=[("ps", 6), "vf"], writes=["aTh"])

    hist, ext, t1, qkT = g["hist"], g["ext"], g["t1"], g["qkT"]
    gsb, l1, nb, nbL, wv_, eo, ec, eL, carry, ctmp = (g[k] for k in ("gsb", "l1", "nb", "nbL", "wv_", "eo", "ec", "eL", "carry", "ctmp"))
    vaug, C32, vH = (g[k] for k in ("vaug", "C32", "vH"))
    sig, kk, bcs, qs, ebuf, qdec = (g[k] for k in ("sig", "kk", "bcs", "qs", "ebuf", "qdec"))
    hcarry, S32, gts, wifg, gbias = (g[k] for k in ("hcarry", "S32", "gts", "wifg", "gbias"))
    OHs = X
    xsrc = g["xsrc"]
    ppi = [0]

    def pp_next():
        b = ppi[0] % 2
        ppi[0] += 1
        return b

    for t in range(NT):
        tok0 = t * TT
        S.op("sp", lambda e, tok0=tok0, xsrc=xsrc: e.dma_start(
            out=X[:], in_=xsrc[tok0:tok0 + TT, :].rearrange("(m p) d -> p m d", p=128)),
            reads=["XN"], writes=["X"], dma=True)
        emit_transpose_affine(S, T, "X", V_GU1, V_BU1)
        pg = _bank(ps, 2)
        wg3 = wifg[:].rearrange("p (k c) -> p k c", c=8)
        for m in range(4):
            for kc in range(KC):
                S.op("pe", lambda e, m=m, kc=kc: e.matmul(pg[:, m * 8:(m + 1) * 8], aT[:, kc, m * 128:(m + 1) * 128], wg3[:, kc, :],
                                                          start=(kc == 0), stop=(kc == KC - 1)),
                     reads=[("aT", kc), "wifg"], writes=[("ps", 2)])
        S.op("dve", lambda e: e.tensor_tensor(out=gsb[:], in0=pg[:, 0:32].rearrange("p (m c) -> p m c", c=8),
                                              in1=gbias[:, 0:8].unsqueeze(1).broadcast_to([128, 4, 8]), op=ALU.add),
             reads=[("ps", 2), "gbias"], writes=["gsb"])
        S.op("act", lambda e: e.activation(out=l1[:], in_=gsb[:, :, 4:8], func=AF.Exp, scale=-1.0), reads=["gsb"], writes=["l1"])
        S.op("act", lambda e: e.activation(out=l1[:], in_=l1[:], func=AF.Ln, bias=1.0), writes=["l1"])
        pc = _bank(ps, 2)
        for m in range(4):
            S.op("pe", lambda e, m=m: e.matmul(pc[:, 64 + m * 4:64 + m * 4 + 4], utri[:], l1[:, m, :], start=True, stop=True),
                 reads=["utri", "l1"], writes=[("ps", 2)])
            S.op("pe", lambda e, m=m: e.matmul(pc[:, 96 + m * 4:96 + m * 4 + 4], ones[:], l1[:, m, :], start=True, stop=True),
                 reads=["ones", "l1"], writes=[("ps", 2)])
        S.op("dve", lambda e: e.tensor_copy(out=nb[:], in_=pc[:, 64:80].rearrange("p (m c) -> p m c", c=4)), reads=[("ps", 2)], writes=["nb"])
        S.op("dve", lambda e: e.tensor_copy(out=nbL[:], in_=pc[:, 96:112].rearrange("p (m c) -> p m c", c=4)), reads=[("ps", 2)], writes=["nbL"])
        S.op("dve", lambda e: e.tensor_tensor(out=ctmp[:], in0=gsb[:, :, 0:4], in1=nb[:], op=ALU.add), reads=["gsb", "nb"], writes=["ctmp"])
        S.op("act", lambda e: e.activation(out=wv_[:], in_=ctmp[:], func=AF.Exp), reads=["ctmp"], writes=["wv"])
        S.op("act", lambda e: e.activation(out=eo[:], in_=nb[:], func=AF.Exp, scale=-1.0, bias=g["lnsc"][:, 0:1]), reads=["nb", "lnsc"], writes=["eo"])
        S.op("act", lambda e: e.activation(out=eL[:], in_=nbL[:], func=AF.Exp, scale=-1.0), reads=["nbL"], writes=["eL"])
        for m in range(4):
            S.op("act", lambda e, m=m: e.activation(out=ctmp[:, m, :], in_=carry[:], func=AF.Exp), reads=["carry"], writes=["ctmp"])
            S.op("dve", lambda e, m=m: e.tensor_tensor(out=ec[:, m, :], in0=eo[:, m, :], in1=ctmp[:, m, :], op=ALU.mult),
                 reads=["eo", "ctmp"], writes=["ec"])
            S.op("dve", lambda e, m=m: e.tensor_tensor(out=carry[:], in0=carry[:], in1=nbL[:, m, :], op=ALU.subtract),
                 reads=["nbL"], writes=["carry"])
        S.op("sp", lambda e, tok0=tok0: e.dma_start(out=g["EM"][tok0:tok0 + TT, :].rearrange("(m p) c -> p m c", p=128), in_=ec[:]),
             reads=["ec"], writes=[("EM", t)], dma=True)
        for qk in range(2):
            wt, wkey = ws.get(g["w1fm"][qk], WSLOT)
            wv4 = wt.rearrange("p (b k c) -> p b k c", b=4, k=KC)
            for b in range(4):
                blk = qk * 4 + b
                bk = pp_next()
                pq = _bank(ps, bk)
                for kc in range(KC):
                    S.op("pe", lambda e, pq=pq, wv4=wv4, b=b, kc=kc: e.matmul(pq, wv4[:, b, kc, :], aT[:, kc, :], start=(kc == 0), stop=(kc == KC - 1)),
                         reads=[wkey, ("aT", kc)], writes=[("ps", bk)])
                if t == 0:
                    ph = _bank(ps, 6)
                    for kc in range(KC):
                        S.op("pe", lambda e, ph=ph, wv4=wv4, b=b, kc=kc, blk=blk: e.matmul(ph[:, 64 + blk * 4:64 + blk * 4 + 3], wv4[:, b, kc, :], aTh[:, kc, :],
                                                                                       start=(kc == 0), stop=(kc == KC - 1)),
                             reads=[wkey, "aTh"], writes=[("ps", 6)])
                    S.op("act", lambda e, ph=ph, blk=blk: e.activation(out=ext[:, 0:3], in_=ph[:, 64 + blk * 4:64 + blk * 4 + 3], func=AF.Copy, scale=g["hfl"][:, 0:1]),
                         reads=[("ps", 6), "hfl"], writes=["ext"])
                else:
                    S.op("act", lambda e, blk=blk: e.activation(out=ext[:, 0:3], in_=hist[:, blk, :], func=AF.Copy),
                         reads=["hist"], writes=["ext"])
                S.op("act", lambda e, pq=pq: e.activation(out=ext[:, 3:TT + 3], in_=pq, func=AF.Copy), reads=[("ps", bk)], writes=["ext"])
                S.op("act", lambda e, blk=blk: e.activation(out=hist[:, blk, :], in_=ext[:, TT:TT + 3], func=AF.Copy), reads=["ext"], writes=["hist"])
                cw = V_CONVW + blk * 4
                S.op("dve", lambda e, cw=cw, blk=blk: e.tensor_scalar(out=t1[:], in0=ext[:, 0:TT], scalar1=vf[:, cw:cw + 1], scalar2=vf[:, V_CONVB + blk:V_CONVB + blk + 1],
                                                                   op0=ALU.mult, op1=ALU.add), reads=["ext", "vf"], writes=["t1"])
                for j in range(1, 4):
                    S.op("dve", lambda e, cw=cw, j=j: e.scalar_tensor_tensor(out=t1[:], in0=ext[:, j:TT + j], scalar=vf[:, cw + j:cw + j + 1], in1=t1[:],
                                                                           op0=ALU.mult, op1=ALU.add), reads=["ext", "vf"], writes=["t1"])
                S.op("act", lambda e, blk=blk: e.activation(out=qkT[:, blk, :], in_=t1[:], func=AF.Silu), reads=["t1"], writes=[("qkT", blk)])
        S.op("sp", lambda e, tok0=tok0: e.dma_start(out=g["QM"][:, tok0:tok0 + TT].rearrange("(h p) t -> p h t", p=128), in_=qkT[:, 0:4, :]),
             reads=[("qkT", b) for b in range(4)], writes=[("QM", t)], dma=True)
        for half in range(2):
            wt, wkey = ws.get(g["w1tm"][half], WSLOT)
            wv3 = wt.rearrange("p (k c) -> p k c", k=KC)
            for m in range(4):
                bk = pp_next()
                pv = _bank(ps, bk)
                for kc in range(KC):
                    S.op("pe", lambda e, pv=pv, wv3=wv3, m=m, kc=kc: e.matmul(pv, aT[:, kc, m * 128:(m + 1) * 128], wv3[:, kc, :], start=(kc == 0), stop=(kc == KC - 1)),
                         reads=[wkey, ("aT", kc)], writes=[("ps", bk)])
                for hh in range(2):
                    h = half * 2 + hh
                    S.op("act", lambda e, pv=pv, m=m, h=h, hh=hh: e.activation(out=vaug[:, m, h, 0:256], in_=pv[:, hh * 256:(hh + 1) * 256], func=AF.Copy,
                                                                            scale=wv_[:, m, h:h + 1]),
                         reads=[("ps", bk), "wv"], writes=[("vaug", m)])
        for m in range(4):
            S.op("dve", lambda e, m=m: e.tensor_copy(out=vaug[:, m, :, 256], in_=wv_[:, m, :]), reads=["wv"], writes=[("vaug", m)])
        pa = _bank(ps, 3)
        ktok4, A_sb4, dCs, Cbf4 = g["ktok4"], g["A_sb4"], g["dCs"], g["Cbf4"]

        def ml_abc(h):
            par = h % 2
            p2 = _bank(ps, 2).bitcast(BF16)
            for m in range(4):
                sl = slice(m * 128, (m + 1) * 128)
                S.op("pe", lambda e, m=m, sl=sl: e.transpose(p2[:, m * 128:(m + 1) * 128], qkT[:, 4 + h, sl], identb[:]),
                     reads=[("qkT", 4 + h), "identb"], writes=[("ps", 2)])
            for m in range(4):
                S.op("act", lambda e, m=m: e.activation(out=ktok4[:, par, m, :], in_=p2[:, m * 128:(m + 1) * 128], func=AF.Copy),
                     reads=[("ps", 2)], writes=[("ktok4", par, m)])
            for m in range(4):
                sl = slice(m * 128, (m + 1) * 128)
                S.op("pe", lambda e, m=m, sl=sl: e.matmul(pa[:, m * 128:(m + 1) * 128], qkT[:, 4 + h, sl], qkT[:, h, sl], start=True, stop=True),
                     reads=[("qkT", 4 + h), ("qkT", h)], writes=[("ps", 3)])
            for m in range(4):
                S.op("dve", lambda e, m=m: e.copy_predicated(out=A_sb4[:, par, m, :], mask=tri[:], data=pa[:, m * 128:(m + 1) * 128]),
                     reads=[("ps", 3), "tri"], writes=[("A_sb4", par, m)])
            for m in range(4):
                bk = 6 + m % 2
                pC = _bank(ps, bk)
                S.op("pe", lambda e, pC=pC, m=m: e.matmul(pC[:, 0:257], ktok4[:, par, m, :], vaug[:, m, h, 0:257], start=True, stop=True),
                     reads=[("ktok4", par, m), ("vaug", m)], writes=[("ps", bk)])
                S.op("act", lambda e, pC=pC, m=m: e.activation(out=dCs[:, m, 0:257], in_=pC[:, 0:257], func=AF.Copy),
                     reads=[("ps", bk)], writes=[("dCs", m)])
            for m in range(4):
                S.op("dve", lambda e, m=m: e.tensor_copy(out=Cbf4[:, par, m, :], in_=C32[:, h, :]), reads=[("C32", h)], writes=[("Cbf4", par, m)])
                S.op("dve", lambda e, m=m: e.tensor_scalar(out=C32[:, h, 0:257], in0=C32[:, h, 0:257], scalar1=eL[:, m, h:h + 1], scalar2=None, op0=ALU.mult),
                     reads=["eL"], writes=[("C32", h)])
                S.op("dve", lambda e, m=m: e.scalar_tensor_tensor(out=C32[:, h, 0:257], in0=dCs[:, m, 0:257], scalar=eL[:, m, h:h + 1], in1=C32[:, h, 0:257],
                                                                 op0=ALU.mult, op1=ALU.add), reads=[("dCs", m), "eL"], writes=[("C32", h)])

        def ml_d(h):
            par = h % 2
            for m in range(4):
                sl = slice(m * 128, (m + 1) * 128)
                bo = 4 + m % 2
                pP = _bank(ps, bo)
                S.op("pe", lambda e, pP=pP, m=m: e.matmul(pP[:, 0:257], A_sb4[:, par, m, :], vaug[:, m, h, 0:257], start=True, stop=False),
                     reads=[("A_sb4", par, m), ("vaug", m)], writes=[("ps", bo)])
                S.op("pe", lambda e, pP=pP, m=m, sl=sl: e.matmul(pP[:, 0:257], qkT[:, h, sl], Cbf4[:, par, m, 0:257], start=False, stop=True),
                     reads=[("qkT", h), ("Cbf4", par, m)], writes=[("ps", bo)])
                r = m % 2
                S.op("act", lambda e, pP=pP, m=m, r=r: e.activation(out=g["OMs"][:, r, 0:257], in_=pP[:, 0:257], func=AF.Copy, scale=eo[:, m, h:h + 1]),
                     reads=[("ps", bo), "eo"], writes=[("OMs", r)])
                S.op("sp", lambda e, m=m, r=r, tok0=tok0: e.dma_start(out=g["OM"][tok0 + m * 128:tok0 + (m + 1) * 128, h * MW:h * MW + 257], in_=g["OMs"][:, r, 0:257]),
                     reads=[("OMs", r)], writes=[("OM", t, m, h)], dma=True)
        for h in range(4):
            if SKIP_ML:
                break
            ml_abc(h)
            if PIPE_ML:
                if h > 0:
                    ml_d(h - 1)
            else:
                ml_d(h)
        if PIPE_ML and not SKIP_ML:
            ml_d(3)
        for half in range(2):
            wt, wkey = ws.get(g["w1tm"][2 + half], WSLOT)
            wv3 = wt.rearrange("p (k c) -> p k c", k=KC)
            for m in range(4):
                bk = pp_next()
                pv = _bank(ps, bk)
                for kc in range(KC):
                    S.op("pe", lambda e, pv=pv, wv3=wv3, m=m, kc=kc: e.matmul(pv, aT[:, kc, m * 128:(m + 1) * 128], wv3[:, kc, :], start=(kc == 0), stop=(kc == KC - 1)),
                         reads=[wkey, ("aT", kc)], writes=[("ps", bk)])
                S.op("act", lambda e, pv=pv, m=m, half=half: e.activation(out=vH[:, m, half * 512:(half + 1) * 512], in_=pv, func=AF.Copy),
                     reads=[("ps", bk)], writes=[("vH", m)])
        gstate = {}
        ws.ahead = NWS - 1

        def gate_unit(u):
            gi, m = u // 4, u % 4
            if m == 0:
                wt, wkey = ws.get(g["w1tm"][4 + gi], WSLOT)
                gstate["wv3"], gstate["wkey"] = wt.rearrange("p (k c) -> p k c", k=KC), wkey
            wv3, wkey = gstate["wv3"], gstate["wkey"]
            fn = AF.Sigmoid if gi < 2 else AF.Silu
            bk = pp_next()
            pv = _bank(ps, bk)
            for kc in range(KC):
                S.op("pe", lambda e, kc=kc: e.matmul(pv, aT[:, kc, m * 128:(m + 1) * 128], wv3[:, kc, :], start=(kc == 0), stop=(kc == KC - 1)),
                     reads=[wkey, ("aT", kc)], writes=[("ps", bk)])
            g2 = u % 2
            S.op("act", lambda e: e.activation(out=gts[:, g2, :], in_=pv, func=fn), reads=[("ps", bk)], writes=[("gts", g2)])
            S.op("sp", lambda e, tok0=tok0: e.dma_start(out=g["GT"][tok0 + m * 128:tok0 + (m + 1) * 128, gi * 512:(gi + 1) * 512], in_=gts[:, g2, :]),
                 reads=[("gts", g2)], writes=[("GT", t, gi, m)], dma=True)
        A_h8, Sp8 = g["A_h8"], g["Sp8"]
        K6, K7 = ("ps", 6), ("ps", 7)

        def hg_abc(h, par, first):
            hsc = g["hsc2"][:, par]
            q_in = g["q_in2"][:, par, :]
            k_in = g["k_in2"][:, par, :]
            p2 = _bank(ps, 2).bitcast(BF16)
            for m in range(4):
                S.op("pe", lambda e, m=m: e.transpose(p2[:, m * 128:(m + 1) * 128], k_in[:, m * 128:(m + 1) * 128], identb[:]),
                     reads=[("k_in", par), "identb"], writes=[("ps", 2)])
            for m in range(4):
                S.op("act", lambda e, m=m: e.activation(out=ktok4[:, par, m, :], in_=p2[:, m * 128:(m + 1) * 128], func=AF.Copy),
                     reads=[("ps", 2)], writes=[("ktok4", par, m)])
            for cc in range(8):
                m, hf = cc // 2, cc % 2
                c0, p0 = cc * 64, hf * 64
                pA = pa[p0:p0 + 64, m * 64:(m + 1) * 64]
                S.op("pe", lambda e, pA=pA, c0=c0: e.matmul(pA, k_in[:, c0:c0 + 64], q_in[:, c0:c0 + 64], start=True, stop=True),
                     reads=[("k_in", par), ("q_in", par)], writes=[("ps", 3)])
            for cc in range(8):
                m, hf = cc // 2, cc % 2
                p0 = hf * 64
                pA = pa[p0:p0 + 64, m * 64:(m + 1) * 64]
                S.op("dve", lambda e, pA=pA, p0=p0, m=m: e.copy_predicated(out=A_h8[p0:p0 + 64, par, m, :], mask=tri[p0:p0 + 64, p0:p0 + 64], data=pA),
                     reads=[("ps", 3), "tri"], writes=[("A_h8", par, cc)])
            for cc in range(8):
                m, hf = cc // 2, cc % 2
                p0 = hf * 64
                bk = 6 + hf
                pS = _bank(ps, bk)[:, m * 128:(m + 1) * 128]
                S.op("pe", lambda e, pS=pS, p0=p0, m=m: e.matmul(pS, ktok4[p0:p0 + 64, par, m, :], vH[p0:p0 + 64, m, h * 128:(h + 1) * 128], start=True, stop=True),
                     reads=[("ktok4", par, m), ("vH", m)], writes=[("ps", bk)])
            for cc in range(8):
                bk = 6 + cc % 2
                pS = _bank(ps, bk)[:, (cc // 2) * 128:(cc // 2 + 1) * 128]
                S.op("dve", lambda e, cc=cc: e.tensor_scalar(out=Sp8[:, par, cc, :], in0=S32[:, h, :], scalar1=hsc[:, cc, 2:3], scalar2=None, op0=ALU.mult),
                     reads=[("S32", h), ("hsc", par)], writes=[("Sp8", par, cc)])
                S.op("dve", lambda e, cc=cc: e.tensor_scalar(out=S32[:, h, :], in0=S32[:, h, :], scalar1=hsc[:, cc, 3:4], scalar2=None, op0=ALU.mult),
                     reads=[("hsc", par)], writes=[("S32", h)])
                S.op("dve", lambda e, pS=pS, cc=cc: e.scalar_tensor_tensor(out=S32[:, h, :], in0=pS, scalar=hsc[:, cc, 4:5], in1=S32[:, h, :], op0=ALU.mult, op1=ALU.add),
                     reads=[("ps", bk), ("hsc", par)], writes=[("S32", h)])

        def hg_d(h, par):
            q_in = g["q_in2"][:, par, :]
            for cc in range(8):
                m, hf = cc // 2, cc % 2
                c0, p0 = cc * 64, hf * 64
                bo = 4 + cc % 2
                pO = _bank(ps, bo)[p0:p0 + 64, 0:128]
                S.op("pe", lambda e, pO=pO, p0=p0, m=m: e.matmul(pO, A_h8[p0:p0 + 64, par, m, :], vH[p0:p0 + 64, m, h * 128:(h + 1) * 128], start=True, stop=False),
                     reads=[("A_h8", par, cc), ("vH", m)], writes=[("ps", bo)])
                S.op("pe", lambda e, pO=pO, c0=c0, cc=cc: e.matmul(pO, q_in[:, c0:c0 + 64], Sp8[:, par, cc, :], start=False, stop=True),
                     reads=[("q_in", par), ("Sp8", par, cc)], writes=[("ps", bo)])
                S.op("act", lambda e, pO=pO, p0=p0, m=m: e.activation(out=OHs[p0:p0 + 64, m, h * 128:(h + 1) * 128], in_=pO, func=AF.Copy),
                     reads=[("ps", bo)], writes=["X"])
        for hp in range(4):
            wt, wkey = ws.get(g["w1fm"][2 + hp], WSLOT)
            wv4 = wt.rearrange("p (b k c) -> p b k c", b=4, k=KC)
            for hh in range(2):
                def _prep(hh=hh, wv4=wv4, wkey=wkey):
                    h = hp * 2 + hh
                    par = h % 2
                    hsc = g["hsc2"][:, par]
                    q_in = g["q_in2"][:, par, :]
                    k_in = g["k_in2"][:, par, :]
                    bq = pp_next()
                    pq = _bank(ps, bq)
                    for kc in range(KC):
                        S.op("pe", lambda e, pq=pq, wv4=wv4, hh=hh, kc=kc: e.matmul(pq, wv4[:, hh * 2, kc, :], aT[:, kc, :], start=(kc == 0), stop=(kc == KC - 1)),
                             reads=[wkey, ("aT", kc)], writes=[("ps", bq)])
                    bf = pp_next()
                    pf = _bank(ps, bf)
                    for kc in range(KC):
                        S.op("pe", lambda e, pf=pf, wv4=wv4, hh=hh, kc=kc: e.matmul(pf, wv4[:, hh * 2 + 1, kc, :], aT[:, kc, :], start=(kc == 0), stop=(kc == KC - 1)),
                             reads=[wkey, ("aT", kc)], writes=[("ps", bf)])
                    S.op("act", lambda e, pq=pq: e.activation(out=qs[:], in_=pq, func=AF.Silu), reads=[("ps", bq)], writes=["qs"])
                    S.op("act", lambda e, pf=pf: e.activation(out=sig[:], in_=pf, func=AF.Sigmoid), reads=[("ps", bf)], writes=["sig"])
                    S.op("dve", lambda e, h=h: e.tensor_scalar(out=sig[:], in0=sig[:], scalar1=vf[:, V_OML + h:V_OML + h + 1], scalar2=vf[:, V_LB + h:V_LB + h + 1],
                                                              op0=ALU.mult, op1=ALU.add), reads=["vf"], writes=["sig"])
                    S.op("dve", lambda e: e.tensor_scalar(out=kk[:], in0=sig[:], scalar1=-1.0, scalar2=1.0, op0=ALU.mult, op1=ALU.add), reads=["sig"], writes=["kk"])
                    S.op("act", lambda e: e.activation(out=sig[:], in_=sig[:], func=AF.Ln), writes=["sig"])
                    S.op("dve", lambda e: e.tensor_tensor_scan(out=bcs[:], data0=m0[:], data1=sig[:], initial=0.0, op0=ALU.mult, op1=ALU.add),
                         reads=["m0", "sig"], writes=["bcs"])
                    b3 = bcs[:].rearrange("p (c t) -> p c t", t=64)
                    S.op("dve", lambda e: e.tensor_copy(out=hsc[:, :, 0], in_=b3[:, :, 31]), reads=["bcs"], writes=[("hsc", par)])
                    S.op("dve", lambda e: e.tensor_copy(out=hsc[:, :, 1], in_=b3[:, :, 63]), reads=["bcs"], writes=[("hsc", par)])
                    S.op("dve", lambda e: e.tensor_tensor(out=hsc[:, :, 7], in0=hsc[:, :, 1], in1=hsc[:, :, 0], op=ALU.subtract), writes=[("hsc", par)])
                    S.op("act", lambda e: e.activation(out=hsc[:, :, 2], in_=hsc[:, :, 0], func=AF.Exp), writes=[("hsc", par)])
                    S.op("act", lambda e: e.activation(out=hsc[:, :, 3], in_=hsc[:, :, 1], func=AF.Exp), writes=[("hsc", par)])
                    S.op("act", lambda e: e.activation(out=hsc[:, :, 4], in_=hsc[:, :, 7], func=AF.Exp), writes=[("hsc", par)])
                    S.op("dve", lambda e, h=h: e.tensor_tensor_scan(out=hsc[:, :, 7], data0=ones[:, 0:8], data1=hsc[:, :, 1], initial=hcarry[:, h:h + 1],
                                                                   op0=ALU.mult, op1=ALU.add), reads=["ones", "hcarry"], writes=[("hsc", par)])
                    S.op("dve", lambda e: e.tensor_tensor(out=hsc[:, :, 5], in0=hsc[:, :, 7], in1=hsc[:, :, 1], op=ALU.subtract), writes=[("hsc", par)])
                    S.op("dve", lambda e, h=h: e.tensor_copy(out=hcarry[:, h:h + 1], in_=hsc[:, 7, 7:8]), reads=[("hsc", par)], writes=["hcarry"])
                    S.op("dve", lambda e: e.tensor_tensor(out=hsc[:, :, 6], in0=hsc[:, :, 5], in1=hsc[:, :, 0], op=ALU.add), writes=[("hsc", par)])
                    S.op("act", lambda e: e.activation(out=hsc[:, :, 6], in_=hsc[:, :, 6], func=AF.Exp), writes=[("hsc", par)])
                    S.op("dve", lambda e: e.tensor_tensor(out=b3, in0=b3, in1=hsc[:, :, 0:1].broadcast_to([128, 8, 64]), op=ALU.subtract),
                         reads=[("hsc", par)], writes=["bcs"])
                    S.op("act", lambda e: e.activation(out=ebuf[:], in_=bcs[:], func=AF.Exp), reads=["bcs"], writes=["ebuf"])
                    S.op("dve", lambda e: e.tensor_tensor(out=q_in, in0=qs[:], in1=ebuf[:], op=ALU.mult), reads=["qs", "ebuf"], writes=[("q_in", par)])
                    S.op("pool", lambda e: e.tensor_tensor(out=qs[:], in0=qs[:], in1=ebuf[:], op=ALU.mult), reads=["ebuf"], writes=["qs"])
                    S.op("pool", lambda e: e.tensor_tensor(out=qdec[:].rearrange("p (c t) -> p c t", t=64), in0=qs[:].rearrange("p (c t) -> p c t", t=64),
                                                           in1=hsc[:, :, 6:7].broadcast_to([128, 8, 64]), op=ALU.mult), reads=["qs", ("hsc", par)], writes=["qdec"])
                    S.op("act", lambda e: e.activation(out=ebuf[:], in_=bcs[:], func=AF.Exp, scale=-1.0), reads=["bcs"], writes=["ebuf"])
                    S.op("dve", lambda e: e.tensor_tensor(out=k_in, in0=kk[:], in1=ebuf[:], op=ALU.mult), reads=["kk", "ebuf"], writes=[("k_in", par)])
                    S.op("sp", lambda e, h=h, tok0=tok0: e.dma_start(out=g["QH"][h * 128:(h + 1) * 128, tok0:tok0 + TT], in_=qdec[:]),
                         reads=["qdec"], writes=[("QH", t, h)], dma=True)
                    return h, par
                h, par = _prep()
                hg_abc(h, par, first=(h == 0))
                if h > 0:
                    hg_d(h - 1, (h - 1) % 2)
                gate_unit(2 * h)
                gate_unit(2 * h + 1)
        hg_d(7, 1)
        ws.ahead = NWS
        S.op("sp", lambda e, tok0=tok0: e.dma_start(out=g["OH"][tok0:tok0 + TT, :].rearrange("(m p) c -> p m c", p=128), in_=OHs[:, :, 0:1024]),
             reads=["X"], writes=[("OH", t)], dma=True)
    stsb = X[:].rearrange("p m d -> p (m d)")[:, 0:NST]
    S.op("dve", lambda e: e.memset(stsb[:, 2064:NST], 0.0), writes=["X"])
    S.op("dve", lambda e: e.tensor_copy(out=stsb[:, 0:1024], in_=S32[:].rearrange("p h c -> p (h c)")), reads=[("S32", h) for h in range(8)], writes=["X"])
    S.op("dve", lambda e: e.tensor_copy(out=stsb[:, 1024:1024 + 4 * MW], in_=C32[:].rearrange("p h c -> p (h c)")), reads=[("C32", h) for h in range(4)], writes=["X"])
    S.op("act", lambda e: e.activation(out=stsb[:, 2064:2072], in_=hcarry[:], func=AF.Exp), reads=["hcarry"], writes=["X"])
    S.op("act", lambda e: e.activation(out=stsb[:, 2072:2076], in_=carry[:], func=AF.Exp), reads=["carry"], writes=["X"])
    S.op("sp", lambda e: e.dma_start(out=g["STATE_SRC"].rearrange("k p c -> p k c"), in_=stsb.rearrange("p (k c) -> p k c", c=256)),
         reads=["X"], writes=["STATE_SRC"], dma=True)
    for k in range(NPC):
        S.op("pool", lambda e, k=k: e.collective_compute("AllGather", ALU.bypass, replica_groups=[[0, 1, 2, 3], [4, 5, 6, 7]],
                                                         ins=[g["STATE_SRC"][k].opt()], outs=[g["STATES"][k].opt()]),
             reads=["STATE_SRC"], writes=[("STATES", k)], dma="cc")


def _emit_p2(S, ws, nc, T, g, NT, l, last):
    X, aT, ps, vf, bc, arena = T["X"], T["aT"], T["ps"], T["vf"], T["bc"], g["arena"]
    ident, identb, selt, Sst, Cst, EMt, hst, mst = (g[k] for k in ("ident", "identb", "selt", "Sst", "Cst", "EMt", "hst", "mst"))
    xsrc = g["xsrc"]

    def hk(a, b):
        return [("hT", j) for j in range(a // TT, (b + TT - 1) // TT)]

    def f32v(off, n):
        return arena[:, off:off + 2 * n].bitcast(F32)

    def dma_in(dst, src, keys, eng="sp", reads=()):
        S.op(eng, lambda e: e.dma_start(out=dst, in_=src), reads=list(reads), writes=list(keys), dma=True)
    blobs = X[:].rearrange("p m d -> p (m d)")
    for k in range(3):
        dma_in(blobs[:, k * NST:(k + 1) * NST].rearrange("p (j c) -> p j c", c=256), g["STATES"][:, k * 128:(k + 1) * 128, :].rearrange("j p c -> p j c"),
               ["X"], reads=[("STATES", j) for j in range(NPC)])
    KS = hk(0, 4 * NST)
    Scur = f32v(0, NST)
    Sacc = f32v(2 * NST, NST)
    S.op("dve", lambda e: e.tensor_copy(out=Scur, in_=blobs[:, 0:NST]), reads=["X"], writes=KS)
    S.op("dve", lambda e: e.tensor_scalar(out=Sacc, in0=Scur, scalar1=selt[:, 1:2], scalar2=None, op0=ALU.mult), reads=["selt"], writes=KS)
    for k in (1, 2):
        bk = blobs[:, k * NST:(k + 1) * NST]
        S.op("dve", lambda e, bk=bk: e.tensor_tensor(out=Scur[:, 0:1024].rearrange("p (h c) -> p h c", c=128), in0=Scur[:, 0:1024].rearrange("p (h c) -> p h c", c=128),
                                                     in1=bk[:, 2064:2072].unsqueeze(2).broadcast_to([128, 8, 128]), op=ALU.mult), reads=["X"], writes=KS)
        S.op("dve", lambda e, bk=bk: e.tensor_tensor(out=Scur[:, 1024:1024 + 4 * MW].rearrange("p (h c) -> p h c", c=MW), in0=Scur[:, 1024:1024 + 4 * MW].rearrange("p (h c) -> p h c", c=MW),
                                                     in1=bk[:, 2072:2076].unsqueeze(2).broadcast_to([128, 4, MW]), op=ALU.mult), reads=["X"], writes=KS)
        S.op("dve", lambda e, bk=bk: e.tensor_tensor(out=Scur[:, 0:2064], in0=Scur[:, 0:2064], in1=bk[:, 0:2064], op=ALU.add), reads=["X"], writes=KS)
        S.op("dve", lambda e, k=k: e.scalar_tensor_tensor(out=Sacc[:, 0:2064], in0=Scur[:, 0:2064], scalar=selt[:, k + 1:k + 2], in1=Sacc[:, 0:2064], op0=ALU.mult, op1=ALU.add),
             reads=["selt"], writes=KS)
    S.op("dve", lambda e: e.tensor_copy(out=Sst[:].rearrange("p h c -> p (h c)"), in_=Sacc[:, 0:1024]), reads=KS, writes=["Sst"])
    S.op("dve", lambda e: e.tensor_copy(out=Cst[:].rearrange("p h c -> p (h c)"), in_=Sacc[:, 1024:1024 + 4 * MW]), reads=KS, writes=["Cst"])

    O_OH, O_OM, O_GT, O_QH, O_QM, O_YC, O_W1, O_W2, O_WM, O_WH = 0, 2048, 4608, 6656, 10752, 12800, 14848, 16896, 18944, 19968
    OHm, K_OH = f32v(O_OH, 1024), hk(O_OH, O_OH + 2048)
    OMm, K_OM = f32v(O_OM, 4 * MW), hk(O_OM, O_OM + 8 * MW)
    GTm, K_GT = arena[:, O_GT:O_GT + 2048], hk(O_GT, O_GT + 2048)
    QHt, K_QH = arena[:, O_QH:O_QH + 4096].rearrange("p (h t) -> p h t", t=TT), hk(O_QH, O_QH + 4096)
    QMt, K_QM = arena[:, O_QM:O_QM + 2048].rearrange("p (h t) -> p h t", t=TT), hk(O_QM, O_QM + 2048)
    ycat, K_YC = arena[:, O_YC:O_YC + 2048], hk(O_YC, O_YC + 2048)
    wk1, K_W1 = f32v(O_W1, 1024), hk(O_W1, O_W1 + 2048)
    wk2, K_W2 = f32v(O_W2, 1024), hk(O_W2, O_W2 + 2048)
    wkm, K_WM = f32v(O_WM, MW), hk(O_WM, O_WM + 2 * MW)
    wkh, K_WH = f32v(O_WH, 256), hk(O_WH, O_WH + 512)

    for t in range(NT):
        tok0 = t * TT
        dma_in(QHt, g["QH"][:, tok0:tok0 + TT].rearrange("(h p) t -> p h t", p=128), K_QH)
        dma_in(QMt, g["QM"][:, tok0:tok0 + TT].rearrange("(h p) t -> p h t", p=128), K_QM)
        dma_in(EMt[:], g["EM"][tok0:tok0 + TT, :].rearrange("(m p) c -> p m c", p=128), ["EMt"])
        for i in range(3):
            dma_in(bc[:, i, :], g["ROWS"][i:i + 1, :].partition_broadcast(128), [("bc", i)], reads=["ROWS"])
        for m in range(4):
            r0 = tok0 + m * 128
            msl = slice(m * 128, (m + 1) * 128)
            dma_in(OHm, g["OH"][r0:r0 + 128, :], K_OH)
            dma_in(OMm, g["OM"][r0:r0 + 128, :], K_OM)
            dma_in(GTm, g["GT"][r0:r0 + 128, :], K_GT)
            for h in range(8):
                bk = 4 + h // 4
                pcor = _bank(ps, bk)[:, (h % 4) * 128:(h % 4 + 1) * 128]
                S.op("pe", lambda e, pcor=pcor, h=h, msl=msl: e.matmul(pcor, QHt[:, h, msl], Sst[:, h, :], start=True, stop=True),
                     reads=K_QH + ["Sst"], writes=[("ps", bk)])
            for hb in range(2):
                S.op("dve", lambda e, hb=hb: e.tensor_tensor(out=wk1[:, hb * 512:(hb + 1) * 512], in0=_bank(ps, 4 + hb), in1=OHm[:, hb * 512:(hb + 1) * 512], op=ALU.add),
                     reads=[("ps", 4 + hb)] + K_OH, writes=K_W1)
            S.op("pool", lambda e: e.tensor_tensor(out=wk2, in0=wk1, in1=wk1, op=ALU.mult), reads=K_W1, writes=K_W2)
            S.op("dve", lambda e: e.tensor_reduce(out=hst[:, :, 0], in_=wk2.rearrange("p (h c) -> p h c", c=128), axis=AX.X, op=ALU.add), reads=K_W2, writes=["hst"])
            S.op("act", lambda e: e.activation(out=hst[:, :, 1], in_=hst[:, :, 0], func=AF.Sqrt, scale=1.0 / 128.0, bias=g["epsb"][:, 0:1]), reads=["hst", "epsb"], writes=["hst"])
            S.op("dve", lambda e: e.reciprocal(out=hst[:, :, 2], in_=hst[:, :, 1]), writes=["hst"])
            S.op("dve", lambda e: e.tensor_tensor(out=wk1.rearrange("p (h c) -> p h c", c=128), in0=wk1.rearrange("p (h c) -> p h c", c=128),
                                                  in1=hst[:, :, 2:3].broadcast_to([128, 8, 128]), op=ALU.mult), reads=["hst"], writes=K_W1)
            S.op("pool", lambda e: e.tensor_tensor(out=ycat[:, 1024:2048], in0=wk1, in1=GTm[:, 1024:2048], op=ALU.mult), reads=K_W1 + K_GT, writes=K_YC)
            for h in range(4):
                bk = 6 + h % 2
                pP = _bank(ps, bk)
                S.op("pe", lambda e, pP=pP, h=h, msl=msl: e.matmul(pP[:, 0:257], QMt[:, h, msl], Cst[:, h, 0:257], start=True, stop=True),
                     reads=K_QM + ["Cst"], writes=[("ps", bk)])
                S.op("dve", lambda e, pP=pP, h=h, m=m: e.scalar_tensor_tensor(out=wkm[:, 0:257], in0=pP[:, 0:257], scalar=EMt[:, m, h:h + 1], in1=OMm[:, h * MW:h * MW + 257],
                                                                             op0=ALU.mult, op1=ALU.add), reads=[("ps", bk), "EMt"] + K_OM, writes=K_WM)
                ms = mst[:, h, :]
                S.op("dve", lambda e, ms=ms: e.tensor_scalar(out=ms[:, 0:1], in0=wkm[:, 256:257], scalar1=-1.0, scalar2=None, op0=ALU.mult), reads=K_WM, writes=[("mst", h)])
                S.op("dve", lambda e, ms=ms: e.tensor_tensor(out=ms[:, 0:1], in0=ms[:, 0:1], in1=wkm[:, 256:257], op=ALU.max), reads=K_WM, writes=[("mst", h)])
                S.op("dve", lambda e, ms=ms: e.tensor_scalar(out=ms[:, 0:1], in0=ms[:, 0:1], scalar1=1.0, scalar2=None, op0=ALU.max), writes=[("mst", h)])
                S.op("dve", lambda e, ms=ms: e.reciprocal(out=ms[:, 1:2], in_=ms[:, 0:1]), writes=[("mst", h)])
                S.op("act", lambda e, ms=ms: e.activation(out=wkh, in_=wkm[:, 0:256], func=AF.Copy, scale=ms[:, 1:2]), reads=K_WM + [("mst", h)], writes=K_WH)
                S.op("dve", lambda e, ms=ms: e.bn_stats(out=ms[:, 2:8], in_=wkh), reads=K_WH, writes=[("mst", h)])
                S.op("dve", lambda e, ms=ms: e.bn_aggr(out=ms[:, 8:10], in_=ms[:, 2:8]), writes=[("mst", h)])
                S.op("act", lambda e, ms=ms: e.activation(out=ms[:, 10:11], in_=ms[:, 9:10], func=AF.Sqrt, bias=g["epsb"][:, 0:1]), reads=["epsb"], writes=[("mst", h)])
                S.op("dve", lambda e, ms=ms: e.reciprocal(out=ms[:, 11:12], in_=ms[:, 10:11]), writes=[("mst", h)])
                S.op("dve", lambda e, ms=ms: e.scalar_tensor_tensor(out=ms[:, 12:13], in0=ms[:, 8:9], scalar=-1.0, in1=ms[:, 11:12], op0=ALU.mult, op1=ALU.mult), writes=[("mst", h)])
                S.op("act", lambda e, ms=ms: e.activation(out=wkh, in_=wkh, func=AF.Identity, scale=ms[:, 11:12], bias=ms[:, 12:13]), reads=[("mst", h)], writes=K_WH)
                S.op("dve", lambda e, h=h: e.tensor_tensor(out=ycat[:, h * 256:(h + 1) * 256], in0=wkh, in1=GTm[:, h * 256:(h + 1) * 256], op=ALU.mult),
                     reads=K_WH + K_GT, writes=K_YC)
            for kc in range(KC):
                t2 = kc % 2
                pt = _bank(ps, 4 + t2).bitcast(BF16)[:, 0:128]
                S.op("pe", lambda e, pt=pt, kc=kc: e.transpose(pt, ycat[:, kc * 128:(kc + 1) * 128], identb[:]), reads=K_YC + ["identb"], writes=[("ps", 4 + t2)])
                S.op("act", lambda e, pt=pt, kc=kc, msl=msl: e.activation(out=aT[:, kc, msl], in_=pt, func=AF.Copy, scale=vf[:, V_NRMW + kc:V_NRMW + kc + 1]),
                     reads=[("ps", 4 + t2), "vf"], writes=[("aT", kc)])
        S.op("sp", lambda e, tok0=tok0, xsrc=xsrc: e.dma_start(out=X[:], in_=xsrc[tok0:tok0 + TT, :].rearrange("(m p) d -> p m d", p=128)),
             reads=["XN"], writes=["X"], dma=True)
        emit_resid_prep(S, T, 0, 1)
        for n in range(4):
            wt, wkey = ws.get(g["wout"][n], WSLOT)
            wv3 = wt.rearrange("p (k c) -> p k c", k=KC)
            for m in range(4):
                pm = _bank(ps, m)
                for kc in range(KC):
                    S.op("pe", lambda e, pm=pm, wv3=wv3, m=m, kc=kc: e.matmul(pm, aT[:, kc, m * 128:(m + 1) * 128], wv3[:, kc, :], start=(kc == 0), stop=(kc == KC - 1)),
                         reads=[wkey, ("aT", kc)], writes=[("ps", m)])
            for m in range(4):
                emit_epilogue_block(S, T, m, n, 2)
        emit_ln_inplace(S, T)
        for i in range(3):
            dma_in(bc[:, i, :], g["ROWS"][3 + i:4 + i, :].partition_broadcast(128), [("bc", i)], reads=["ROWS"])
        emit_transpose_affine(S, T, "X", V_GU2, V_BU2)
        emit_ffn2(S, ws, T, g["wgu"], g["wd"])
        if not last:
            S.op("sp", lambda e, tok0=tok0: e.dma_start(out=g["XN"][tok0:tok0 + TT, :].rearrange("(m p) d -> p m d", p=128), in_=X[:]),
                 reads=["X"], writes=["XN"], dma=True)
            if t == NT - 1:
                S.op("sp", lambda e: e.dma_start(out=g["HALO_SRC"][0:24, :].rearrange("(r c) w -> r (c w)", r=3), in_=X[125:128, 3, :]), reads=["X"], writes=["HALO_SRC"], dma=True)
                S.op("pool", lambda e: e.collective_compute("AllGather", ALU.bypass, replica_groups=[[0, 1, 2, 3], [4, 5, 6, 7]],
                                                            ins=[g["HALO_SRC"].opt()], outs=[g["HALO_DST"].opt()]),
                     reads=["HALO_SRC"], writes=["HALO_DST"], dma="cc")
        else:
            for i in range(2):
                dma_in(bc[:, i, :], g["fin"][i:i + 1, :].partition_broadcast(128), [("bc", i)])
            emit_resid_prep(S, T, 0, 1)
            S.op("sp", lambda e, tok0=tok0: e.dma_start(out=g["OUT"][tok0:tok0 + TT, :].rearrange("(m p) d -> p m d", p=128), in_=X[:]),
                 reads=["X"], writes=[("OUT", t)], dma=True)


def emit_ffn2(S, ws, T, wgu, wd):
    aT, hT, ps, sg = T["aT"], T["hT"], T["ps"], T["sg"]
    for j2 in range(FC // 2):
        wt, wkey = ws.get(wgu[j2], 2 * 2 * KC * 128)
        wv = wt.rearrange("p (b g k c) -> p b g k c", b=2, g=2, k=KC)
        for b in range(2):
            j = j2 * 2 + b
            bg = 4 + (j % 2) * 2
            bu = bg + 1
            pg, pu = _bank(ps, bg), _bank(ps, bu)
            for gi, pp, bk in ((0, pg, bg), (1, pu, bu)):
                for kc in range(KC):
                    S.op("pe", lambda e, pp=pp, wv=wv, b=b, gi=gi, kc=kc: e.matmul(
                        pp, wv[:, b, gi, kc, :], aT[:, kc, :], start=(kc == 0), stop=(kc == KC - 1)),
                        reads=[wkey, ("aT", kc)], writes=[("ps", bk)])
            sgt = sg[:, (j % 2) * 512:(j % 2 + 1) * 512]
            S.op("act", lambda e, sgt=sgt, pg=pg: e.activation(out=sgt, in_=pg, func=AF.Silu),
                 reads=[("ps", bg)], writes=[("sg", j % 2)])
            S.op("dve", lambda e, sgt=sgt, pu=pu, j=j: e.tensor_tensor(out=hT[:, j, :], in0=pu, in1=sgt, op=ALU.mult),
                 reads=[("ps", bu), ("sg", j % 2)], writes=[("hT", j)])
    emit_resid_prep(S, T, 0, 1)
    for n in range(4):
        for jg in range(4):
            wt, wkey = ws.get(wd[n, jg], 11 * 512)
            wv = wt.rearrange("p (j c) -> p j c", j=11)
            for m in range(4):
                pm = _bank(ps, m)
                for jj in range(11):
                    j = jg * 11 + jj
                    S.op("pe", lambda e, pm=pm, wv=wv, jj=jj, j=j, m=m: e.matmul(
                        pm, hT[:, j, m * 128:(m + 1) * 128], wv[:, jj, :], start=(j == 0), stop=(j == FC - 1)),
                        reads=[wkey, ("hT", j)], writes=[("ps", m)])
        for m in range(4):
            emit_epilogue_block(S, T, m, n, 2)
    emit_ln_inplace(S, T)


def _fm_block(W, col, width=128):
    return W[:, col:col + width].reshape(KC, 128, width).transpose(1, 0, 2)


def _fm16(v):
    return np.ascontiguousarray(v.reshape(-1, 128).T)


def _layer_arrays(inp, l, depth):
    W = inp["w_in"][l]
    fm_tiles = []
    fm_tiles.append(np.stack([_fm_block(W, h * 128) for h in range(4)], axis=1))
    fm_tiles.append(np.stack([_fm_block(W, 512 + h * 128) for h in range(4)], axis=1))
    for hp in range(4):
        blks = []
        for hh in range(2):
            h = hp * 2 + hh
            blks += [_fm_block(W, 3080 + h * 128), _fm_block(W, 4104 + h * 128)]
        fm_tiles.append(np.stack(blks, axis=1))
    w1fm = np.ascontiguousarray(np.stack(fm_tiles)).reshape(6, 128, WSLOT)
    tm_cols = [1024, 1536, 5128, 5640, 2048, 2560, 6152, 6664]
    w1tm = np.ascontiguousarray(np.stack([_fm_block(W, c, 512) for c in tm_cols])).reshape(8, 128, WSLOT)
    w1ifg = np.ascontiguousarray(_fm_block(W, 3072, 8)).reshape(128, KC * 8)
    Wm = inp["w_mod"][l]
    wmod = np.ascontiguousarray(np.stack([np.stack([_fm_block(Wm, t * 512 + b * 128) for b in range(4)], axis=1)
                                          for t in range(24)])).reshape(24, 128, WSLOT)
    Wo = inp["w_out"][l]
    wout = np.ascontiguousarray(np.stack([_fm_block(Wo, n * 512, 512) for n in range(4)])).reshape(4, 128, WSLOT)
    g4 = inp["w_gate"][l].reshape(KC, 128, FC // 2, 2, 128)
    u4 = inp["w_up"][l].reshape(KC, 128, FC // 2, 2, 128)
    gu = np.stack([g4, u4], axis=0)
    wgu = np.ascontiguousarray(gu.transpose(3, 2, 4, 0, 1, 5)).reshape(FC // 2, 128, WSLOT)
    wd = np.ascontiguousarray(inp["w_down"][l].reshape(4, 11, 128, 4, 512).transpose(3, 0, 2, 1, 4)).reshape(4, 4, 128, 11 * 512)
    vecs = np.zeros((128, V_IN), np.float32)
    vecs[:, V_BMOD:V_BMOD + 96] = _fm16(inp["b_mod"][l])
    if l == 0:
        vecs[:, V_LNPG:V_LNPG + 16] = np.ones((128, 16), np.float32)
    else:
        vecs[:, V_LNPG:V_LNPG + 16] = _fm16(inp["ln2_g"][l - 1])
        vecs[:, V_LNPB:V_LNPB + 16] = _fm16(inp["ln2_b"][l - 1])
    vecs[:, V_LN1G:V_LN1G + 16] = _fm16(inp["ln1_g"][l])
    vecs[:, V_LN1B:V_LN1B + 16] = _fm16(inp["ln1_b"][l])
    vecs[:, V_CONVW:V_CONVW + 32] = inp["conv_w"][l].reshape(4, 8, 128).transpose(2, 1, 0).reshape(128, 32)
    vecs[:, V_CONVB:V_CONVB + 8] = _fm16(inp["conv_b"][l])
    vecs[:, V_LBL:V_LBL + 8 * depth] = inp["lb_logits"].reshape(depth, 8, 128).transpose(2, 1, 0).reshape(128, 8 * depth)
    nrmw = _fm16(np.concatenate([inp["mlstm_norm_w"][l], inp["hgrn_norm_w"][l]]))
    vecs[:, V_NRMW:V_NRMW + 16] = nrmw
    grow = np.zeros((1, 8 + DEPTH), np.float32)
    grow[0, 0:4] = inp["b_igate"][l]
    grow[0, 4:8] = inp["b_fgate"][l]
    for i in range(1, l + 1):
        grow[0, 8 + i] = np.float32(1.0)
    return dict(w1fm=w1fm, w1tm=w1tm, w1ifg=w1ifg, wmod=wmod, wout=wout, wgu=wgu, wd=wd, vecs=vecs, grow=grow, nrmw=nrmw)


def build_fused(TOK, depth):
    NT = TOK // TT
    nc = bass.Bass("TRN2", target_bir_lowering=False)

    def din(name, shape, dt=F32):
        return nc.dram_tensor(name, shape, dt, kind="ExternalInput").ap()

    def dint(name, shape, dt=F32):
        return nc.dram_tensor(name, shape, dt, kind="Internal").ap()
    g = {}
    x_in = din("x_in", [TOK, D])
    g["halo0"] = din("halo0", [3, D])
    g["hflag"] = din("hflag", [128, 1])
    g["c_fm"] = din("c_fm", [128, KC])
    g["sel"] = din("sel", [128, 4])
    g["selprev"] = din("selprev", [128, 4])
    g["fin"] = din("fin", [2, D])
    g["identd"] = din("identd", [128, 128])
    g["trid"] = din("trid", [128, 128], U8)
    wmod = din("wmod", [depth * 24, 128, WSLOT])
    w1fm = din("w1fm", [depth * 6, 128, WSLOT])
    w1tm = din("w1tm", [depth * 8, 128, WSLOT])
    w1ifg = din("w1ifg", [depth * 128, KC * 8])
    wout = din("wout", [depth * 4, 128, WSLOT])
    wgu = din("wgu", [depth * (FC // 2), 128, WSLOT])
    wd = din("wd", [depth * 4, 4, 128, 11 * 512])
    vecs = din("vecs", [depth * 128, V_IN])
    grow = din("grow", [depth, 8 + DEPTH])
    g["OUT"] = nc.dram_tensor("OUT", [TOK, D], F32, kind="ExternalOutput").ap()
    g["XN"] = dint("XN", [TOK, D])
    g["OH"] = dint("OH", [TOK, 1024])
    g["OM"] = dint("OM", [TOK, 4 * MW])
    g["QH"] = dint("QH", [1024, TOK], BF16)
    g["QM"] = dint("QM", [512, TOK], BF16)
    g["EM"] = dint("EM", [TOK, 4])
    g["GT"] = dint("GT", [TOK, 2048], BF16)
    g["ROWS"] = dint("ROWS", [6, D])
    g["STATE_SRC"] = dint("STATE_SRC", [NPC, 128, 256])
    g["STATES"] = dint("STATES", [NPC, 4 * 128, 256])
    g["HALO_SRC"] = dint("HALO_SRC", [128, 256])
    g["HALO_DST"] = dint("HALO_DST", [4 * 128, 256])

    with contextlib.ExitStack() as es:
        def sb(name, shape, dt=F32):
            return es.enter_context(nc.sbuf_tensor(name, shape, dt))
        T = {}
        X = T["X"] = sb("X", [128, 4, D])
        T["aT"] = sb("aT", [128, KC, TT], BF16)
        wbuf = sb("wbuf", [128, NWS * WSLOT], BF16)
        T["ident"] = g["ident"] = sb("ident", [128, 128])
        g["identb"] = sb("identb", [128, 128], BF16)
        g["tri"] = sb("tri", [128, 128], U8)
        g["utri"] = sb("utri", [128, 128])
        g["ones"] = sb("ones", [128, 128])
        g["m0"] = sb("m0", [128, TT])
        T["vf"] = sb("vf", [128, V_TOT])
        g["cact"] = sb("cact", [128, KC])
        g["cactb"] = sb("cactb", [128, KC], BF16)
        g["hfl"] = sb("hfl", [128, 1])
        g["selt"] = sb("selt", [128, 4])
        g["selp"] = sb("selp", [128, 4])
        g["lnsc"] = sb("lnsc", [128, 1])
        T["epsb"] = g["epsb"] = sb("epsb", [128, 1])
        g["A_sb4"] = sb("A_sb4", [128, 2, 4, 128], BF16)
        g["A_h8"] = sb("A_h8", [128, 2, 4, 64], BF16)
        T["st"] = sb("st", [128, 4, 4, 6])
        T["mv"] = sb("mv", [128, 4, 8])
        fsc = sb("fsc", [128, 1])
        PAR = 41216
        par = sb("par", [128, PAR], BF16)
        T["ps"] = es.enter_context(nc.psum_tensor("ps", [128, 8 * 512], F32))
        g["hx"] = X[0:3, 0, :]
        g["hx2"] = X[0:3, 1, :]
        off = [0]

        def cv(n_bf16, dt, shape_str=None, **kw):
            a = par[:, off[0]:off[0] + n_bf16]
            off[0] += (n_bf16 + 15) // 16 * 16
            if dt == F32:
                a = a.bitcast(F32)
            if shape_str:
                a = a.rearrange(shape_str, **kw)
            return a
        g["aTh"] = cv(48, BF16, "p (k t) -> p k t", t=3)
        g["hist"] = cv(48, F32, "p (k t) -> p k t", t=3)
        g["ext"] = cv(2 * (TT + 4), F32)
        g["t1"] = cv(2 * TT, F32)
        g["qkT"] = cv(8 * TT, BF16, "p (k t) -> p k t", t=TT)
        g["wifg"] = cv(KC * 8, BF16)
        g["gsb"] = cv(64, F32, "p (m c) -> p m c", c=8)
        for nm in ("l1", "nb", "nbL", "wv_", "eo", "ec", "eL", "ctmp"):
            g[nm] = cv(32, F32, "p (m c) -> p m c", c=4)
        g["carry"] = cv(8, F32)
        g["gbias"] = cv(2 * (8 + DEPTH), F32)
        g["vaug"] = cv(16 * MW, BF16, "p (m h c) -> p m h c", m=4, h=4)
        g["ktok4"] = cv(1024, BF16, "p (a m c) -> p a m c", a=2, m=4)
        g["C32"] = cv(8 * MW, F32, "p (h c) -> p h c", h=4)
        g["Cbf4"] = cv(8 * MW, BF16, "p (a m c) -> p a m c", a=2, m=4)
        g["dCs"] = cv(8 * MW, F32, "p (m c) -> p m c", m=4)
        g["vH"] = cv(4096, BF16, "p (m c) -> p m c", m=4)
        for nm in ("sig", "kk", "bcs", "qs", "ebuf"):
            g[nm] = cv(2 * TT, F32)
        g["qdec"] = cv(TT, BF16)
        g["q_in2"] = cv(2 * TT, BF16, "p (a t) -> p a t", a=2)
        g["k_in2"] = cv(2 * TT, BF16, "p (a t) -> p a t", a=2)
        g["hsc2"] = cv(256, F32, "p (a c s) -> p a c s", a=2, s=8)
        g["hcarry"] = cv(16, F32)
        g["S32"] = cv(2048, F32, "p (h c) -> p h c", h=8)
        g["Sp8"] = cv(2048, BF16, "p (a c d) -> p a c d", a=2, c=8)
        g["gts"] = cv(2 * TT, BF16, "p (a c) -> p a c", a=2)
        g["rowsb"] = cv(2 * 768, F32)
        g["OMs"] = cv(2 * 2 * MW, F32, "p (m c) -> p m c", m=2)
        print("P1 arena", off[0])
        assert off[0] <= PAR, off[0]
        off[0] = 0
        g["arena"] = cv(FC * TT, BF16)
        T["hT"] = g["arena"].rearrange("p (j t) -> p j t", t=TT)
        T["bc"] = cv(2 * 3 * D, F32, "p (i d) -> p i d", i=3)
        T["sg"] = cv(2 * 1024, F32)
        T["tmp"] = cv(2 * 1024, F32)
        g["Sst"] = cv(1024, BF16, "p (h c) -> p h c", h=8)
        g["Cst"] = cv(4 * MW, BF16, "p (h c) -> p h c", h=4)
        g["EMt"] = cv(32, F32, "p (m c) -> p m c", c=4)
        g["hst"] = cv(64, F32, "p (h c) -> p h c", c=4)
        g["mst"] = cv(128, F32, "p (h c) -> p h c", c=16)
        print("P2 arena", off[0])
        assert off[0] <= PAR, off[0]
        rec = None
        for pas in range(2):
            S = Sched(nc)
            ws = WStream(S, wbuf, record=rec)
            _emit_prologue(S, T, g)
            for l in range(depth):
                g["xsrc"] = x_in if l == 0 else g["XN"]
                g["wmod"] = wmod[l * 24:(l + 1) * 24]
                g["w1fm"] = w1fm[l * 6:(l + 1) * 6]
                g["w1tm"] = w1tm[l * 8:(l + 1) * 8]
                g["w1ifg"] = w1ifg[l * 128:(l + 1) * 128, :]
                g["wout"] = wout[l * 4:(l + 1) * 4]
                g["wgu"] = wgu[l * (FC // 2):(l + 1) * (FC // 2)]
                g["wd"] = wd[l * 4:(l + 1) * 4]
                g["vecs"] = vecs[l * 128:(l + 1) * 128, :]
                g["grow"] = grow[l:l + 1, :]
                _emit_p1(S, ws, nc, T, g, NT, l)
                S.fence(fsc[:])
                _emit_p2(S, ws, nc, T, g, NT, l, l == depth - 1)
                S.fence(fsc[:])
            S.barrier_all("sp")
            rec = ws.seq
        S.run()
    return nc


def _fm_block(W, col, width=128):
    return W[:, col:col + width].reshape(KC, 128, width).transpose(1, 0, 2)


def _fm16(v):
    return np.ascontiguousarray(v.reshape(-1, 128).T)


def _layer_arrays(inp, l, depth):
    W = inp["w_in"][l]
    fm_tiles = []
    fm_tiles.append(np.stack([_fm_block(W, h * 128) for h in range(4)], axis=1))
    fm_tiles.append(np.stack([_fm_block(W, 512 + h * 128) for h in range(4)], axis=1))
    for hp in range(4):
        blks = []
        for hh in range(2):
            h = hp * 2 + hh
            blks += [_fm_block(W, 3080 + h * 128), _fm_block(W, 4104 + h * 128)]
        fm_tiles.append(np.stack(blks, axis=1))
    w1fm = np.ascontiguousarray(np.stack(fm_tiles)).reshape(6, 128, WSLOT)
    tm_cols = [1024, 1536, 5128, 5640, 2048, 2560, 6152, 6664]
    w1tm = np.ascontiguousarray(np.stack([_fm_block(W, c, 512) for c in tm_cols])).reshape(8, 128, WSLOT)
    w1ifg = np.ascontiguousarray(_fm_block(W, 3072, 8)).reshape(128, KC * 8)
    Wm = inp["w_mod"][l]
    wmod = np.ascontiguousarray(np.stack([np.stack([_fm_block(Wm, t * 512 + b * 128) for b in range(4)], axis=1)
                                          for t in range(24)])).reshape(24, 128, WSLOT)
    Wo = inp["w_out"][l]
    wout = np.ascontiguousarray(np.stack([_fm_block(Wo, n * 512, 512) for n in range(4)])).reshape(4, 128, WSLOT)
    g4 = inp["w_gate"][l].reshape(KC, 128, FC // 2, 2, 128)
    u4 = inp["w_up"][l].reshape(KC, 128, FC // 2, 2, 128)
    gu = np.stack([g4, u4], axis=0)
    wgu = np.ascontiguousarray(gu.transpose(3, 2, 4, 0, 1, 5)).reshape(FC // 2, 128, WSLOT)
    wd = np.ascontiguousarray(inp["w_down"][l].reshape(4, 11, 128, 4, 512).transpose(3, 0, 2, 1, 4)).reshape(4, 4, 128, 11 * 512)
    vecs = np.zeros((128, V_IN), np.float32)
    vecs[:, V_BMOD:V_BMOD + 96] = _fm16(inp["b_mod"][l])
    if l == 0:
        vecs[:, V_LNPG:V_LNPG + 16] = np.ones((128, 16), np.float32)
    else:
        vecs[:, V_LNPG:V_LNPG + 16] = _fm16(inp["ln2_g"][l - 1])
        vecs[:, V_LNPB:V_LNPB + 16] = _fm16(inp["ln2_b"][l - 1])
    vecs[:, V_LN1G:V_LN1G + 16] = _fm16(inp["ln1_g"][l])
    vecs[:, V_LN1B:V_LN1B + 16] = _fm16(inp["ln1_b"][l])
    vecs[:, V_CONVW:V_CONVW + 32] = inp["conv_w"][l].reshape(4, 8, 128).transpose(2, 1, 0).reshape(128, 32)
    vecs[:, V_CONVB:V_CONVB + 8] = _fm16(inp["conv_b"][l])
    vecs[:, V_LBL:V_LBL + 8 * depth] = inp["lb_logits"].reshape(depth, 8, 128).transpose(2, 1, 0).reshape(128, 8 * depth)
    vecs[:, V_NRMW:V_NRMW + 16] = _fm16(np.concatenate([inp["mlstm_norm_w"][l], inp["hgrn_norm_w"][l]]))
    grow = np.zeros((1, 8 + DEPTH), np.float32)
    grow[0, 0:4] = inp["b_igate"][l]
    grow[0, 4:8] = inp["b_fgate"][l]
    for i in range(1, l + 1):
        grow[0, 8 + i] = np.float32(1.0)
    return dict(w1fm=w1fm, w1tm=w1tm, w1ifg=w1ifg, wmod=wmod, wout=wout, wgu=wgu, wd=wd, vecs=vecs, grow=grow)


_PROGS = {}


def kernel(**inputs):
    inp = {k: np.asarray(v) for k, v in inputs.items()}
    x = inp["x"]
    B, T_, _ = x.shape
    depth = inp["w_in"].shape[0]
    SEG = NCORE // B
    TOK = T_ // SEG
    key = (TOK, depth, DEPTH)
    if key not in _PROGS:
        _PROGS[key] = build_fused(TOK, depth)
    nc = _PROGS[key]
    LA = [_layer_arrays(inp, l, depth) for l in range(depth)]
    shared = {k: np.ascontiguousarray(np.concatenate([LA[l][k] for l in range(depth)], axis=0))
              for k in ("wmod", "w1fm", "w1tm", "w1ifg", "wout", "wgu", "wd", "vecs", "grow")}
    del LA
    shared["identd"] = np.eye(128, dtype=np.float32)
    shared["trid"] = np.triu(np.ones((128, 128), np.uint8))
    shared["fin"] = np.ascontiguousarray(np.stack([inp["ln2_g"][depth - 1], inp["ln2_b"][depth - 1]]).astype(np.float32))
    in_maps = []
    for i in range(NCORE):
        b, sgi = i // SEG, i % SEG
        sel = np.zeros((128, 4), np.float32)
        sel[:, sgi] = 1.0
        selp = np.zeros((128, 4), np.float32)
        if sgi > 0:
            selp[:, sgi - 1] = 1.0
        halo = np.zeros((3, D), np.float32) if sgi == 0 else np.ascontiguousarray(x[b, sgi * TOK - 3:sgi * TOK, :])
        m = dict(shared)
        m.update(x_in=np.ascontiguousarray(x[b, sgi * TOK:(sgi + 1) * TOK, :]), halo0=halo,
                 hflag=np.full((128, 1), 0.0 if sgi == 0 else 1.0, np.float32), c_fm=_fm16(inp["c"][b]), sel=sel, selprev=selp)
        in_maps.append(m)
    res = run_bass_kernel_spmd(nc, in_maps, core_ids=list(range(NCORE))).results
    y = np.empty((B, T_, D), np.float32)
    for i in range(NCORE):
        y[i // SEG, (i % SEG) * TOK:(i % SEG + 1) * TOK, :] = res[i]["OUT"]
    return y


_DEBUG = None
```

```python
import contextlib
import math
import numpy as np
import ml_dtypes
import concourse.bass as bass
import concourse.mybir as mybir
from concourse.bass_utils import run_bass_kernel_spmd

F32 = mybir.dt.float32
BF16 = mybir.dt.bfloat16
U8 = mybir.dt.uint8
AF = mybir.ActivationFunctionType
ALU = mybir.AluOpType
AX = mybir.AxisListType

D = 2048
DFF = 5632
KC = D // 128
FC = DFF // 128
TT = 512
EPS = 1e-5
DEPTH = 4
NCORE = 8
ALPHA = (2 * DEPTH) ** 0.25
WSLOT = 8192
NWS = 3
MW = 260
NST = 2304
NPC = NST // 256
LNSC = math.log(128.0 ** -0.5)

NO_CC = False
PIPE_ML = True
SKIP_ML = False
SKIP_HG = False
PIPE_HG = True
ENGS = ("pe", "act", "dve", "pool", "sp")
SEM_EPOCH = 30000
NDS = 20


class _Op:
    __slots__ = ("eng", "fn", "waits", "is_dma", "needed", "sem", "val")

    def __init__(self, eng, fn, is_dma):
        self.eng = eng
        self.fn = fn
        self.is_dma = is_dma
        self.waits = []
        self.needed = False
        self.sem = None
        self.val = None


class Sched:
    def __init__(self, nc):
        self.nc = nc
        self.q = {e: [] for e in ENGS}
        self.last_w = {}
        self.readers = {}
        self.all_ops = []

    def op(self, eng, fn, reads=(), writes=(), dma=False, nofence=False):
        o = _Op(eng, fn, dma)
        deps = []
        seen = set()
        if not nofence:
            reads = list(reads) + ["__fence__"]

        def add(d):
            if d is None or id(d) in seen:
                return
            seen.add(id(d))
            deps.append(d)
        for k in reads:
            add(self.last_w.get(k))
        for k in writes:
            add(self.last_w.get(k))
            for r in self.readers.get(k, ()):
                add(r)
        for d in deps:
            if d.eng == "pe" and eng == "pe" and not d.is_dma and not dma:
                continue
            d.needed = True
            o.waits.append(d)
        for k in reads:
            self.readers.setdefault(k, []).append(o)
        for k in writes:
            self.last_w[k] = o
            self.readers[k] = []
        self.q[eng].append(o)
        self.all_ops.append(o)
        return o

    def fence(self, scratch):
        self.op("dve", lambda e: e.memset(scratch, 0.0), writes=["__fence__"], nofence=True)

    def barrier_all(self, eng="sp"):
        o = _Op(eng, None, False)
        for d in self.all_ops:
            if d.is_dma:
                d.needed = True
                o.waits.append(d)
        self.q[eng].append(o)
        self.all_ops.append(o)

    def run(self):
        nc = self.nc
        cnt = {e: 0 for e in ENGS}
        di = {e: 0 for e in ENGS}
        dlast = {}
        dcount = {}
        for o in self.all_ops:
            if not o.needed:
                continue
            e = o.eng
            if o.is_dma == "cc":
                o.sem, o.val = f"cc_{di[e]}", 1
                di[e] += 1
            elif o.is_dma:
                name = f"d_{e}_{di[e] % NDS}"
                di[e] += 1
                prev = dlast.get(name)
                if prev is not None and prev not in o.waits:
                    o.waits.append(prev)
                dcount[name] = dcount.get(name, 0) + 16
                o.sem, o.val = name, dcount[name]
                dlast[name] = o
            else:
                n = cnt[e]
                o.sem = f"c_{e}_{n // SEM_EPOCH}"
                o.val = n % SEM_EPOCH + 1
                cnt[e] = n + 1
        names = sorted({o.sem for o in self.all_ops if o.sem is not None})
        for o in self.all_ops:
            best = {}
            for d in o.waits:
                if d.sem not in best or d.val > best[d.sem].val:
                    best[d.sem] = d
            o.waits = list(best.values())
        with contextlib.ExitStack() as es:
            S = {n: es.enter_context(nc.semaphore(n)) for n in names}
            block = es.enter_context(nc.Block())

            def body(eng_name):
                def _f(eng):
                    known = {}
                    for o in self.q[eng_name]:
                        for d in o.waits:
                            if known.get(d.sem, 0) >= d.val:
                                continue
                            eng.wait_ge(S[d.sem], d.val)
                            known[d.sem] = d.val
                        if o.fn is None:
                            continue
                        ins = o.fn(eng)
                        if o.needed:
                            ins.then_inc(S[o.sem], 16 if (o.is_dma and o.is_dma != "cc") else 1)
                return _f
            block.tensor(body("pe"))
            block.scalar(body("act"))
            block.vector(body("dve"))
            block.gpsimd(body("pool"))
            block.sync(body("sp"))
        return len(names)


class WStream:
    def __init__(self, S, buf, record=None):
        self.S = S
        self.buf = buf
        self.seq = [] if record is None else record
        self.recording = record is None
        self.k = 0
        self.issued = 0
        self.ahead = NWS

    def _issue(self, j):
        ap, n = self.seq[j]
        slot = j % NWS
        dst = self.buf[:, slot * WSLOT: slot * WSLOT + n]
        self.S.op("pool", lambda e, dst=dst, ap=ap: e.dma_start(out=dst, in_=ap),
                  writes=[("w", slot)], dma=True, nofence=True)

    def get(self, ap, n):
        if self.recording:
            self.seq.append((ap, n))
            return self.buf[:, 0:n], ("w", 0)
        k = self.k
        while self.issued < min(len(self.seq), k + self.ahead):
            self._issue(self.issued)
            self.issued += 1
        self.k += 1
        slot = k % NWS
        return self.buf[:, slot * WSLOT: slot * WSLOT + n], ("w", slot)


def _bank(ps, b):
    return ps[:, b * 512:(b + 1) * 512]


def emit_transpose_affine(S, T, src_key, gcol, bcol, extra_reads=()):
    X, aT, ps, ident, vf = T["X"], T["aT"], T["ps"], T["ident"], T["vf"]
    for kc in range(KC):
        bank = 4 + (kc % 2)
        pst = _bank(ps, bank)
        for m in range(4):
            S.op("pe", lambda e, pst=pst, m=m, kc=kc: e.transpose(
                pst[:, m * 128:(m + 1) * 128], X[:, m, kc * 128:(kc + 1) * 128], ident[:]),
                reads=[src_key, "ident"], writes=[("ps", bank)])
        S.op("act", lambda e, pst=pst, kc=kc: e.activation(
            out=aT[:, kc, :], in_=pst, func=AF.Identity,
            scale=vf[:, gcol + kc:gcol + kc + 1], bias=vf[:, bcol + kc:bcol + kc + 1]),
            reads=[("ps", bank), "vf", *extra_reads], writes=[("aT", kc)])


def emit_resid_prep(S, T, gi, bi):
    X, bc = T["X"], T["bc"]
    for m in range(4):
        S.op("pool", lambda e, m=m: e.tensor_tensor(out=X[:, m, :], in0=X[:, m, :], in1=bc[:, gi, :], op=ALU.mult),
             reads=[("bc", gi)], writes=["X"])
        S.op("pool", lambda e, m=m: e.tensor_tensor(out=X[:, m, :], in0=X[:, m, :], in1=bc[:, bi, :], op=ALU.add),
             reads=[("bc", bi)], writes=["X"])


def emit_epilogue_block(S, T, m, n, gy):
    X, bc, ps, tmp = T["X"], T["bc"], T["ps"], T["tmp"]
    pm = _bank(ps, m)
    tq = tmp[:, (m % 2) * 512:(m % 2 + 1) * 512]
    S.op("dve", lambda e: e.tensor_tensor(out=tq, in0=pm, in1=bc[:, gy, n * 512:(n + 1) * 512], op=ALU.mult),
         reads=[("ps", m), ("bc", gy)], writes=[("tmp", m % 2)])
    S.op("pool", lambda e: e.tensor_tensor(out=X[:, m, n * 512:(n + 1) * 512],
                                           in0=X[:, m, n * 512:(n + 1) * 512], in1=tq, op=ALU.add),
         reads=[("tmp", m % 2)], writes=["X"])


def emit_ln_inplace(S, T):
    X, st, mv, epsb = T["X"], T["st"], T["mv"], T["epsb"]
    for m in range(4):
        for i in range(4):
            S.op("dve", lambda e, m=m, i=i: e.bn_stats(out=st[:, m, i, :], in_=X[:, m, i * 512:(i + 1) * 512]),
                 reads=["X"], writes=[("st", m)])
        S.op("dve", lambda e, m=m: e.bn_aggr(out=mv[:, m, 0:2], in_=st[:, m, :, :].rearrange("p a b -> p (a b)")),
             reads=[("st", m)], writes=[("mv", m)])
        S.op("act", lambda e, m=m: e.activation(out=mv[:, m, 2:3], in_=mv[:, m, 1:2], func=AF.Sqrt, bias=epsb[:, 0:1]),
             reads=[("mv", m), "epsb"], writes=[("mv", m)])
        S.op("dve", lambda e, m=m: e.reciprocal(out=mv[:, m, 3:4], in_=mv[:, m, 2:3]),
             reads=[("mv", m)], writes=[("mv", m)])
        S.op("dve", lambda e, m=m: e.scalar_tensor_tensor(
            out=mv[:, m, 4:5], in0=mv[:, m, 0:1], scalar=-1.0, in1=mv[:, m, 3:4], op0=ALU.mult, op1=ALU.mult),
            reads=[("mv", m)], writes=[("mv", m)])
        S.op("act", lambda e, m=m: e.activation(out=X[:, m, :], in_=X[:, m, :], func=AF.Identity,
                                                scale=mv[:, m, 3:4], bias=mv[:, m, 4:5]),
             reads=[("mv", m)], writes=["X"])


def emit_ffn(S, ws, T, wgu, wd, gy):
    aT, hT, ps, sg = T["aT"], T["hT"], T["ps"], T["sg"]
    for j2 in range(FC // 2):
        wt, wkey = ws.get(wgu[j2], 2 * 2 * KC * 128)
        wv = wt.rearrange("p (b g k c) -> p b g k c", b=2, g=2, k=KC)
        for b in range(2):
            j = j2 * 2 + b
            bg = 4 + (j % 2) * 2
            bu = bg + 1
            pg, pu = _bank(ps, bg), _bank(ps, bu)
            for g, pp, bk in ((0, pg, bg), (1, pu, bu)):
                for kc in range(KC):
                    S.op("pe", lambda e, pp=pp, wv=wv, b=b, g=g, kc=kc: e.matmul(
                        pp, wv[:, b, g, kc, :], aT[:, kc, :], start=(kc == 0), stop=(kc == KC - 1)),
                        reads=[wkey, ("aT", kc)], writes=[("ps", bk)])
            sgt = sg[:, (j % 2) * 512:(j % 2 + 1) * 512]
            S.op("act", lambda e, sgt=sgt, pg=pg: e.activation(out=sgt, in_=pg, func=AF.Silu),
                 reads=[("ps", bg)], writes=[("sg", j % 2)])
            S.op("dve", lambda e, sgt=sgt, pu=pu, j=j: e.tensor_tensor(out=hT[:, j, :], in0=pu, in1=sgt, op=ALU.mult),
                 reads=[("ps", bu), ("sg", j % 2)], writes=[("hT", j)])
    emit_resid_prep(S, T, 3, 4)
    for n in range(4):
        for jg in range(4):
            wt, wkey = ws.get(wd[n, jg], 11 * 512)
            wv = wt.rearrange("p (j c) -> p j c", j=11)
            for m in range(4):
                pm = _bank(ps, m)
                for jj in range(11):
                    j = jg * 11 + jj
                    S.op("pe", lambda e, pm=pm, wv=wv, jj=jj, j=j, m=m: e.matmul(
                        pm, hT[:, j, m * 128:(m + 1) * 128], wv[:, jj, :], start=(j == 0), stop=(j == FC - 1)),
                        reads=[wkey, ("hT", j)], writes=[("ps", m)])
        for m in range(4):
            emit_epilogue_block(S, T, m, n, gy)
    emit_ln_inplace(S, T)


V_BMOD = 0
V_LNPG = 96
V_LNPB = 112
V_LN1G = 128
V_LN1B = 144
V_CONVW = 160
V_CONVB = 192
V_LBL = 200
V_NRMW = 232
V_IN = 248
V_MOD = 256
V_GU1 = 352
V_BU1 = 368
V_GU2 = 384
V_BU2 = 400
V_ROWS = 416
V_LB = 512
V_OML = 520
V_TOT = 544


def _emit_prologue(S, T, g):
    ident, identb, tri, utri, ones, m0 = g["ident"], g["identb"], g["tri"], g["utri"], g["ones"], g["m0"]

    def dma_in(dst, src, key, eng="sp"):
        S.op(eng, lambda e: e.dma_start(out=dst, in_=src), writes=[key], dma=True)
    dma_in(ident[:], g["identd"], "ident")
    dma_in(identb[:], g["identd"], "identb", eng="pool")
    dma_in(tri[:], g["trid"], "tri")
    dma_in(g["cact"][:], g["c_fm"], "cact")
    dma_in(g["hfl"][:], g["hflag"], "hfl")
    dma_in(g["selt"][:], g["sel"], "selt")
    dma_in(g["selp"][:], g["selprev"], "selp")
    S.op("dve", lambda e: e.tensor_copy(out=utri[:], in_=tri[:]), reads=["tri"], writes=["utri"])
    S.op("dve", lambda e: e.memset(ones[:], 1.0), writes=["ones"])
    S.op("dve", lambda e: e.memset(g["lnsc"][:], LNSC), writes=["lnsc"])
    S.op("dve", lambda e: e.memset(g["epsb"][:], EPS), writes=["epsb"])
    S.op("dve", lambda e: e.memset(m0[:], 1.0), writes=["m0"])
    S.op("dve", lambda e: e.memset(m0[:].rearrange("p (c t) -> p c t", t=64)[:, :, 0:1], 0.0), writes=["m0"])
    S.op("dve", lambda e: e.memset(g["A_sb4"][:], 0.0), writes=[("A_sb4", p, m) for p in range(2) for m in range(4)])
    S.op("dve", lambda e: e.memset(g["A_h8"][:], 0.0), writes=[("A_h8", p, c) for p in range(2) for c in range(8)])
    S.op("act", lambda e: e.activation(out=g["cactb"][:], in_=g["cact"][:], func=AF.Silu), reads=["cact"], writes=["cactb"])


def _emit_p1(S, ws, nc, T, g, NT, l):
    X, aT, ps, vf = T["X"], T["aT"], T["ps"], T["vf"]
    ident, identb, tri, utri, ones, m0 = g["ident"], g["identb"], g["tri"], g["utri"], g["ones"], g["m0"]

    def dma_in(dst, src, key, eng="sp"):
        S.op(eng, lambda e: e.dma_start(out=dst, in_=src), writes=[key], dma=True)
    dma_in(vf[:, 0:V_IN], g["vecs"], "vf")
    dma_in(g["gbias"][:], g["grow"].partition_broadcast(128), "gbias")
    dma_in(g["wifg"][:], g["w1ifg"], "wifg", eng="pool")
    S.op("dve", lambda e: e.memset(g["C32"][:], 0.0), writes=[("C32", h) for h in range(4)])
    S.op("dve", lambda e: e.memset(g["S32"][:], 0.0), writes=[("S32", h) for h in range(8)])
    S.op("dve", lambda e: e.memset(g["carry"][:], 0.0), writes=["carry"])
    S.op("dve", lambda e: e.memset(g["hcarry"][:], 0.0), writes=["hcarry"])
    S.op("dve", lambda e: e.memset(g["vaug"][:], 0.0), writes=[("vaug", m) for m in range(4)])
    cact, cactb = g["cact"], g["cactb"]
    psm = _bank(ps, 7)
    for t in range(24):
        wt, wkey = ws.get(g["wmod"][t], WSLOT)
        wv = wt.rearrange("p (b k c) -> p b k c", b=4, k=KC)
        for b in range(4):
            j = t * 4 + b
            for kc in range(KC):
                S.op("pe", lambda e, wv=wv, b=b, kc=kc, j=j: e.matmul(
                    psm[:, j:j + 1], wv[:, b, kc, :], cactb[:, kc:kc + 1], start=(kc == 0), stop=(kc == KC - 1)),
                    reads=[wkey, "cactb"], writes=[("ps", 7)])
    S.op("dve", lambda e: e.tensor_tensor(out=vf[:, V_MOD:V_MOD + 96], in0=psm[:, 0:96], in1=vf[:, V_BMOD:V_BMOD + 96], op=ALU.add),
         reads=[("ps", 7), "vf"], writes=["vf"])

    def vop(fn):
        S.op("dve", fn, writes=["vf"])
    SH1, SC1, G1, SH2, SC2, G2 = (V_MOD + 16 * i for i in range(6))

    def c(off):
        return vf[:, off:off + 16]
    for (lg, lb_, sc, sh, gu, bu) in ((V_LNPG, V_LNPB, SC1, SH1, V_GU1, V_BU1), (V_LN1G, V_LN1B, SC2, SH2, V_GU2, V_BU2)):
        vop(lambda e, lg=lg, sc=sc, gu=gu: e.scalar_tensor_tensor(out=c(gu), in0=c(sc), scalar=1.0, in1=c(lg), op0=ALU.add, op1=ALU.mult))
        vop(lambda e, lb_=lb_, sc=sc, bu=bu: e.scalar_tensor_tensor(out=c(bu), in0=c(sc), scalar=1.0, in1=c(lb_), op0=ALU.add, op1=ALU.mult))
        vop(lambda e, sh=sh, bu=bu: e.tensor_tensor(out=c(bu), in0=c(bu), in1=c(sh), op=ALU.add))
    for i, (lg, lb_, gg) in enumerate(((V_LNPG, V_LNPB, G1), (V_LN1G, V_LN1B, G2))):
        r0 = V_ROWS + 48 * i
        vop(lambda e, lg=lg, r0=r0: e.tensor_scalar(out=c(r0), in0=c(lg), scalar1=ALPHA, scalar2=None, op0=ALU.mult))
        vop(lambda e, lb_=lb_, r0=r0: e.tensor_scalar(out=c(r0 + 16), in0=c(lb_), scalar1=ALPHA, scalar2=None, op0=ALU.mult))
        vop(lambda e, gg=gg, r0=r0: e.tensor_scalar(out=c(r0 + 32), in0=c(gg), scalar1=1.0, scalar2=None, op0=ALU.add))
    lbl = vf[:, V_LBL:V_LBL + 8 * DEPTH].rearrange("p (h l) -> p h l", l=DEPTH)
    sm = g["hsc2"][:, 0]
    vop(lambda e: e.tensor_reduce(out=sm[:, :, 0], in_=lbl, axis=AX.X, op=ALU.max))
    vop(lambda e: e.tensor_tensor(out=lbl, in0=lbl, in1=sm[:, :, 0:1].broadcast_to([128, 8, DEPTH]), op=ALU.subtract))
    S.op("act", lambda e: e.activation(out=lbl, in_=lbl, func=AF.Exp), reads=["vf"], writes=["vf"])
    vop(lambda e: e.tensor_reduce(out=sm[:, :, 0], in_=lbl, axis=AX.X, op=ALU.add))
    vop(lambda e: e.reciprocal(out=sm[:, :, 1], in_=sm[:, :, 0]))
    S.op("dve", lambda e: e.tensor_tensor(out=lbl, in0=lbl, in1=g["gbias"][:, 8:8 + DEPTH].unsqueeze(1).broadcast_to([128, 8, DEPTH]), op=ALU.mult),
         reads=["gbias"], writes=["vf"])
    vop(lambda e: e.tensor_reduce(out=sm[:, :, 0], in_=lbl, axis=AX.X, op=ALU.add))
    vop(lambda e: e.tensor_tensor(out=vf[:, V_LB:V_LB + 8], in0=sm[:, :, 0], in1=sm[:, :, 1], op=ALU.mult))
    vop(lambda e: e.tensor_scalar(out=vf[:, V_OML:V_OML + 8], in0=vf[:, V_LB:V_LB + 8], scalar1=-1.0, scalar2=1.0, op0=ALU.mult, op1=ALU.add))
    prw = _bank(ps, 5)
    rowsb = g["rowsb"]
    for k in range(6):
        bk = 5 if k < 4 else 4
        dst = _bank(ps, bk)[0:16, (k % 4) * 128:(k % 4 + 1) * 128]
        S.op("pe", lambda e, k=k, dst=dst: e.transpose(dst, vf[:, V_ROWS + 16 * k:V_ROWS + 16 * k + 16], ident[:]),
             reads=["vf", "ident"], writes=[("ps", bk)])
    S.op("act", lambda e: e.activation(out=rowsb[0:16, 0:512], in_=prw[0:16, 0:512], func=AF.Copy), reads=[("ps", 5)], writes=["rowsb"])
    S.op("act", lambda e: e.activation(out=rowsb[0:16, 512:768], in_=_bank(ps, 4)[0:16, 0:256], func=AF.Copy), reads=[("ps", 4)], writes=["rowsb"])
    S.op("sp", lambda e: e.dma_start(out=g["ROWS"].rearrange("k (c p) -> c k p", p=128), in_=rowsb[0:16, :].rearrange("c (k p) -> c k p", p=128)),
         reads=["rowsb"], writes=["ROWS"], dma=True)

    hx, aTh, hfl = g["hx"], g["aTh"], g["hfl"]
    if l == 0:
        dma_in(hx, g["halo0"], "X")
    else:
        hx2, selp = g["hx2"], g["selp"]
        for k in range(4):
            S.op("sp", lambda e, k=k: e.dma_start(out=hx2, in_=g["HALO_DST"][k * 128:k * 128 + 24, :].rearrange("(r c) w -> r (c w)", r=3)), reads=["HALO_DST"], writes=["hx2"], dma=True)
            if k == 0:
                S.op("dve", lambda e: e.tensor_scalar(out=hx, in0=hx2, scalar1=selp[0:3, 0:1], scalar2=None, op0=ALU.mult),
                     reads=["hx2", "selp"], writes=["X"])
            else:
                S.op("dve", lambda e, k=k: e.scalar_tensor_tensor(out=hx, in0=hx2, scalar=selp[0:3, k:k + 1], in1=hx, op0=ALU.mult, op1=ALU.add),
                     reads=["hx2", "selp"], writes=["X"])
    psh = _bank(ps, 6)
    for kc in range(KC):
        S.op("pe", lambda e, kc=kc: e.transpose(psh[:, kc * 4:kc * 4 + 3], hx[:, kc * 128:(kc + 1) * 128], ident[0:3, 0:3]),
             reads=["X", "ident"], writes=[("ps", 6)])
    for kc in range(KC):
        S.op("act", lambda e, kc=kc: e.activation(out=aTh[:, kc, :], in_=psh[:, kc * 4:kc * 4 + 3], func=AF.Identity,
                                                  scale=vf[:, V_GU1 + kc:V_GU1 + kc + 1], bias=vf[:, V_BU1 + kc:V_BU1 + kc + 1]),
             reads=[("ps", 6), "vf"], writes=["aTh"])

    hist, ext, t1, qkT = g["hist"], g["ext"], g["t1"], g["qkT"]
    gsb, l1, nb, nbL, wv_, eo, ec, eL, carry, ctmp = (g[k] for k in ("gsb", "l1", "nb", "nbL", "wv_", "eo", "ec", "eL", "carry", "ctmp"))
    vaug, C32, vH = (g[k] for k in ("vaug", "C32", "vH"))
    sig, kk, bcs, qs, ebuf, qdec = (g[k] for k in ("sig", "kk", "bcs", "qs", "ebuf", "qdec"))
    hcarry, S32, gts, wifg, gbias = (g[k] for k in ("hcarry", "S32", "gts", "wifg", "gbias"))
    OHs = X
    xsrc = g["xsrc"]
    ppi = [0]

    def pp_next():
        b = ppi[0] % 2
        ppi[0] += 1
        return b

    for t in range(NT):
        tok0 = t * TT
        S.op("sp", lambda e, tok0=tok0, xsrc=xsrc: e.dma_start(
            out=X[:], in_=xsrc[tok0:tok0 + TT, :].rearrange("(m p) d -> p m d", p=128)),
            reads=["XN"], writes=["X"], dma=True)
        emit_transpose_affine(S, T, "X", V_GU1, V_BU1)
        pg = _bank(ps, 2)
        wg3 = wifg[:].rearrange("p (k c) -> p k c", c=8)
        for m in range(4):
            for kc in range(KC):
                S.op("pe", lambda e, m=m, kc=kc: e.matmul(pg[:, m * 8:(m + 1) * 8], aT[:, kc, m * 128:(m + 1) * 128], wg3[:, kc, :],
                                                          start=(kc == 0), stop=(kc == KC - 1)),
                     reads=[("aT", kc), "wifg"], writes=[("ps", 2)])
        S.op("dve", lambda e: e.tensor_tensor(out=gsb[:], in0=pg[:, 0:32].rearrange("p (m c) -> p m c", c=8),
                                              in1=gbias[:, 0:8].unsqueeze(1).broadcast_to([128, 4, 8]), op=ALU.add),
             reads=[("ps", 2), "gbias"], writes=["gsb"])
        S.op("act", lambda e: e.activation(out=l1[:], in_=gsb[:, :, 4:8], func=AF.Exp, scale=-1.0), reads=["gsb"], writes=["l1"])
        S.op("act", lambda e: e.activation(out=l1[:], in_=l1[:], func=AF.Ln, bias=1.0), writes=["l1"])
        pc = _bank(ps, 2)
        for m in range(4):
            S.op("pe", lambda e, m=m: e.matmul(pc[:, 64 + m * 4:64 + m * 4 + 4], utri[:], l1[:, m, :], start=True, stop=True),
                 reads=["utri", "l1"], writes=[("ps", 2)])
            S.op("pe", lambda e, m=m: e.matmul(pc[:, 96 + m * 4:96 + m * 4 + 4], ones[:], l1[:, m, :], start=True, stop=True),
                 reads=["ones", "l1"], writes=[("ps", 2)])
        S.op("dve", lambda e: e.tensor_copy(out=nb[:], in_=pc[:, 64:80].rearrange("p (m c) -> p m c", c=4)), reads=[("ps", 2)], writes=["nb"])
        S.op("dve", lambda e: e.tensor_copy(out=nbL[:], in_=pc[:, 96:112].rearrange("p (m c) -> p m c", c=4)), reads=[("ps", 2)], writes=["nbL"])
        S.op("dve", lambda e: e.tensor_tensor(out=ctmp[:], in0=gsb[:, :, 0:4], in1=nb[:], op=ALU.add), reads=["gsb", "nb"], writes=["ctmp"])
        S.op("act", lambda e: e.activation(out=wv_[:], in_=ctmp[:], func=AF.Exp), reads=["ctmp"], writes=["wv"])
        S.op("act", lambda e: e.activation(out=eo[:], in_=nb[:], func=AF.Exp, scale=-1.0, bias=g["lnsc"][:, 0:1]), reads=["nb", "lnsc"], writes=["eo"])
        S.op("act", lambda e: e.activation(out=eL[:], in_=nbL[:], func=AF.Exp, scale=-1.0), reads=["nbL"], writes=["eL"])
        for m in range(4):
            S.op("act", lambda e, m=m: e.activation(out=ctmp[:, m, :], in_=carry[:], func=AF.Exp), reads=["carry"], writes=["ctmp"])
            S.op("dve", lambda e, m=m: e.tensor_tensor(out=ec[:, m, :], in0=eo[:, m, :], in1=ctmp[:, m, :], op=ALU.mult),
                 reads=["eo", "ctmp"], writes=["ec"])
            S.op("dve", lambda e, m=m: e.tensor_tensor(out=carry[:], in0=carry[:], in1=nbL[:, m, :], op=ALU.subtract),
                 reads=["nbL"], writes=["carry"])
        S.op("sp", lambda e, tok0=tok0: e.dma_start(out=g["EM"][tok0:tok0 + TT, :].rearrange("(m p) c -> p m c", p=128), in_=ec[:]),
             reads=["ec"], writes=[("EM", t)], dma=True)
        for qk in range(2):
            wt, wkey = ws.get(g["w1fm"][qk], WSLOT)
            wv4 = wt.rearrange("p (b k c) -> p b k c", b=4, k=KC)
            for b in range(4):
                blk = qk * 4 + b
                bk = pp_next()
                pq = _bank(ps, bk)
                for kc in range(KC):
                    S.op("pe", lambda e, pq=pq, wv4=wv4, b=b, kc=kc: e.matmul(pq, wv4[:, b, kc, :], aT[:, kc, :], start=(kc == 0), stop=(kc == KC - 1)),
                         reads=[wkey, ("aT", kc)], writes=[("ps", bk)])
                if t == 0:
                    ph = _bank(ps, 6)
                    for kc in range(KC):
                        S.op("pe", lambda e, ph=ph, wv4=wv4, b=b, kc=kc, blk=blk: e.matmul(ph[:, 64 + blk * 4:64 + blk * 4 + 3], wv4[:, b, kc, :], aTh[:, kc, :],
                                                                                       start=(kc == 0), stop=(kc == KC - 1)),
                             reads=[wkey, "aTh"], writes=[("ps", 6)])
                    S.op("act", lambda e, ph=ph, blk=blk: e.activation(out=ext[:, 0:3], in_=ph[:, 64 + blk * 4:64 + blk * 4 + 3], func=AF.Copy, scale=g["hfl"][:, 0:1]),
                         reads=[("ps", 6), "hfl"], writes=["ext"])
                else:
                    S.op("act", lambda e, blk=blk: e.activation(out=ext[:, 0:3], in_=hist[:, blk, :], func=AF.Copy),
                         reads=["hist"], writes=["ext"])
                S.op("act", lambda e, pq=pq: e.activation(out=ext[:, 3:TT + 3], in_=pq, func=AF.Copy), reads=[("ps", bk)], writes=["ext"])
                S.op("act", lambda e, blk=blk: e.activation(out=hist[:, blk, :], in_=ext[:, TT:TT + 3], func=AF.Copy), reads=["ext"], writes=["hist"])
                cw = V_CONVW + blk * 4
                S.op("dve", lambda e, cw=cw, blk=blk: e.tensor_scalar(out=t1[:], in0=ext[:, 0:TT], scalar1=vf[:, cw:cw + 1], scalar2=vf[:, V_CONVB + blk:V_CONVB + blk + 1],
                                                                   op0=ALU.mult, op1=ALU.add), reads=["ext", "vf"], writes=["t1"])
                for j in range(1, 4):
                    S.op("dve", lambda e, cw=cw, j=j: e.scalar_tensor_tensor(out=t1[:], in0=ext[:, j:TT + j], scalar=vf[:, cw + j:cw + j + 1], in1=t1[:],
                                                                           op0=ALU.mult, op1=ALU.add), reads=["ext", "vf"], writes=["t1"])
                S.op("act", lambda e, blk=blk: e.activation(out=qkT[:, blk, :], in_=t1[:], func=AF.Silu), reads=["t1"], writes=[("qkT", blk)])
        S.op("sp", lambda e, tok0=tok0: e.dma_start(out=g["QM"][:, tok0:tok0 + TT].rearrange("(h p) t -> p h t", p=128), in_=qkT[:, 0:4, :]),
             reads=[("qkT", b) for b in range(4)], writes=[("QM", t)], dma=True)
        for half in range(2):
            wt, wkey = ws.get(g["w1tm"][half], WSLOT)
            wv3 = wt.rearrange("p (k c) -> p k c", k=KC)
            for m in range(4):
                bk = pp_next()
                pv = _bank(ps, bk)
                for kc in range(KC):
                    S.op("pe", lambda e, pv=pv, wv3=wv3, m=m, kc=kc: e.matmul(pv, aT[:, kc, m * 128:(m + 1) * 128], wv3[:, kc, :], start=(kc == 0), stop=(kc == KC - 1)),
                         reads=[wkey, ("aT", kc)], writes=[("ps", bk)])
                for hh in range(2):
                    h = half * 2 + hh
                    S.op("act", lambda e, pv=pv, m=m, h=h, hh=hh: e.activation(out=vaug[:, m, h, 0:256], in_=pv[:, hh * 256:(hh + 1) * 256], func=AF.Copy,
                                                                            scale=wv_[:, m, h:h + 1]),
                         reads=[("ps", bk), "wv"], writes=[("vaug", m)])
        for m in range(4):
            S.op("dve", lambda e, m=m: e.tensor_copy(out=vaug[:, m, :, 256], in_=wv_[:, m, :]), reads=["wv"], writes=[("vaug", m)])
        pa = _bank(ps, 3)
        ktok4, A_sb4, dCs, Cbf4 = g["ktok4"], g["A_sb4"], g["dCs"], g["Cbf4"]

        def ml_abc(h):
            par = h % 2
            p2 = _bank(ps, 2).bitcast(BF16)
            for m in range(4):
                sl = slice(m * 128, (m + 1) * 128)
                S.op("pe", lambda e, m=m, sl=sl: e.transpose(p2[:, m * 128:(m + 1) * 128], qkT[:, 4 + h, sl], identb[:]),
                     reads=[("qkT", 4 + h), "identb"], writes=[("ps", 2)])
            for m in range(4):
                S.op("act", lambda e, m=m: e.activation(out=ktok4[:, par, m, :], in_=p2[:, m * 128:(m + 1) * 128], func=AF.Copy),
                     reads=[("ps", 2)], writes=[("ktok4", par, m)])
            for m in range(4):
                sl = slice(m * 128, (m + 1) * 128)
                S.op("pe", lambda e, m=m, sl=sl: e.matmul(pa[:, m * 128:(m + 1) * 128], qkT[:, 4 + h, sl], qkT[:, h, sl], start=True, stop=True),
                     reads=[("qkT", 4 + h), ("qkT", h)], writes=[("ps", 3)])
            for m in range(4):
                S.op("dve", lambda e, m=m: e.copy_predicated(out=A_sb4[:, par, m, :], mask=tri[:], data=pa[:, m * 128:(m + 1) * 128]),
                     reads=[("ps", 3), "tri"], writes=[("A_sb4", par, m)])
            for m in range(4):
                bk = 6 + m % 2
                pC = _bank(ps, bk)
                S.op("pe", lambda e, pC=pC, m=m: e.matmul(pC[:, 0:257], ktok4[:, par, m, :], vaug[:, m, h, 0:257], start=True, stop=True),
                     reads=[("ktok4", par, m), ("vaug", m)], writes=[("ps", bk)])
                S.op("act", lambda e, pC=pC, m=m: e.activation(out=dCs[:, m, 0:257], in_=pC[:, 0:257], func=AF.Copy),
                     reads=[("ps", bk)], writes=[("dCs", m)])
            for m in range(4):
                S.op("dve", lambda e, m=m: e.tensor_copy(out=Cbf4[:, par, m, :], in_=C32[:, h, :]), reads=[("C32", h)], writes=[("Cbf4", par, m)])
                S.op("dve", lambda e, m=m: e.tensor_scalar(out=C32[:, h, 0:257], in0=C32[:, h, 0:257], scalar1=eL[:, m, h:h + 1], scalar2=None, op0=ALU.mult),
                     reads=["eL"], writes=[("C32", h)])
                S.op("dve", lambda e, m=m: e.scalar_tensor_tensor(out=C32[:, h, 0:257], in0=dCs[:, m, 0:257], scalar=eL[:, m, h:h + 1], in1=C32[:, h, 0:257],
                                                                 op0=ALU.mult, op1=ALU.add), reads=[("dCs", m), "eL"], writes=[("C32", h)])

        def ml_d(h):
            par = h % 2
            for m in range(4):
                sl = slice(m * 128, (m + 1) * 128)
                bo = 4 + m % 2
                pP = _bank(ps, bo)
                S.op("pe", lambda e, pP=pP, m=m: e.matmul(pP[:, 0:257], A_sb4[:, par, m, :], vaug[:, m, h, 0:257], start=True, stop=False),
                     reads=[("A_sb4", par, m), ("vaug", m)], writes=[("ps", bo)])
                S.op("pe", lambda e, pP=pP, m=m, sl=sl: e.matmul(pP[:, 0:257], qkT[:, h, sl], Cbf4[:, par, m, 0:257], start=False, stop=True),
                     reads=[("qkT", h), ("Cbf4", par, m)], writes=[("ps", bo)])
                r = m % 2
                S.op("act", lambda e, pP=pP, m=m, r=r: e.activation(out=g["OMs"][:, r, 0:257], in_=pP[:, 0:257], func=AF.Copy, scale=eo[:, m, h:h + 1]),
                     reads=[("ps", bo), "eo"], writes=[("OMs", r)])
                S.op("sp", lambda e, m=m, r=r, tok0=tok0: e.dma_start(out=g["OM"][tok0 + m * 128:tok0 + (m + 1) * 128, h * MW:h * MW + 257], in_=g["OMs"][:, r, 0:257]),
                     reads=[("OMs", r)], writes=[("OM", t, m, h)], dma=True)
        for h in range(4):
            if SKIP_ML:
                break
            ml_abc(h)
            if PIPE_ML:
                if h > 0:
                    ml_d(h - 1)
            else:
                ml_d(h)
        if PIPE_ML and not SKIP_ML:
            ml_d(3)
        for half in range(2):
            wt, wkey = ws.get(g["w1tm"][2 + half], WSLOT)
            wv3 = wt.rearrange("p (k c) -> p k c", k=KC)
            for m in range(4):
                bk = pp_next()
                pv = _bank(ps, bk)
                for kc in range(KC):
                    S.op("pe", lambda e, pv=pv, wv3=wv3, m=m, kc=kc: e.matmul(pv, aT[:, kc, m * 128:(m + 1) * 128], wv3[:, kc, :], start=(kc == 0), stop=(kc == KC - 1)),
                         reads=[wkey, ("aT", kc)], writes=[("ps", bk)])
                S.op("act", lambda e, pv=pv, m=m, half=half: e.activation(out=vH[:, m, half * 512:(half + 1) * 512], in_=pv, func=AF.Copy),
                     reads=[("ps", bk)], writes=[("vH", m)])
        gstate = {}
        ws.ahead = NWS - 1

        def gate_unit(u):
            gi, m = u // 4, u % 4
            if m == 0:
                wt, wkey = ws.get(g["w1tm"][4 + gi], WSLOT)
                gstate["wv3"], gstate["wkey"] = wt.rearrange("p (k c) -> p k c", k=KC), wkey
            wv3, wkey = gstate["wv3"], gstate["wkey"]
            fn = AF.Sigmoid if gi < 2 else AF.Silu
            bk = pp_next()
            pv = _bank(ps, bk)
            for kc in range(KC):
                S.op("pe", lambda e, kc=kc: e.matmul(pv, aT[:, kc, m * 128:(m + 1) * 128], wv3[:, kc, :], start=(kc == 0), stop=(kc == KC - 1)),
                     reads=[wkey, ("aT", kc)], writes=[("ps", bk)])
            g2 = u % 2
            S.op("act", lambda e: e.activation(out=gts[:, g2, :], in_=pv, func=fn), reads=[("ps", bk)], writes=[("gts", g2)])
            S.op("sp", lambda e, tok0=tok0: e.dma_start(out=g["GT"][tok0 + m * 128:tok0 + (m + 1) * 128, gi * 512:(gi + 1) * 512], in_=gts[:, g2, :]),
                 reads=[("gts", g2)], writes=[("GT", t, gi, m)], dma=True)
        A_h8, Sp8 = g["A_h8"], g["Sp8"]
        K6, K7 = ("ps", 6), ("ps", 7)

        def hg_abc(h, par, first):
            hsc = g["hsc2"][:, par]
            q_in = g["q_in2"][:, par, :]
            k_in = g["k_in2"][:, par, :]
            p2 = _bank(ps, 2).bitcast(BF16)
            for m in range(4):
                S.op("pe", lambda e, m=m: e.transpose(p2[:, m * 128:(m + 1) * 128], k_in[:, m * 128:(m + 1) * 128], identb[:]),
                     reads=[("k_in", par), "identb"], writes=[("ps", 2)])
            for m in range(4):
                S.op("act", lambda e, m=m: e.activation(out=ktok4[:, par, m, :], in_=p2[:, m * 128:(m + 1) * 128], func=AF.Copy),
                     reads=[("ps", 2)], writes=[("ktok4", par, m)])
            for cc in range(8):
                m, hf = cc // 2, cc % 2
                c0, p0 = cc * 64, hf * 64
                pA = pa[p0:p0 + 64, m * 64:(m + 1) * 64]
                S.op("pe", lambda e, pA=pA, c0=c0: e.matmul(pA, k_in[:, c0:c0 + 64], q_in[:, c0:c0 + 64], start=True, stop=True),
                     reads=[("k_in", par), ("q_in", par)], writes=[("ps", 3)])
            for cc in range(8):
                m, hf = cc // 2, cc % 2
                p0 = hf * 64
                pA = pa[p0:p0 + 64, m * 64:(m + 1) * 64]
                S.op("dve", lambda e, pA=pA, p0=p0, m=m: e.copy_predicated(out=A_h8[p0:p0 + 64, par, m, :], mask=tri[p0:p0 + 64, p0:p0 + 64], data=pA),
                     reads=[("ps", 3), "tri"], writes=[("A_h8", par, cc)])
            for cc in range(8):
                m, hf = cc // 2, cc % 2
                p0 = hf * 64
                bk = 6 + hf
                pS = _bank(ps, bk)[:, m * 128:(m + 1) * 128]
                S.op("pe", lambda e, pS=pS, p0=p0, m=m: e.matmul(pS, ktok4[p0:p0 + 64, par, m, :], vH[p0:p0 + 64, m, h * 128:(h + 1) * 128], start=True, stop=True),
                     reads=[("ktok4", par, m), ("vH", m)], writes=[("ps", bk)])
            for cc in range(8):
                bk = 6 + cc % 2
                pS = _bank(ps, bk)[:, (cc // 2) * 128:(cc // 2 + 1) * 128]
                S.op("dve", lambda e, cc=cc: e.tensor_scalar(out=Sp8[:, par, cc, :], in0=S32[:, h, :], scalar1=hsc[:, cc, 2:3], scalar2=None, op0=ALU.mult),
                     reads=[("S32", h), ("hsc", par)], writes=[("Sp8", par, cc)])
                S.op("dve", lambda e, cc=cc: e.tensor_scalar(out=S32[:, h, :], in0=S32[:, h, :], scalar1=hsc[:, cc, 3:4], scalar2=None, op0=ALU.mult),
                     reads=[("hsc", par)], writes=[("S32", h)])
                S.op("dve", lambda e, pS=pS, cc=cc: e.scalar_tensor_tensor(out=S32[:, h, :], in0=pS, scalar=hsc[:, cc, 4:5], in1=S32[:, h, :], op0=ALU.mult, op1=ALU.add),
                     reads=[("ps", bk), ("hsc", par)], writes=[("S32", h)])

        def hg_d(h, par):
            q_in = g["q_in2"][:, par, :]
            for cc in range(8):
                m, hf = cc // 2, cc % 2
                c0, p0 = cc * 64, hf * 64
                bo = 4 + cc % 2
                pO = _bank(ps, bo)[p0:p0 + 64, 0:128]
                S.op("pe", lambda e, pO=pO, p0=p0, m=m: e.matmul(pO, A_h8[p0:p0 + 64, par, m, :], vH[p0:p0 + 64, m, h * 128:(h + 1) * 128], start=True, stop=False),
                     reads=[("A_h8", par, cc), ("vH", m)], writes=[("ps", bo)])
                S.op("pe", lambda e, pO=pO, c0=c0, cc=cc: e.matmul(pO, q_in[:, c0:c0 + 64], Sp8[:, par, cc, :], start=False, stop=True),
                     reads=[("q_in", par), ("Sp8", par, cc)], writes=[("ps", bo)])
                S.op("act", lambda e, pO=pO, p0=p0, m=m: e.activation(out=OHs[p0:p0 + 64, m, h * 128:(h + 1) * 128], in_=pO, func=AF.Copy),
                     reads=[("ps", bo)], writes=["X"])
        for hp in range(4):
            wt, wkey = ws.get(g["w1fm"][2 + hp], WSLOT)
            wv4 = wt.rearrange("p (b k c) -> p b k c", b=4, k=KC)
            for hh in range(2):
                def _prep(hh=hh, wv4=wv4, wkey=wkey):
                    h = hp * 2 + hh
                    par = h % 2
                    hsc = g["hsc2"][:, par]
                    q_in = g["q_in2"][:, par, :]
                    k_in = g["k_in2"][:, par, :]
                    bq = pp_next()
                    pq = _bank(ps, bq)
                    for kc in range(KC):
                        S.op("pe", lambda e, pq=pq, wv4=wv4, hh=hh, kc=kc: e.matmul(pq, wv4[:, hh * 2, kc, :], aT[:, kc, :], start=(kc == 0), stop=(kc == KC - 1)),
                             reads=[wkey, ("aT", kc)], writes=[("ps", bq)])
                    bf = pp_next()
                    pf = _bank(ps, bf)
                    for kc in range(KC):
                        S.op("pe", lambda e, pf=pf, wv4=wv4, hh=hh, kc=kc: e.matmul(pf, wv4[:, hh * 2 + 1, kc, :], aT[:, kc, :], start=(kc == 0), stop=(kc == KC - 1)),
                             reads=[wkey, ("aT", kc)], writes=[("ps", bf)])
                    S.op("act", lambda e, pq=pq: e.activation(out=qs[:], in_=pq, func=AF.Silu), reads=[("ps", bq)], writes=["qs"])
                    S.op("act", lambda e, pf=pf: e.activation(out=sig[:], in_=pf, func=AF.Sigmoid), reads=[("ps", bf)], writes=["sig"])
                    S.op("dve", lambda e, h=h: e.tensor_scalar(out=sig[:], in0=sig[:], scalar1=vf[:, V_OML + h:V_OML + h + 1], scalar2=vf[:, V_LB + h:V_LB + h + 1],
                                                              op0=ALU.mult, op1=ALU.add), reads=["vf"], writes=["sig"])
                    S.op("dve", lambda e: e.tensor_scalar(out=kk[:], in0=sig[:], scalar1=-1.0, scalar2=1.0, op0=ALU.mult, op1=ALU.add), reads=["sig"], writes=["kk"])
                    S.op("act", lambda e: e.activation(out=sig[:], in_=sig[:], func=AF.Ln), writes=["sig"])
                    S.op("dve", lambda e: e.tensor_tensor_scan(out=bcs[:], data0=m0[:], data1=sig[:], initial=0.0, op0=ALU.mult, op1=ALU.add),
                         reads=["m0", "sig"], writes=["bcs"])
                    b3 = bcs[:].rearrange("p (c t) -> p c t", t=64)
                    S.op("dve", lambda e: e.tensor_copy(out=hsc[:, :, 0], in_=b3[:, :, 31]), reads=["bcs"], writes=[("hsc", par)])
                    S.op("dve", lambda e: e.tensor_copy(out=hsc[:, :, 1], in_=b3[:, :, 63]), reads=["bcs"], writes=[("hsc", par)])
                    S.op("dve", lambda e: e.tensor_tensor(out=hsc[:, :, 7], in0=hsc[:, :, 1], in1=hsc[:, :, 0], op=ALU.subtract), writes=[("hsc", par)])
                    S.op("act", lambda e: e.activation(out=hsc[:, :, 2], in_=hsc[:, :, 0], func=AF.Exp), writes=[("hsc", par)])
                    S.op("act", lambda e: e.activation(out=hsc[:, :, 3], in_=hsc[:, :, 1], func=AF.Exp), writes=[("hsc", par)])
                    S.op("act", lambda e: e.activation(out=hsc[:, :, 4], in_=hsc[:, :, 7], func=AF.Exp), writes=[("hsc", par)])
                    S.op("dve", lambda e, h=h: e.tensor_tensor_scan(out=hsc[:, :, 7], data0=ones[:, 0:8], data1=hsc[:, :, 1], initial=hcarry[:, h:h + 1],
                                                                   op0=ALU.mult, op1=ALU.add), reads=["ones", "hcarry"], writes=[("hsc", par)])
                    S.op("dve", lambda e: e.tensor_tensor(out=hsc[:, :, 5], in0=hsc[:, :, 7], in1=hsc[:, :, 1], op=ALU.subtract), writes=[("hsc", par)])
                    S.op("dve", lambda e, h=h: e.tensor_copy(out=hcarry[:, h:h + 1], in_=hsc[:, 7, 7:8]), reads=[("hsc", par)], writes=["hcarry"])
                    S.op("dve", lambda e: e.tensor_tensor(out=hsc[:, :, 6], in0=hsc[:, :, 5], in1=hsc[:, :, 0], op=ALU.add), writes=[("hsc", par)])
                    S.op("act", lambda e: e.activation(out=hsc[:, :, 6], in_=hsc[:, :, 6], func=AF.Exp), writes=[("hsc", par)])
                    S.op("dve", lambda e: e.tensor_tensor(out=b3, in0=b3, in1=hsc[:, :, 0:1].broadcast_to([128, 8, 64]), op=ALU.subtract),
                         reads=[("hsc", par)], writes=["bcs"])
                    S.op("act", lambda e: e.activation(out=ebuf[:], in_=bcs[:], func=AF.Exp), reads=["bcs"], writes=["ebuf"])
                    S.op("dve", lambda e: e.tensor_tensor(out=q_in, in0=qs[:], in1=ebuf[:], op=ALU.mult), reads=["qs", "ebuf"], writes=[("q_in", par)])
                    S.op("pool", lambda e: e.tensor_tensor(out=qs[:], in0=qs[:], in1=ebuf[:], op=ALU.mult), reads=["ebuf"], writes=["qs"])
                    S.op("pool", lambda e: e.tensor_tensor(out=qdec[:].rearrange("p (c t) -> p c t", t=64), in0=qs[:].rearrange("p (c t) -> p c t", t=64),
                                                           in1=hsc[:, :, 6:7].broadcast_to([128, 8, 64]), op=ALU.mult), reads=["qs", ("hsc", par)], writes=["qdec"])
                    S.op("act", lambda e: e.activation(out=ebuf[:], in_=bcs[:], func=AF.Exp, scale=-1.0), reads=["bcs"], writes=["ebuf"])
                    S.op("dve", lambda e: e.tensor_tensor(out=k_in, in0=kk[:], in1=ebuf[:], op=ALU.mult), reads=["kk", "ebuf"], writes=[("k_in", par)])
                    S.op("sp", lambda e, h=h, tok0=tok0: e.dma_start(out=g["QH"][h * 128:(h + 1) * 128, tok0:tok0 + TT], in_=qdec[:]),
                         reads=["qdec"], writes=[("QH", t, h)], dma=True)
                    return h, par
                h, par = _prep()
                if h > 0:
                    hg_d(h - 1, (h - 1) % 2)
                gate_unit(2 * h)
                gate_unit(2 * h + 1)
                hg_abc(h, par, first=(h == 0))
        hg_d(7, 1)
        ws.ahead = NWS
        S.op("sp", lambda e, tok0=tok0: e.dma_start(out=g["OH"][tok0:tok0 + TT, :].rearrange("(m p) c -> p m c", p=128), in_=OHs[:, :, 0:1024]),
             reads=["X"], writes=[("OH", t)], dma=True)
    stsb = X[:].rearrange("p m d -> p (m d)")[:, 0:NST]
    S.op("dve", lambda e: e.memset(stsb[:, 2064:NST], 0.0), writes=["X"])
    S.op("dve", lambda e: e.tensor_copy(out=stsb[:, 0:1024], in_=S32[:].rearrange("p h c -> p (h c)")), reads=[("S32", h) for h in range(8)], writes=["X"])
    S.op("dve", lambda e: e.tensor_copy(out=stsb[:, 1024:1024 + 4 * MW], in_=C32[:].rearrange("p h c -> p (h c)")), reads=[("C32", h) for h in range(4)], writes=["X"])
    S.op("act", lambda e: e.activation(out=stsb[:, 2064:2072], in_=hcarry[:], func=AF.Exp), reads=["hcarry"], writes=["X"])
    S.op("act", lambda e: e.activation(out=stsb[:, 2072:2076], in_=carry[:], func=AF.Exp), reads=["carry"], writes=["X"])
    S.op("sp", lambda e: e.dma_start(out=g["STATE_SRC"].rearrange("k p c -> p k c"), in_=stsb.rearrange("p (k c) -> p k c", c=256)),
         reads=["X"], writes=["STATE_SRC"], dma=True)
    for k in range(NPC):
        S.op("pool", lambda e, k=k: e.collective_compute("AllGather", ALU.bypass, replica_groups=[[0, 1, 2, 3], [4, 5, 6, 7]],
                                                         ins=[g["STATE_SRC"][k].opt()], outs=[g["STATES"][k].opt()]),
             reads=["STATE_SRC"], writes=[("STATES", k)], dma="cc")


def _emit_p2(S, ws, nc, T, g, NT, l, last):
    X, aT, ps, vf, bc, arena = T["X"], T["aT"], T["ps"], T["vf"], T["bc"], g["arena"]
    ident, identb, selt, Sst, Cst, EMt, hst, mst = (g[k] for k in ("ident", "identb", "selt", "Sst", "Cst", "EMt", "hst", "mst"))
    xsrc = g["xsrc"]

    def hk(a, b):
        return [("hT", j) for j in range(a // TT, (b + TT - 1) // TT)]

    def f32v(off, n):
        return arena[:, off:off + 2 * n].bitcast(F32)

    def dma_in(dst, src, keys, eng="sp", reads=()):
        S.op(eng, lambda e: e.dma_start(out=dst, in_=src), reads=list(reads), writes=list(keys), dma=True)
    blobs = X[:].rearrange("p m d -> p (m d)")
    for k in range(3):
        dma_in(blobs[:, k * NST:(k + 1) * NST].rearrange("p (j c) -> p j c", c=256), g["STATES"][:, k * 128:(k + 1) * 128, :].rearrange("j p c -> p j c"),
               ["X"], reads=[("STATES", j) for j in range(NPC)])
    KS = hk(0, 4 * NST)
    Scur = f32v(0, NST)
    Sacc = f32v(2 * NST, NST)
    S.op("dve", lambda e: e.tensor_copy(out=Scur, in_=blobs[:, 0:NST]), reads=["X"], writes=KS)
    S.op("dve", lambda e: e.tensor_scalar(out=Sacc, in0=Scur, scalar1=selt[:, 1:2], scalar2=None, op0=ALU.mult), reads=["selt"], writes=KS)
    for k in (1, 2):
        bk = blobs[:, k * NST:(k + 1) * NST]
        S.op("dve", lambda e, bk=bk: e.tensor_tensor(out=Scur[:, 0:1024].rearrange("p (h c) -> p h c", c=128), in0=Scur[:, 0:1024].rearrange("p (h c) -> p h c", c=128),
                                                     in1=bk[:, 2064:2072].unsqueeze(2).broadcast_to([128, 8, 128]), op=ALU.mult), reads=["X"], writes=KS)
        S.op("dve", lambda e, bk=bk: e.tensor_tensor(out=Scur[:, 1024:1024 + 4 * MW].rearrange("p (h c) -> p h c", c=MW), in0=Scur[:, 1024:1024 + 4 * MW].rearrange("p (h c) -> p h c", c=MW),
                                                     in1=bk[:, 2072:2076].unsqueeze(2).broadcast_to([128, 4, MW]), op=ALU.mult), reads=["X"], writes=KS)
        S.op("dve", lambda e, bk=bk: e.tensor_tensor(out=Scur[:, 0:2064], in0=Scur[:, 0:2064], in1=bk[:, 0:2064], op=ALU.add), reads=["X"], writes=KS)
        S.op("dve", lambda e, k=k: e.scalar_tensor_tensor(out=Sacc[:, 0:2064], in0=Scur[:, 0:2064], scalar=selt[:, k + 1:k + 2], in1=Sacc[:, 0:2064], op0=ALU.mult, op1=ALU.add),
             reads=["selt"], writes=KS)
    S.op("dve", lambda e: e.tensor_copy(out=Sst[:].rearrange("p h c -> p (h c)"), in_=Sacc[:, 0:1024]), reads=KS, writes=["Sst"])
    S.op("dve", lambda e: e.tensor_copy(out=Cst[:].rearrange("p h c -> p (h c)"), in_=Sacc[:, 1024:1024 + 4 * MW]), reads=KS, writes=["Cst"])

    O_OH, O_OM, O_GT, O_QH, O_QM, O_YC, O_W1, O_W2, O_WM, O_WH = 0, 2048, 4608, 6656, 10752, 12800, 14848, 16896, 18944, 19968
    OHm, K_OH = f32v(O_OH, 1024), hk(O_OH, O_OH + 2048)
    OMm, K_OM = f32v(O_OM, 4 * MW), hk(O_OM, O_OM + 8 * MW)
    GTm, K_GT = arena[:, O_GT:O_GT + 2048], hk(O_GT, O_GT + 2048)
    QHt, K_QH = arena[:, O_QH:O_QH + 4096].rearrange("p (h t) -> p h t", t=TT), hk(O_QH, O_QH + 4096)
    QMt, K_QM = arena[:, O_QM:O_QM + 2048].rearrange("p (h t) -> p h t", t=TT), hk(O_QM, O_QM + 2048)
    ycat, K_YC = arena[:, O_YC:O_YC + 2048], hk(O_YC, O_YC + 2048)
    wk1, K_W1 = f32v(O_W1, 1024), hk(O_W1, O_W1 + 2048)
    wk2, K_W2 = f32v(O_W2, 1024), hk(O_W2, O_W2 + 2048)
    wkm, K_WM = f32v(O_WM, MW), hk(O_WM, O_WM + 2 * MW)
    wkh, K_WH = f32v(O_WH, 256), hk(O_WH, O_WH + 512)

    for t in range(NT):
        tok0 = t * TT
        dma_in(QHt, g["QH"][:, tok0:tok0 + TT].rearrange("(h p) t -> p h t", p=128), K_QH)
        dma_in(QMt, g["QM"][:, tok0:tok0 + TT].rearrange("(h p) t -> p h t", p=128), K_QM)
        dma_in(EMt[:], g["EM"][tok0:tok0 + TT, :].rearrange("(m p) c -> p m c", p=128), ["EMt"])
        for i in range(3):
            dma_in(bc[:, i, :], g["ROWS"][i:i + 1, :].partition_broadcast(128), [("bc", i)], reads=["ROWS"])
        for m in range(4):
            r0 = tok0 + m * 128
            msl = slice(m * 128, (m + 1) * 128)
            dma_in(OHm, g["OH"][r0:r0 + 128, :], K_OH)
            dma_in(OMm, g["OM"][r0:r0 + 128, :], K_OM)
            dma_in(GTm, g["GT"][r0:r0 + 128, :], K_GT)
            for h in range(8):
                bk = 4 + h // 4
                pcor = _bank(ps, bk)[:, (h % 4) * 128:(h % 4 + 1) * 128]
                S.op("pe", lambda e, pcor=pcor, h=h, msl=msl: e.matmul(pcor, QHt[:, h, msl], Sst[:, h, :], start=True, stop=True),
                     reads=K_QH + ["Sst"], writes=[("ps", bk)])
            for hb in range(2):
                S.op("dve", lambda e, hb=hb: e.tensor_tensor(out=wk1[:, hb * 512:(hb + 1) * 512], in0=_bank(ps, 4 + hb), in1=OHm[:, hb * 512:(hb + 1) * 512], op=ALU.add),
                     reads=[("ps", 4 + hb)] + K_OH, writes=K_W1)
            S.op("pool", lambda e: e.tensor_tensor(out=wk2, in0=wk1, in1=wk1, op=ALU.mult), reads=K_W1, writes=K_W2)
            S.op("dve", lambda e: e.tensor_reduce(out=hst[:, :, 0], in_=wk2.rearrange("p (h c) -> p h c", c=128), axis=AX.X, op=ALU.add), reads=K_W2, writes=["hst"])
            S.op("act", lambda e: e.activation(out=hst[:, :, 1], in_=hst[:, :, 0], func=AF.Sqrt, scale=1.0 / 128.0, bias=g["epsb"][:, 0:1]), reads=["hst", "epsb"], writes=["hst"])
            S.op("dve", lambda e: e.reciprocal(out=hst[:, :, 2], in_=hst[:, :, 1]), writes=["hst"])
            S.op("dve", lambda e: e.tensor_tensor(out=wk1.rearrange("p (h c) -> p h c", c=128), in0=wk1.rearrange("p (h c) -> p h c", c=128),
                                                  in1=hst[:, :, 2:3].broadcast_to([128, 8, 128]), op=ALU.mult), reads=["hst"], writes=K_W1)
            S.op("pool", lambda e: e.tensor_tensor(out=ycat[:, 1024:2048], in0=wk1, in1=GTm[:, 1024:2048], op=ALU.mult), reads=K_W1 + K_GT, writes=K_YC)
            for h in range(4):
                bk = 6 + h % 2
                pP = _bank(ps, bk)
                S.op("pe", lambda e, pP=pP, h=h, msl=msl: e.matmul(pP[:, 0:257], QMt[:, h, msl], Cst[:, h, 0:257], start=True, stop=True),
                     reads=K_QM + ["Cst"], writes=[("ps", bk)])
                S.op("dve", lambda e, pP=pP, h=h, m=m: e.scalar_tensor_tensor(out=wkm[:, 0:257], in0=pP[:, 0:257], scalar=EMt[:, m, h:h + 1], in1=OMm[:, h * MW:h * MW + 257],
                                                                             op0=ALU.mult, op1=ALU.add), reads=[("ps", bk), "EMt"] + K_OM, writes=K_WM)
                ms = mst[:, h, :]
                S.op("dve", lambda e, ms=ms: e.tensor_scalar(out=ms[:, 0:1], in0=wkm[:, 256:257], scalar1=-1.0, scalar2=None, op0=ALU.mult), reads=K_WM, writes=[("mst", h)])
                S.op("dve", lambda e, ms=ms: e.tensor_tensor(out=ms[:, 0:1], in0=ms[:, 0:1], in1=wkm[:, 256:257], op=ALU.max), reads=K_WM, writes=[("mst", h)])
                S.op("dve", lambda e, ms=ms: e.tensor_scalar(out=ms[:, 0:1], in0=ms[:, 0:1], scalar1=1.0, scalar2=None, op0=ALU.max), writes=[("mst", h)])
                S.op("dve", lambda e, ms=ms: e.reciprocal(out=ms[:, 1:2], in_=ms[:, 0:1]), writes=[("mst", h)])
                S.op("act", lambda e, ms=ms: e.activation(out=wkh, in_=wkm[:, 0:256], func=AF.Copy, scale=ms[:, 1:2]), reads=K_WM + [("mst", h)], writes=K_WH)
                S.op("dve", lambda e, ms=ms: e.bn_stats(out=ms[:, 2:8], in_=wkh), reads=K_WH, writes=[("mst", h)])
                S.op("dve", lambda e, ms=ms: e.bn_aggr(out=ms[:, 8:10], in_=ms[:, 2:8]), writes=[("mst", h)])
                S.op("act", lambda e, ms=ms: e.activation(out=ms[:, 10:11], in_=ms[:, 9:10], func=AF.Sqrt, bias=g["epsb"][:, 0:1]), reads=["epsb"], writes=[("mst", h)])
                S.op("dve", lambda e, ms=ms: e.reciprocal(out=ms[:, 11:12], in_=ms[:, 10:11]), writes=[("mst", h)])
                S.op("dve", lambda e, ms=ms: e.scalar_tensor_tensor(out=ms[:, 12:13], in0=ms[:, 8:9], scalar=-1.0, in1=ms[:, 11:12], op0=ALU.mult, op1=ALU.mult), writes=[("mst", h)])
                S.op("act", lambda e, ms=ms: e.activation(out=wkh, in_=wkh, func=AF.Identity, scale=ms[:, 11:12], bias=ms[:, 12:13]), reads=[("mst", h)], writes=K_WH)
                S.op("dve", lambda e, h=h: e.tensor_tensor(out=ycat[:, h * 256:(h + 1) * 256], in0=wkh, in1=GTm[:, h * 256:(h + 1) * 256], op=ALU.mult),
                     reads=K_WH + K_GT, writes=K_YC)
            for kc in range(KC):
                t2 = kc % 2
                pt = _bank(ps, 4 + t2).bitcast(BF16)[:, 0:128]
                S.op("pe", lambda e, pt=pt, kc=kc: e.transpose(pt, ycat[:, kc * 128:(kc + 1) * 128], identb[:]), reads=K_YC + ["identb"], writes=[("ps", 4 + t2)])
                S.op("act", lambda e, pt=pt, kc=kc, msl=msl: e.activation(out=aT[:, kc, msl], in_=pt, func=AF.Copy, scale=vf[:, V_NRMW + kc:V_NRMW + kc + 1]),
                     reads=[("ps", 4 + t2), "vf"], writes=[("aT", kc)])
        S.op("sp", lambda e, tok0=tok0, xsrc=xsrc: e.dma_start(out=X[:], in_=xsrc[tok0:tok0 + TT, :].rearrange("(m p) d -> p m d", p=128)),
             reads=["XN"], writes=["X"], dma=True)
        emit_resid_prep(S, T, 0, 1)
        for n in range(4):
            wt, wkey = ws.get(g["wout"][n], WSLOT)
            wv3 = wt.rearrange("p (k c) -> p k c", k=KC)
            for m in range(4):
                pm = _bank(ps, m)
                for kc in range(KC):
                    S.op("pe", lambda e, pm=pm, wv3=wv3, m=m, kc=kc: e.matmul(pm, aT[:, kc, m * 128:(m + 1) * 128], wv3[:, kc, :], start=(kc == 0), stop=(kc == KC - 1)),
                         reads=[wkey, ("aT", kc)], writes=[("ps", m)])
            for m in range(4):
                emit_epilogue_block(S, T, m, n, 2)
        emit_ln_inplace(S, T)
        for i in range(3):
            dma_in(bc[:, i, :], g["ROWS"][3 + i:4 + i, :].partition_broadcast(128), [("bc", i)], reads=["ROWS"])
        emit_transpose_affine(S, T, "X", V_GU2, V_BU2)
        emit_ffn2(S, ws, T, g["wgu"], g["wd"])
        if not last:
            S.op("sp", lambda e, tok0=tok0: e.dma_start(out=g["XN"][tok0:tok0 + TT, :].rearrange("(m p) d -> p m d", p=128), in_=X[:]),
                 reads=["X"], writes=["XN"], dma=True)
            if t == NT - 1:
                S.op("sp", lambda e: e.dma_start(out=g["HALO_SRC"][0:24, :].rearrange("(r c) w -> r (c w)", r=3), in_=X[125:128, 3, :]), reads=["X"], writes=["HALO_SRC"], dma=True)
                S.op("pool", lambda e: e.collective_compute("AllGather", ALU.bypass, replica_groups=[[0, 1, 2, 3], [4, 5, 6, 7]],
                                                            ins=[g["HALO_SRC"].opt()], outs=[g["HALO_DST"].opt()]),
                     reads=["HALO_SRC"], writes=["HALO_DST"], dma="cc")
        else:
            for i in range(2):
                dma_in(bc[:, i, :], g["fin"][i:i + 1, :].partition_broadcast(128), [("bc", i)])
            emit_resid_prep(S, T, 0, 1)
            S.op("sp", lambda e, tok0=tok0: e.dma_start(out=g["OUT"][tok0:tok0 + TT, :].rearrange("(m p) d -> p m d", p=128), in_=X[:]),
                 reads=["X"], writes=[("OUT", t)], dma=True)


def emit_ffn2(S, ws, T, wgu, wd):
    aT, hT, ps, sg = T["aT"], T["hT"], T["ps"], T["sg"]
    for j2 in range(FC // 2):
        wt, wkey = ws.get(wgu[j2], 2 * 2 * KC * 128)
        wv = wt.rearrange("p (b g k c) -> p b g k c", b=2, g=2, k=KC)
        for b in range(2):
            j = j2 * 2 + b
            bg = 4 + (j % 2) * 2
            bu = bg + 1
            pg, pu = _bank(ps, bg), _bank(ps, bu)
            for gi, pp, bk in ((0, pg, bg), (1, pu, bu)):
                for kc in range(KC):
                    S.op("pe", lambda e, pp=pp, wv=wv, b=b, gi=gi, kc=kc: e.matmul(
                        pp, wv[:, b, gi, kc, :], aT[:, kc, :], start=(kc == 0), stop=(kc == KC - 1)),
                        reads=[wkey, ("aT", kc)], writes=[("ps", bk)])
            sgt = sg[:, (j % 2) * 512:(j % 2 + 1) * 512]
            S.op("act", lambda e, sgt=sgt, pg=pg: e.activation(out=sgt, in_=pg, func=AF.Silu),
                 reads=[("ps", bg)], writes=[("sg", j % 2)])
            S.op("dve", lambda e, sgt=sgt, pu=pu, j=j: e.tensor_tensor(out=hT[:, j, :], in0=pu, in1=sgt, op=ALU.mult),
                 reads=[("ps", bu), ("sg", j % 2)], writes=[("hT", j)])
    emit_resid_prep(S, T, 0, 1)
    for n in range(4):
        for jg in range(4):
            wt, wkey = ws.get(wd[n, jg], 11 * 512)
            wv = wt.rearrange("p (j c) -> p j c", j=11)
            for m in range(4):
                pm = _bank(ps, m)
                for jj in range(11):
                    j = jg * 11 + jj
                    S.op("pe", lambda e, pm=pm, wv=wv, jj=jj, j=j, m=m: e.matmul(
                        pm, hT[:, j, m * 128:(m + 1) * 128], wv[:, jj, :], start=(j == 0), stop=(j == FC - 1)),
                        reads=[wkey, ("hT", j)], writes=[("ps", m)])
        for m in range(4):
            emit_epilogue_block(S, T, m, n, 2)
    emit_ln_inplace(S, T)


def _fm_block(W, col, width=128):
    return W[:, col:col + width].reshape(KC, 128, width).transpose(1, 0, 2)


def _fm16(v):
    return np.ascontiguousarray(v.reshape(-1, 128).T)


def _layer_arrays(inp, l, depth):
    W = inp["w_in"][l]
    fm_tiles = []
    fm_tiles.append(np.stack([_fm_block(W, h * 128) for h in range(4)], axis=1))
    fm_tiles.append(np.stack([_fm_block(W, 512 + h * 128) for h in range(4)], axis=1))
    for hp in range(4):
        blks = []
        for hh in range(2):
            h = hp * 2 + hh
            blks += [_fm_block(W, 3080 + h * 128), _fm_block(W, 4104 + h * 128)]
        fm_tiles.append(np.stack(blks, axis=1))
    w1fm = np.ascontiguousarray(np.stack(fm_tiles)).reshape(6, 128, WSLOT)
    tm_cols = [1024, 1536, 5128, 5640, 2048, 2560, 6152, 6664]
    w1tm = np.ascontiguousarray(np.stack([_fm_block(W, c, 512) for c in tm_cols])).reshape(8, 128, WSLOT)
    w1ifg = np.ascontiguousarray(_fm_block(W, 3072, 8)).reshape(128, KC * 8)
    Wm = inp["w_mod"][l]
    wmod = np.ascontiguousarray(np.stack([np.stack([_fm_block(Wm, t * 512 + b * 128) for b in range(4)], axis=1)
                                          for t in range(24)])).reshape(24, 128, WSLOT)
    Wo = inp["w_out"][l]
    wout = np.ascontiguousarray(np.stack([_fm_block(Wo, n * 512, 512) for n in range(4)])).reshape(4, 128, WSLOT)
    g4 = inp["w_gate"][l].reshape(KC, 128, FC // 2, 2, 128)
    u4 = inp["w_up"][l].reshape(KC, 128, FC // 2, 2, 128)
    gu = np.stack([g4, u4], axis=0)
    wgu = np.ascontiguousarray(gu.transpose(3, 2, 4, 0, 1, 5)).reshape(FC // 2, 128, WSLOT)
    wd = np.ascontiguousarray(inp["w_down"][l].reshape(4, 11, 128, 4, 512).transpose(3, 0, 2, 1, 4)).reshape(4, 4, 128, 11 * 512)
    vecs = np.zeros((128, V_IN), np.float32)
    vecs[:, V_BMOD:V_BMOD + 96] = _fm16(inp["b_mod"][l])
    if l == 0:
        vecs[:, V_LNPG:V_LNPG + 16] = np.ones((128, 16), np.float32)
    else:
        vecs[:, V_LNPG:V_LNPG + 16] = _fm16(inp["ln2_g"][l - 1])
        vecs[:, V_LNPB:V_LNPB + 16] = _fm16(inp["ln2_b"][l - 1])
    vecs[:, V_LN1G:V_LN1G + 16] = _fm16(inp["ln1_g"][l])
    vecs[:, V_LN1B:V_LN1B + 16] = _fm16(inp["ln1_b"][l])
    vecs[:, V_CONVW:V_CONVW + 32] = inp["conv_w"][l].reshape(4, 8, 128).transpose(2, 1, 0).reshape(128, 32)
    vecs[:, V_CONVB:V_CONVB + 8] = _fm16(inp["conv_b"][l])
    vecs[:, V_LBL:V_LBL + 8 * depth] = inp["lb_logits"].reshape(depth, 8, 128).transpose(2, 1, 0).reshape(128, 8 * depth)
    nrmw = _fm16(np.concatenate([inp["mlstm_norm_w"][l], inp["hgrn_norm_w"][l]]))
    vecs[:, V_NRMW:V_NRMW + 16] = nrmw
    grow = np.zeros((1, 8 + DEPTH), np.float32)
    grow[0, 0:4] = inp["b_igate"][l]
    grow[0, 4:8] = inp["b_fgate"][l]
    for i in range(1, l + 1):
        grow[0, 8 + i] = np.float32(1.0)
    return dict(w1fm=w1fm, w1tm=w1tm, w1ifg=w1ifg, wmod=wmod, wout=wout, wgu=wgu, wd=wd, vecs=vecs, grow=grow, nrmw=nrmw)


def build_fused(TOK, depth):
    NT = TOK // TT
    nc = bass.Bass("TRN2", target_bir_lowering=False)

    def din(name, shape, dt=F32):
        return nc.dram_tensor(name, shape, dt, kind="ExternalInput").ap()

    def dint(name, shape, dt=F32):
        return nc.dram_tensor(name, shape, dt, kind="Internal").ap()
    g = {}
    x_in = din("x_in", [TOK, D])
    g["halo0"] = din("halo0", [3, D])
    g["hflag"] = din("hflag", [128, 1])
    g["c_fm"] = din("c_fm", [128, KC])
    g["sel"] = din("sel", [128, 4])
    g["selprev"] = din("selprev", [128, 4])
    g["fin"] = din("fin", [2, D])
    g["identd"] = din("identd", [128, 128])
    g["trid"] = din("trid", [128, 128], U8)
    wmod = din("wmod", [depth * 24, 128, WSLOT])
    w1fm = din("w1fm", [depth * 6, 128, WSLOT])
    w1tm = din("w1tm", [depth * 8, 128, WSLOT])
    w1ifg = din("w1ifg", [depth * 128, KC * 8])
    wout = din("wout", [depth * 4, 128, WSLOT])
    wgu = din("wgu", [depth * (FC // 2), 128, WSLOT])
    wd = din("wd", [depth * 4, 4, 128, 11 * 512])
    vecs = din("vecs", [depth * 128, V_IN])
    grow = din("grow", [depth, 8 + DEPTH])
    g["OUT"] = nc.dram_tensor("OUT", [TOK, D], F32, kind="ExternalOutput").ap()
    g["XN"] = dint("XN", [TOK, D])
    g["OH"] = dint("OH", [TOK, 1024])
    g["OM"] = dint("OM", [TOK, 4 * MW])
    g["QH"] = dint("QH", [1024, TOK], BF16)
    g["QM"] = dint("QM", [512, TOK], BF16)
    g["EM"] = dint("EM", [TOK, 4])
    g["GT"] = dint("GT", [TOK, 2048], BF16)
    g["ROWS"] = dint("ROWS", [6, D])
    g["STATE_SRC"] = dint("STATE_SRC", [NPC, 128, 256])
    g["STATES"] = dint("STATES", [NPC, 4 * 128, 256])
    g["HALO_SRC"] = dint("HALO_SRC", [128, 256])
    g["HALO_DST"] = dint("HALO_DST", [4 * 128, 256])

    with contextlib.ExitStack() as es:
        def sb(name, shape, dt=F32):
            return es.enter_context(nc.sbuf_tensor(name, shape, dt))
        T = {}
        X = T["X"] = sb("X", [128, 4, D])
        T["aT"] = sb("aT", [128, KC, TT], BF16)
        wbuf = sb("wbuf", [128, NWS * WSLOT], BF16)
        T["ident"] = g["ident"] = sb("ident", [128, 128])
        g["identb"] = sb("identb", [128, 128], BF16)
        g["tri"] = sb("tri", [128, 128], U8)
        g["utri"] = sb("utri", [128, 128])
        g["ones"] = sb("ones", [128, 128])
        g["m0"] = sb("m0", [128, TT])
        T["vf"] = sb("vf", [128, V_TOT])
        g["cact"] = sb("cact", [128, KC])
        g["cactb"] = sb("cactb", [128, KC], BF16)
        g["hfl"] = sb("hfl", [128, 1])
        g["selt"] = sb("selt", [128, 4])
        g["selp"] = sb("selp", [128, 4])
        g["lnsc"] = sb("lnsc", [128, 1])
        T["epsb"] = g["epsb"] = sb("epsb", [128, 1])
        g["A_sb4"] = sb("A_sb4", [128, 2, 4, 128], BF16)
        g["A_h8"] = sb("A_h8", [128, 2, 4, 64], BF16)
        T["st"] = sb("st", [128, 4, 4, 6])
        T["mv"] = sb("mv", [128, 4, 8])
        fsc = sb("fsc", [128, 1])
        PAR = 41216
        par = sb("par", [128, PAR], BF16)
        T["ps"] = es.enter_context(nc.psum_tensor("ps", [128, 8 * 512], F32))
        g["hx"] = X[0:3, 0, :]
        g["hx2"] = X[0:3, 1, :]
        off = [0]

        def cv(n_bf16, dt, shape_str=None, **kw):
            a = par[:, off[0]:off[0] + n_bf16]
            off[0] += (n_bf16 + 15) // 16 * 16
            if dt == F32:
                a = a.bitcast(F32)
            if shape_str:
                a = a.rearrange(shape_str, **kw)
            return a
        g["aTh"] = cv(48, BF16, "p (k t) -> p k t", t=3)
        g["hist"] = cv(48, F32, "p (k t) -> p k t", t=3)
        g["ext"] = cv(2 * (TT + 4), F32)
        g["t1"] = cv(2 * TT, F32)
        g["qkT"] = cv(8 * TT, BF16, "p (k t) -> p k t", t=TT)
        g["wifg"] = cv(KC * 8, BF16)
        g["gsb"] = cv(64, F32, "p (m c) -> p m c", c=8)
        for nm in ("l1", "nb", "nbL", "wv_", "eo", "ec", "eL", "ctmp"):
            g[nm] = cv(32, F32, "p (m c) -> p m c", c=4)
        g["carry"] = cv(8, F32)
        g["gbias"] = cv(2 * (8 + DEPTH), F32)
        g["vaug"] = cv(16 * MW, BF16, "p (m h c) -> p m h c", m=4, h=4)
        g["ktok4"] = cv(1024, BF16, "p (a m c) -> p a m c", a=2, m=4)
        g["C32"] = cv(8 * MW, F32, "p (h c) -> p h c", h=4)
        g["Cbf4"] = cv(8 * MW, BF16, "p (a m c) -> p a m c", a=2, m=4)
        g["dCs"] = cv(8 * MW, F32, "p (m c) -> p m c", m=4)
        g["vH"] = cv(4096, BF16, "p (m c) -> p m c", m=4)
        for nm in ("sig", "kk", "bcs", "qs", "ebuf"):
            g[nm] = cv(2 * TT, F32)
        g["qdec"] = cv(TT, BF16)
        g["q_in2"] = cv(2 * TT, BF16, "p (a t) -> p a t", a=2)
        g["k_in2"] = cv(2 * TT, BF16, "p (a t) -> p a t", a=2)
        g["hsc2"] = cv(256, F32, "p (a c s) -> p a c s", a=2, s=8)
        g["hcarry"] = cv(16, F32)
        g["S32"] = cv(2048, F32, "p (h c) -> p h c", h=8)
        g["Sp8"] = cv(2048, BF16, "p (a c d) -> p a c d", a=2, c=8)
        g["gts"] = cv(2 * TT, BF16, "p (a c) -> p a c", a=2)
        g["rowsb"] = cv(2 * 768, F32)
        g["OMs"] = cv(2 * 2 * MW, F32, "p (m c) -> p m c", m=2)
        print("P1 arena", off[0])
        assert off[0] <= PAR, off[0]
        off[0] = 0
        g["arena"] = cv(FC * TT, BF16)
        T["hT"] = g["arena"].rearrange("p (j t) -> p j t", t=TT)
        T["bc"] = cv(2 * 3 * D, F32, "p (i d) -> p i d", i=3)
        T["sg"] = cv(2 * 1024, F32)
        T["tmp"] = cv(2 * 1024, F32)
        g["Sst"] = cv(1024, BF16, "p (h c) -> p h c", h=8)
        g["Cst"] = cv(4 * MW, BF16, "p (h c) -> p h c", h=4)
        g["EMt"] = cv(32, F32, "p (m c) -> p m c", c=4)
        g["hst"] = cv(64, F32, "p (h c) -> p h c", c=4)
        g["mst"] = cv(128, F32, "p (h c) -> p h c", c=16)
        print("P2 arena", off[0])
        assert off[0] <= PAR, off[0]
        rec = None
        for pas in range(2):
            S = Sched(nc)
            ws = WStream(S, wbuf, record=rec)
            _emit_prologue(S, T, g)
            for l in range(depth):
                g["xsrc"] = x_in if l == 0 else g["XN"]
                g["wmod"] = wmod[l * 24:(l + 1) * 24]
                g["w1fm"] = w1fm[l * 6:(l + 1) * 6]
                g["w1tm"] = w1tm[l * 8:(l + 1) * 8]
                g["w1ifg"] = w1ifg[l * 128:(l + 1) * 128, :]
                g["wout"] = wout[l * 4:(l + 1) * 4]
                g["wgu"] = wgu[l * (FC // 2):(l + 1) * (FC // 2)]
                g["wd"] = wd[l * 4:(l + 1) * 4]
                g["vecs"] = vecs[l * 128:(l + 1) * 128, :]
                g["grow"] = grow[l:l + 1, :]
                _emit_p1(S, ws, nc, T, g, NT, l)
                S.fence(fsc[:])
                _emit_p2(S, ws, nc, T, g, NT, l, l == depth - 1)
                S.fence(fsc[:])
            S.barrier_all("sp")
            rec = ws.seq
        S.run()
    return nc


def _fm_block(W, col, width=128):
    return W[:, col:col + width].reshape(KC, 128, width).transpose(1, 0, 2)


def _fm16(v):
    return np.ascontiguousarray(v.reshape(-1, 128).T)


def _layer_arrays(inp, l, depth):
    W = inp["w_in"][l]
    fm_tiles = []
    fm_tiles.append(np.stack([_fm_block(W, h * 128) for h in range(4)], axis=1))
    fm_tiles.append(np.stack([_fm_block(W, 512 + h * 128) for h in range(4)], axis=1))
    for hp in range(4):
        blks = []
        for hh in range(2):
            h = hp * 2 + hh
            blks += [_fm_block(W, 3080 + h * 128), _fm_block(W, 4104 + h * 128)]
        fm_tiles.append(np.stack(blks, axis=1))
    w1fm = np.ascontiguousarray(np.stack(fm_tiles)).reshape(6, 128, WSLOT)
    tm_cols = [1024, 1536, 5128, 5640, 2048, 2560, 6152, 6664]
    w1tm = np.ascontiguousarray(np.stack([_fm_block(W, c, 512) for c in tm_cols])).reshape(8, 128, WSLOT)
    w1ifg = np.ascontiguousarray(_fm_block(W, 3072, 8)).reshape(128, KC * 8)
    Wm = inp["w_mod"][l]
    wmod = np.ascontiguousarray(np.stack([np.stack([_fm_block(Wm, t * 512 + b * 128) for b in range(4)], axis=1)
                                          for t in range(24)])).reshape(24, 128, WSLOT)
    Wo = inp["w_out"][l]
    wout = np.ascontiguousarray(np.stack([_fm_block(Wo, n * 512, 512) for n in range(4)])).reshape(4, 128, WSLOT)
    g4 = inp["w_gate"][l].reshape(KC, 128, FC // 2, 2, 128)
    u4 = inp["w_up"][l].reshape(KC, 128, FC // 2, 2, 128)
    gu = np.stack([g4, u4], axis=0)
    wgu = np.ascontiguousarray(gu.transpose(3, 2, 4, 0, 1, 5)).reshape(FC // 2, 128, WSLOT)
    wd = np.ascontiguousarray(inp["w_down"][l].reshape(4, 11, 128, 4, 512).transpose(3, 0, 2, 1, 4)).reshape(4, 4, 128, 11 * 512)
    vecs = np.zeros((128, V_IN), np.float32)
    vecs[:, V_BMOD:V_BMOD + 96] = _fm16(inp["b_mod"][l])
    if l == 0:
        vecs[:, V_LNPG:V_LNPG + 16] = np.ones((128, 16), np.float32)
    else:
        vecs[:, V_LNPG:V_LNPG + 16] = _fm16(inp["ln2_g"][l - 1])
        vecs[:, V_LNPB:V_LNPB + 16] = _fm16(inp["ln2_b"][l - 1])
    vecs[:, V_LN1G:V_LN1G + 16] = _fm16(inp["ln1_g"][l])
    vecs[:, V_LN1B:V_LN1B + 16] = _fm16(inp["ln1_b"][l])
    vecs[:, V_CONVW:V_CONVW + 32] = inp["conv_w"][l].reshape(4, 8, 128).transpose(2, 1, 0).reshape(128, 32)
    vecs[:, V_CONVB:V_CONVB + 8] = _fm16(inp["conv_b"][l])
    vecs[:, V_LBL:V_LBL + 8 * depth] = inp["lb_logits"].reshape(depth, 8, 128).transpose(2, 1, 0).reshape(128, 8 * depth)
    vecs[:, V_NRMW:V_NRMW + 16] = _fm16(np.concatenate([inp["mlstm_norm_w"][l], inp["hgrn_norm_w"][l]]))
    grow = np.zeros((1, 8 + DEPTH), np.float32)
    grow[0, 0:4] = inp["b_igate"][l]
    grow[0, 4:8] = inp["b_fgate"][l]
    for i in range(1, l + 1):
        grow[0, 8 + i] = np.float32(1.0)
    return dict(w1fm=w1fm, w1tm=w1tm, w1ifg=w1ifg, wmod=wmod, wout=wout, wgu=wgu, wd=wd, vecs=vecs, grow=grow)


_PROGS = {}


def kernel(**inputs):
    inp = {k: np.asarray(v) for k, v in inputs.items()}
    x = inp["x"]
    B, T_, _ = x.shape
    depth = inp["w_in"].shape[0]
    SEG = NCORE // B
    TOK = T_ // SEG
    key = (TOK, depth, DEPTH)
    if key not in _PROGS:
        _PROGS[key] = build_fused(TOK, depth)
    nc = _PROGS[key]
    LA = [_layer_arrays(inp, l, depth) for l in range(depth)]
    shared = {k: np.ascontiguousarray(np.concatenate([LA[l][k] for l in range(depth)], axis=0))
              for k in ("wmod", "w1fm", "w1tm", "w1ifg", "wout", "wgu", "wd", "vecs", "grow")}
    del LA
    shared["identd"] = np.eye(128, dtype=np.float32)
    shared["trid"] = np.triu(np.ones((128, 128), np.uint8))
    shared["fin"] = np.ascontiguousarray(np.stack([inp["ln2_g"][depth - 1], inp["ln2_b"][depth - 1]]).astype(np.float32))
    in_maps = []
    for i in range(NCORE):
        b, sgi = i // SEG, i % SEG
        sel = np.zeros((128, 4), np.float32)
        sel[:, sgi] = 1.0
        selp = np.zeros((128, 4), np.float32)
        if sgi > 0:
            selp[:, sgi - 1] = 1.0
        halo = np.zeros((3, D), np.float32) if sgi == 0 else np.ascontiguousarray(x[b, sgi * TOK - 3:sgi * TOK, :])
        m = dict(shared)
        m.update(x_in=np.ascontiguousarray(x[b, sgi * TOK:(sgi + 1) * TOK, :]), halo0=halo,
                 hflag=np.full((128, 1), 0.0 if sgi == 0 else 1.0, np.float32), c_fm=_fm16(inp["c"][b]), sel=sel, selprev=selp)
        in_maps.append(m)
    res = run_bass_kernel_spmd(nc, in_maps, core_ids=list(range(NCORE))).results
    y = np.empty((B, T_, D), np.float32)
    for i in range(NCORE):
        y[i // SEG, (i % SEG) * TOK:(i % SEG + 1) * TOK, :] = res[i]["OUT"]
    return y


_DEBUG = None
```

```python
import contextlib
import math
import numpy as np
import ml_dtypes
import concourse.bass as bass
import concourse.mybir as mybir
from concourse.bass_utils import run_bass_kernel_spmd

F32 = mybir.dt.float32
BF16 = mybir.dt.bfloat16
U8 = mybir.dt.uint8
AF = mybir.ActivationFunctionType
ALU = mybir.AluOpType
AX = mybir.AxisListType

D = 2048
DFF = 5632
KC = D // 128
FC = DFF // 128
TT = 512
EPS = 1e-5
DEPTH = 4
NCORE = 8
ALPHA = (2 * DEPTH) ** 0.25
WSLOT = 8192
NWS = 3
MW = 260
NST = 2304
NPC = NST // 256
LNSC = math.log(128.0 ** -0.5)

NO_CC = False
PIPE_ML = True
SKIP_ML = False
SKIP_HG = False
PIPE_HG = True
ENGS = ("pe", "act", "dve", "pool", "sp")
SEM_EPOCH = 30000
NDS = 20


class _Op:
    __slots__ = ("eng", "fn", "waits", "is_dma", "needed", "sem", "val")

    def __init__(self, eng, fn, is_dma):
        self.eng = eng
        self.fn = fn
        self.is_dma = is_dma
        self.waits = []
        self.needed = False
        self.sem = None
        self.val = None


class Sched:
    def __init__(self, nc):
        self.nc = nc
        self.q = {e: [] for e in ENGS}
        self.last_w = {}
        self.readers = {}
        self.all_ops = []

    def op(self, eng, fn, reads=(), writes=(), dma=False, nofence=False):
        o = _Op(eng, fn, dma)
        deps = []
        seen = set()
        if not nofence:
            reads = list(reads) + ["__fence__"]

        def add(d):
            if d is None or id(d) in seen:
                return
            seen.add(id(d))
            deps.append(d)
        for k in reads:
            add(self.last_w.get(k))
        for k in writes:
            add(self.last_w.get(k))
            for r in self.readers.get(k, ()):
                add(r)
        for d in deps:
            if d.eng == "pe" and eng == "pe" and not d.is_dma and not dma:
                continue
            d.needed = True
            o.waits.append(d)
        for k in reads:
            self.readers.setdefault(k, []).append(o)
        for k in writes:
            self.last_w[k] = o
            self.readers[k] = []
        self.q[eng].append(o)
        self.all_ops.append(o)
        return o

    def fence(self, scratch):
        self.op("dve", lambda e: e.memset(scratch, 0.0), writes=["__fence__"], nofence=True)

    def barrier_all(self, eng="sp"):
        o = _Op(eng, None, False)
        for d in self.all_ops:
            if d.is_dma:
                d.needed = True
                o.waits.append(d)
        self.q[eng].append(o)
        self.all_ops.append(o)

    def run(self):
        nc = self.nc
        cnt = {e: 0 for e in ENGS}
        di = {e: 0 for e in ENGS}
        dlast = {}
        dcount = {}
        for o in self.all_ops:
            if not o.needed:
                continue
            e = o.eng
            if o.is_dma == "cc":
                o.sem, o.val = f"cc_{di[e]}", 1
                di[e] += 1
            elif o.is_dma:
                name = f"d_{e}_{di[e] % NDS}"
                di[e] += 1
                prev = dlast.get(name)
                if prev is not None and prev not in o.waits:
                    o.waits.append(prev)
                dcount[name] = dcount.get(name, 0) + 16
                o.sem, o.val = name, dcount[name]
                dlast[name] = o
            else:
                n = cnt[e]
                o.sem = f"c_{e}_{n // SEM_EPOCH}"
                o.val = n % SEM_EPOCH + 1
                cnt[e] = n + 1
        names = sorted({o.sem for o in self.all_ops if o.sem is not None})
        for o in self.all_ops:
            best = {}
            for d in o.waits:
                if d.sem not in best or d.val > best[d.sem].val:
                    best[d.sem] = d
            o.waits = list(best.values())
        with contextlib.ExitStack() as es:
            S = {n: es.enter_context(nc.semaphore(n)) for n in names}
            block = es.enter_context(nc.Block())

            def body(eng_name):
                def _f(eng):
                    known = {}
                    for o in self.q[eng_name]:
                        for d in o.waits:
                            if known.get(d.sem, 0) >= d.val:
                                continue
                            eng.wait_ge(S[d.sem], d.val)
                            known[d.sem] = d.val
                        if o.fn is None:
                            continue
                        ins = o.fn(eng)
                        if o.needed:
                            ins.then_inc(S[o.sem], 16 if (o.is_dma and o.is_dma != "cc") else 1)
                return _f
            block.tensor(body("pe"))
            block.scalar(body("act"))
            block.vector(body("dve"))
            block.gpsimd(body("pool"))
            block.sync(body("sp"))
        return len(names)


class WStream:
    def __init__(self, S, buf, record=None):
        self.S = S
        self.buf = buf
        self.seq = [] if record is None else record
        self.recording = record is None
        self.k = 0
        self.issued = 0
        self.ahead = NWS

    def _issue(self, j):
        ap, n = self.seq[j]
        slot = j % NWS
        dst = self.buf[:, slot * WSLOT: slot * WSLOT + n]
        self.S.op("pool", lambda e, dst=dst, ap=ap: e.dma_start(out=dst, in_=ap),
                  writes=[("w", slot)], dma=True, nofence=True)

    def get(self, ap, n):
        if self.recording:
            self.seq.append((ap, n))
            return self.buf[:, 0:n], ("w", 0)
        k = self.k
        while self.issued < min(len(self.seq), k + self.ahead):
            self._issue(self.issued)
            self.issued += 1
        self.k += 1
        slot = k % NWS
        return self.buf[:, slot * WSLOT: slot * WSLOT + n], ("w", slot)


def _bank(ps, b):
    return ps[:, b * 512:(b + 1) * 512]


def emit_transpose_affine(S, T, src_key, gcol, bcol, extra_reads=()):
    X, aT, ps, ident, vf = T["X"], T["aT"], T["ps"], T["ident"], T["vf"]
    for kc in range(KC):
        bank = 4 + (kc % 2)
        pst = _bank(ps, bank)
        for m in range(4):
            S.op("pe", lambda e, pst=pst, m=m, kc=kc: e.transpose(
                pst[:, m * 128:(m + 1) * 128], X[:, m, kc * 128:(kc + 1) * 128], ident[:]),
                reads=[src_key, "ident"], writes=[("ps", bank)])
        S.op("act", lambda e, pst=pst, kc=kc: e.activation(
            out=aT[:, kc, :], in_=pst, func=AF.Identity,
            scale=vf[:, gcol + kc:gcol + kc + 1], bias=vf[:, bcol + kc:bcol + kc + 1]),
            reads=[("ps", bank), "vf", *extra_reads], writes=[("aT", kc)])


def emit_resid_prep(S, T, gi, bi):
    X, bc = T["X"], T["bc"]
    for m in range(4):
        S.op("pool", lambda e, m=m: e.tensor_tensor(out=X[:, m, :], in0=X[:, m, :], in1=bc[:, gi, :], op=ALU.mult),
             reads=[("bc", gi)], writes=["X"])
        S.op("pool", lambda e, m=m: e.tensor_tensor(out=X[:, m, :], in0=X[:, m, :], in1=bc[:, bi, :], op=ALU.add),
             reads=[("bc", bi)], writes=["X"])


def emit_epilogue_block(S, T, m, n, gy):
    X, bc, ps, tmp = T["X"], T["bc"], T["ps"], T["tmp"]
    pm = _bank(ps, m)
    tq = tmp[:, (m % 2) * 512:(m % 2 + 1) * 512]
    S.op("dve", lambda e: e.tensor_tensor(out=tq, in0=pm, in1=bc[:, gy, n * 512:(n + 1) * 512], op=ALU.mult),
         reads=[("ps", m), ("bc", gy)], writes=[("tmp", m % 2)])
    S.op("pool", lambda e: e.tensor_tensor(out=X[:, m, n * 512:(n + 1) * 512],
                                           in0=X[:, m, n * 512:(n + 1) * 512], in1=tq, op=ALU.add),
         reads=[("tmp", m % 2)], writes=["X"])


def emit_ln_inplace(S, T):
    X, st, mv, epsb = T["X"], T["st"], T["mv"], T["epsb"]
    for m in range(4):
        for i in range(4):
            S.op("dve", lambda e, m=m, i=i: e.bn_stats(out=st[:, m, i, :], in_=X[:, m, i * 512:(i + 1) * 512]),
                 reads=["X"], writes=[("st", m)])
        S.op("dve", lambda e, m=m: e.bn_aggr(out=mv[:, m, 0:2], in_=st[:, m, :, :].rearrange("p a b -> p (a b)")),
             reads=[("st", m)], writes=[("mv", m)])
        S.op("act", lambda e, m=m: e.activation(out=mv[:, m, 2:3], in_=mv[:, m, 1:2], func=AF.Sqrt, bias=epsb[:, 0:1]),
             reads=[("mv", m), "epsb"], writes=[("mv", m)])
        S.op("dve", lambda e, m=m: e.reciprocal(out=mv[:, m, 3:4], in_=mv[:, m, 2:3]),
             reads=[("mv", m)], writes=[("mv", m)])
        S.op("dve", lambda e, m=m: e.scalar_tensor_tensor(
            out=mv[:, m, 4:5], in0=mv[:, m, 0:1], scalar=-1.0, in1=mv[:, m, 3:4], op0=ALU.mult, op1=ALU.mult),
            reads=[("mv", m)], writes=[("mv", m)])
        S.op("act", lambda e, m=m: e.activation(out=X[:, m, :], in_=X[:, m, :], func=AF.Identity,
                                                scale=mv[:, m, 3:4], bias=mv[:, m, 4:5]),
             reads=[("mv", m)], writes=["X"])


def emit_ffn(S, ws, T, wgu, wd, gy):
    aT, hT, ps, sg = T["aT"], T["hT"], T["ps"], T["sg"]
    for j2 in range(FC // 2):
        wt, wkey = ws.get(wgu[j2], 2 * 2 * KC * 128)
        wv = wt.rearrange("p (b g k c) -> p b g k c", b=2, g=2, k=KC)
        for b in range(2):
            j = j2 * 2 + b
            bg = 4 + (j % 2) * 2
            bu = bg + 1
            pg, pu = _bank(ps, bg), _bank(ps, bu)
            for g, pp, bk in ((0, pg, bg), (1, pu, bu)):
                for kc in range(KC):
                    S.op("pe", lambda e, pp=pp, wv=wv, b=b, g=g, kc=kc: e.matmul(
                        pp, wv[:, b, g, kc, :], aT[:, kc, :], start=(kc == 0), stop=(kc == KC - 1)),
                        reads=[wkey, ("aT", kc)], writes=[("ps", bk)])
            sgt = sg[:, (j % 2) * 512:(j % 2 + 1) * 512]
            S.op("act", lambda e, sgt=sgt, pg=pg: e.activation(out=sgt, in_=pg, func=AF.Silu),
                 reads=[("ps", bg)], writes=[("sg", j % 2)])
            S.op("dve", lambda e, sgt=sgt, pu=pu, j=j: e.tensor_tensor(out=hT[:, j, :], in0=pu, in1=sgt, op=ALU.mult),
                 reads=[("ps", bu), ("sg", j % 2)], writes=[("hT", j)])
    emit_resid_prep(S, T, 3, 4)
    for n in range(4):
        for jg in range(4):
            wt, wkey = ws.get(wd[n, jg], 11 * 512)
            wv = wt.rearrange("p (j c) -> p j c", j=11)
            for m in range(4):
                pm = _bank(ps, m)
                for jj in range(11):
                    j = jg * 11 + jj
                    S.op("pe", lambda e, pm=pm, wv=wv, jj=jj, j=j, m=m: e.matmul(
                        pm, hT[:, j, m * 128:(m + 1) * 128], wv[:, jj, :], start=(j == 0), stop=(j == FC - 1)),
                        reads=[wkey, ("hT", j)], writes=[("ps", m)])
        for m in range(4):
            emit_epilogue_block(S, T, m, n, gy)
    emit_ln_inplace(S, T)


V_BMOD = 0
V_LNPG = 96
V_LNPB = 112
V_LN1G = 128
V_LN1B = 144
V_CONVW = 160
V_CONVB = 192
V_LBL = 200
V_NRMW = 232
V_IN = 248
V_MOD = 256
V_GU1 = 352
V_BU1 = 368
V_GU2 = 384
V_BU2 = 400
V_ROWS = 416
V_LB = 512
V_OML = 520
V_TOT = 544


def emit_mod_tiles(S, ws, T, g, wmod, tiles, bank):
    psm = _bank(T["ps"], bank)
    cactb = g["cactb"]
    tiles = list(tiles)
    if not tiles:
        return
    for t in tiles:
        wt, wkey = ws.get(wmod[t], WSLOT)
        wv = wt.rearrange("p (b k c) -> p b k c", b=4, k=KC)
        for b in range(4):
            j = t * 4 + b
            for kc in range(KC):
                S.op("pe", lambda e, wv=wv, b=b, kc=kc, j=j: e.matmul(
                    psm[:, j:j + 1], wv[:, b, kc, :], cactb[:, kc:kc + 1], start=(kc == 0), stop=(kc == KC - 1)),
                    reads=[wkey, "cactb"], writes=[("ps", bank)])
    j0, j1 = tiles[0] * 4, tiles[-1] * 4 + 4
    S.op("dve", lambda e: e.tensor_copy(out=g["modacc"][:, j0:j1], in_=psm[:, j0:j1]), reads=[("ps", bank)], writes=["modacc"])


def _emit_prologue(S, T, g):
    ident, identb, tri, utri, ones, m0 = g["ident"], g["identb"], g["tri"], g["utri"], g["ones"], g["m0"]

    def dma_in(dst, src, key, eng="sp"):
        S.op(eng, lambda e: e.dma_start(out=dst, in_=src), writes=[key], dma=True)
    dma_in(ident[:], g["identd"], "ident")
    dma_in(identb[:], g["identd"], "identb", eng="pool")
    dma_in(tri[:], g["trid"], "tri")
    dma_in(g["cact"][:], g["c_fm"], "cact")
    dma_in(g["hfl"][:], g["hflag"], "hfl")
    dma_in(g["selt"][:], g["sel"], "selt")
    dma_in(g["selp"][:], g["selprev"], "selp")
    S.op("dve", lambda e: e.tensor_copy(out=utri[:], in_=tri[:]), reads=["tri"], writes=["utri"])
    S.op("dve", lambda e: e.memset(ones[:], 1.0), writes=["ones"])
    S.op("dve", lambda e: e.memset(g["lnsc"][:], LNSC), writes=["lnsc"])
    S.op("dve", lambda e: e.memset(g["epsb"][:], EPS), writes=["epsb"])
    S.op("dve", lambda e: e.memset(m0[:], 1.0), writes=["m0"])
    S.op("dve", lambda e: e.memset(m0[:].rearrange("p (c t) -> p c t", t=64)[:, :, 0:1], 0.0), writes=["m0"])
    S.op("dve", lambda e: e.memset(g["A_sb4"][:], 0.0), writes=[("A_sb4", p, m) for p in range(2) for m in range(4)])
    S.op("dve", lambda e: e.memset(g["A_h8"][:], 0.0), writes=[("A_h8", p, c) for p in range(2) for c in range(8)])
    S.op("act", lambda e: e.activation(out=g["cactb"][:], in_=g["cact"][:], func=AF.Silu), reads=["cact"], writes=["cactb"])


def _emit_p1(S, ws, nc, T, g, NT, l):
    X, aT, ps, vf = T["X"], T["aT"], T["ps"], T["vf"]
    ident, identb, tri, utri, ones, m0 = g["ident"], g["identb"], g["tri"], g["utri"], g["ones"], g["m0"]

    def dma_in(dst, src, key, eng="sp"):
        S.op(eng, lambda e: e.dma_start(out=dst, in_=src), writes=[key], dma=True)
    dma_in(vf[:, 0:V_IN], g["vecs"], "vf")
    dma_in(g["gbias"][:], g["grow"].partition_broadcast(128), "gbias")
    dma_in(g["wifg"][:], g["w1ifg"], "wifg", eng="pool")
    S.op("dve", lambda e: e.memset(g["C32"][:], 0.0), writes=[("C32", h) for h in range(4)])
    S.op("dve", lambda e: e.memset(g["S32"][:], 0.0), writes=[("S32", h) for h in range(8)])
    S.op("dve", lambda e: e.memset(g["carry"][:], 0.0), writes=["carry"])
    S.op("dve", lambda e: e.memset(g["hcarry"][:], 0.0), writes=["hcarry"])
    S.op("dve", lambda e: e.memset(g["vaug"][:], 0.0), writes=[("vaug", m) for m in range(4)])
    if l == 0:
        emit_mod_tiles(S, ws, T, g, g["wmod"], range(24), 7)
    S.op("dve", lambda e: e.tensor_tensor(out=vf[:, V_MOD:V_MOD + 96], in0=g["modacc"][:, 0:96], in1=vf[:, V_BMOD:V_BMOD + 96], op=ALU.add),
         reads=["modacc", "vf"], writes=["vf"])

    def vop(fn):
        S.op("dve", fn, writes=["vf"])
    SH1, SC1, G1, SH2, SC2, G2 = (V_MOD + 16 * i for i in range(6))

    def c(off):
        return vf[:, off:off + 16]
    for (lg, lb_, sc, sh, gu, bu) in ((V_LNPG, V_LNPB, SC1, SH1, V_GU1, V_BU1), (V_LN1G, V_LN1B, SC2, SH2, V_GU2, V_BU2)):
        vop(lambda e, lg=lg, sc=sc, gu=gu: e.scalar_tensor_tensor(out=c(gu), in0=c(sc), scalar=1.0, in1=c(lg), op0=ALU.add, op1=ALU.mult))
        vop(lambda e, lb_=lb_, sc=sc, bu=bu: e.scalar_tensor_tensor(out=c(bu), in0=c(sc), scalar=1.0, in1=c(lb_), op0=ALU.add, op1=ALU.mult))
        vop(lambda e, sh=sh, bu=bu: e.tensor_tensor(out=c(bu), in0=c(bu), in1=c(sh), op=ALU.add))
    for i, (lg, lb_, gg) in enumerate(((V_LNPG, V_LNPB, G1), (V_LN1G, V_LN1B, G2))):
        r0 = V_ROWS + 48 * i
        vop(lambda e, lg=lg, r0=r0: e.tensor_scalar(out=c(r0), in0=c(lg), scalar1=ALPHA, scalar2=None, op0=ALU.mult))
        vop(lambda e, lb_=lb_, r0=r0: e.tensor_scalar(out=c(r0 + 16), in0=c(lb_), scalar1=ALPHA, scalar2=None, op0=ALU.mult))
        vop(lambda e, gg=gg, r0=r0: e.tensor_scalar(out=c(r0 + 32), in0=c(gg), scalar1=1.0, scalar2=None, op0=ALU.add))
    lbl = vf[:, V_LBL:V_LBL + 8 * DEPTH].rearrange("p (h l) -> p h l", l=DEPTH)
    sm = g["hsc2"][:, 0]
    vop(lambda e: e.tensor_reduce(out=sm[:, :, 0], in_=lbl, axis=AX.X, op=ALU.max))
    vop(lambda e: e.tensor_tensor(out=lbl, in0=lbl, in1=sm[:, :, 0:1].broadcast_to([128, 8, DEPTH]), op=ALU.subtract))
    S.op("act", lambda e: e.activation(out=lbl, in_=lbl, func=AF.Exp), reads=["vf"], writes=["vf"])
    vop(lambda e: e.tensor_reduce(out=sm[:, :, 0], in_=lbl, axis=AX.X, op=ALU.add))
    vop(lambda e: e.reciprocal(out=sm[:, :, 1], in_=sm[:, :, 0]))
    S.op("dve", lambda e: e.tensor_tensor(out=lbl, in0=lbl, in1=g["gbias"][:, 8:8 + DEPTH].unsqueeze(1).broadcast_to([128, 8, DEPTH]), op=ALU.mult),
         reads=["gbias"], writes=["vf"])
    vop(lambda e: e.tensor_reduce(out=sm[:, :, 0], in_=lbl, axis=AX.X, op=ALU.add))
    vop(lambda e: e.tensor_tensor(out=vf[:, V_LB:V_LB + 8], in0=sm[:, :, 0], in1=sm[:, :, 1], op=ALU.mult))
    vop(lambda e: e.tensor_scalar(out=vf[:, V_OML:V_OML + 8], in0=vf[:, V_LB:V_LB + 8], scalar1=-1.0, scalar2=1.0, op0=ALU.mult, op1=ALU.add))
    prw = _bank(ps, 5)
    rowsb = g["rowsb"]
    for k in range(6):
        bk = 5 if k < 4 else 4
        dst = _bank(ps, bk)[0:16, (k % 4) * 128:(k % 4 + 1) * 128]
        S.op("pe", lambda e, k=k, dst=dst: e.transpose(dst, vf[:, V_ROWS + 16 * k:V_ROWS + 16 * k + 16], ident[:]),
             reads=["vf", "ident"], writes=[("ps", bk)])
    S.op("act", lambda e: e.activation(out=rowsb[0:16, 0:512], in_=prw[0:16, 0:512], func=AF.Copy), reads=[("ps", 5)], writes=["rowsb"])
    S.op("act", lambda e: e.activation(out=rowsb[0:16, 512:768], in_=_bank(ps, 4)[0:16, 0:256], func=AF.Copy), reads=[("ps", 4)], writes=["rowsb"])
    S.op("sp", lambda e: e.dma_start(out=g["ROWS"].rearrange("k (c p) -> c k p", p=128), in_=rowsb[0:16, :].rearrange("c (k p) -> c k p", p=128)),
         reads=["rowsb"], writes=["ROWS"], dma=True)

    hx, aTh, hfl = g["hx"], g["aTh"], g["hfl"]
    if l == 0:
        dma_in(hx, g["halo0"], "X")
    else:
        hx2, selp = g["hx2"], g["selp"]
        for k in range(4):
            S.op("sp", lambda e, k=k: e.dma_start(out=hx2, in_=g["HALO_DST"][k * 128:k * 128 + 24, :].rearrange("(r c) w -> r (c w)", r=3)), reads=["HALO_DST"], writes=["hx2"], dma=True)
            if k == 0:
                S.op("dve", lambda e: e.tensor_scalar(out=hx, in0=hx2, scalar1=selp[0:3, 0:1], scalar2=None, op0=ALU.mult),
                     reads=["hx2", "selp"], writes=["X"])
            else:
                S.op("dve", lambda e, k=k: e.scalar_tensor_tensor(out=hx, in0=hx2, scalar=selp[0:3, k:k + 1], in1=hx, op0=ALU.mult, op1=ALU.add),
                     reads=["hx2", "selp"], writes=["X"])
    psh = _bank(ps, 6)
    for kc in range(KC):
        S.op("pe", lambda e, kc=kc: e.transpose(psh[:, kc * 4:kc * 4 + 3], hx[:, kc * 128:(kc + 1) * 128], ident[0:3, 0:3]),
             reads=["X", "ident"], writes=[("ps", 6)])
    for kc in range(KC):
        S.op("act", lambda e, kc=kc: e.activation(out=aTh[:, kc, :], in_=psh[:, kc * 4:kc * 4 + 3], func=AF.Identity,
                                                  scale=vf[:, V_GU1 + kc:V_GU1 + kc + 1], bias=vf[:, V_BU1 + kc:V_BU1 + kc + 1]),
             reads=[("ps", 6), "vf"], writes=["aTh"])

    hist, ext, t1, qkT = g["hist"], g["ext"], g["t1"], g["qkT"]
    gsb, l1, nb, nbL, wv_, eo, ec, eL, carry, ctmp = (g[k] for k in ("gsb", "l1", "nb", "nbL", "wv_", "eo", "ec", "eL", "carry", "ctmp"))
    vaug, C32, vH = (g[k] for k in ("vaug", "C32", "vH"))
    sig, kk, bcs, qs, ebuf, qdec = (g[k] for k in ("sig", "kk", "bcs", "qs", "ebuf", "qdec"))
    hcarry, S32, gts, wifg, gbias = (g[k] for k in ("hcarry", "S32", "gts", "wifg", "gbias"))
    OHs = X
    xsrc = g["xsrc"]
    ppi = [0]

    def pp_next():
        b = ppi[0] % 2
        ppi[0] += 1
        return b

    for t in range(NT):
        tok0 = t * TT
        S.op("sp", lambda e, tok0=tok0, xsrc=xsrc: e.dma_start(
            out=X[:], in_=xsrc[tok0:tok0 + TT, :].rearrange("(m p) d -> p m d", p=128)),
            reads=["XN"], writes=["X"], dma=True)
        emit_transpose_affine(S, T, "X", V_GU1, V_BU1)
        pg = _bank(ps, 2)
        wg3 = wifg[:].rearrange("p (k c) -> p k c", c=8)
        for m in range(4):
            for kc in range(KC):
                S.op("pe", lambda e, m=m, kc=kc: e.matmul(pg[:, m * 8:(m + 1) * 8], aT[:, kc, m * 128:(m + 1) * 128], wg3[:, kc, :],
                                                          start=(kc == 0), stop=(kc == KC - 1)),
                     reads=[("aT", kc), "wifg"], writes=[("ps", 2)])
        S.op("dve", lambda e: e.tensor_tensor(out=gsb[:], in0=pg[:, 0:32].rearrange("p (m c) -> p m c", c=8),
                                              in1=gbias[:, 0:8].unsqueeze(1).broadcast_to([128, 4, 8]), op=ALU.add),
             reads=[("ps", 2), "gbias"], writes=["gsb"])
        S.op("act", lambda e: e.activation(out=l1[:], in_=gsb[:, :, 4:8], func=AF.Exp, scale=-1.0), reads=["gsb"], writes=["l1"])
        S.op("act", lambda e: e.activation(out=l1[:], in_=l1[:], func=AF.Ln, bias=1.0), writes=["l1"])
        pc = _bank(ps, 2)
        for m in range(4):
            S.op("pe", lambda e, m=m: e.matmul(pc[:, 64 + m * 4:64 + m * 4 + 4], utri[:], l1[:, m, :], start=True, stop=True),
                 reads=["utri", "l1"], writes=[("ps", 2)])
            S.op("pe", lambda e, m=m: e.matmul(pc[:, 96 + m * 4:96 + m * 4 + 4], ones[:], l1[:, m, :], start=True, stop=True),
                 reads=["ones", "l1"], writes=[("ps", 2)])
        S.op("dve", lambda e: e.tensor_copy(out=nb[:], in_=pc[:, 64:80].rearrange("p (m c) -> p m c", c=4)), reads=[("ps", 2)], writes=["nb"])
        S.op("dve", lambda e: e.tensor_copy(out=nbL[:], in_=pc[:, 96:112].rearrange("p (m c) -> p m c", c=4)), reads=[("ps", 2)], writes=["nbL"])
        S.op("dve", lambda e: e.tensor_tensor(out=ctmp[:], in0=gsb[:, :, 0:4], in1=nb[:], op=ALU.add), reads=["gsb", "nb"], writes=["ctmp"])
        S.op("act", lambda e: e.activation(out=wv_[:], in_=ctmp[:], func=AF.Exp), reads=["ctmp"], writes=["wv"])
        S.op("act", lambda e: e.activation(out=eo[:], in_=nb[:], func=AF.Exp, scale=-1.0, bias=g["lnsc"][:, 0:1]), reads=["nb", "lnsc"], writes=["eo"])
        S.op("act", lambda e: e.activation(out=eL[:], in_=nbL[:], func=AF.Exp, scale=-1.0), reads=["nbL"], writes=["eL"])
        for m in range(4):
            S.op("act", lambda e, m=m: e.activation(out=ctmp[:, m, :], in_=carry[:], func=AF.Exp), reads=["carry"], writes=["ctmp"])
            S.op("dve", lambda e, m=m: e.tensor_tensor(out=ec[:, m, :], in0=eo[:, m, :], in1=ctmp[:, m, :], op=ALU.mult),
                 reads=["eo", "ctmp"], writes=["ec"])
            S.op("dve", lambda e, m=m: e.tensor_tensor(out=carry[:], in0=carry[:], in1=nbL[:, m, :], op=ALU.subtract),
                 reads=["nbL"], writes=["carry"])
        S.op("sp", lambda e, tok0=tok0: e.dma_start(out=g["EM"][tok0:tok0 + TT, :].rearrange("(m p) c -> p m c", p=128), in_=ec[:]),
             reads=["ec"], writes=[("EM", t)], dma=True)
        for qk in range(2):
            wt, wkey = ws.get(g["w1fm"][qk], WSLOT)
            wv4 = wt.rearrange("p (b k c) -> p b k c", b=4, k=KC)
            for b in range(4):
                blk = qk * 4 + b
                bk = pp_next()
                pq = _bank(ps, bk)
                for kc in range(KC):
                    S.op("pe", lambda e, pq=pq, wv4=wv4, b=b, kc=kc: e.matmul(pq, wv4[:, b, kc, :], aT[:, kc, :], start=(kc == 0), stop=(kc == KC - 1)),
                         reads=[wkey, ("aT", kc)], writes=[("ps", bk)])
                if t == 0:
                    ph = _bank(ps, 6)
                    for kc in range(KC):
                        S.op("pe", lambda e, ph=ph, wv4=wv4, b=b, kc=kc, blk=blk: e.matmul(ph[:, 64 + blk * 4:64 + blk * 4 + 3], wv4[:, b, kc, :], aTh[:, kc, :],
                                                                                       start=(kc == 0), stop=(kc == KC - 1)),
                             reads=[wkey, "aTh"], writes=[("ps", 6)])
                    S.op("act", lambda e, ph=ph, blk=blk: e.activation(out=ext[:, 0:3], in_=ph[:, 64 + blk * 4:64 + blk * 4 + 3], func=AF.Copy, scale=g["hfl"][:, 0:1]),
                         reads=[("ps", 6), "hfl"], writes=["ext"])
                else:
                    S.op("act", lambda e, blk=blk: e.activation(out=ext[:, 0:3], in_=hist[:, blk, :], func=AF.Copy),
                         reads=["hist"], writes=["ext"])
                S.op("act", lambda e, pq=pq: e.activation(out=ext[:, 3:TT + 3], in_=pq, func=AF.Copy), reads=[("ps", bk)], writes=["ext"])
                S.op("act", lambda e, blk=blk: e.activation(out=hist[:, blk, :], in_=ext[:, TT:TT + 3], func=AF.Copy), reads=["ext"], writes=["hist"])
                cw = V_CONVW + blk * 4
                S.op("dve", lambda e, cw=cw, blk=blk: e.tensor_scalar(out=t1[:], in0=ext[:, 0:TT], scalar1=vf[:, cw:cw + 1], scalar2=vf[:, V_CONVB + blk:V_CONVB + blk + 1],
                                                                   op0=ALU.mult, op1=ALU.add), reads=["ext", "vf"], writes=["t1"])
                for j in range(1, 4):
                    S.op("dve", lambda e, cw=cw, j=j: e.scalar_tensor_tensor(out=t1[:], in0=ext[:, j:TT + j], scalar=vf[:, cw + j:cw + j + 1], in1=t1[:],
                                                                           op0=ALU.mult, op1=ALU.add), reads=["ext", "vf"], writes=["t1"])
                S.op("act", lambda e, blk=blk: e.activation(out=qkT[:, blk, :], in_=t1[:], func=AF.Silu), reads=["t1"], writes=[("qkT", blk)])
        S.op("sp", lambda e, tok0=tok0: e.dma_start(out=g["QM"][:, tok0:tok0 + TT].rearrange("(h p) t -> p h t", p=128), in_=qkT[:, 0:4, :]),
             reads=[("qkT", b) for b in range(4)], writes=[("QM", t)], dma=True)
        for half in range(2):
            wt, wkey = ws.get(g["w1tm"][half], WSLOT)
            wv3 = wt.rearrange("p (k c) -> p k c", k=KC)
            for m in range(4):
                bk = pp_next()
                pv = _bank(ps, bk)
                for kc in range(KC):
                    S.op("pe", lambda e, pv=pv, wv3=wv3, m=m, kc=kc: e.matmul(pv, aT[:, kc, m * 128:(m + 1) * 128], wv3[:, kc, :], start=(kc == 0), stop=(kc == KC - 1)),
                         reads=[wkey, ("aT", kc)], writes=[("ps", bk)])
                for hh in range(2):
                    h = half * 2 + hh
                    S.op("act", lambda e, pv=pv, m=m, h=h, hh=hh: e.activation(out=vaug[:, m, h, 0:256], in_=pv[:, hh * 256:(hh + 1) * 256], func=AF.Copy,
                                                                            scale=wv_[:, m, h:h + 1]),
                         reads=[("ps", bk), "wv"], writes=[("vaug", m)])
        for m in range(4):
            S.op("dve", lambda e, m=m: e.tensor_copy(out=vaug[:, m, :, 256], in_=wv_[:, m, :]), reads=["wv"], writes=[("vaug", m)])
        pa = _bank(ps, 3)
        ktok4, A_sb4, dCs, Cbf4 = g["ktok4"], g["A_sb4"], g["dCs"], g["Cbf4"]

        def ml_abc(h):
            par = h % 2
            p2 = _bank(ps, 2).bitcast(BF16)
            for m in range(4):
                sl = slice(m * 128, (m + 1) * 128)
                S.op("pe", lambda e, m=m, sl=sl: e.transpose(p2[:, m * 128:(m + 1) * 128], qkT[:, 4 + h, sl], identb[:]),
                     reads=[("qkT", 4 + h), "identb"], writes=[("ps", 2)])
            for m in range(4):
                S.op("act", lambda e, m=m: e.activation(out=ktok4[:, par, m, :], in_=p2[:, m * 128:(m + 1) * 128], func=AF.Copy),
                     reads=[("ps", 2)], writes=[("ktok4", par, m)])
            for m in range(4):
                sl = slice(m * 128, (m + 1) * 128)
                S.op("pe", lambda e, m=m, sl=sl: e.matmul(pa[:, m * 128:(m + 1) * 128], qkT[:, 4 + h, sl], qkT[:, h, sl], start=True, stop=True),
                     reads=[("qkT", 4 + h), ("qkT", h)], writes=[("ps", 3)])
            for m in range(4):
                S.op("dve", lambda e, m=m: e.copy_predicated(out=A_sb4[:, par, m, :], mask=tri[:], data=pa[:, m * 128:(m + 1) * 128]),
                     reads=[("ps", 3), "tri"], writes=[("A_sb4", par, m)])
            for m in range(4):
                bk = 6 + m % 2
                pC = _bank(ps, bk)
                S.op("pe", lambda e, pC=pC, m=m: e.matmul(pC[:, 0:257], ktok4[:, par, m, :], vaug[:, m, h, 0:257], start=True, stop=True),
                     reads=[("ktok4", par, m), ("vaug", m)], writes=[("ps", bk)])
                S.op("act", lambda e, pC=pC, m=m: e.activation(out=dCs[:, m, 0:257], in_=pC[:, 0:257], func=AF.Copy),
                     reads=[("ps", bk)], writes=[("dCs", m)])
            for m in range(4):
                S.op("dve", lambda e, m=m: e.tensor_copy(out=Cbf4[:, par, m, :], in_=C32[:, h, :]), reads=[("C32", h)], writes=[("Cbf4", par, m)])
                S.op("dve", lambda e, m=m: e.tensor_scalar(out=C32[:, h, 0:257], in0=C32[:, h, 0:257], scalar1=eL[:, m, h:h + 1], scalar2=None, op0=ALU.mult),
                     reads=["eL"], writes=[("C32", h)])
                S.op("dve", lambda e, m=m: e.scalar_tensor_tensor(out=C32[:, h, 0:257], in0=dCs[:, m, 0:257], scalar=eL[:, m, h:h + 1], in1=C32[:, h, 0:257],
                                                                 op0=ALU.mult, op1=ALU.add), reads=[("dCs", m), "eL"], writes=[("C32", h)])

        def ml_d(h):
            par = h % 2
            for m in range(4):
                sl = slice(m * 128, (m + 1) * 128)
                bo = 4 + m % 2
                pP = _bank(ps, bo)
                S.op("pe", lambda e, pP=pP, m=m: e.matmul(pP[:, 0:257], A_sb4[:, par, m, :], vaug[:, m, h, 0:257], start=True, stop=False),
                     reads=[("A_sb4", par, m), ("vaug", m)], writes=[("ps", bo)])
                S.op("pe", lambda e, pP=pP, m=m, sl=sl: e.matmul(pP[:, 0:257], qkT[:, h, sl], Cbf4[:, par, m, 0:257], start=False, stop=True),
                     reads=[("qkT", h), ("Cbf4", par, m)], writes=[("ps", bo)])
                r = m % 2
                S.op("act", lambda e, pP=pP, m=m, r=r: e.activation(out=g["OMs"][:, r, 0:257], in_=pP[:, 0:257], func=AF.Copy, scale=eo[:, m, h:h + 1]),
                     reads=[("ps", bo), "eo"], writes=[("OMs", r)])
                S.op("sp", lambda e, m=m, r=r, tok0=tok0: e.dma_start(out=g["OM"][tok0 + m * 128:tok0 + (m + 1) * 128, h * MW:h * MW + 257], in_=g["OMs"][:, r, 0:257]),
                     reads=[("OMs", r)], writes=[("OM", t, m, h)], dma=True)
        for h in range(4):
            if SKIP_ML:
                break
            ml_abc(h)
            if PIPE_ML:
                if h > 0:
                    ml_d(h - 1)
            else:
                ml_d(h)
        if PIPE_ML and not SKIP_ML:
            ml_d(3)
        for half in range(2):
            wt, wkey = ws.get(g["w1tm"][2 + half], WSLOT)
            wv3 = wt.rearrange("p (k c) -> p k c", k=KC)
            for m in range(4):
                bk = pp_next()
                pv = _bank(ps, bk)
                for kc in range(KC):
                    S.op("pe", lambda e, pv=pv, wv3=wv3, m=m, kc=kc: e.matmul(pv, aT[:, kc, m * 128:(m + 1) * 128], wv3[:, kc, :], start=(kc == 0), stop=(kc == KC - 1)),
                         reads=[wkey, ("aT", kc)], writes=[("ps", bk)])
                S.op("act", lambda e, pv=pv, m=m, half=half: e.activation(out=vH[:, m, half * 512:(half + 1) * 512], in_=pv, func=AF.Copy),
                     reads=[("ps", bk)], writes=[("vH", m)])
        gstate = {}
        ws.ahead = NWS - 1

        def gate_unit(u):
            gi, m = u // 4, u % 4
            if m == 0:
                wt, wkey = ws.get(g["w1tm"][4 + gi], WSLOT)
                gstate["wv3"], gstate["wkey"] = wt.rearrange("p (k c) -> p k c", k=KC), wkey
            wv3, wkey = gstate["wv3"], gstate["wkey"]
            fn = AF.Sigmoid if gi < 2 else AF.Silu
            bk = pp_next()
            pv = _bank(ps, bk)
            for kc in range(KC):
                S.op("pe", lambda e, kc=kc: e.matmul(pv, aT[:, kc, m * 128:(m + 1) * 128], wv3[:, kc, :], start=(kc == 0), stop=(kc == KC - 1)),
                     reads=[wkey, ("aT", kc)], writes=[("ps", bk)])
            g2 = u % 2
            S.op("act", lambda e: e.activation(out=gts[:, g2, :], in_=pv, func=fn), reads=[("ps", bk)], writes=[("gts", g2)])
            S.op("sp", lambda e, tok0=tok0: e.dma_start(out=g["GT"][tok0 + m * 128:tok0 + (m + 1) * 128, gi * 512:(gi + 1) * 512], in_=gts[:, g2, :]),
                 reads=[("gts", g2)], writes=[("GT", t, gi, m)], dma=True)
        A_h8, Sp8 = g["A_h8"], g["Sp8"]
        K6, K7 = ("ps", 6), ("ps", 7)

        def hg_abc(h, par, first):
            hsc = g["hsc2"][:, par]
            q_in = g["q_in2"][:, par, :]
            k_in = g["k_in2"][:, par, :]
            p2 = _bank(ps, 2).bitcast(BF16)
            for m in range(4):
                S.op("pe", lambda e, m=m: e.transpose(p2[:, m * 128:(m + 1) * 128], k_in[:, m * 128:(m + 1) * 128], identb[:]),
                     reads=[("k_in", par), "identb"], writes=[("ps", 2)])
            for m in range(4):
                S.op("act", lambda e, m=m: e.activation(out=ktok4[:, par, m, :], in_=p2[:, m * 128:(m + 1) * 128], func=AF.Copy),
                     reads=[("ps", 2)], writes=[("ktok4", par, m)])
            for cc in range(8):
                m, hf = cc // 2, cc % 2
                c0, p0 = cc * 64, hf * 64
                pA = pa[p0:p0 + 64, m * 64:(m + 1) * 64]
                S.op("pe", lambda e, pA=pA, c0=c0: e.matmul(pA, k_in[:, c0:c0 + 64], q_in[:, c0:c0 + 64], start=True, stop=True),
                     reads=[("k_in", par), ("q_in", par)], writes=[("ps", 3)])
            for cc in range(8):
                m, hf = cc // 2, cc % 2
                p0 = hf * 64
                pA = pa[p0:p0 + 64, m * 64:(m + 1) * 64]
                S.op("dve", lambda e, pA=pA, p0=p0, m=m: e.copy_predicated(out=A_h8[p0:p0 + 64, par, m, :], mask=tri[p0:p0 + 64, p0:p0 + 64], data=pA),
                     reads=[("ps", 3), "tri"], writes=[("A_h8", par, cc)])
            for cc in range(8):
                m, hf = cc // 2, cc % 2
                p0 = hf * 64
                bk = 6 + hf
                pS = _bank(ps, bk)[:, m * 128:(m + 1) * 128]
                S.op("pe", lambda e, pS=pS, p0=p0, m=m: e.matmul(pS, ktok4[p0:p0 + 64, par, m, :], vH[p0:p0 + 64, m, h * 128:(h + 1) * 128], start=True, stop=True),
                     reads=[("ktok4", par, m), ("vH", m)], writes=[("ps", bk)])
            for cc in range(8):
                bk = 6 + cc % 2
                pS = _bank(ps, bk)[:, (cc // 2) * 128:(cc // 2 + 1) * 128]
                S.op("dve", lambda e, cc=cc: e.tensor_scalar(out=Sp8[:, par, cc, :], in0=S32[:, h, :], scalar1=hsc[:, cc, 2:3], scalar2=None, op0=ALU.mult),
                     reads=[("S32", h), ("hsc", par)], writes=[("Sp8", par, cc)])
                S.op("dve", lambda e, cc=cc: e.tensor_scalar(out=S32[:, h, :], in0=S32[:, h, :], scalar1=hsc[:, cc, 3:4], scalar2=None, op0=ALU.mult),
                     reads=[("hsc", par)], writes=[("S32", h)])
                S.op("dve", lambda e, pS=pS, cc=cc: e.scalar_tensor_tensor(out=S32[:, h, :], in0=pS, scalar=hsc[:, cc, 4:5], in1=S32[:, h, :], op0=ALU.mult, op1=ALU.add),
                     reads=[("ps", bk), ("hsc", par)], writes=[("S32", h)])

        def hg_d(h, par):
            q_in = g["q_in2"][:, par, :]
            for cc in range(8):
                m, hf = cc // 2, cc % 2
                c0, p0 = cc * 64, hf * 64
                bo = 4 + cc % 2
                pO = _bank(ps, bo)[p0:p0 + 64, 0:128]
                S.op("pe", lambda e, pO=pO, p0=p0, m=m: e.matmul(pO, A_h8[p0:p0 + 64, par, m, :], vH[p0:p0 + 64, m, h * 128:(h + 1) * 128], start=True, stop=False),
                     reads=[("A_h8", par, cc), ("vH", m)], writes=[("ps", bo)])
                S.op("pe", lambda e, pO=pO, c0=c0, cc=cc: e.matmul(pO, q_in[:, c0:c0 + 64], Sp8[:, par, cc, :], start=False, stop=True),
                     reads=[("q_in", par), ("Sp8", par, cc)], writes=[("ps", bo)])
                S.op("act", lambda e, pO=pO, p0=p0, m=m: e.activation(out=OHs[p0:p0 + 64, m, h * 128:(h + 1) * 128], in_=pO, func=AF.Copy),
                     reads=[("ps", bo)], writes=["X"])
        for hp in range(4):
            wt, wkey = ws.get(g["w1fm"][2 + hp], WSLOT)
            wv4 = wt.rearrange("p (b k c) -> p b k c", b=4, k=KC)
            for hh in range(2):
                def _prep(hh=hh, wv4=wv4, wkey=wkey):
                    h = hp * 2 + hh
                    par = h % 2
                    hsc = g["hsc2"][:, par]
                    q_in = g["q_in2"][:, par, :]
                    k_in = g["k_in2"][:, par, :]
                    bq = pp_next()
                    pq = _bank(ps, bq)
                    for kc in range(KC):
                        S.op("pe", lambda e, pq=pq, wv4=wv4, hh=hh, kc=kc: e.matmul(pq, wv4[:, hh * 2, kc, :], aT[:, kc, :], start=(kc == 0), stop=(kc == KC - 1)),
                             reads=[wkey, ("aT", kc)], writes=[("ps", bq)])
                    bf = pp_next()
                    pf = _bank(ps, bf)
                    for kc in range(KC):
                        S.op("pe", lambda e, pf=pf, wv4=wv4, hh=hh, kc=kc: e.matmul(pf, wv4[:, hh * 2 + 1, kc, :], aT[:, kc, :], start=(kc == 0), stop=(kc == KC - 1)),
                             reads=[wkey, ("aT", kc)], writes=[("ps", bf)])
                    S.op("act", lambda e, pq=pq: e.activation(out=qs[:], in_=pq, func=AF.Silu), reads=[("ps", bq)], writes=["qs"])
                    S.op("act", lambda e, pf=pf: e.activation(out=sig[:], in_=pf, func=AF.Sigmoid), reads=[("ps", bf)], writes=["sig"])
                    S.op("dve", lambda e, h=h: e.tensor_scalar(out=sig[:], in0=sig[:], scalar1=vf[:, V_OML + h:V_OML + h + 1], scalar2=vf[:, V_LB + h:V_LB + h + 1],
                                                              op0=ALU.mult, op1=ALU.add), reads=["vf"], writes=["sig"])
                    S.op("dve", lambda e: e.tensor_scalar(out=kk[:], in0=sig[:], scalar1=-1.0, scalar2=1.0, op0=ALU.mult, op1=ALU.add), reads=["sig"], writes=["kk"])
                    S.op("act", lambda e: e.activation(out=sig[:], in_=sig[:], func=AF.Ln), writes=["sig"])
                    S.op("dve", lambda e: e.tensor_tensor_scan(out=bcs[:], data0=m0[:], data1=sig[:], initial=0.0, op0=ALU.mult, op1=ALU.add),
                         reads=["m0", "sig"], writes=["bcs"])
                    b3 = bcs[:].rearrange("p (c t) -> p c t", t=64)
                    S.op("dve", lambda e: e.tensor_copy(out=hsc[:, :, 0], in_=b3[:, :, 31]), reads=["bcs"], writes=[("hsc", par)])
                    S.op("dve", lambda e: e.tensor_copy(out=hsc[:, :, 1], in_=b3[:, :, 63]), reads=["bcs"], writes=[("hsc", par)])
                    S.op("dve", lambda e: e.tensor_tensor(out=hsc[:, :, 7], in0=hsc[:, :, 1], in1=hsc[:, :, 0], op=ALU.subtract), writes=[("hsc", par)])
                    S.op("act", lambda e: e.activation(out=hsc[:, :, 2], in_=hsc[:, :, 0], func=AF.Exp), writes=[("hsc", par)])
                    S.op("act", lambda e: e.activation(out=hsc[:, :, 3], in_=hsc[:, :, 1], func=AF.Exp), writes=[("hsc", par)])
                    S.op("act", lambda e: e.activation(out=hsc[:, :, 4], in_=hsc[:, :, 7], func=AF.Exp), writes=[("hsc", par)])
                    S.op("dve", lambda e, h=h: e.tensor_tensor_scan(out=hsc[:, :, 7], data0=ones[:, 0:8], data1=hsc[:, :, 1], initial=hcarry[:, h:h + 1],
                                                                   op0=ALU.mult, op1=ALU.add), reads=["ones", "hcarry"], writes=[("hsc", par)])
                    S.op("dve", lambda e: e.tensor_tensor(out=hsc[:, :, 5], in0=hsc[:, :, 7], in1=hsc[:, :, 1], op=ALU.subtract), writes=[("hsc", par)])
                    S.op("dve", lambda e, h=h: e.tensor_copy(out=hcarry[:, h:h + 1], in_=hsc[:, 7, 7:8]), reads=[("hsc", par)], writes=["hcarry"])
                    S.op("dve", lambda e: e.tensor_tensor(out=hsc[:, :, 6], in0=hsc[:, :, 5], in1=hsc[:, :, 0], op=ALU.add), writes=[("hsc", par)])
                    S.op("act", lambda e: e.activation(out=hsc[:, :, 6], in_=hsc[:, :, 6], func=AF.Exp), writes=[("hsc", par)])
                    S.op("dve", lambda e: e.tensor_tensor(out=b3, in0=b3, in1=hsc[:, :, 0:1].broadcast_to([128, 8, 64]), op=ALU.subtract),
                         reads=[("hsc", par)], writes=["bcs"])
                    S.op("act", lambda e: e.activation(out=ebuf[:], in_=bcs[:], func=AF.Exp), reads=["bcs"], writes=["ebuf"])
                    S.op("dve", lambda e: e.tensor_tensor(out=q_in, in0=qs[:], in1=ebuf[:], op=ALU.mult), reads=["qs", "ebuf"], writes=[("q_in", par)])
                    S.op("pool", lambda e: e.tensor_tensor(out=qs[:], in0=qs[:], in1=ebuf[:], op=ALU.mult), reads=["ebuf"], writes=["qs"])
                    S.op("pool", lambda e: e.tensor_tensor(out=qdec[:].rearrange("p (c t) -> p c t", t=64), in0=qs[:].rearrange("p (c t) -> p c t", t=64),
                                                           in1=hsc[:, :, 6:7].broadcast_to([128, 8, 64]), op=ALU.mult), reads=["qs", ("hsc", par)], writes=["qdec"])
                    S.op("act", lambda e: e.activation(out=ebuf[:], in_=bcs[:], func=AF.Exp, scale=-1.0), reads=["bcs"], writes=["ebuf"])
                    S.op("dve", lambda e: e.tensor_tensor(out=k_in, in0=kk[:], in1=ebuf[:], op=ALU.mult), reads=["kk", "ebuf"], writes=[("k_in", par)])
                    S.op("sp", lambda e, h=h, tok0=tok0: e.dma_start(out=g["QH"][h * 128:(h + 1) * 128, tok0:tok0 + TT], in_=qdec[:]),
                         reads=["qdec"], writes=[("QH", t, h)], dma=True)
                    return h, par
                h, par = _prep()
                if h > 0:
                    hg_d(h - 1, (h - 1) % 2)
                gate_unit(2 * h)
                gate_unit(2 * h + 1)
                hg_abc(h, par, first=(h == 0))
        hg_d(7, 1)
        ws.ahead = NWS
        S.op("sp", lambda e, tok0=tok0: e.dma_start(out=g["OH"][tok0:tok0 + TT, :].rearrange("(m p) c -> p m c", p=128), in_=OHs[:, :, 0:1024]),
             reads=["X"], writes=[("OH", t)], dma=True)
    stsb = X[:].rearrange("p m d -> p (m d)")[:, 0:NST]
    S.op("dve", lambda e: e.memset(stsb[:, 2064:NST], 0.0), writes=["X"])
    S.op("dve", lambda e: e.tensor_copy(out=stsb[:, 0:1024], in_=S32[:].rearrange("p h c -> p (h c)")), reads=[("S32", h) for h in range(8)], writes=["X"])
    S.op("dve", lambda e: e.tensor_copy(out=stsb[:, 1024:1024 + 4 * MW], in_=C32[:].rearrange("p h c -> p (h c)")), reads=[("C32", h) for h in range(4)], writes=["X"])
    S.op("act", lambda e: e.activation(out=stsb[:, 2064:2072], in_=hcarry[:], func=AF.Exp), reads=["hcarry"], writes=["X"])
    S.op("act", lambda e: e.activation(out=stsb[:, 2072:2076], in_=carry[:], func=AF.Exp), reads=["carry"], writes=["X"])
    S.op("sp", lambda e: e.dma_start(out=g["STATE_SRC"].rearrange("k p c -> p k c"), in_=stsb.rearrange("p (k c) -> p k c", c=256)),
         reads=["X"], writes=["STATE_SRC"], dma=True)
    for k in range(NPC):
        S.op("pool", lambda e, k=k: e.collective_compute("AllGather", ALU.bypass, replica_groups=[[0, 1, 2, 3], [4, 5, 6, 7]],
                                                         ins=[g["STATE_SRC"][k].opt()], outs=[g["STATES"][k].opt()]),
             reads=["STATE_SRC"], writes=[("STATES", k)], dma="cc")


def _emit_p2(S, ws, nc, T, g, NT, l, last):
    X, aT, ps, vf, bc, arena = T["X"], T["aT"], T["ps"], T["vf"], T["bc"], g["arena"]
    ident, identb, selt, Sst, Cst, EMt, hst, mst = (g[k] for k in ("ident", "identb", "selt", "Sst", "Cst", "EMt", "hst", "mst"))
    xsrc = g["xsrc"]

    def hk(a, b):
        return [("hT", j) for j in range(a // TT, (b + TT - 1) // TT)]

    def f32v(off, n):
        return arena[:, off:off + 2 * n].bitcast(F32)

    def dma_in(dst, src, keys, eng="sp", reads=()):
        S.op(eng, lambda e: e.dma_start(out=dst, in_=src), reads=list(reads), writes=list(keys), dma=True)
    blobs = X[:].rearrange("p m d -> p (m d)")
    for k in range(3):
        dma_in(blobs[:, k * NST:(k + 1) * NST].rearrange("p (j c) -> p j c", c=256), g["STATES"][:, k * 128:(k + 1) * 128, :].rearrange("j p c -> p j c"),
               ["X"], reads=[("STATES", j) for j in range(NPC)])
    KS = hk(0, 4 * NST)
    Scur = f32v(0, NST)
    Sacc = f32v(2 * NST, NST)
    S.op("dve", lambda e: e.tensor_copy(out=Scur, in_=blobs[:, 0:NST]), reads=["X"], writes=KS)
    S.op("dve", lambda e: e.tensor_scalar(out=Sacc, in0=Scur, scalar1=selt[:, 1:2], scalar2=None, op0=ALU.mult), reads=["selt"], writes=KS)
    for k in (1, 2):
        bk = blobs[:, k * NST:(k + 1) * NST]
        S.op("dve", lambda e, bk=bk: e.tensor_tensor(out=Scur[:, 0:1024].rearrange("p (h c) -> p h c", c=128), in0=Scur[:, 0:1024].rearrange("p (h c) -> p h c", c=128),
                                                     in1=bk[:, 2064:2072].unsqueeze(2).broadcast_to([128, 8, 128]), op=ALU.mult), reads=["X"], writes=KS)
        S.op("dve", lambda e, bk=bk: e.tensor_tensor(out=Scur[:, 1024:1024 + 4 * MW].rearrange("p (h c) -> p h c", c=MW), in0=Scur[:, 1024:1024 + 4 * MW].rearrange("p (h c) -> p h c", c=MW),
                                                     in1=bk[:, 2072:2076].unsqueeze(2).broadcast_to([128, 4, MW]), op=ALU.mult), reads=["X"], writes=KS)
        S.op("dve", lambda e, bk=bk: e.tensor_tensor(out=Scur[:, 0:2064], in0=Scur[:, 0:2064], in1=bk[:, 0:2064], op=ALU.add), reads=["X"], writes=KS)
        S.op("dve", lambda e, k=k: e.scalar_tensor_tensor(out=Sacc[:, 0:2064], in0=Scur[:, 0:2064], scalar=selt[:, k + 1:k + 2], in1=Sacc[:, 0:2064], op0=ALU.mult, op1=ALU.add),
             reads=["selt"], writes=KS)
    S.op("dve", lambda e: e.tensor_copy(out=Sst[:].rearrange("p h c -> p (h c)"), in_=Sacc[:, 0:1024]), reads=KS, writes=["Sst"])
    S.op("dve", lambda e: e.tensor_copy(out=Cst[:].rearrange("p h c -> p (h c)"), in_=Sacc[:, 1024:1024 + 4 * MW]), reads=KS, writes=["Cst"])

    O_OH, O_OM, O_GT, O_QH, O_QM, O_YC, O_W1, O_W2, O_WM, O_WH = 0, 2048, 4608, 6656, 10752, 12800, 14848, 16896, 18944, 19968
    OHm, K_OH = f32v(O_OH, 1024), hk(O_OH, O_OH + 2048)
    OMm, K_OM = f32v(O_OM, 4 * MW), hk(O_OM, O_OM + 8 * MW)
    GTm, K_GT = arena[:, O_GT:O_GT + 2048], hk(O_GT, O_GT + 2048)
    QHt, K_QH = arena[:, O_QH:O_QH + 4096].rearrange("p (h t) -> p h t", t=TT), hk(O_QH, O_QH + 4096)
    QMt, K_QM = arena[:, O_QM:O_QM + 2048].rearrange("p (h t) -> p h t", t=TT), hk(O_QM, O_QM + 2048)
    ycat, K_YC = arena[:, O_YC:O_YC + 2048], hk(O_YC, O_YC + 2048)
    wk1, K_W1 = f32v(O_W1, 1024), hk(O_W1, O_W1 + 2048)
    wk2, K_W2 = f32v(O_W2, 1024), hk(O_W2, O_W2 + 2048)
    wkm, K_WM = f32v(O_WM, MW), hk(O_WM, O_WM + 2 * MW)
    wkh, K_WH = f32v(O_WH, 256), hk(O_WH, O_WH + 512)

    for t in range(NT):
        tok0 = t * TT
        dma_in(QHt, g["QH"][:, tok0:tok0 + TT].rearrange("(h p) t -> p h t", p=128), K_QH)
        dma_in(QMt, g["QM"][:, tok0:tok0 + TT].rearrange("(h p) t -> p h t", p=128), K_QM)
        dma_in(EMt[:], g["EM"][tok0:tok0 + TT, :].rearrange("(m p) c -> p m c", p=128), ["EMt"])
        for i in range(3):
            dma_in(bc[:, i, :], g["ROWS"][i:i + 1, :].partition_broadcast(128), [("bc", i)], reads=["ROWS"])
        for m in range(4):
            r0 = tok0 + m * 128
            msl = slice(m * 128, (m + 1) * 128)
            dma_in(OHm, g["OH"][r0:r0 + 128, :], K_OH)
            dma_in(OMm, g["OM"][r0:r0 + 128, :], K_OM)
            dma_in(GTm, g["GT"][r0:r0 + 128, :], K_GT)
            for h in range(8):
                bk = 4 + h // 4
                pcor = _bank(ps, bk)[:, (h % 4) * 128:(h % 4 + 1) * 128]
                S.op("pe", lambda e, pcor=pcor, h=h, msl=msl: e.matmul(pcor, QHt[:, h, msl], Sst[:, h, :], start=True, stop=True),
                     reads=K_QH + ["Sst"], writes=[("ps", bk)])
            for hb in range(2):
                S.op("dve", lambda e, hb=hb: e.tensor_tensor(out=wk1[:, hb * 512:(hb + 1) * 512], in0=_bank(ps, 4 + hb), in1=OHm[:, hb * 512:(hb + 1) * 512], op=ALU.add),
                     reads=[("ps", 4 + hb)] + K_OH, writes=K_W1)
            S.op("pool", lambda e: e.tensor_tensor(out=wk2, in0=wk1, in1=wk1, op=ALU.mult), reads=K_W1, writes=K_W2)
            S.op("dve", lambda e: e.tensor_reduce(out=hst[:, :, 0], in_=wk2.rearrange("p (h c) -> p h c", c=128), axis=AX.X, op=ALU.add), reads=K_W2, writes=["hst"])
            S.op("act", lambda e: e.activation(out=hst[:, :, 1], in_=hst[:, :, 0], func=AF.Sqrt, scale=1.0 / 128.0, bias=g["epsb"][:, 0:1]), reads=["hst", "epsb"], writes=["hst"])
            S.op("dve", lambda e: e.reciprocal(out=hst[:, :, 2], in_=hst[:, :, 1]), writes=["hst"])
            S.op("dve", lambda e: e.tensor_tensor(out=wk1.rearrange("p (h c) -> p h c", c=128), in0=wk1.rearrange("p (h c) -> p h c", c=128),
                                                  in1=hst[:, :, 2:3].broadcast_to([128, 8, 128]), op=ALU.mult), reads=["hst"], writes=K_W1)
            S.op("pool", lambda e: e.tensor_tensor(out=ycat[:, 1024:2048], in0=wk1, in1=GTm[:, 1024:2048], op=ALU.mult), reads=K_W1 + K_GT, writes=K_YC)
            for h in range(4):
                bk = 6 + h % 2
                pP = _bank(ps, bk)
                S.op("pe", lambda e, pP=pP, h=h, msl=msl: e.matmul(pP[:, 0:257], QMt[:, h, msl], Cst[:, h, 0:257], start=True, stop=True),
                     reads=K_QM + ["Cst"], writes=[("ps", bk)])
                S.op("dve", lambda e, pP=pP, h=h, m=m: e.scalar_tensor_tensor(out=wkm[:, 0:257], in0=pP[:, 0:257], scalar=EMt[:, m, h:h + 1], in1=OMm[:, h * MW:h * MW + 257],
                                                                             op0=ALU.mult, op1=ALU.add), reads=[("ps", bk), "EMt"] + K_OM, writes=K_WM)
                ms = mst[:, h, :]
                S.op("dve", lambda e, ms=ms: e.tensor_scalar(out=ms[:, 0:1], in0=wkm[:, 256:257], scalar1=-1.0, scalar2=None, op0=ALU.mult), reads=K_WM, writes=[("mst", h)])
                S.op("dve", lambda e, ms=ms: e.tensor_tensor(out=ms[:, 0:1], in0=ms[:, 0:1], in1=wkm[:, 256:257], op=ALU.max), reads=K_WM, writes=[("mst", h)])
                S.op("dve", lambda e, ms=ms: e.tensor_scalar(out=ms[:, 0:1], in0=ms[:, 0:1], scalar1=1.0, scalar2=None, op0=ALU.max), writes=[("mst", h)])
                S.op("dve", lambda e, ms=ms: e.reciprocal(out=ms[:, 1:2], in_=ms[:, 0:1]), writes=[("mst", h)])
                S.op("act", lambda e, ms=ms: e.activation(out=wkh, in_=wkm[:, 0:256], func=AF.Copy, scale=ms[:, 1:2]), reads=K_WM + [("mst", h)], writes=K_WH)
                S.op("dve", lambda e, ms=ms: e.bn_stats(out=ms[:, 2:8], in_=wkh), reads=K_WH, writes=[("mst", h)])
                S.op("dve", lambda e, ms=ms: e.bn_aggr(out=ms[:, 8:10], in_=ms[:, 2:8]), writes=[("mst", h)])
                S.op("act", lambda e, ms=ms: e.activation(out=ms[:, 10:11], in_=ms[:, 9:10], func=AF.Sqrt, bias=g["epsb"][:, 0:1]), reads=["epsb"], writes=[("mst", h)])
                S.op("dve", lambda e, ms=ms: e.reciprocal(out=ms[:, 11:12], in_=ms[:, 10:11]), writes=[("mst", h)])
                S.op("dve", lambda e, ms=ms: e.scalar_tensor_tensor(out=ms[:, 12:13], in0=ms[:, 8:9], scalar=-1.0, in1=ms[:, 11:12], op0=ALU.mult, op1=ALU.mult), writes=[("mst", h)])
                S.op("act", lambda e, ms=ms: e.activation(out=wkh, in_=wkh, func=AF.Identity, scale=ms[:, 11:12], bias=ms[:, 12:13]), reads=[("mst", h)], writes=K_WH)
                S.op("dve", lambda e, h=h: e.tensor_tensor(out=ycat[:, h * 256:(h + 1) * 256], in0=wkh, in1=GTm[:, h * 256:(h + 1) * 256], op=ALU.mult),
                     reads=K_WH + K_GT, writes=K_YC)
            for kc in range(KC):
                t2 = kc % 2
                pt = _bank(ps, 4 + t2).bitcast(BF16)[:, 0:128]
                S.op("pe", lambda e, pt=pt, kc=kc: e.transpose(pt, ycat[:, kc * 128:(kc + 1) * 128], identb[:]), reads=K_YC + ["identb"], writes=[("ps", 4 + t2)])
                S.op("act", lambda e, pt=pt, kc=kc, msl=msl: e.activation(out=aT[:, kc, msl], in_=pt, func=AF.Copy, scale=vf[:, V_NRMW + kc:V_NRMW + kc + 1]),
                     reads=[("ps", 4 + t2), "vf"], writes=[("aT", kc)])
        S.op("sp", lambda e, tok0=tok0, xsrc=xsrc: e.dma_start(out=X[:], in_=xsrc[tok0:tok0 + TT, :].rearrange("(m p) d -> p m d", p=128)),
             reads=["XN"], writes=["X"], dma=True)
        emit_resid_prep(S, T, 0, 1)
        for n in range(4):
            wt, wkey = ws.get(g["wout"][n], WSLOT)
            wv3 = wt.rearrange("p (k c) -> p k c", k=KC)
            for m in range(4):
                pm = _bank(ps, m)
                for kc in range(KC):
                    S.op("pe", lambda e, pm=pm, wv3=wv3, m=m, kc=kc: e.matmul(pm, aT[:, kc, m * 128:(m + 1) * 128], wv3[:, kc, :], start=(kc == 0), stop=(kc == KC - 1)),
                         reads=[wkey, ("aT", kc)], writes=[("ps", m)])
            for m in range(4):
                emit_epilogue_block(S, T, m, n, 2)
        emit_ln_inplace(S, T)
        for i in range(3):
            dma_in(bc[:, i, :], g["ROWS"][3 + i:4 + i, :].partition_broadcast(128), [("bc", i)], reads=["ROWS"])
        emit_transpose_affine(S, T, "X", V_GU2, V_BU2)
        hook = None
        if not last:
            mt = list(range(t * 24 // NT, (t + 1) * 24 // NT))
            npos = FC // 2
            at = {(i + 1) * npos // (len(mt) + 1): mt[i] for i in range(len(mt))} if len(mt) <= npos - 1 else None

            def hook(j2, at=at, mt=mt):
                if at is None:
                    if j2 == 1:
                        emit_mod_tiles(S, ws, T, g, g["wmod_next"], mt, 3)
                elif j2 in at:
                    emit_mod_tiles(S, ws, T, g, g["wmod_next"], [at[j2]], 3)
        emit_ffn2(S, ws, T, g["wgu"], g["wd"], hook)
        if not last:
            S.op("sp", lambda e, tok0=tok0: e.dma_start(out=g["XN"][tok0:tok0 + TT, :].rearrange("(m p) d -> p m d", p=128), in_=X[:]),
                 reads=["X"], writes=["XN"], dma=True)
            if t == NT - 1:
                S.op("sp", lambda e: e.dma_start(out=g["HALO_SRC"][0:24, :].rearrange("(r c) w -> r (c w)", r=3), in_=X[125:128, 3, :]), reads=["X"], writes=["HALO_SRC"], dma=True)
                S.op("pool", lambda e: e.collective_compute("AllGather", ALU.bypass, replica_groups=[[0, 1, 2, 3], [4, 5, 6, 7]],
                                                            ins=[g["HALO_SRC"].opt()], outs=[g["HALO_DST"].opt()]),
                     reads=["HALO_SRC"], writes=["HALO_DST"], dma="cc")
        else:
            for i in range(2):
                dma_in(bc[:, i, :], g["fin"][i:i + 1, :].partition_broadcast(128), [("bc", i)])
            emit_resid_prep(S, T, 0, 1)
            S.op("sp", lambda e, tok0=tok0: e.dma_start(out=g["OUT"][tok0:tok0 + TT, :].rearrange("(m p) d -> p m d", p=128), in_=X[:]),
                 reads=["X"], writes=[("OUT", t)], dma=True)


def emit_ffn2(S, ws, T, wgu, wd, hook=None):
    aT, hT, ps, sg = T["aT"], T["hT"], T["ps"], T["sg"]
    for j2 in range(FC // 2):
        if hook is not None:
            hook(j2)
        wt, wkey = ws.get(wgu[j2], 2 * 2 * KC * 128)
        wv = wt.rearrange("p (b g k c) -> p b g k c", b=2, g=2, k=KC)
        for b in range(2):
            j = j2 * 2 + b
            bg = 4 + (j % 2) * 2
            bu = bg + 1
            pg, pu = _bank(ps, bg), _bank(ps, bu)
            for gi, pp, bk in ((0, pg, bg), (1, pu, bu)):
                for kc in range(KC):
                    S.op("pe", lambda e, pp=pp, wv=wv, b=b, gi=gi, kc=kc: e.matmul(
                        pp, wv[:, b, gi, kc, :], aT[:, kc, :], start=(kc == 0), stop=(kc == KC - 1)),
                        reads=[wkey, ("aT", kc)], writes=[("ps", bk)])
            sgt = sg[:, (j % 2) * 512:(j % 2 + 1) * 512]
            S.op("act", lambda e, sgt=sgt, pg=pg: e.activation(out=sgt, in_=pg, func=AF.Silu),
                 reads=[("ps", bg)], writes=[("sg", j % 2)])
            S.op("dve", lambda e, sgt=sgt, pu=pu, j=j: e.tensor_tensor(out=hT[:, j, :], in0=pu, in1=sgt, op=ALU.mult),
                 reads=[("ps", bu), ("sg", j % 2)], writes=[("hT", j)])
    emit_resid_prep(S, T, 0, 1)
    for n in range(4):
        for jg in range(4):
            wt, wkey = ws.get(wd[n, jg], 11 * 512)
            wv = wt.rearrange("p (j c) -> p j c", j=11)
            for m in range(4):
                pm = _bank(ps, m)
                for jj in range(11):
                    j = jg * 11 + jj
                    S.op("pe", lambda e, pm=pm, wv=wv, jj=jj, j=j, m=m: e.matmul(
                        pm, hT[:, j, m * 128:(m + 1) * 128], wv[:, jj, :], start=(j == 0), stop=(j == FC - 1)),
                        reads=[wkey, ("hT", j)], writes=[("ps", m)])
        for m in range(4):
            emit_epilogue_block(S, T, m, n, 2)
    emit_ln_inplace(S, T)


def _fm_block(W, col, width=128):
    return W[:, col:col + width].reshape(KC, 128, width).transpose(1, 0, 2)


def _fm16(v):
    return np.ascontiguousarray(v.reshape(-1, 128).T)


def _layer_arrays(inp, l, depth):
    W = inp["w_in"][l]
    fm_tiles = []
    fm_tiles.append(np.stack([_fm_block(W, h * 128) for h in range(4)], axis=1))
    fm_tiles.append(np.stack([_fm_block(W, 512 + h * 128) for h in range(4)], axis=1))
    for hp in range(4):
        blks = []
        for hh in range(2):
            h = hp * 2 + hh
            blks += [_fm_block(W, 3080 + h * 128), _fm_block(W, 4104 + h * 128)]
        fm_tiles.append(np.stack(blks, axis=1))
    w1fm = np.ascontiguousarray(np.stack(fm_tiles)).reshape(6, 128, WSLOT)
    tm_cols = [1024, 1536, 5128, 5640, 2048, 2560, 6152, 6664]
    w1tm = np.ascontiguousarray(np.stack([_fm_block(W, c, 512) for c in tm_cols])).reshape(8, 128, WSLOT)
    w1ifg = np.ascontiguousarray(_fm_block(W, 3072, 8)).reshape(128, KC * 8)
    Wm = inp["w_mod"][l]
    wmod = np.ascontiguousarray(np.stack([np.stack([_fm_block(Wm, t * 512 + b * 128) for b in range(4)], axis=1)
                                          for t in range(24)])).reshape(24, 128, WSLOT)
    Wo = inp["w_out"][l]
    wout = np.ascontiguousarray(np.stack([_fm_block(Wo, n * 512, 512) for n in range(4)])).reshape(4, 128, WSLOT)
    g4 = inp["w_gate"][l].reshape(KC, 128, FC // 2, 2, 128)
    u4 = inp["w_up"][l].reshape(KC, 128, FC // 2, 2, 128)
    gu = np.stack([g4, u4], axis=0)
    wgu = np.ascontiguousarray(gu.transpose(3, 2, 4, 0, 1, 5)).reshape(FC // 2, 128, WSLOT)
    wd = np.ascontiguousarray(inp["w_down"][l].reshape(4, 11, 128, 4, 512).transpose(3, 0, 2, 1, 4)).reshape(4, 4, 128, 11 * 512)
    vecs = np.zeros((128, V_IN), np.float32)
    vecs[:, V_BMOD:V_BMOD + 96] = _fm16(inp["b_mod"][l])
    if l == 0:
        vecs[:, V_LNPG:V_LNPG + 16] = np.ones((128, 16), np.float32)
    else:
        vecs[:, V_LNPG:V_LNPG + 16] = _fm16(inp["ln2_g"][l - 1])
        vecs[:, V_LNPB:V_LNPB + 16] = _fm16(inp["ln2_b"][l - 1])
    vecs[:, V_LN1G:V_LN1G + 16] = _fm16(inp["ln1_g"][l])
    vecs[:, V_LN1B:V_LN1B + 16] = _fm16(inp["ln1_b"][l])
    vecs[:, V_CONVW:V_CONVW + 32] = inp["conv_w"][l].reshape(4, 8, 128).transpose(2, 1, 0).reshape(128, 32)
    vecs[:, V_CONVB:V_CONVB + 8] = _fm16(inp["conv_b"][l])
    vecs[:, V_LBL:V_LBL + 8 * depth] = inp["lb_logits"].reshape(depth, 8, 128).transpose(2, 1, 0).reshape(128, 8 * depth)
    nrmw = _fm16(np.concatenate([inp["mlstm_norm_w"][l], inp["hgrn_norm_w"][l]]))
    vecs[:, V_NRMW:V_NRMW + 16] = nrmw
    grow = np.zeros((1, 8 + DEPTH), np.float32)
    grow[0, 0:4] = inp["b_igate"][l]
    grow[0, 4:8] = inp["b_fgate"][l]
    for i in range(1, l + 1):
        grow[0, 8 + i] = np.float32(1.0)
    return dict(w1fm=w1fm, w1tm=w1tm, w1ifg=w1ifg, wmod=wmod, wout=wout, wgu=wgu, wd=wd, vecs=vecs, grow=grow, nrmw=nrmw)


def build_fused(TOK, depth):
    NT = TOK // TT
    nc = bass.Bass("TRN2", target_bir_lowering=False)

    def din(name, shape, dt=F32):
        return nc.dram_tensor(name, shape, dt, kind="ExternalInput").ap()

    def dint(name, shape, dt=F32):
        return nc.dram_tensor(name, shape, dt, kind="Internal").ap()
    g = {}
    x_in = din("x_in", [TOK, D])
    g["halo0"] = din("halo0", [3, D])
    g["hflag"] = din("hflag", [128, 1])
    g["c_fm"] = din("c_fm", [128, KC])
    g["sel"] = din("sel", [128, 4])
    g["selprev"] = din("selprev", [128, 4])
    g["fin"] = din("fin", [2, D])
    g["identd"] = din("identd", [128, 128])
    g["trid"] = din("trid", [128, 128], U8)
    wmod = din("wmod", [depth * 24, 128, WSLOT])
    w1fm = din("w1fm", [depth * 6, 128, WSLOT])
    w1tm = din("w1tm", [depth * 8, 128, WSLOT])
    w1ifg = din("w1ifg", [depth * 128, KC * 8])
    wout = din("wout", [depth * 4, 128, WSLOT])
    wgu = din("wgu", [depth * (FC // 2), 128, WSLOT])
    wd = din("wd", [depth * 4, 4, 128, 11 * 512])
    vecs = din("vecs", [depth * 128, V_IN])
    grow = din("grow", [depth, 8 + DEPTH])
    g["OUT"] = nc.dram_tensor("OUT", [TOK, D], F32, kind="ExternalOutput").ap()
    g["XN"] = dint("XN", [TOK, D])
    g["OH"] = dint("OH", [TOK, 1024])
    g["OM"] = dint("OM", [TOK, 4 * MW])
    g["QH"] = dint("QH", [1024, TOK], BF16)
    g["QM"] = dint("QM", [512, TOK], BF16)
    g["EM"] = dint("EM", [TOK, 4])
    g["GT"] = dint("GT", [TOK, 2048], BF16)
    g["ROWS"] = dint("ROWS", [6, D])
    g["STATE_SRC"] = dint("STATE_SRC", [NPC, 128, 256])
    g["STATES"] = dint("STATES", [NPC, 4 * 128, 256])
    g["HALO_SRC"] = dint("HALO_SRC", [128, 256])
    g["HALO_DST"] = dint("HALO_DST", [4 * 128, 256])

    with contextlib.ExitStack() as es:
        def sb(name, shape, dt=F32):
            return es.enter_context(nc.sbuf_tensor(name, shape, dt))
        T = {}
        X = T["X"] = sb("X", [128, 4, D])
        T["aT"] = sb("aT", [128, KC, TT], BF16)
        wbuf = sb("wbuf", [128, NWS * WSLOT], BF16)
        T["ident"] = g["ident"] = sb("ident", [128, 128])
        g["identb"] = sb("identb", [128, 128], BF16)
        g["tri"] = sb("tri", [128, 128], U8)
        g["utri"] = sb("utri", [128, 128])
        g["ones"] = sb("ones", [128, 128])
        g["m0"] = sb("m0", [128, TT])
        T["vf"] = sb("vf", [128, V_TOT])
        g["cact"] = sb("cact", [128, KC])
        g["cactb"] = sb("cactb", [128, KC], BF16)
        g["hfl"] = sb("hfl", [128, 1])
        g["selt"] = sb("selt", [128, 4])
        g["selp"] = sb("selp", [128, 4])
        g["lnsc"] = sb("lnsc", [128, 1])
        T["epsb"] = g["epsb"] = sb("epsb", [128, 1])
        g["A_sb4"] = sb("A_sb4", [128, 2, 4, 128], BF16)
        g["A_h8"] = sb("A_h8", [128, 2, 4, 64], BF16)
        T["st"] = sb("st", [128, 4, 4, 6])
        T["mv"] = sb("mv", [128, 4, 8])
        fsc = sb("fsc", [128, 1])
        g["modacc"] = sb("modacc", [128, 96])
        PAR = 41216
        par = sb("par", [128, PAR], BF16)
        T["ps"] = es.enter_context(nc.psum_tensor("ps", [128, 8 * 512], F32))
        g["hx"] = X[0:3, 0, :]
        g["hx2"] = X[0:3, 1, :]
        off = [0]

        def cv(n_bf16, dt, shape_str=None, **kw):
            a = par[:, off[0]:off[0] + n_bf16]
            off[0] += (n_bf16 + 15) // 16 * 16
            if dt == F32:
                a = a.bitcast(F32)
            if shape_str:
                a = a.rearrange(shape_str, **kw)
            return a
        g["aTh"] = cv(48, BF16, "p (k t) -> p k t", t=3)
        g["hist"] = cv(48, F32, "p (k t) -> p k t", t=3)
        g["ext"] = cv(2 * (TT + 4), F32)
        g["t1"] = cv(2 * TT, F32)
        g["qkT"] = cv(8 * TT, BF16, "p (k t) -> p k t", t=TT)
        g["wifg"] = cv(KC * 8, BF16)
        g["gsb"] = cv(64, F32, "p (m c) -> p m c", c=8)
        for nm in ("l1", "nb", "nbL", "wv_", "eo", "ec", "eL", "ctmp"):
            g[nm] = cv(32, F32, "p (m c) -> p m c", c=4)
        g["carry"] = cv(8, F32)
        g["gbias"] = cv(2 * (8 + DEPTH), F32)
        g["vaug"] = cv(16 * MW, BF16, "p (m h c) -> p m h c", m=4, h=4)
        g["ktok4"] = cv(1024, BF16, "p (a m c) -> p a m c", a=2, m=4)
        g["C32"] = cv(8 * MW, F32, "p (h c) -> p h c", h=4)
        g["Cbf4"] = cv(8 * MW, BF16, "p (a m c) -> p a m c", a=2, m=4)
        g["dCs"] = cv(8 * MW, F32, "p (m c) -> p m c", m=4)
        g["vH"] = cv(4096, BF16, "p (m c) -> p m c", m=4)
        for nm in ("sig", "kk", "bcs", "qs", "ebuf"):
            g[nm] = cv(2 * TT, F32)
        g["qdec"] = cv(TT, BF16)
        g["q_in2"] = cv(2 * TT, BF16, "p (a t) -> p a t", a=2)
        g["k_in2"] = cv(2 * TT, BF16, "p (a t) -> p a t", a=2)
        g["hsc2"] = cv(256, F32, "p (a c s) -> p a c s", a=2, s=8)
        g["hcarry"] = cv(16, F32)
        g["S32"] = cv(2048, F32, "p (h c) -> p h c", h=8)
        g["Sp8"] = cv(2048, BF16, "p (a c d) -> p a c d", a=2, c=8)
        g["gts"] = cv(2 * TT, BF16, "p (a c) -> p a c", a=2)
        g["rowsb"] = cv(2 * 768, F32)
        g["OMs"] = cv(2 * 2 * MW, F32, "p (m c) -> p m c", m=2)
        print("P1 arena", off[0])
        assert off[0] <= PAR, off[0]
        off[0] = 0
        g["arena"] = cv(FC * TT, BF16)
        T["hT"] = g["arena"].rearrange("p (j t) -> p j t", t=TT)
        T["bc"] = cv(2 * 3 * D, F32, "p (i d) -> p i d", i=3)
        T["sg"] = cv(2 * 1024, F32)
        T["tmp"] = cv(2 * 1024, F32)
        g["Sst"] = cv(1024, BF16, "p (h c) -> p h c", h=8)
        g["Cst"] = cv(4 * MW, BF16, "p (h c) -> p h c", h=4)
        g["EMt"] = cv(32, F32, "p (m c) -> p m c", c=4)
        g["hst"] = cv(64, F32, "p (h c) -> p h c", c=4)
        g["mst"] = cv(128, F32, "p (h c) -> p h c", c=16)
        print("P2 arena", off[0])
        assert off[0] <= PAR, off[0]
        rec = None
        for pas in range(2):
            S = Sched(nc)
            ws = WStream(S, wbuf, record=rec)
            _emit_prologue(S, T, g)
            for l in range(depth):
                g["xsrc"] = x_in if l == 0 else g["XN"]
                g["wmod"] = wmod[l * 24:(l + 1) * 24]
                g["wmod_next"] = wmod[(l + 1) * 24:(l + 2) * 24] if l + 1 < depth else None
                g["w1fm"] = w1fm[l * 6:(l + 1) * 6]
                g["w1tm"] = w1tm[l * 8:(l + 1) * 8]
                g["w1ifg"] = w1ifg[l * 128:(l + 1) * 128, :]
                g["wout"] = wout[l * 4:(l + 1) * 4]
                g["wgu"] = wgu[l * (FC // 2):(l + 1) * (FC // 2)]
                g["wd"] = wd[l * 4:(l + 1) * 4]
                g["vecs"] = vecs[l * 128:(l + 1) * 128, :]
                g["grow"] = grow[l:l + 1, :]
                _emit_p1(S, ws, nc, T, g, NT, l)
                S.fence(fsc[:])
                _emit_p2(S, ws, nc, T, g, NT, l, l == depth - 1)
                S.fence(fsc[:])
            S.barrier_all("sp")
            rec = ws.seq
        S.run()
    return nc


def _fm_block(W, col, width=128):
    return W[:, col:col + width].reshape(KC, 128, width).transpose(1, 0, 2)


def _fm16(v):
    return np.ascontiguousarray(v.reshape(-1, 128).T)


def _layer_arrays(inp, l, depth):
    W = inp["w_in"][l]
    fm_tiles = []
    fm_tiles.append(np.stack([_fm_block(W, h * 128) for h in range(4)], axis=1))
    fm_tiles.append(np.stack([_fm_block(W, 512 + h * 128) for h in range(4)], axis=1))
    for hp in range(4):
        blks = []
        for hh in range(2):
            h = hp * 2 + hh
            blks += [_fm_block(W, 3080 + h * 128), _fm_block(W, 4104 + h * 128)]
        fm_tiles.append(np.stack(blks, axis=1))
    w1fm = np.ascontiguousarray(np.stack(fm_tiles)).reshape(6, 128, WSLOT)
    tm_cols = [1024, 1536, 5128, 5640, 2048, 2560, 6152, 6664]
    w1tm = np.ascontiguousarray(np.stack([_fm_block(W, c, 512) for c in tm_cols])).reshape(8, 128, WSLOT)
    w1ifg = np.ascontiguousarray(_fm_block(W, 3072, 8)).reshape(128, KC * 8)
    Wm = inp["w_mod"][l]
    wmod = np.ascontiguousarray(np.stack([np.stack([_fm_block(Wm, t * 512 + b * 128) for b in range(4)], axis=1)
                                          for t in range(24)])).reshape(24, 128, WSLOT)
    Wo = inp["w_out"][l]
    wout = np.ascontiguousarray(np.stack([_fm_block(Wo, n * 512, 512) for n in range(4)])).reshape(4, 128, WSLOT)
    g4 = inp["w_gate"][l].reshape(KC, 128, FC // 2, 2, 128)
    u4 = inp["w_up"][l].reshape(KC, 128, FC // 2, 2, 128)
    gu = np.stack([g4, u4], axis=0)
    wgu = np.ascontiguousarray(gu.transpose(3, 2, 4, 0, 1, 5)).reshape(FC // 2, 128, WSLOT)
    wd = np.ascontiguousarray(inp["w_down"][l].reshape(4, 11, 128, 4, 512).transpose(3, 0, 2, 1, 4)).reshape(4, 4, 128, 11 * 512)
    vecs = np.zeros((128, V_IN), np.float32)
    vecs[:, V_BMOD:V_BMOD + 96] = _fm16(inp["b_mod"][l])
    if l == 0:
        vecs[:, V_LNPG:V_LNPG + 16] = np.ones((128, 16), np.float32)
    else:
        vecs[:, V_LNPG:V_LNPG + 16] = _fm16(inp["ln2_g"][l - 1])
        vecs[:, V_LNPB:V_LNPB + 16] = _fm16(inp["ln2_b"][l - 1])
    vecs[:, V_LN1G:V_LN1G + 16] = _fm16(inp["ln1_g"][l])
    vecs[:, V_LN1B:V_LN1B + 16] = _fm16(inp["ln1_b"][l])
    vecs[:, V_CONVW:V_CONVW + 32] = inp["conv_w"][l].reshape(4, 8, 128).transpose(2, 1, 0).reshape(128, 32)
    vecs[:, V_CONVB:V_CONVB + 8] = _fm16(inp["conv_b"][l])
    vecs[:, V_LBL:V_LBL + 8 * depth] = inp["lb_logits"].reshape(depth, 8, 128).transpose(2, 1, 0).reshape(128, 8 * depth)
    vecs[:, V_NRMW:V_NRMW + 16] = _fm16(np.concatenate([inp["mlstm_norm_w"][l], inp["hgrn_norm_w"][l]]))
    grow = np.zeros((1, 8 + DEPTH), np.float32)
    grow[0, 0:4] = inp["b_igate"][l]
    grow[0, 4:8] = inp["b_fgate"][l]
    for i in range(1, l + 1):
        grow[0, 8 + i] = np.float32(1.0)
    return dict(w1fm=w1fm, w1tm=w1tm, w1ifg=w1ifg, wmod=wmod, wout=wout, wgu=wgu, wd=wd, vecs=vecs, grow=grow)


_PROGS = {}


def kernel(**inputs):
    inp = {k: np.asarray(v) for k, v in inputs.items()}
    x = inp["x"]
    B, T_, _ = x.shape
    depth = inp["w_in"].shape[0]
    SEG = NCORE // B
    TOK = T_ // SEG
    key = (TOK, depth, DEPTH)
    if key not in _PROGS:
        _PROGS[key] = build_fused(TOK, depth)
    nc = _PROGS[key]
    LA = [_layer_arrays(inp, l, depth) for l in range(depth)]
    shared = {k: np.ascontiguousarray(np.concatenate([LA[l][k] for l in range(depth)], axis=0))
              for k in ("wmod", "w1fm", "w1tm", "w1ifg", "wout", "wgu", "wd", "vecs", "grow")}
    del LA
    shared["identd"] = np.eye(128, dtype=np.float32)
    shared["trid"] = np.triu(np.ones((128, 128), np.uint8))
    shared["fin"] = np.ascontiguousarray(np.stack([inp["ln2_g"][depth - 1], inp["ln2_b"][depth - 1]]).astype(np.float32))
    in_maps = []
    for i in range(NCORE):
        b, sgi = i // SEG, i % SEG
        sel = np.zeros((128, 4), np.float32)
        sel[:, sgi] = 1.0
        selp = np.zeros((128, 4), np.float32)
        if sgi > 0:
            selp[:, sgi - 1] = 1.0
        halo = np.zeros((3, D), np.float32) if sgi == 0 else np.ascontiguousarray(x[b, sgi * TOK - 3:sgi * TOK, :])
        m = dict(shared)
        m.update(x_in=np.ascontiguousarray(x[b, sgi * TOK:(sgi + 1) * TOK, :]), halo0=halo,
                 hflag=np.full((128, 1), 0.0 if sgi == 0 else 1.0, np.float32), c_fm=_fm16(inp["c"][b]), sel=sel, selprev=selp)
        in_maps.append(m)
    res = run_bass_kernel_spmd(nc, in_maps, core_ids=list(range(NCORE))).results
    y = np.empty((B, T_, D), np.float32)
    for i in range(NCORE):
        y[i // SEG, (i % SEG) * TOK:(i % SEG + 1) * TOK, :] = res[i]["OUT"]
    return y


_DEBUG = None
```
